# Optimizing a Trainium2 kernel written in Bass

```python
import jax, jax.numpy as jnp
from jax import lax
import numpy as np

D_MODEL = 2048
BATCH = 2
SEQ = 4096
DEPTH = 1

MLA_HEADS = 8
MLA_Q_RANK = 512
MLA_KV_RANK = 512
MLA_NOPE = 128
MLA_ROPE = 64
MLA_V = 128
ROPE_THETA = 10000.0
FOX_HEADS = 8
FOX_HEAD_DIM = 128
MIX_WIDTH = MLA_HEADS * MLA_V + FOX_HEADS * FOX_HEAD_DIM
Q_BLOCK = 128
OFF_CQ = 0
OFF_CKV = OFF_CQ + MLA_Q_RANK
OFF_KR = OFF_CKV + MLA_KV_RANK
OFF_FQ = OFF_KR + MLA_ROPE
OFF_FK = OFF_FQ + FOX_HEADS * FOX_HEAD_DIM
OFF_FV = OFF_FK + FOX_HEADS * FOX_HEAD_DIM
OFF_FF = OFF_FV + FOX_HEADS * FOX_HEAD_DIM
IN_WIDTH = OFF_FF + FOX_HEADS
PEER_HEADS = 8
PEER_NKEYS = 128
PEER_EXPERTS = PEER_NKEYS * PEER_NKEYS
PEER_KEY_DIM = 128
PEER_TOPK = 16
PEER_TOKEN_BLOCK = 128
PLE_DIM = 256
ALPHA = (2 * DEPTH) ** 0.25
BETA = (8 * DEPTH) ** -0.25
NORM_EPS = 1e-6
NEG_INF = -1e30

kernel_name = "hybrid_mla_fox_peer_deepnorm"


def layer_norm(x, g, b):
    xf = x.astype(jnp.float32)
    mu = jnp.mean(xf, axis=-1, keepdims=True)
    var = jnp.mean(jnp.square(xf - mu), axis=-1, keepdims=True)
    y = (xf - mu) * lax.rsqrt(var + NORM_EPS)
    return (y * g.astype(jnp.float32) + b.astype(jnp.float32)).astype(x.dtype)


def rms_norm(x, g):
    xf = x.astype(jnp.float32)
    y = xf * lax.rsqrt(jnp.mean(jnp.square(xf), axis=-1, keepdims=True) + NORM_EPS)
    return (y * g.astype(jnp.float32)).astype(x.dtype)


def rope(x, cos, sin):
    half = x.shape[-1] // 2
    x1, x2 = x[..., :half], x[..., half:]
    cos = cos.astype(x.dtype)
    sin = sin.astype(x.dtype)
    return jnp.concatenate([x1 * cos - x2 * sin, x1 * sin + x2 * cos], axis=-1)


def to_blocks(t):
    b, s = t.shape[0], t.shape[1]
    return jnp.swapaxes(t.reshape((b, s // Q_BLOCK, Q_BLOCK) + t.shape[2:]), 0, 1)


def from_blocks(t):
    t = jnp.swapaxes(t, 0, 1)
    return t.reshape((t.shape[0], t.shape[1] * t.shape[2]) + t.shape[3:])


def causal_mask(start, seq):
    q_idx = start + jnp.arange(Q_BLOCK)
    k_idx = jnp.arange(seq)
    return k_idx[None, :] <= q_idx[:, None]


def mla_attention(q_nope, q_rope, k_nope, k_rope, v):
    seq = q_nope.shape[1]
    scale = (MLA_NOPE + MLA_ROPE) ** -0.5
    starts = jnp.arange(seq // Q_BLOCK) * Q_BLOCK

    def one_block(args):
        qn, qr, start = args
        s = (jnp.einsum('bqhd,bkhd->bhqk', qn, k_nope, preferred_element_type=jnp.float32)
             + jnp.einsum('bqhr,bkr->bhqk', qr, k_rope, preferred_element_type=jnp.float32)) * scale
        s = jnp.where(causal_mask(start, seq), s, NEG_INF)
        pr = jax.nn.softmax(s, axis=-1).astype(v.dtype)
        return jnp.einsum('bhqk,bkhd->bqhd', pr, v)

    out = lax.map(one_block, (to_blocks(q_nope), to_blocks(q_rope), starts))
    return from_blocks(out)


def fox_attention(q, k, v, cum_log_f):
    seq = q.shape[1]
    scale = FOX_HEAD_DIM ** -0.5
    starts = jnp.arange(seq // Q_BLOCK) * Q_BLOCK
    f_k = jnp.transpose(cum_log_f, (0, 2, 1))

    def one_block(args):
        qb, fq, start = args
        s = jnp.einsum('bqhd,bkhd->bhqk', qb, k, preferred_element_type=jnp.float32) * scale
        s = s + jnp.transpose(fq, (0, 2, 1))[..., :, None] - f_k[:, :, None, :]
        s = jnp.where(causal_mask(start, seq), s, NEG_INF)
        pr = jax.nn.softmax(s, axis=-1).astype(v.dtype)
        return jnp.einsum('bhqk,bkhd->bqhd', pr, v)

    out = lax.map(one_block, (to_blocks(q), to_blocks(cum_log_f), starts))
    return from_blocks(out)


def peer(h, w_q, keys1, keys2, u_tab, v_tab):
    b, s, d = h.shape
    half = PEER_KEY_DIM // 2
    q = (h @ w_q).reshape(b, s, PEER_HEADS, PEER_KEY_DIM)
    s1 = jnp.einsum('bshd,hnd->bshn', q[..., :half], keys1, preferred_element_type=jnp.float32)
    s2 = jnp.einsum('bshd,hnd->bshn', q[..., half:], keys2, preferred_element_type=jnp.float32)
    v1, i1 = lax.top_k(s1, PEER_TOPK)
    v2, i2 = lax.top_k(s2, PEER_TOPK)
    n_cand = PEER_TOPK * PEER_TOPK
    cand = (v1[..., :, None] + v2[..., None, :]).reshape(b, s, PEER_HEADS, n_cand)
    cidx = (i1[..., :, None] * PEER_NKEYS + i2[..., None, :]).reshape(b, s, PEER_HEADS, n_cand)
    sc, pos = lax.top_k(cand, PEER_TOPK)
    eidx = jnp.take_along_axis(cidx, pos, axis=-1)
    gate = jax.nn.softmax(sc, axis=-1).astype(h.dtype)

    t = PEER_TOKEN_BLOCK
    n = (b * s) // t
    hb = h.reshape(n, t, d)
    eb = eidx.reshape(n, t, PEER_HEADS, PEER_TOPK)
    gb = gate.reshape(n, t, PEER_HEADS, PEER_TOPK)

    def one_block(args):
        hx, e, g = args
        u = u_tab[e]
        a = jnp.einsum('thkd,td->thk', u, hx)
        act = jax.nn.gelu(a, approximate=False) * g
        return jnp.einsum('thk,thkd->td', act, v_tab[e])

    return lax.map(one_block, (hb, eb, gb)).reshape(b, s, d)


def setup_inputs(seed: int = 0) -> dict:
    key = jax.random.key(seed)
    ks = jax.random.split(key, 24)
    f32 = jnp.float32

    def nrm(k, shape, scale):
        return jax.random.normal(k, shape, f32) * scale

    L = DEPTH
    return {
        "x": nrm(ks[0], (BATCH, SEQ, D_MODEL), 1.0),
        "p": nrm(ks[1], (DEPTH, BATCH, SEQ, PLE_DIM), 1.0),
        "positions": jnp.broadcast_to(jnp.arange(SEQ, dtype=jnp.int32)[None, :], (BATCH, SEQ)),
        "w_in": nrm(ks[2], (L, D_MODEL, IN_WIDTH), D_MODEL ** -0.5),
        "b_forget": 1.0 + nrm(ks[3], (L, FOX_HEADS), 0.5),
        "g_q_norm": 1.0 + nrm(ks[4], (L, MLA_Q_RANK), 0.02),
        "w_uq": nrm(ks[5], (L, MLA_Q_RANK, MLA_HEADS * (MLA_NOPE + MLA_ROPE)), MLA_Q_RANK ** -0.5),
        "g_kv_norm": 1.0 + nrm(ks[6], (L, MLA_KV_RANK), 0.02),
        "w_ukv": nrm(ks[7], (L, MLA_KV_RANK, MLA_HEADS * (MLA_NOPE + MLA_V)), MLA_KV_RANK ** -0.5),
        "w_out": nrm(ks[8], (L, MIX_WIDTH, D_MODEL), BETA * MIX_WIDTH ** -0.5),
        "ln1_g": 1.0 + nrm(ks[9], (L, D_MODEL), 0.02),
        "ln1_b": nrm(ks[10], (L, D_MODEL), 0.02),
        "peer_wq": nrm(ks[11], (L, D_MODEL, PEER_HEADS * PEER_KEY_DIM), D_MODEL ** -0.5),
        "peer_keys1": nrm(ks[12], (L, PEER_HEADS, PEER_NKEYS, PEER_KEY_DIM // 2), (PEER_KEY_DIM // 2) ** -0.5),
        "peer_keys2": nrm(ks[13], (L, PEER_HEADS, PEER_NKEYS, PEER_KEY_DIM // 2), (PEER_KEY_DIM // 2) ** -0.5),
        "peer_u": nrm(ks[14], (L, PEER_EXPERTS, D_MODEL), D_MODEL ** -0.5),
        "peer_v": nrm(ks[15], (L, PEER_EXPERTS, D_MODEL), BETA * PEER_HEADS ** -0.5),
        "ple_wgate": nrm(ks[16], (L, D_MODEL, D_MODEL), D_MODEL ** -0.5),
        "ple_wproj": nrm(ks[17], (L, PLE_DIM, D_MODEL), BETA * PLE_DIM ** -0.5),
        "ln2_g": 1.0 + nrm(ks[18], (L, D_MODEL), 0.02),
        "ln2_b": nrm(ks[19], (L, D_MODEL), 0.02),
    }


def reference(x, p, positions, w_in, b_forget, g_q_norm, w_uq, g_kv_norm, w_ukv, w_out,
              ln1_g, ln1_b, peer_wq, peer_keys1, peer_keys2, peer_u, peer_v,
              ple_wgate, ple_wproj, ln2_g, ln2_b):
    b, s, _ = x.shape
    inv_freq = 1.0 / (ROPE_THETA ** (jnp.arange(0, MLA_ROPE, 2, dtype=jnp.float32) / MLA_ROPE))
    ang = positions.astype(jnp.float32)[..., None] * inv_freq
    cos, sin = jnp.cos(ang), jnp.sin(ang)

    h = x
    for i in range(DEPTH):
        z = h @ w_in[i]

        cq = rms_norm(z[..., OFF_CQ:OFF_CKV], g_q_norm[i])
        q = (cq @ w_uq[i]).reshape(b, s, MLA_HEADS, MLA_NOPE + MLA_ROPE)
        q_nope = q[..., :MLA_NOPE]
        q_rope = rope(q[..., MLA_NOPE:], cos[:, :, None, :], sin[:, :, None, :])
        ckv = rms_norm(z[..., OFF_CKV:OFF_KR], g_kv_norm[i])
        kv = (ckv @ w_ukv[i]).reshape(b, s, MLA_HEADS, MLA_NOPE + MLA_V)
        k_nope, v_mla = kv[..., :MLA_NOPE], kv[..., MLA_NOPE:]
        k_rope = rope(z[..., OFF_KR:OFF_FQ], cos, sin)
        o_mla = mla_attention(q_nope, q_rope, k_nope, k_rope, v_mla).reshape(b, s, MLA_HEADS * MLA_V)

        fq = z[..., OFF_FQ:OFF_FK].reshape(b, s, FOX_HEADS, FOX_HEAD_DIM)
        fk = z[..., OFF_FK:OFF_FV].reshape(b, s, FOX_HEADS, FOX_HEAD_DIM)
        fv = z[..., OFF_FV:OFF_FF].reshape(b, s, FOX_HEADS, FOX_HEAD_DIM)
        log_f = jax.nn.log_sigmoid(z[..., OFF_FF:IN_WIDTH].astype(jnp.float32) + b_forget[i].astype(jnp.float32))
        cum_log_f = lax.cumsum(log_f, axis=1)
        o_fox = fox_attention(fq, fk, fv, cum_log_f).reshape(b, s, FOX_HEADS * FOX_HEAD_DIM)

        mix = jnp.concatenate([o_mla, o_fox], axis=-1) @ w_out[i]
        h = layer_norm(ALPHA * h + mix, ln1_g[i], ln1_b[i])

        ffn = peer(h, peer_wq[i], peer_keys1[i], peer_keys2[i], peer_u[i], peer_v[i])
        ple = (p[i] @ ple_wproj[i]) * jax.nn.sigmoid(h @ ple_wgate[i])
        h = layer_norm(ALPHA * h + ffn + ple, ln2_g[i], ln2_b[i])
    return h
```

```python
import numpy as np
from contextlib import ExitStack
import concourse.bass as bass
import concourse.mybir as mybir
from concourse.bass_utils import run_bass_kernel_spmd
from concourse.alu_op_type import AluOpType as ALU

F32 = mybir.dt.float32
BF16 = mybir.dt.bfloat16
I32 = mybir.dt.int32
U32 = mybir.dt.uint32
AF = mybir.ActivationFunctionType


class Buf:
    def __init__(self, name, ap):
        self.name = name
        self.ap = ap
        self.last_w = []
        self.reads = []
        self.dsem = None
        self.dcount = 0
        self.is_dram = False
        self.excl = False

    def __getitem__(self, k):
        return self.ap[k]


class Sched:
    ENG = ["pe", "dve", "act", "pool", "sp"]

    def __init__(self, nc):
        self.nc = nc
        self.stack = ExitStack()
        self.eobj = {"pe": nc.tensor, "dve": nc.vector, "act": nc.scalar, "pool": nc.gpsimd, "sp": nc.sync}
        self.q = {e: [] for e in self.ENG}
        self.cnt = {e: 0 for e in self.ENG}
        self.esem = {e: self.stack.enter_context(nc.semaphore("es_" + e)) for e in self.ENG}
        self.waited = {e: {} for e in self.ENG}
        self.nbuf = 0
        self.dma_tokens = []
        self.free_dsems = []

    def sbuf(self, name, shape, dtype, stack=None):
        t = (stack or self.stack).enter_context(self.nc.sbuf_tensor(name, list(shape), dtype))
        return Buf(name, t[:])

    def psum(self, name, shape, dtype, stack=None):
        t = (stack or self.stack).enter_context(self.nc.psum_tensor(name, list(shape), dtype))
        b = Buf(name, t[:])
        b.excl = True
        return b

    def dram(self, ap, name="dram"):
        b = Buf(name, ap)
        b.is_dram = True
        return b

    def _dsem(self, buf):
        if buf.dsem is None:
            self.nbuf += 1
            buf.dsem = self.stack.enter_context(self.nc.semaphore("ds%d" % self.nbuf))
        return buf.dsem

    def _deps(self, eng, r, w):
        deps = []
        for b in r:
            deps.extend(b.last_w)
        for b in w:
            deps.extend(b.last_w)
            deps.extend(b.reads)
        out = {}
        for (sem, val, e) in deps:
            if e == eng and eng in ("pe", "sp"):
                continue
            key = id(sem)
            if self.waited[eng].get(key, 0) >= val:
                continue
            if key not in out or out[key][1] < val:
                out[key] = (sem, val)
        for key, (sem, val) in out.items():
            self.waited[eng][key] = val
        return list(out.values())

    def _commit(self, tok, r, w, accumulate=False):
        for b in r:
            b.reads.append(tok)
        for b in w:
            if accumulate:
                b.last_w = [t for t in b.last_w if t[0] is not tok[0]] + [tok]
            else:
                b.last_w = [tok]
            b.reads = []

    def op(self, eng, fn, r=(), w=(), dma=False):
        r = list(r); w = list(w)
        for b in list(r):
            if b.excl:
                r.remove(b)
                if b not in w:
                    w.append(b)
        deps = self._deps(eng, r, w)
        if dma:
            wb = w[0]
            own = r[0] if (wb.is_dram and r) else wb
            sem = self._dsem(own)
            own.dcount += 16
            tok = (sem, own.dcount, "dma")
            self.q[eng].append((deps, fn, sem, 16))
            self.dma_tokens.append(tok)
            self._commit(tok, r, w, accumulate=wb.is_dram)
            return
        else:
            self.cnt[eng] += 1
            tok = (self.esem[eng], self.cnt[eng], eng)
            self.q[eng].append((deps, fn, self.esem[eng], 1))
        self._commit(tok, r, w)

    def dma(self, eng, wbuf, out_ap, rbuf, in_ap, **kw):
        r = [rbuf] if rbuf is not None else []
        self.op(eng, lambda e: e.dma_start(out=out_ap, in_=in_ap, **kw), r=r, w=[wbuf], dma=True)

    def barrier(self):
        toks = [(self.esem[e], self.cnt[e], e) for e in self.ENG if self.cnt[e] > 0]
        toks += self.dma_tokens
        self.dma_tokens = []
        for eng in self.ENG:
            out = {}
            for (sem, val, e) in toks:
                if e == eng:
                    continue
                key = id(sem)
                if self.waited[eng].get(key, 0) >= val:
                    continue
                if key not in out or out[key][1] < val:
                    out[key] = (sem, val)
            for key, (sem, val) in out.items():
                self.waited[eng][key] = val
            if out:
                self.q[eng].append((list(out.values()), None, None, 0))

    def finish(self, out_bufs):
        deps = []
        for b in out_bufs:
            for t in b.last_w:
                deps.append((t[0], t[1]))
        self.q["sp"].append((deps, None, None, 0))
        with self.nc.Block() as block:
            def mk(eng):
                def body(e):
                    for (deps, fn, sem, inc) in self.q[eng]:
                        for (s, v) in deps:
                            e.wait_ge(s, v)
                        if fn is not None:
                            ins = fn(e)
                            ins.then_inc(sem, inc)
                return body
            block.tensor(mk("pe"))
            block.vector(mk("dve"))
            block.scalar(mk("act"))
            block.gpsimd(mk("pool"))
            block.sync(mk("sp"))
        self.stack.close()


D = 2048
NT = 4096
NQ = 1024
TT = 256
ALPHA = 2.0 ** 0.25
EPS = 1e-6
PI = float(np.pi)
SC_MLA = 192.0 ** -0.5
SC_FOX = 128.0 ** -0.5
NEG = -1.0e30


class RR:
    def __init__(self, items):
        self.items = list(items); self.i = 0

    def next(self):
        x = self.items[self.i % len(self.items)]; self.i += 1
        return x


class Region:
    def __init__(self, arena, start_kb, end_kb):
        self.arena = arena; self.off = int(start_kb * 1024); self.end = int(end_kb * 1024)

    def alloc(self, name, shape, dtype):
        n = int(np.prod(shape[1:]))
        esz = 2 if dtype == BF16 else 4
        nb = (n * esz + 63) // 64 * 64
        assert self.off + nb <= self.end, (name, self.off, nb, self.end)
        o = self.off // 2
        ap = self.arena[0:shape[0], o:o + n * esz // 2]
        if dtype != BF16:
            ap = ap.bitcast(dtype)
        if len(shape) == 3:
            ap = ap.rearrange("p (a b) -> p a b", b=shape[2])
        elif len(shape) == 4:
            ap = ap.rearrange("p (a b c) -> p a b c", b=shape[2], c=shape[3])
        self.off += nb
        return Buf(name, ap)


import os
SECT = os.environ.get('SECT', 'mla,rope,fox,forget').split(',')
NTILES = int(os.environ.get('NTILES', '0'))


def build_nc(debug=False, stop_after=99):
    nc = bass.Bass("TRN2", target_bir_lowering=False)
    dbg_outs = {}

    def IN(name, shape, dtype=F32):
        return nc.dram_tensor(name, list(shape), dtype, kind="ExternalInput").ap()

    def SCR(name, shape, dtype):
        return nc.dram_tensor(name, list(shape), dtype, kind=("ExternalOutput" if debug else "Internal")).ap()

    xkv = IN("xkv", [NT, D]); xq = IN("xq", [NQ, D]); pq = IN("pq", [NQ, 256])
    pos_kv = IN("pos_kv", [1, NT], I32); pos_q = IN("pos_q", [1, NQ], I32)
    w_in = IN("w_in", [D, 4168]); w_uq = IN("w_uq", [512, 1536]); w_ukv = IN("w_ukv", [512, 2048])
    w_out = IN("w_out", [D, D]); peer_wq = IN("peer_wq", [D, 1024])
    keys1 = IN("keys1", [8, 128, 64]); keys2 = IN("keys2", [8, 128, 64])
    peer_u = IN("peer_u", [16384, D]); peer_v = IN("peer_v", [16384, D])
    ple_wgate = IN("ple_wgate", [D, D]); ple_wproj = IN("ple_wproj", [256, D])
    gqT = IN("gqT", [128, 4]); gkvT = IN("gkvT", [128, 4]); bfg = IN("bfg", [8, 1])
    ln1g = IN("ln1g", [1, D]); ln1b = IN("ln1b", [1, D]); ln2g = IN("ln2g", [1, D]); ln2b = IN("ln2b", [1, D])
    negmask_in = IN("negmask", [128, 4, 128]); sel_in = IN("sel", [128, 4, 128])
    invf_in = IN("invf", [64, 1]); sgn_in = IN("sgn", [64, 1])
    out_d = nc.dram_tensor("out", [NQ, D], F32, kind="ExternalOutput").ap()

    kTm_d = SCR("kTm_d", [8, 128, NT], BF16); krT_d = SCR("krT_d", [64, NT], BF16)
    vm_d = SCR("vm_d", [8, 128, 32, 128], BF16)
    kTf_d = SCR("kTf_d", [8, 128, NT], BF16); vf_d = SCR("vf_d", [8, 128, 32, 128], BF16)
    Ub_d = nc.dram_tensor("Ub_d", [16384, D], BF16, kind="Internal").ap()
    Vb_d = nc.dram_tensor("Vb_d", [16384, D], BF16, kind="Internal").ap()

    S = Sched(nc)
    arena = S.stack.enter_context(nc.sbuf_tensor("arena", [128, 95 * 1024], BF16))
    D_kTm = S.dram(kTm_d); D_krT = S.dram(krT_d); D_vm = S.dram(vm_d); D_kTf = S.dram(kTf_d); D_vf = S.dram(vf_d)
    D_out = S.dram(out_d)
    D_Ub = S.dram(Ub_d); D_Vb = S.dram(Vb_d)
    final_bufs = [D_out]

    def dbg(name, buf, shape, dtype):
        if not debug:
            return
        t = nc.dram_tensor("dbg_" + name, list(shape), dtype, kind="ExternalOutput").ap()
        b = S.dram(t)
        S.dma("sp", b, t, buf, buf.ap)
        final_bufs.append(b)

    def mm(out, lhsT, rhs, start, stop, r, w):
        S.op("pe", lambda e: e.matmul(out, lhsT, rhs, start=start, stop=stop), r=r, w=w)

    def tr(out, in_, idn, r, w):
        S.op("pe", lambda e: e.transpose(out, in_, idn), r=r, w=w)

    def act(out, in_, func, r, w, bias=None, scale=None, accum=None):
        kw = {}
        if bias is not None: kw["bias"] = bias
        if scale is not None: kw["scale"] = scale
        if accum is not None: kw["accum_out"] = accum
        S.op("act", lambda e: e.activation(out, in_, func, **kw), r=r, w=w)

    def tt(out, a, b, op, r, w, eng="dve"):
        S.op(eng, lambda e: e.tensor_tensor(out, a, b, op), r=r, w=w)

    def ts(out, a, s1, s2, op0, op1, r, w, eng="dve", accum=None):
        if op1 is None:
            S.op(eng, lambda e: e.tensor_scalar(out, a, s1, None, op0), r=r, w=w)
        elif accum is not None:
            S.op(eng, lambda e: e.tensor_scalar(out, a, s1, s2, op0, op1, accum_out=accum), r=r, w=w)
        else:
            S.op(eng, lambda e: e.tensor_scalar(out, a, s1, s2, op0, op1), r=r, w=w)

    def stt(out, a, sc, b, op0, op1, r, w, accum=None):
        if accum is None:
            S.op("dve", lambda e: e.scalar_tensor_tensor(out=out, in0=a, scalar=sc, in1=b, op0=op0, op1=op1), r=r, w=w)
        else:
            S.op("dve", lambda e: e.scalar_tensor_tensor(out=out, in0=a, scalar=sc, in1=b, op0=op0, op1=op1, accum_out=accum), r=r, w=w)

    def cp(out, in_, r, w, eng="dve"):
        if eng == "act":
            S.op("act", lambda e: e.copy(out, in_), r=r, w=w)
        else:
            S.op(eng, lambda e: e.tensor_copy(out, in_), r=r, w=w)

    evac_i = [0]

    def evac(out, in_, r, w):
        evac_i[0] += 1
        cp(out, in_, r, w, eng=("act" if evac_i[0] % 2 else "dve"))

    PF = [S.psum("pf%d" % i, [128, 512], F32) for i in range(6)]
    PB = [S.psum("pb%d" % i, [128, 1024], BF16) for i in range(2)]
    PFR = RR(PF); PBR = RR(PB)

    RC = Region(arena, 0, 8)
    identf = RC.alloc("identf", [128, 128], F32)
    ident = RC.alloc("ident", [128, 128], BF16)
    ones_bf = RC.alloc("ones_bf", [128, 128], BF16)
    ones_f = RC.alloc("ones_f", [128, 256], F32)
    epsT = RC.alloc("epsT", [128, 1], F32)
    oneT = RC.alloc("oneT", [128, 1], F32)
    invf = RC.alloc("invf", [64, 1], F32)
    sgn = RC.alloc("sgn", [64, 1], F32)
    negb = RC.alloc("negb", [8, 1], F32)
    gkv = RC.alloc("gkv", [128, 4], F32)
    gq = RC.alloc("gq", [128, 4], F32)
    negF_tok = RC.alloc("negF_tok", [128, 32, 8], F32)
    negmask = RC.alloc("negmask", [128, 4, 128], BF16)
    EI = RC.alloc("EI", [128, 128], I32)
    GT = RC.alloc("GT", [128, 128], F32)

    S.op("pool", lambda e: e.memset(identf.ap, 0.0), w=[identf])
    S.op("pool", lambda e: e.affine_select(out=identf.ap, in_=identf.ap, pattern=[[-1, 128]], compare_op=ALU.not_equal,
                                           fill=1.0, base=0, channel_multiplier=1), r=[identf], w=[identf])
    cp(ident.ap, identf.ap, [identf], [ident])
    S.op("dve", lambda e: e.memset(ones_bf.ap, 1.0), w=[ones_bf])
    S.op("dve", lambda e: e.memset(ones_f.ap, 1.0), w=[ones_f])
    S.op("dve", lambda e: e.memset(epsT.ap, EPS), w=[epsT])
    S.op("dve", lambda e: e.memset(oneT.ap, 1.0), w=[oneT])
    S.dma("sp", invf, invf.ap, None, invf_in)
    S.dma("sp", sgn, sgn.ap, None, sgn_in)
    S.dma("sp", negb, negb.ap, None, bfg)
    ts(negb.ap, negb.ap, -1.0, None, ALU.mult, None, [negb], [negb])
    S.dma("sp", gkv, gkv.ap, None, gkvT)
    S.dma("sp", gq, gq.ap, None, gqT)
    S.dma("pool", negmask, negmask.ap, None, negmask_in)

    w_in_v = w_in.rearrange("(c p) n -> p c n", p=128)

    def rope_tables(R, pos_ap, c0, n, tag):
        posi = R.alloc("posi" + tag, [64, n], I32)
        posf = R.alloc("posf" + tag, [64, n], F32)
        ang = R.alloc("ang" + tag, [64, n], F32)
        tq = R.alloc("tq" + tag, [64, n], F32)
        ki = R.alloc("ki" + tag, [64, n], I32)
        cosb = R.alloc("cos" + tag, [64, n], F32)
        sinb = R.alloc("sin" + tag, [64, n], F32)

        def emit(c0):
            S.dma("sp", posi, posi.ap, None, pos_ap[0:1, c0:c0 + n].partition_broadcast(64))
            cp(posf.ap, posi.ap, [posi], [posf])
            for (phase, dst) in ((0.0, sinb), (PI / 2, cosb)):
                ts(ang.ap, posf.ap, invf.ap[:, 0:1], phase, ALU.mult, ALU.add, [posf, invf], [ang])
                ts(tq.ap, ang.ap, 1.0 / (2 * PI), None, ALU.mult, None, [ang], [tq])
                cp(ki.ap, tq.ap, [tq], [ki])
                cp(tq.ap, ki.ap, [ki], [tq])
                stt(ang.ap, tq.ap, -2 * PI, ang.ap, ALU.mult, ALU.add, [tq, ang], [ang])
                ts(tq.ap, ang.ap, PI, -2 * PI, ALU.is_gt, ALU.mult, [ang], [tq])
                tt(ang.ap, ang.ap, tq.ap, ALU.add, [ang, tq], [ang])
                ts(tq.ap, ang.ap, -PI, 2 * PI, ALU.is_lt, ALU.mult, [ang], [tq])
                tt(ang.ap, ang.ap, tq.ap, ALU.add, [ang, tq], [ang])
                ts(ang.ap, ang.ap, PI, -PI, ALU.min, ALU.max, [ang], [ang])
                act(dst.ap, ang.ap, AF.Sin, [ang], [dst])
            ts(sinb.ap, sinb.ap, sgn.ap[:, 0:1], None, ALU.mult, None, [sinb, sgn], [sinb])
        return cosb, sinb, emit

    def transpose_in(xb, xT, ns):
        for ck in range(16):
            pb = PBR.next()
            for s_ in range(ns):
                tr(pb.ap[:, s_ * 128:(s_ + 1) * 128], xb.ap[:, s_, ck * 128:(ck + 1) * 128], ident.ap, [xb, ident], [pb])
            evac(xT.ap[:, ck, :], pb.ap[:, 0:ns * 128], [pb], [xT])

    def proj_rms(R, Wt, col0, xT, n, tag):
        raw = R.alloc("raw" + tag, [128, 4, n], F32)
        sq = R.alloc("sq" + tag, [128, 4, n], BF16)
        nrm = R.alloc("nrm" + tag, [128, 4, n], BF16)
        Rs = R.alloc("Rs" + tag, [128, n], F32)

        def emit():
            for fc in range(4):
                pf = PFR.next()
                for ck in range(16):
                    mm(pf.ap[:, 0:n], Wt.ap[:, ck, col0 + fc * 128:col0 + (fc + 1) * 128], xT.ap[:, ck, :], ck == 0, ck == 15, [Wt, xT], [pf])
                act(sq.ap[:, fc, :], pf.ap[:, 0:n], AF.Square, [pf], [sq])
                cp(raw.ap[:, fc, :], pf.ap[:, 0:n], [pf], [raw])
            pf = PFR.next()
            for fc in range(4):
                mm(pf.ap[:, 0:n], ones_bf.ap, sq.ap[:, fc, :], fc == 0, fc == 3, [ones_bf, sq], [pf])
            act(Rs.ap, pf.ap[:, 0:n], AF.Sqrt, [pf, epsT], [Rs], bias=epsT.ap[:, 0:1], scale=1.0 / 512)
            S.op("dve", lambda e: e.reciprocal(Rs.ap, Rs.ap), r=[Rs], w=[Rs])
            for fc in range(4):
                tt(nrm.ap[:, fc, :], raw.ap[:, fc, :], Rs.ap, ALU.mult, [raw, Rs], [nrm])
        return nrm, emit

    R1 = Region(arena, 8, 190)
    Frow = R1.alloc("Frow", [8, NT], F32)
    Wm = R1.alloc("Wm", [128, 16, 576], BF16)
    Wf = R1.alloc("Wf", [128, 16, 2056], BF16)
    Wukv = R1.alloc("Wukv", [128, 4, 2048], BF16)
    Wkrsw = R1.alloc("Wkrsw", [128, 16, 64], BF16)
    mark = R1.off
    stg = R1.alloc("stg", [128, 4, 2048], F32)
    S.dma("pool", Wm, Wm.ap, None, w_in_v[:, :, 512:1088])
    S.dma("pool", Wf, Wf.ap[:, :, 0:1028], None, w_in_v[:, :, 2112:3140])
    S.dma("pool", Wf, Wf.ap[:, :, 1028:2056], None, w_in_v[:, :, 3140:4168])
    S.dma("sp", stg, stg.ap, None, w_ukv.rearrange("(c p) n -> p c n", p=128))
    for fc in range(4):
        ts(Wukv.ap[:, fc, :], stg.ap[:, fc, :], gkv.ap[:, fc:fc + 1], None, ALU.mult, None, [stg, gkv], [Wukv])
    cp(Wkrsw.ap[:, :, 0:32], Wm.ap[:, :, 544:576], [Wm], [Wkrsw])
    cp(Wkrsw.ap[:, :, 32:64], Wm.ap[:, :, 512:544], [Wm], [Wkrsw])
    S.barrier()
    R1.off = mark
    ns = TT // 128
    xb = R1.alloc("xb", [128, ns, D], BF16)
    xT = R1.alloc("xT", [128, 16, TT], BF16)
    ckvn, emit_ckv = proj_rms(R1, Wm, 0, xT, TT, "kv")
    kn_st = RR([R1.alloc("kn_st%d" % i, [128, 8, TT], BF16) for i in range(2)])
    fk_st = RR([R1.alloc("fk_st%d" % i, [128, 8, TT], BF16) for i in range(2)])
    v_st = R1.alloc("v_st", [128, ns, 1024], BF16)
    fv_st = R1.alloc("fv_st", [128, ns, 1024], BF16)
    cosb, sinb, emit_rope = rope_tables(R1, pos_kv, 0, TT, "kv")
    T1 = R1.alloc("T1", [64, TT], F32); T2 = R1.alloc("T2", [64, TT], F32)
    kr_st = R1.alloc("kr_st", [64, TT], BF16)
    exb = R1.alloc("exb", [8, TT], F32); lnb = R1.alloc("lnb", [8, TT], F32)

    def sec_load(T, t0):
        S.dma("pool", xb, xb.ap, None, xkv[t0:t0 + TT, :].rearrange("(s p) d -> p s d", p=128))
        transpose_in(xb, xT, ns)
    MLAK = int(os.environ.get('MLAK', '9'))

    def sec_mla(T, t0):
        emit_ckv()
        if MLAK < 2: return
        kst = kn_st.next()
        for h in range(8):
            pf = PFR.next()
            for fc in range(4):
                mm(pf.ap[:, 0:TT], Wukv.ap[:, fc, h * 256:h * 256 + 128], ckvn.ap[:, fc, :], fc == 0, fc == 3, [Wukv, ckvn], [pf])
            evac(kst.ap[:, h, :], pf.ap[:, 0:TT], [pf], [kst])
        S.dma("sp", D_kTm, kTm_d.rearrange("h d t -> d h t")[:, :, t0:t0 + TT], kst, kst.ap)
        if MLAK < 3: return
        for s_ in range(ns):
            for hh in range(2):
                pf = PFR.next()
                for fc in range(4):
                    rhs = Wukv.ap[:, fc, :].rearrange("p (h c) -> p h c", c=256)[:, hh * 4:(hh + 1) * 4, 128:256]
                    mm(pf.ap, ckvn.ap[:, fc, s_ * 128:(s_ + 1) * 128], rhs, fc == 0, fc == 3, [Wukv, ckvn], [pf])
                evac(v_st.ap[:, s_, hh * 512:(hh + 1) * 512], pf.ap, [pf], [v_st])
            blk = T * ns + s_
            S.dma("sp", D_vm, vm_d.rearrange("h p b d -> p b h d")[:, blk, :, :], v_st,
                  v_st.ap[:, s_, :].rearrange("p (h d) -> p h d", d=128))
    def sec_rope(T, t0):
        pk = PFR.next(); pks = PFR.next()
        for ck in range(16):
            mm(pk.ap[0:64, 0:TT], Wm.ap[:, ck, 512:576], xT.ap[:, ck, :], ck == 0, ck == 15, [Wm, xT], [pk])
        for ck in range(16):
            mm(pks.ap[0:64, 0:TT], Wkrsw.ap[:, ck, :], xT.ap[:, ck, :], ck == 0, ck == 15, [Wkrsw, xT], [pks])
        emit_rope(t0)
        tt(T1.ap, pk.ap[0:64, 0:TT], cosb.ap, ALU.mult, [pk, cosb], [T1])
        tt(T2.ap, pks.ap[0:64, 0:TT], sinb.ap, ALU.mult, [pks, sinb], [T2])
        tt(kr_st.ap, T1.ap, T2.ap, ALU.add, [T1, T2], [kr_st])
        S.dma("sp", D_krT, krT_d[:, t0:t0 + TT], kr_st, kr_st.ap)
    def sec_fox(T, t0):
        fst = fk_st.next()
        for h in range(8):
            pf = PFR.next()
            for ck in range(16):
                mm(pf.ap[:, 0:TT], Wf.ap[:, ck, h * 128:(h + 1) * 128], xT.ap[:, ck, :], ck == 0, ck == 15, [Wf, xT], [pf])
            evac(fst.ap[:, h, :], pf.ap[:, 0:TT], [pf], [fst])
        S.dma("sp", D_kTf, kTf_d.rearrange("h d t -> d h t")[:, :, t0:t0 + TT], fst, fst.ap)
        for s_ in range(ns):
            for hh in range(2):
                pf = PFR.next()
                for ck in range(16):
                    mm(pf.ap, xT.ap[:, ck, s_ * 128:(s_ + 1) * 128], Wf.ap[:, ck, 1024 + hh * 512:1024 + (hh + 1) * 512], ck == 0, ck == 15, [Wf, xT], [pf])
                evac(fv_st.ap[:, s_, hh * 512:(hh + 1) * 512], pf.ap, [pf], [fv_st])
            blk = T * ns + s_
            S.dma("sp", D_vf, vf_d.rearrange("h p b d -> p b h d")[:, blk, :, :], fv_st,
                  fv_st.ap[:, s_, :].rearrange("p (h d) -> p h d", d=128))
    def sec_forget(T, t0):
        pff = PFR.next()
        for ck in range(16):
            mm(pff.ap[0:8, 0:TT], Wf.ap[:, ck, 2048:2056], xT.ap[:, ck, :], ck == 0, ck == 15, [Wf, xT], [pff])
        act(exb.ap, pff.ap[0:8, 0:TT], AF.Exp, [pff, negb], [exb], bias=negb.ap[:, 0:1], scale=-1.0)
        act(lnb.ap, exb.ap, AF.Ln, [exb, oneT], [lnb], bias=oneT.ap[0:8, 0:1])
        init = 0.0 if T == 0 else Frow.ap[:, t0 - 1:t0]
        S.op("dve", lambda e, init=init, t0=t0: e.tensor_tensor_scan(Frow.ap[:, t0:t0 + TT], ones_f.ap[0:8, 0:TT], lnb.ap, init, ALU.mult, ALU.subtract),
             r=[Frow, ones_f, lnb], w=[Frow])
        for s_ in range(ns):
            blk = T * ns + s_
            pf = PFR.next()
            tr(pf.ap[:, 0:8], Frow.ap[0:8, blk * 128:(blk + 1) * 128], identf.ap[0:8, 0:8], [Frow, identf], [pf])
            ts(negF_tok.ap[:, blk, :], pf.ap[:, 0:8], -1.0, None, ALU.mult, None, [pf], [negF_tok])

    for T in range(NTILES or (NT // TT)):
        t0 = T * TT
        sec_load(T, t0)
        r0 = T * 1024
        S.dma("pool", D_Ub, Ub_d[r0:r0 + 1024, :], None, peer_u[r0:r0 + 1024, :])
        S.dma("pool", D_Vb, Vb_d[r0:r0 + 1024, :], None, peer_v[r0:r0 + 1024, :])
        if 'mla' in SECT: sec_mla(T, t0)
        if 'rope' in SECT: sec_rope(T, t0)
        if 'fox' in SECT: sec_fox(T, t0)
        if 'forget' in SECT: sec_forget(T, t0)
    dbg("negF", negF_tok, [128, 32, 8], F32)
    dbg("Frow", Frow, [8, NT], F32)
    if stop_after <= 1:
        final_bufs.extend([D_kTm, D_krT, D_vm, D_kTf, D_vf])
        S.finish(final_bufs)
        return nc

    S.barrier()
    RP = Region(arena, 24, 88)
    QN = RP.alloc("QN", [128, 8, NQ], BF16)
    QR = RP.alloc("QR", [64, 8, NQ], BF16)
    FQ = RP.alloc("FQ", [128, 8, NQ], BF16)
    Fq_row = RP.alloc("Fq_row", [1, 8, NQ], BF16)
    R2 = Region(arena, 88, 190)
    Wqc = R2.alloc("Wqc", [128, 16, 512], BF16)
    Wqf = R2.alloc("Wqf", [128, 16, 1024], BF16)
    Wuq = R2.alloc("Wuq", [128, 4, 1536], BF16)
    Wuqsw = R2.alloc("Wuqsw", [128, 4, 8, 64], BF16)
    mark = R2.off
    stg2 = R2.alloc("stg2", [128, 4, 1536], F32)
    S.dma("pool", Wqc, Wqc.ap, None, w_in_v[:, :, 0:512])
    S.dma("pool", Wqf, Wqf.ap, None, w_in_v[:, :, 1088:2112])
    S.dma("sp", stg2, stg2.ap, None, w_uq.rearrange("(c p) n -> p c n", p=128))
    for fc in range(4):
        ts(Wuq.ap[:, fc, :], stg2.ap[:, fc, :], gq.ap[:, fc:fc + 1], None, ALU.mult, None, [stg2, gq], [Wuq])
    for fc in range(4):
        src = Wuq.ap[:, fc, :].rearrange("p (h c) -> p h c", c=192)
        cp(Wuqsw.ap[:, fc, :, 0:32], src[:, :, 160:192], [Wuq], [Wuqsw])
        cp(Wuqsw.ap[:, fc, :, 32:64], src[:, :, 128:160], [Wuq], [Wuqsw])
    S.barrier()
    R2.off = mark
    xb2 = R2.alloc("xb2", [128, ns, D], BF16)
    xT2 = R2.alloc("xT2", [128, 16, TT], BF16)
    cqn, emit_cq = proj_rms(R2, Wqc, 0, xT2, TT, "q")
    cosq, sinq, emit_ropeq = rope_tables(R2, pos_q, 0, TT, "q")
    T1q = R2.alloc("T1q", [64, TT], F32); T2q = R2.alloc("T2q", [64, TT], F32)
    selb = R2.alloc("selb", [128, 4, 128], F32)
    S.dma("sp", selb, selb.ap, None, sel_in)
    for T in range(NQ // TT):
        t0 = T * TT
        S.dma("pool", xb2, xb2.ap, None, xq[t0:t0 + TT, :].rearrange("(s p) d -> p s d", p=128))
        transpose_in(xb2, xT2, ns)
        emit_cq()
        emit_ropeq(t0)
        for h in range(8):
            pf = PFR.next()
            for fc in range(4):
                mm(pf.ap[:, 0:TT], Wuq.ap[:, fc, h * 192:h * 192 + 128], cqn.ap[:, fc, :], fc == 0, fc == 3, [Wuq, cqn], [pf])
            evac(QN.ap[:, h, t0:t0 + TT], pf.ap[:, 0:TT], [pf], [QN])
            pk = PFR.next(); pks = PFR.next()
            for fc in range(4):
                mm(pk.ap[0:64, 0:TT], Wuq.ap[:, fc, h * 192 + 128:h * 192 + 192], cqn.ap[:, fc, :], fc == 0, fc == 3, [Wuq, cqn], [pk])
            for fc in range(4):
                mm(pks.ap[0:64, 0:TT], Wuqsw.ap[:, fc, h, :], cqn.ap[:, fc, :], fc == 0, fc == 3, [Wuqsw, cqn], [pks])
            tt(T1q.ap, pk.ap[0:64, 0:TT], cosq.ap, ALU.mult, [pk, cosq], [T1q])
            tt(T2q.ap, pks.ap[0:64, 0:TT], sinq.ap, ALU.mult, [pks, sinq], [T2q])
            tt(QR.ap[:, h, t0:t0 + TT], T1q.ap, T2q.ap, ALU.add, [T1q, T2q], [QR])
            pf = PFR.next()
            for ck in range(16):
                mm(pf.ap[:, 0:TT], Wqf.ap[:, ck, h * 128:(h + 1) * 128], xT2.ap[:, ck, :], ck == 0, ck == 15, [Wqf, xT2], [pf])
            evac(FQ.ap[:, h, t0:t0 + TT], pf.ap[:, 0:TT], [pf], [FQ])
    for i in range(8):
        for h in range(8):
            pf = PFR.next()
            for m in range(4):
                mm(pf.ap[0:1, 0:128], negF_tok.ap[:, 4 * i + m, h:h + 1], selb.ap[:, m, :], m == 0, m == 3, [negF_tok, selb], [pf])
            ts(Fq_row.ap[0:1, h, i * 128:(i + 1) * 128], pf.ap[0:1, 0:128], -1.0 / SC_FOX, None, ALU.mult, None, [pf], [Fq_row])
    dbg("QN", QN, [128, 8, NQ], BF16)
    dbg("QR", QR, [64, 8, NQ], BF16)
    dbg("FQ", FQ, [128, 8, NQ], BF16)
    dbg("Fqrow", Fq_row, [1, 8, NQ], BF16)
    if stop_after <= 2:
        S.finish(final_bufs)
        return nc

    S.barrier()
    OT = Region(arena, 158, 190).alloc("OT", [128, 16, NQ], BF16)
    R3 = Region(arena, 88, 158)
    KT = RR([R3.alloc("KT%d" % i, [128, NT], BF16) for i in range(2)])
    KR = R3.alloc("KR", [64, NT], BF16)
    VV = RR([R3.alloc("V%d" % i, [128, 32, 129], BF16) for i in range(2)])
    PT = RR([R3.alloc("PT%d" % i, [128, 512], BF16) for i in range(4)])
    OTl = RR([R3.alloc("Otl%d" % i, [128, 128], BF16) for i in range(2)])
    rec = RR([R3.alloc("rec%d" % i, [128, 1], F32) for i in range(2)])
    for v in VV.items:
        S.op("dve", lambda e, v=v: e.memset(v.ap[:, :, 128:129], 1.0), w=[v])
    S.dma("sp", KR, KR.ap, D_krT, krT_d)
    PS = RR(PF[0:4]); PO = RR(PF[4:6])
    NHD = int(os.environ.get("NHD", "16"))
    for hd in range(NHD):
        mla = hd < 8; h = hd % 8
        kt = KT.next(); v = VV.next()
        S.dma("sp", kt, kt.ap, (D_kTm if mla else D_kTf), (kTm_d if mla else kTf_d)[h])
        S.dma("sp", v, v.ap[:, :, 0:128], (D_vm if mla else D_vf), (vm_d if mla else vf_d)[h])
        for i in range(8):
            po = PO.next()
            q0 = i * 128
            nkb = 4 * i + 4
            for g in range(i + 1):
                ps = PS.next(); pt = PT.next()
                diag = (g == i)
                for m in range(4):
                    kb = 4 * g + m
                    out = ps.ap[:, m * 128:(m + 1) * 128]
                    if mla:
                        mm(out, kt.ap[:, kb * 128:(kb + 1) * 128], QN.ap[:, h, q0:q0 + 128], True, False, [kt, QN], [ps])
                        mm(out, KR.ap[0:64, kb * 128:(kb + 1) * 128], QR.ap[0:64, h, q0:q0 + 128], False, not diag, [KR, QR], [ps])
                    else:
                        mm(out, kt.ap[:, kb * 128:(kb + 1) * 128], FQ.ap[:, h, q0:q0 + 128], True, False, [kt, FQ], [ps])
                        mm(out, ones_bf.ap[0:1, 0:128], Fq_row.ap[0:1, h, q0:q0 + 128], False, not diag, [ones_bf, Fq_row], [ps])
                    if diag:
                        mm(out, ident.ap, negmask.ap[:, m, :], False, True, [ident, negmask], [ps])
                if mla:
                    act(pt.ap, ps.ap, AF.Exp, [ps], [pt], scale=SC_MLA)
                else:
                    for m in range(4):
                        act(pt.ap[:, m * 128:(m + 1) * 128], ps.ap[:, m * 128:(m + 1) * 128], AF.Exp, [ps, negF_tok], [pt],
                            bias=negF_tok.ap[:, 4 * g + m, h:h + 1], scale=SC_FOX)
                for m in range(4):
                    kb = 4 * g + m
                    mm(po.ap[:, 0:129], pt.ap[:, m * 128:(m + 1) * 128], v.ap[:, kb, :], kb == 0, kb == nkb - 1, [pt, v], [po])
            r_ = rec.next(); ot = OTl.next()
            S.op("dve", lambda e, r_=r_, po=po: e.reciprocal(r_.ap, po.ap[:, 128:129]), r=[po], w=[r_])
            ts(ot.ap, po.ap[:, 0:128], r_.ap[:, 0:1], None, ALU.mult, None, [po, r_], [ot])
            pb = PBR.next()
            tr(pb.ap[:, 0:128], ot.ap, ident.ap, [ot, ident], [pb])
            evac(OT.ap[:, hd, q0:q0 + 128], pb.ap[:, 0:128], [pb], [OT])
    dbg("OT", OT, [128, 16, NQ], BF16)
    if stop_after <= 3:
        S.finish(final_bufs)
        return nc

    S.barrier()
    H1 = Region(arena, 8, 72).alloc("H1", [128, 8, D], F32)
    R4 = Region(arena, 72, 158)
    WoutC = RR([R4.alloc("WoutC%d" % i, [128, 16, 512], BF16) for i in range(2)])
    xqt = RR([R4.alloc("xqt%d" % i, [128, D], F32) for i in range(2)])
    Gbc = R4.alloc("Gbc", [128, D], F32)
    Bbc = R4.alloc("Bbc", [128, D], F32)
    stats = R4.alloc("stats", [128, 4, 6], F32)
    mv = R4.alloc("mv", [128, 2], F32)
    rstd = R4.alloc("rstd", [128, 1], F32)
    w_out_v = w_out.rearrange("(c p) n -> p c n", p=128)

    def layer_norm_rows(Xap, Xbuf, Gb, Bb, stats, mv, rstd):
        for c4 in range(4):
            S.op("dve", lambda e, c4=c4, stats=stats: e.bn_stats(stats.ap[:, c4, :], Xap[:, c4 * 512:(c4 + 1) * 512]), r=[Xbuf], w=[stats])
        S.op("dve", lambda e, stats=stats, mv=mv: e.bn_aggr(mv.ap, stats.ap), r=[stats], w=[mv])
        act(rstd.ap, mv.ap[:, 1:2], AF.Sqrt, [mv, epsT], [rstd], bias=epsT.ap[:, 0:1])
        S.op("dve", lambda e, rstd=rstd: e.reciprocal(rstd.ap, rstd.ap), r=[rstd], w=[rstd])
        ts(Xap, Xap, mv.ap[:, 0:1], rstd.ap[:, 0:1], ALU.subtract, ALU.mult, [Xbuf, mv, rstd], [Xbuf])
        if Gb is not None:
            tt(Xap, Xap, Gb.ap, ALU.mult, [Xbuf, Gb], [Xbuf])
            tt(Xap, Xap, Bb.ap, ALU.add, [Xbuf, Bb], [Xbuf])

    S.dma("sp", Gbc, Gbc.ap, None, ln1g.partition_broadcast(128))
    S.dma("sp", Bbc, Bbc.ap, None, ln1b.partition_broadcast(128))
    for nf in range(4):
        wc = WoutC.next()
        for half in range(2):
            S.dma("pool", wc, wc.ap[:, half * 8:(half + 1) * 8, :], None, w_out_v[:, half * 8:(half + 1) * 8, nf * 512:(nf + 1) * 512])
        for i in range(8):
            if nf == 0:
                xt_ = xqt.next()
                S.dma("sp", xt_, xt_.ap, None, xq[i * 128:(i + 1) * 128, :])
                S.op("act", lambda e, xt_=xt_, i=i: e.mul(H1.ap[:, i, :], xt_.ap, ALPHA), r=[xt_], w=[H1])
            pf = PFR.next()
            for cc in range(16):
                mm(pf.ap, OT.ap[:, cc, i * 128:(i + 1) * 128], wc.ap[:, cc, :], cc == 0, cc == 15, [OT, wc], [pf])
            tt(H1.ap[:, i, nf * 512:(nf + 1) * 512], H1.ap[:, i, nf * 512:(nf + 1) * 512], pf.ap, ALU.add, [H1, pf], [H1])
    for i in range(8):
        layer_norm_rows(H1.ap[:, i, :], H1, Gbc, Bbc, stats, mv, rstd)
    dbg("H1", H1, [128, 8, D], F32)
    S.barrier()
    RB = Region(arena, 72, 136)
    H1B = RB.alloc("H1B", [128, 8, D], BF16)
    H1T = RB.alloc("H1T", [128, 16, NQ], BF16)
    for i in range(8):
        cp(H1B.ap[:, i, :], H1.ap[:, i, :], [H1], [H1B], eng=("act" if i % 2 else "dve"))
    for cc in range(16):
        for ig in range(2):
            pb = PBR.next()
            for k in range(4):
                i = ig * 4 + k
                tr(pb.ap[:, k * 128:(k + 1) * 128], H1B.ap[:, i, cc * 128:(cc + 1) * 128], ident.ap, [H1B, ident], [pb])
            evac(H1T.ap[:, cc, ig * 512:(ig + 1) * 512], pb.ap[:, 0:512], [pb], [H1T])
    if stop_after <= 4:
        S.finish(final_bufs)
        return nc

    R5 = Region(arena, 136, 190)
    WgC = RR([R5.alloc("WgC%d" % i, [128, 16, 512], BF16) for i in range(2)])
    Wp = R5.alloc("Wp", [128, 2, D], BF16)
    pT = R5.alloc("pT", [128, 2, NQ], BF16)
    pb16 = R5.alloc("pb16", [128, 8, 256], BF16)
    sig = RR([R5.alloc("sig%d" % i, [128, 512], F32) for i in range(2)])
    S.dma("pool", Wp, Wp.ap, None, ple_wproj.rearrange("(c p) n -> p c n", p=128))
    S.dma("pool", pb16, pb16.ap, None, pq.rearrange("(i p) d -> p i d", p=128))
    for c2 in range(2):
        for ig in range(2):
            pb = PBR.next()
            for k in range(4):
                i = ig * 4 + k
                tr(pb.ap[:, k * 128:(k + 1) * 128], pb16.ap[:, i, c2 * 128:(c2 + 1) * 128], ident.ap, [pb16, ident], [pb])
            evac(pT.ap[:, c2, ig * 512:(ig + 1) * 512], pb.ap[:, 0:512], [pb], [pT])
    wg_v = ple_wgate.rearrange("(c p) n -> p c n", p=128)
    for nf in range(4):
        wc = WgC.next()
        for half in range(2):
            S.dma("pool", wc, wc.ap[:, half * 8:(half + 1) * 8, :], None, wg_v[:, half * 8:(half + 1) * 8, nf * 512:(nf + 1) * 512])
        for i in range(8):
            pf = PFR.next()
            for cc in range(16):
                mm(pf.ap, H1T.ap[:, cc, i * 128:(i + 1) * 128], wc.ap[:, cc, :], cc == 0, cc == 15, [H1T, wc], [pf])
            sg = sig.next()
            act(sg.ap, pf.ap, AF.Sigmoid, [pf], [sg])
            pf2 = PFR.next()
            for c2 in range(2):
                mm(pf2.ap, pT.ap[:, c2, i * 128:(i + 1) * 128], Wp.ap[:, c2, nf * 512:(nf + 1) * 512], c2 == 0, c2 == 1, [pT, Wp], [pf2])
            tt(sg.ap, sg.ap, pf2.ap, ALU.mult, [sg, pf2], [sg])
            Hs = H1.ap[:, i, nf * 512:(nf + 1) * 512]
            stt(Hs, Hs, ALPHA, sg.ap, ALU.mult, ALU.add, [H1, sg], [H1])
    dbg("Y", H1, [128, 8, D], F32)
    if stop_after <= 5:
        S.finish(final_bufs)
        return nc

    S.barrier()
    RE = Region(arena, 182, 190)
    EI_all = RE.alloc("EI_all", [128, 8, 128], I32)
    GT_all = RE.alloc("GT_all", [128, 8, 128], F32)
    R5b = Region(arena, 136, 182)
    Wpq = R5b.alloc("Wpq", [128, 16, 1024], BF16)
    QPi = R5b.alloc("QPi", [128, 8, 128], BF16)
    KT12 = R5b.alloc("KT12", [128, 8, 128], BF16)
    k12 = R5b.alloc("k12", [128, 128], BF16)
    SCb = R5b.alloc("SCb", [128, 256], F32)
    tmpS = R5b.alloc("tmpS", [128, 128], F32)
    V12 = R5b.alloc("V12", [128, 32], F32)
    I12 = R5b.alloc("I12", [128, 32], U32)
    I12f = R5b.alloc("I12f", [128, 32], F32)
    cand = R5b.alloc("cand", [128, 16, 16], F32)
    cidx = R5b.alloc("cidx", [128, 16, 16], F32)
    tmp256 = R5b.alloc("tmp256", [128, 256], F32)
    junk256 = R5b.alloc("junk256", [128, 256], F32)
    iota_i = R5b.alloc("iota_i", [128, 256], I32)
    iota_f = R5b.alloc("iota_f", [128, 256], F32)
    SCv = R5b.alloc("SCv", [128, 16], F32)
    posu = R5b.alloc("posu", [128, 16], U32)
    posf2 = R5b.alloc("posf2", [128, 16], F32)
    EIf = R5b.alloc("EIf", [128, 16], F32)
    gexp = R5b.alloc("gexp", [128, 16], F32)
    negm = R5b.alloc("negm", [128, 1], F32)
    Zs = R5b.alloc("Zs", [128, 1], F32)
    pwq_v = peer_wq.rearrange("(c p) n -> p c n", p=128)
    for half in range(2):
        S.dma("pool", Wpq, Wpq.ap[:, half * 8:(half + 1) * 8, :], None, pwq_v[:, half * 8:(half + 1) * 8, :])
    S.op("pool", lambda e: e.iota(iota_i.ap, pattern=[[1, 256]], base=0, channel_multiplier=0), w=[iota_i])
    cp(iota_f.ap, iota_i.ap, [iota_i], [iota_f])
    for h in range(8):
        S.dma("pool", k12, k12.ap[:, 0:64], None, keys1[h])
        S.dma("pool", k12, k12.ap[:, 64:128], None, keys2[h])
        pb = PBR.next()
        tr(pb.ap[:, 0:128], k12.ap, ident.ap, [k12, ident], [pb])
        evac(KT12.ap[:, h, :], pb.ap[:, 0:128], [pb], [KT12])
    cand_f = cand.ap.rearrange("p a b -> p (a b)")
    cidx_f = cidx.ap.rearrange("p a b -> p (a b)")

    def top16(vals_ap, vbuf, out_v, out_i, obufs, scratch, n):
        S.op("dve", lambda e: e.max(out_v[:, 0:8], vals_ap), r=[vbuf], w=[obufs[0]])
        S.op("dve", lambda e: e.max_index(out_i[:, 0:8], out_v[:, 0:8], vals_ap), r=[vbuf, obufs[0]], w=[obufs[1]])
        S.op("dve", lambda e: e.match_replace(scratch.ap[:, 0:n], out_v[:, 0:8], vals_ap, NEG), r=[vbuf, obufs[0]], w=[scratch])
        S.op("dve", lambda e: e.max(out_v[:, 8:16], scratch.ap[:, 0:n]), r=[scratch], w=[obufs[0]])
        S.op("dve", lambda e: e.max_index(out_i[:, 8:16], out_v[:, 8:16], scratch.ap[:, 0:n]), r=[scratch, obufs[0]], w=[obufs[1]])

    NTI = int(os.environ.get("NTI", "8"))
    for i in range(NTI):
        for h in range(8):
            pf = PFR.next()
            for ck in range(16):
                mm(pf.ap[:, 0:128], Wpq.ap[:, ck, h * 128:(h + 1) * 128], H1T.ap[:, ck, i * 128:(i + 1) * 128], ck == 0, ck == 15, [Wpq, H1T], [pf])
            evac(QPi.ap[:, h, :], pf.ap[:, 0:128], [pf], [QPi])
        for h in range(8):
            pf = PFR.next(); pf2 = PFR.next()
            mm(pf.ap[:, 0:128], QPi.ap[0:64, h, :], KT12.ap[0:64, h, :], True, True, [QPi, KT12], [pf])
            mm(pf2.ap[:, 0:128], QPi.ap[64:128, h, :], KT12.ap[64:128, h, :], True, True, [QPi, KT12], [pf2])
            cp(SCb.ap[:, 0:128], pf.ap[:, 0:128], [pf], [SCb])
            cp(SCb.ap[:, 128:256], pf2.ap[:, 0:128], [pf2], [SCb])
            for half in range(2):
                top16(SCb.ap[:, half * 128:(half + 1) * 128], SCb, V12.ap[:, half * 16:(half + 1) * 16], I12.ap[:, half * 16:(half + 1) * 16],
                      (V12, I12), tmpS, 128)
            cp(I12f.ap, I12.ap, [I12], [I12f])
            v1b = V12.ap[:, 0:16].unsqueeze(2).to_broadcast([128, 16, 16])
            v2b = V12.ap[:, 16:32].unsqueeze(1).to_broadcast([128, 16, 16])
            i1b = I12f.ap[:, 0:16].unsqueeze(2).to_broadcast([128, 16, 16])
            i2b = I12f.ap[:, 16:32].unsqueeze(1).to_broadcast([128, 16, 16])
            tt(cand.ap, v1b, v2b, ALU.add, [V12], [cand])
            stt(cidx.ap, i1b, 128.0, i2b, ALU.mult, ALU.add, [I12f], [cidx])
            top16(cand_f, cand, SCv.ap, posu.ap, (SCv, posu), tmp256, 256)
            cp(posf2.ap, posu.ap, [posu], [posf2])
            S.op("dve", lambda e: e.memset(EIf.ap, 0.0), w=[EIf])
            for k in range(16):
                stt(junk256.ap, iota_f.ap, posf2.ap[:, k:k + 1], cidx_f, ALU.is_equal, ALU.mult, [iota_f, posf2, cidx], [junk256, EIf],
                    accum=EIf.ap[:, k:k + 1])
            cp(EI_all.ap[:, i, h * 16:(h + 1) * 16], EIf.ap, [EIf], [EI_all])
            ts(negm.ap, SCv.ap[:, 0:1], -1.0, None, ALU.mult, None, [SCv], [negm])
            S.op("dve", lambda e: e.memset(Zs.ap, 0.0), w=[Zs])
            act(gexp.ap, SCv.ap, AF.Exp, [SCv, negm], [gexp, Zs], bias=negm.ap[:, 0:1], accum=Zs.ap[:, 0:1])
            S.op("dve", lambda e: e.reciprocal(Zs.ap, Zs.ap), r=[Zs], w=[Zs])
            ts(GT_all.ap[:, i, h * 16:(h + 1) * 16], gexp.ap, Zs.ap[:, 0:1], None, ALU.mult, None, [gexp, Zs], [GT_all])
    dbg("EI", EI_all, [128, 8, 128], I32)
    dbg("GT", GT_all, [128, 8, 128], F32)
    if stop_after <= 6:
        S.finish(final_bufs)
        return nc

    S.barrier()
    R5c = Region(arena, 104, 182)
    UG = RR([R5c.alloc("UG%d" % k, [128, 4, D], BF16) for k in range(2)])
    VG = RR([R5c.alloc("VG%d" % k, [128, 4, D], BF16) for k in range(2)])
    Adot = R5c.alloc("Adot", [128, 128], F32)
    AW = R5c.alloc("AW", [128, 128], F32)
    DG = RR([R5c.alloc("DG%d" % k, [128, 128], BF16) for k in range(4)])
    stats2 = R5c.alloc("stats2", [128, 4, 6], F32)
    mv2 = R5c.alloc("mv2", [128, 2], F32)
    rstd2 = R5c.alloc("rstd2", [128, 1], F32)
    Gh = R5c.alloc("Gh", [128, 1024], F32)
    Bh = R5c.alloc("Bh", [128, 1024], F32)
    NSL = int(os.environ.get("NSL", "128"))
    for i in range(NTI):
        S.op("dve", lambda e: e.memset(Adot.ap, 0.0), w=[Adot])
        for ch in range(NSL // 4):
            ug = UG.next(); vg = VG.next()
            for c in range(4):
                slot = ch * 4 + c
                S.op("pool", lambda e, ug=ug, c=c, slot=slot, i=i: e.indirect_dma_start(
                    out=ug.ap[:, c, :], out_offset=None, in_=Ub_d,
                    in_offset=bass.IndirectOffsetOnAxis(ap=EI_all.ap[:, i, slot:slot + 1], axis=0)), r=[EI_all, D_Ub], w=[ug], dma=True)
            for c in range(4):
                slot = ch * 4 + c
                S.op("pool", lambda e, vg=vg, c=c, slot=slot, i=i: e.indirect_dma_start(
                    out=vg.ap[:, c, :], out_offset=None, in_=Vb_d,
                    in_offset=bass.IndirectOffsetOnAxis(ap=EI_all.ap[:, i, slot:slot + 1], axis=0)), r=[EI_all, D_Vb], w=[vg], dma=True)
            for c in range(4):
                slot = ch * 4 + c
                stt(ug.ap[:, c, :], ug.ap[:, c, :], 1.0, H1B.ap[:, i, :], ALU.mult, ALU.mult, [ug, H1B], [ug, Adot], accum=Adot.ap[:, slot:slot + 1])
            sl = slice(ch * 4, ch * 4 + 4)
            act(AW.ap[:, sl], Adot.ap[:, sl], AF.Gelu, [Adot], [AW])
            tt(AW.ap[:, sl], AW.ap[:, sl], GT_all.ap[:, i, sl], ALU.mult, [AW, GT_all], [AW])
            for c in range(4):
                slot = ch * 4 + c
                dg = DG.next()
                ts(dg.ap, ident.ap, AW.ap[:, slot:slot + 1], None, ALU.mult, None, [ident, AW], [dg])
                for nf in range(4):
                    mm(PF[nf].ap, dg.ap, vg.ap[:, c, nf * 512:(nf + 1) * 512], slot == 0, slot == NSL - 1, [dg, vg], [PF[nf]])
        for nf in range(4):
            Hs = H1.ap[:, i, nf * 512:(nf + 1) * 512]
            tt(Hs, Hs, PF[nf].ap, ALU.add, [H1, PF[nf]], [H1])
        layer_norm_rows(H1.ap[:, i, :], H1, None, None, stats2, mv2, rstd2)
        for hf in range(2):
            S.dma("sp", Gh, Gh.ap, None, ln2g[:, hf * 1024:(hf + 1) * 1024].partition_broadcast(128))
            S.dma("sp", Bh, Bh.ap, None, ln2b[:, hf * 1024:(hf + 1) * 1024].partition_broadcast(128))
            Xh = H1.ap[:, i, hf * 1024:(hf + 1) * 1024]
            tt(Xh, Xh, Gh.ap, ALU.mult, [H1, Gh], [H1])
            tt(Xh, Xh, Bh.ap, ALU.add, [H1, Bh], [H1])
        S.dma("sp", D_out, out_d[i * 128:(i + 1) * 128, :], H1, H1.ap[:, i, :])
    S.finish(final_bufs)
    return nc


def own_rows(j):
    return np.concatenate([np.arange((4 * i + j) * 128, (4 * i + j + 1) * 128) for i in range(8)])


def core_inputs(inp, c):
    b, j = c // 4, c % 4
    own = own_rows(j)
    f = lambda a: np.ascontiguousarray(a, dtype=np.float32)
    x = inp["x"][b]
    negmask = np.zeros((128, 4, 128), np.float32)
    sel = np.zeros((128, 4, 128), np.float32)
    kk = np.arange(128)[:, None]; qq = np.arange(128)[None, :]
    for m in range(4):
        if m == j:
            negmask[:, m, :] = np.where(kk <= qq, 0.0, NEG)
            sel[:, m, :] = np.eye(128, dtype=np.float32)
        elif m > j:
            negmask[:, m, :] = NEG
    inv_freq = (1.0 / (10000.0 ** (np.arange(0, 64, 2, dtype=np.float32) / 64))).astype(np.float32)
    invf = np.concatenate([inv_freq, inv_freq])[:, None]
    sgn = np.concatenate([-np.ones(32, np.float32), np.ones(32, np.float32)])[:, None]
    return {
        "xkv": f(x), "xq": f(x[own]), "pq": f(inp["p"][0, b][own]),
        "pos_kv": np.ascontiguousarray(inp["positions"][b][None, :], dtype=np.int32),
        "pos_q": np.ascontiguousarray(inp["positions"][b][own][None, :], dtype=np.int32),
        "w_in": f(inp["w_in"][0]), "w_uq": f(inp["w_uq"][0]), "w_ukv": f(inp["w_ukv"][0]),
        "w_out": f(inp["w_out"][0]), "peer_wq": f(inp["peer_wq"][0]),
        "keys1": f(inp["peer_keys1"][0]), "keys2": f(inp["peer_keys2"][0]),
        "peer_u": f(inp["peer_u"][0]), "peer_v": f(inp["peer_v"][0]),
        "ple_wgate": f(inp["ple_wgate"][0]), "ple_wproj": f(inp["ple_wproj"][0]),
        "gqT": f(inp["g_q_norm"][0].reshape(4, 128).T), "gkvT": f(inp["g_kv_norm"][0].reshape(4, 128).T),
        "bfg": f(inp["b_forget"][0][:, None]),
        "ln1g": f(inp["ln1_g"]), "ln1b": f(inp["ln1_b"]), "ln2g": f(inp["ln2_g"]), "ln2b": f(inp["ln2_b"]),
        "negmask": negmask, "sel": sel, "invf": f(invf), "sgn": f(sgn),
    }


_NC_CACHE = {}


def kernel(**inputs):
    inp = {k: np.asarray(v) for k, v in inputs.items()}
    if "nc" not in _NC_CACHE:
        _NC_CACHE["nc"] = build_nc()
    nc = _NC_CACHE["nc"]
    maps = [core_inputs(inp, c) for c in range(8)]
    res = run_bass_kernel_spmd(nc, maps, core_ids=list(range(8)))
    out = np.zeros((2, NT, D), np.float32)
    for c in range(8):
        b, j = c // 4, c % 4
        out[b, own_rows(j)] = np.asarray(res.results[c]["out"], dtype=np.float32)
    return out
```

```python
import numpy as np
from contextlib import ExitStack
import concourse.bass as bass
import concourse.mybir as mybir
from concourse.bass_utils import run_bass_kernel_spmd
from concourse.alu_op_type import AluOpType as ALU

F32 = mybir.dt.float32
BF16 = mybir.dt.bfloat16
I32 = mybir.dt.int32
U32 = mybir.dt.uint32
AF = mybir.ActivationFunctionType


class Buf:
    def __init__(self, name, ap):
        self.name = name
        self.ap = ap
        self.last_w = []
        self.reads = []
        self.dsem = None
        self.dcount = 0
        self.is_dram = False
        self.excl = False

    def __getitem__(self, k):
        return self.ap[k]


class Sched:
    ENG = ["pe", "dve", "act", "pool", "sp"]

    def __init__(self, nc):
        self.nc = nc
        self.stack = ExitStack()
        self.eobj = {"pe": nc.tensor, "dve": nc.vector, "act": nc.scalar, "pool": nc.gpsimd, "sp": nc.sync}
        self.q = {e: [] for e in self.ENG}
        self.cnt = {e: 0 for e in self.ENG}
        self.esem = {e: self.stack.enter_context(nc.semaphore("es_" + e)) for e in self.ENG}
        self.waited = {e: {} for e in self.ENG}
        self.nbuf = 0
        self.dma_tokens = []
        self.free_dsems = []

    def sbuf(self, name, shape, dtype, stack=None):
        t = (stack or self.stack).enter_context(self.nc.sbuf_tensor(name, list(shape), dtype))
        return Buf(name, t[:])

    def psum(self, name, shape, dtype, stack=None):
        t = (stack or self.stack).enter_context(self.nc.psum_tensor(name, list(shape), dtype))
        b = Buf(name, t[:])
        b.excl = True
        return b

    def dram(self, ap, name="dram"):
        b = Buf(name, ap)
        b.is_dram = True
        return b

    def _dsem(self, buf):
        if buf.dsem is None:
            self.nbuf += 1
            buf.dsem = self.stack.enter_context(self.nc.semaphore("ds%d" % self.nbuf))
        return buf.dsem

    def _deps(self, eng, r, w):
        deps = []
        for b in r:
            deps.extend(b.last_w)
        for b in w:
            deps.extend(b.last_w)
            deps.extend(b.reads)
        out = {}
        for (sem, val, e) in deps:
            if e == eng and eng in ("pe", "sp"):
                continue
            key = id(sem)
            if self.waited[eng].get(key, 0) >= val:
                continue
            if key not in out or out[key][1] < val:
                out[key] = (sem, val)
        for key, (sem, val) in out.items():
            self.waited[eng][key] = val
        return list(out.values())

    def _commit(self, tok, r, w, accumulate=False):
        for b in r:
            b.reads.append(tok)
        for b in w:
            if accumulate:
                b.last_w = [t for t in b.last_w if t[0] is not tok[0]] + [tok]
            else:
                b.last_w = [tok]
            b.reads = []

    def op(self, eng, fn, r=(), w=(), dma=False):
        r = list(r); w = list(w)
        for b in list(r):
            if b.excl:
                r.remove(b)
                if b not in w:
                    w.append(b)
        deps = self._deps(eng, r, w)
        if dma:
            wb = w[0]
            own = r[0] if (wb.is_dram and r) else wb
            sem = self._dsem(own)
            own.dcount += 16
            tok = (sem, own.dcount, "dma")
            self.q[eng].append((deps, fn, sem, 16))
            self.dma_tokens.append(tok)
            self._commit(tok, r, w, accumulate=wb.is_dram)
            return
        else:
            self.cnt[eng] += 1
            tok = (self.esem[eng], self.cnt[eng], eng)
            self.q[eng].append((deps, fn, self.esem[eng], 1))
        self._commit(tok, r, w)

    def dma(self, eng, wbuf, out_ap, rbuf, in_ap, **kw):
        r = [rbuf] if rbuf is not None else []
        self.op(eng, lambda e: e.dma_start(out=out_ap, in_=in_ap, **kw), r=r, w=[wbuf], dma=True)

    def barrier(self):
        toks = [(self.esem[e], self.cnt[e], e) for e in self.ENG if self.cnt[e] > 0]
        toks += self.dma_tokens
        self.dma_tokens = []
        for eng in self.ENG:
            out = {}
            for (sem, val, e) in toks:
                if e == eng:
                    continue
                key = id(sem)
                if self.waited[eng].get(key, 0) >= val:
                    continue
                if key not in out or out[key][1] < val:
                    out[key] = (sem, val)
            for key, (sem, val) in out.items():
                self.waited[eng][key] = val
            if out:
                self.q[eng].append((list(out.values()), None, None, 0))

    def finish(self, out_bufs):
        deps = []
        for b in out_bufs:
            for t in b.last_w:
                deps.append((t[0], t[1]))
        self.q["sp"].append((deps, None, None, 0))
        with self.nc.Block() as block:
            def mk(eng):
                def body(e):
                    for (deps, fn, sem, inc) in self.q[eng]:
                        for (s, v) in deps:
                            e.wait_ge(s, v)
                        if fn is not None:
                            ins = fn(e)
                            ins.then_inc(sem, inc)
                return body
            block.tensor(mk("pe"))
            block.vector(mk("dve"))
            block.scalar(mk("act"))
            block.gpsimd(mk("pool"))
            block.sync(mk("sp"))
        self.stack.close()


D = 2048
NT = 4096
NQ = 1024
TT = 256
ALPHA = 2.0 ** 0.25
EPS = 1e-6
PI = float(np.pi)
SC_MLA = 192.0 ** -0.5
SC_FOX = 128.0 ** -0.5
NEG = -1.0e30


class RR:
    def __init__(self, items):
        self.items = list(items); self.i = 0

    def next(self):
        x = self.items[self.i % len(self.items)]; self.i += 1
        return x


class Region:
    def __init__(self, arena, start_kb, end_kb):
        self.arena = arena; self.off = int(start_kb * 1024); self.end = int(end_kb * 1024)

    def alloc(self, name, shape, dtype):
        n = int(np.prod(shape[1:]))
        esz = 2 if dtype == BF16 else 4
        nb = (n * esz + 63) // 64 * 64
        assert self.off + nb <= self.end, (name, self.off, nb, self.end)
        o = self.off // 2
        ap = self.arena[0:shape[0], o:o + n * esz // 2]
        if dtype != BF16:
            ap = ap.bitcast(dtype)
        if len(shape) == 3:
            ap = ap.rearrange("p (a b) -> p a b", b=shape[2])
        elif len(shape) == 4:
            ap = ap.rearrange("p (a b c) -> p a b c", b=shape[2], c=shape[3])
        self.off += nb
        return Buf(name, ap)


import os
SECT = os.environ.get('SECT', 'mla,rope,fox,forget').split(',')
NTILES = int(os.environ.get('NTILES', '0'))


def build_nc(debug=False, stop_after=99):
    nc = bass.Bass("TRN2", target_bir_lowering=False)
    dbg_outs = {}

    def IN(name, shape, dtype=F32):
        return nc.dram_tensor(name, list(shape), dtype, kind="ExternalInput").ap()

    def SCR(name, shape, dtype):
        return nc.dram_tensor(name, list(shape), dtype, kind=("ExternalOutput" if debug else "Internal")).ap()

    xkv = IN("xkv", [NT, D]); xq = IN("xq", [NQ, D]); pq = IN("pq", [NQ, 256])
    pos_kv = IN("pos_kv", [1, NT], I32); pos_q = IN("pos_q", [1, NQ], I32)
    w_in = IN("w_in", [D, 4168]); w_uq = IN("w_uq", [512, 1536]); w_ukv = IN("w_ukv", [512, 2048])
    w_out = IN("w_out", [D, D]); peer_wq = IN("peer_wq", [D, 1024])
    keys1 = IN("keys1", [8, 128, 64]); keys2 = IN("keys2", [8, 128, 64])
    peer_u = IN("peer_u", [16384, D]); peer_v = IN("peer_v", [16384, D])
    ple_wgate = IN("ple_wgate", [D, D]); ple_wproj = IN("ple_wproj", [256, D])
    gqT = IN("gqT", [128, 4]); gkvT = IN("gkvT", [128, 4]); bfg = IN("bfg", [8, 1])
    ln1g = IN("ln1g", [1, D]); ln1b = IN("ln1b", [1, D]); ln2g = IN("ln2g", [1, D]); ln2b = IN("ln2b", [1, D])
    negmask_in = IN("negmask", [128, 4, 128]); sel_in = IN("sel", [128, 4, 128])
    invf_in = IN("invf", [64, 1]); sgn_in = IN("sgn", [64, 1])
    out_d = nc.dram_tensor("out", [NQ, D], F32, kind="ExternalOutput").ap()

    kTm_d = SCR("kTm_d", [8, 128, NT], BF16); krT_d = SCR("krT_d", [64, NT], BF16)
    vm_d = SCR("vm_d", [8, 128, 32, 128], BF16)
    kTf_d = SCR("kTf_d", [8, 128, NT], BF16); vf_d = SCR("vf_d", [8, 128, 32, 128], BF16)
    UVb_d = nc.dram_tensor("UVb_d", [16384, 2, D], BF16, kind="Internal").ap()

    S = Sched(nc)
    arena = S.stack.enter_context(nc.sbuf_tensor("arena", [128, 95 * 1024], BF16))
    D_kTm = S.dram(kTm_d); D_krT = S.dram(krT_d); D_vm = S.dram(vm_d); D_kTf = S.dram(kTf_d); D_vf = S.dram(vf_d)
    D_out = S.dram(out_d)
    D_UV = S.dram(UVb_d)
    final_bufs = [D_out]

    def dbg(name, buf, shape, dtype):
        if not debug:
            return
        t = nc.dram_tensor("dbg_" + name, list(shape), dtype, kind="ExternalOutput").ap()
        b = S.dram(t)
        S.dma("sp", b, t, buf, buf.ap)
        final_bufs.append(b)

    def mm(out, lhsT, rhs, start, stop, r, w):
        S.op("pe", lambda e: e.matmul(out, lhsT, rhs, start=start, stop=stop), r=r, w=w)

    def tr(out, in_, idn, r, w):
        S.op("pe", lambda e: e.transpose(out, in_, idn), r=r, w=w)

    def act(out, in_, func, r, w, bias=None, scale=None, accum=None):
        kw = {}
        if bias is not None: kw["bias"] = bias
        if scale is not None: kw["scale"] = scale
        if accum is not None: kw["accum_out"] = accum
        S.op("act", lambda e: e.activation(out, in_, func, **kw), r=r, w=w)

    def tt(out, a, b, op, r, w, eng="dve"):
        S.op(eng, lambda e: e.tensor_tensor(out, a, b, op), r=r, w=w)

    def ts(out, a, s1, s2, op0, op1, r, w, eng="dve", accum=None):
        if op1 is None:
            S.op(eng, lambda e: e.tensor_scalar(out, a, s1, None, op0), r=r, w=w)
        elif accum is not None:
            S.op(eng, lambda e: e.tensor_scalar(out, a, s1, s2, op0, op1, accum_out=accum), r=r, w=w)
        else:
            S.op(eng, lambda e: e.tensor_scalar(out, a, s1, s2, op0, op1), r=r, w=w)

    def stt(out, a, sc, b, op0, op1, r, w, accum=None):
        if accum is None:
            S.op("dve", lambda e: e.scalar_tensor_tensor(out=out, in0=a, scalar=sc, in1=b, op0=op0, op1=op1), r=r, w=w)
        else:
            S.op("dve", lambda e: e.scalar_tensor_tensor(out=out, in0=a, scalar=sc, in1=b, op0=op0, op1=op1, accum_out=accum), r=r, w=w)

    def cp(out, in_, r, w, eng="dve"):
        if eng == "act":
            S.op("act", lambda e: e.copy(out, in_), r=r, w=w)
        else:
            S.op(eng, lambda e: e.tensor_copy(out, in_), r=r, w=w)

    evac_i = [0]

    def evac(out, in_, r, w):
        evac_i[0] += 1
        cp(out, in_, r, w, eng=("act" if evac_i[0] % 2 else "dve"))

    PF = [S.psum("pf%d" % i, [128, 512], F32) for i in range(6)]
    PB = [S.psum("pb%d" % i, [128, 1024], BF16) for i in range(2)]
    PFR = RR(PF); PBR = RR(PB)

    RC = Region(arena, 0, 8)
    identf = RC.alloc("identf", [128, 128], F32)
    ident = RC.alloc("ident", [128, 128], BF16)
    ones_bf = RC.alloc("ones_bf", [128, 128], BF16)
    ones_f = RC.alloc("ones_f", [128, 256], F32)
    epsT = RC.alloc("epsT", [128, 1], F32)
    oneT = RC.alloc("oneT", [128, 1], F32)
    invf = RC.alloc("invf", [64, 1], F32)
    sgn = RC.alloc("sgn", [64, 1], F32)
    negb = RC.alloc("negb", [8, 1], F32)
    gkv = RC.alloc("gkv", [128, 4], F32)
    gq = RC.alloc("gq", [128, 4], F32)
    negF_tok = RC.alloc("negF_tok", [128, 32, 8], F32)
    negmask = RC.alloc("negmask", [128, 4, 128], BF16)
    EI = RC.alloc("EI", [128, 128], I32)
    GT = RC.alloc("GT", [128, 128], F32)

    S.op("pool", lambda e: e.memset(identf.ap, 0.0), w=[identf])
    S.op("pool", lambda e: e.affine_select(out=identf.ap, in_=identf.ap, pattern=[[-1, 128]], compare_op=ALU.not_equal,
                                           fill=1.0, base=0, channel_multiplier=1), r=[identf], w=[identf])
    cp(ident.ap, identf.ap, [identf], [ident])
    S.op("dve", lambda e: e.memset(ones_bf.ap, 1.0), w=[ones_bf])
    S.op("dve", lambda e: e.memset(ones_f.ap, 1.0), w=[ones_f])
    S.op("dve", lambda e: e.memset(epsT.ap, EPS), w=[epsT])
    S.op("dve", lambda e: e.memset(oneT.ap, 1.0), w=[oneT])
    S.dma("sp", invf, invf.ap, None, invf_in)
    S.dma("sp", sgn, sgn.ap, None, sgn_in)
    S.dma("sp", negb, negb.ap, None, bfg)
    ts(negb.ap, negb.ap, -1.0, None, ALU.mult, None, [negb], [negb])
    S.dma("sp", gkv, gkv.ap, None, gkvT)
    S.dma("sp", gq, gq.ap, None, gqT)
    S.dma("pool", negmask, negmask.ap, None, negmask_in)

    w_in_v = w_in.rearrange("(c p) n -> p c n", p=128)

    def rope_tables(R, pos_ap, c0, n, tag):
        posi = R.alloc("posi" + tag, [64, n], I32)
        posf = R.alloc("posf" + tag, [64, n], F32)
        ang = R.alloc("ang" + tag, [64, n], F32)
        tq = R.alloc("tq" + tag, [64, n], F32)
        ki = R.alloc("ki" + tag, [64, n], I32)
        cosb = R.alloc("cos" + tag, [64, n], F32)
        sinb = R.alloc("sin" + tag, [64, n], F32)

        def emit(c0):
            S.dma("sp", posi, posi.ap, None, pos_ap[0:1, c0:c0 + n].partition_broadcast(64))
            cp(posf.ap, posi.ap, [posi], [posf])
            for (phase, dst) in ((0.0, sinb), (PI / 2, cosb)):
                ts(ang.ap, posf.ap, invf.ap[:, 0:1], phase, ALU.mult, ALU.add, [posf, invf], [ang])
                ts(tq.ap, ang.ap, 1.0 / (2 * PI), None, ALU.mult, None, [ang], [tq])
                cp(ki.ap, tq.ap, [tq], [ki])
                cp(tq.ap, ki.ap, [ki], [tq])
                stt(ang.ap, tq.ap, -2 * PI, ang.ap, ALU.mult, ALU.add, [tq, ang], [ang])
                ts(tq.ap, ang.ap, PI, -2 * PI, ALU.is_gt, ALU.mult, [ang], [tq])
                tt(ang.ap, ang.ap, tq.ap, ALU.add, [ang, tq], [ang])
                ts(tq.ap, ang.ap, -PI, 2 * PI, ALU.is_lt, ALU.mult, [ang], [tq])
                tt(ang.ap, ang.ap, tq.ap, ALU.add, [ang, tq], [ang])
                ts(ang.ap, ang.ap, PI, -PI, ALU.min, ALU.max, [ang], [ang])
                act(dst.ap, ang.ap, AF.Sin, [ang], [dst])
            ts(sinb.ap, sinb.ap, sgn.ap[:, 0:1], None, ALU.mult, None, [sinb, sgn], [sinb])
        return cosb, sinb, emit

    def transpose_in(xb, xT, ns):
        for ck in range(16):
            pb = PBR.next()
            for s_ in range(ns):
                tr(pb.ap[:, s_ * 128:(s_ + 1) * 128], xb.ap[:, s_, ck * 128:(ck + 1) * 128], ident.ap, [xb, ident], [pb])
            evac(xT.ap[:, ck, :], pb.ap[:, 0:ns * 128], [pb], [xT])

    def proj_rms(R, Wt, col0, xT, n, tag):
        raw = R.alloc("raw" + tag, [128, 4, n], F32)
        sq = R.alloc("sq" + tag, [128, 4, n], BF16)
        nrm = R.alloc("nrm" + tag, [128, 4, n], BF16)
        Rs = R.alloc("Rs" + tag, [128, n], F32)

        def emit():
            for fc in range(4):
                pf = PFR.next()
                for ck in range(16):
                    mm(pf.ap[:, 0:n], Wt.ap[:, ck, col0 + fc * 128:col0 + (fc + 1) * 128], xT.ap[:, ck, :], ck == 0, ck == 15, [Wt, xT], [pf])
                act(sq.ap[:, fc, :], pf.ap[:, 0:n], AF.Square, [pf], [sq])
                cp(raw.ap[:, fc, :], pf.ap[:, 0:n], [pf], [raw])
            pf = PFR.next()
            for fc in range(4):
                mm(pf.ap[:, 0:n], ones_bf.ap, sq.ap[:, fc, :], fc == 0, fc == 3, [ones_bf, sq], [pf])
            act(Rs.ap, pf.ap[:, 0:n], AF.Sqrt, [pf, epsT], [Rs], bias=epsT.ap[:, 0:1], scale=1.0 / 512)
            S.op("dve", lambda e: e.reciprocal(Rs.ap, Rs.ap), r=[Rs], w=[Rs])
            for fc in range(4):
                tt(nrm.ap[:, fc, :], raw.ap[:, fc, :], Rs.ap, ALU.mult, [raw, Rs], [nrm])
        return nrm, emit

    R1 = Region(arena, 8, 190)
    Frow = R1.alloc("Frow", [8, NT], F32)
    Wm = R1.alloc("Wm", [128, 16, 576], BF16)
    Wf = R1.alloc("Wf", [128, 16, 2056], BF16)
    Wukv = R1.alloc("Wukv", [128, 4, 2048], BF16)
    Wkrsw = R1.alloc("Wkrsw", [128, 16, 64], BF16)
    mark = R1.off
    stg = R1.alloc("stg", [128, 4, 2048], F32)
    S.dma("pool", Wm, Wm.ap, None, w_in_v[:, :, 512:1088])
    S.dma("pool", Wf, Wf.ap[:, :, 0:1028], None, w_in_v[:, :, 2112:3140])
    S.dma("pool", Wf, Wf.ap[:, :, 1028:2056], None, w_in_v[:, :, 3140:4168])
    S.dma("sp", stg, stg.ap, None, w_ukv.rearrange("(c p) n -> p c n", p=128))
    for fc in range(4):
        ts(Wukv.ap[:, fc, :], stg.ap[:, fc, :], gkv.ap[:, fc:fc + 1], None, ALU.mult, None, [stg, gkv], [Wukv])
    cp(Wkrsw.ap[:, :, 0:32], Wm.ap[:, :, 544:576], [Wm], [Wkrsw])
    cp(Wkrsw.ap[:, :, 32:64], Wm.ap[:, :, 512:544], [Wm], [Wkrsw])
    S.barrier()
    R1.off = mark
    ns = TT // 128
    xb = R1.alloc("xb", [128, ns, D], BF16)
    xT = R1.alloc("xT", [128, 16, TT], BF16)
    ckvn, emit_ckv = proj_rms(R1, Wm, 0, xT, TT, "kv")
    kn_st = RR([R1.alloc("kn_st%d" % i, [128, 8, TT], BF16) for i in range(2)])
    fk_st = RR([R1.alloc("fk_st%d" % i, [128, 8, TT], BF16) for i in range(2)])
    v_st = R1.alloc("v_st", [128, ns, 1024], BF16)
    fv_st = R1.alloc("fv_st", [128, ns, 1024], BF16)
    cosb, sinb, emit_rope = rope_tables(R1, pos_kv, 0, TT, "kv")
    T1 = R1.alloc("T1", [64, TT], F32); T2 = R1.alloc("T2", [64, TT], F32)
    kr_st = R1.alloc("kr_st", [64, TT], BF16)
    exb = R1.alloc("exb", [8, TT], F32); lnb = R1.alloc("lnb", [8, TT], F32)

    def sec_load(T, t0):
        S.dma("pool", xb, xb.ap, None, xkv[t0:t0 + TT, :].rearrange("(s p) d -> p s d", p=128))
        transpose_in(xb, xT, ns)
    MLAK = int(os.environ.get('MLAK', '9'))

    def sec_mla(T, t0):
        emit_ckv()
        if MLAK < 2: return
        kst = kn_st.next()
        for h in range(8):
            pf = PFR.next()
            for fc in range(4):
                mm(pf.ap[:, 0:TT], Wukv.ap[:, fc, h * 256:h * 256 + 128], ckvn.ap[:, fc, :], fc == 0, fc == 3, [Wukv, ckvn], [pf])
            evac(kst.ap[:, h, :], pf.ap[:, 0:TT], [pf], [kst])
        S.dma("sp", D_kTm, kTm_d.rearrange("h d t -> d h t")[:, :, t0:t0 + TT], kst, kst.ap)
        if MLAK < 3: return
        for s_ in range(ns):
            for hh in range(2):
                pf = PFR.next()
                for fc in range(4):
                    rhs = Wukv.ap[:, fc, :].rearrange("p (h c) -> p h c", c=256)[:, hh * 4:(hh + 1) * 4, 128:256]
                    mm(pf.ap, ckvn.ap[:, fc, s_ * 128:(s_ + 1) * 128], rhs, fc == 0, fc == 3, [Wukv, ckvn], [pf])
                evac(v_st.ap[:, s_, hh * 512:(hh + 1) * 512], pf.ap, [pf], [v_st])
            blk = T * ns + s_
            S.dma("sp", D_vm, vm_d.rearrange("h p b d -> p b h d")[:, blk, :, :], v_st,
                  v_st.ap[:, s_, :].rearrange("p (h d) -> p h d", d=128))
    def sec_rope(T, t0):
        pk = PFR.next(); pks = PFR.next()
        for ck in range(16):
            mm(pk.ap[0:64, 0:TT], Wm.ap[:, ck, 512:576], xT.ap[:, ck, :], ck == 0, ck == 15, [Wm, xT], [pk])
        for ck in range(16):
            mm(pks.ap[0:64, 0:TT], Wkrsw.ap[:, ck, :], xT.ap[:, ck, :], ck == 0, ck == 15, [Wkrsw, xT], [pks])
        emit_rope(t0)
        tt(T1.ap, pk.ap[0:64, 0:TT], cosb.ap, ALU.mult, [pk, cosb], [T1])
        tt(T2.ap, pks.ap[0:64, 0:TT], sinb.ap, ALU.mult, [pks, sinb], [T2])
        tt(kr_st.ap, T1.ap, T2.ap, ALU.add, [T1, T2], [kr_st])
        S.dma("sp", D_krT, krT_d[:, t0:t0 + TT], kr_st, kr_st.ap)
    def sec_fox(T, t0):
        fst = fk_st.next()
        for h in range(8):
            pf = PFR.next()
            for ck in range(16):
                mm(pf.ap[:, 0:TT], Wf.ap[:, ck, h * 128:(h + 1) * 128], xT.ap[:, ck, :], ck == 0, ck == 15, [Wf, xT], [pf])
            evac(fst.ap[:, h, :], pf.ap[:, 0:TT], [pf], [fst])
        S.dma("sp", D_kTf, kTf_d.rearrange("h d t -> d h t")[:, :, t0:t0 + TT], fst, fst.ap)
        for s_ in range(ns):
            for hh in range(2):
                pf = PFR.next()
                for ck in range(16):
                    mm(pf.ap, xT.ap[:, ck, s_ * 128:(s_ + 1) * 128], Wf.ap[:, ck, 1024 + hh * 512:1024 + (hh + 1) * 512], ck == 0, ck == 15, [Wf, xT], [pf])
                evac(fv_st.ap[:, s_, hh * 512:(hh + 1) * 512], pf.ap, [pf], [fv_st])
            blk = T * ns + s_
            S.dma("sp", D_vf, vf_d.rearrange("h p b d -> p b h d")[:, blk, :, :], fv_st,
                  fv_st.ap[:, s_, :].rearrange("p (h d) -> p h d", d=128))
    def sec_forget(T, t0):
        pff = PFR.next()
        for ck in range(16):
            mm(pff.ap[0:8, 0:TT], Wf.ap[:, ck, 2048:2056], xT.ap[:, ck, :], ck == 0, ck == 15, [Wf, xT], [pff])
        act(exb.ap, pff.ap[0:8, 0:TT], AF.Exp, [pff, negb], [exb], bias=negb.ap[:, 0:1], scale=-1.0)
        act(lnb.ap, exb.ap, AF.Ln, [exb, oneT], [lnb], bias=oneT.ap[0:8, 0:1])
        init = 0.0 if T == 0 else Frow.ap[:, t0 - 1:t0]
        S.op("dve", lambda e, init=init, t0=t0: e.tensor_tensor_scan(Frow.ap[:, t0:t0 + TT], ones_f.ap[0:8, 0:TT], lnb.ap, init, ALU.mult, ALU.subtract),
             r=[Frow, ones_f, lnb], w=[Frow])
        for s_ in range(ns):
            blk = T * ns + s_
            pf = PFR.next()
            tr(pf.ap[:, 0:8], Frow.ap[0:8, blk * 128:(blk + 1) * 128], identf.ap[0:8, 0:8], [Frow, identf], [pf])
            ts(negF_tok.ap[:, blk, :], pf.ap[:, 0:8], -1.0, None, ALU.mult, None, [pf], [negF_tok])

    for T in range(NTILES or (NT // TT)):
        t0 = T * TT
        sec_load(T, t0)
        r0 = T * 1024
        S.dma("pool", D_UV, UVb_d[r0:r0 + 1024, 0, :], None, peer_u[r0:r0 + 1024, :])
        S.dma("pool", D_UV, UVb_d[r0:r0 + 1024, 1, :], None, peer_v[r0:r0 + 1024, :])
        if 'mla' in SECT: sec_mla(T, t0)
        if 'rope' in SECT: sec_rope(T, t0)
        if 'fox' in SECT: sec_fox(T, t0)
        if 'forget' in SECT: sec_forget(T, t0)
    dbg("negF", negF_tok, [128, 32, 8], F32)
    dbg("Frow", Frow, [8, NT], F32)
    if stop_after <= 1:
        final_bufs.extend([D_kTm, D_krT, D_vm, D_kTf, D_vf])
        S.finish(final_bufs)
        return nc

    S.barrier()
    RP = Region(arena, 24, 88)
    QN = RP.alloc("QN", [128, 8, NQ], BF16)
    QR = RP.alloc("QR", [64, 8, NQ], BF16)
    FQ = RP.alloc("FQ", [128, 8, NQ], BF16)
    Fq_row = RP.alloc("Fq_row", [1, 8, NQ], BF16)
    R2 = Region(arena, 88, 190)
    Wqc = R2.alloc("Wqc", [128, 16, 512], BF16)
    Wqf = R2.alloc("Wqf", [128, 16, 1024], BF16)
    Wuq = R2.alloc("Wuq", [128, 4, 1536], BF16)
    Wuqsw = R2.alloc("Wuqsw", [128, 4, 8, 64], BF16)
    mark = R2.off
    stg2 = R2.alloc("stg2", [128, 4, 1536], F32)
    S.dma("pool", Wqc, Wqc.ap, None, w_in_v[:, :, 0:512])
    S.dma("pool", Wqf, Wqf.ap, None, w_in_v[:, :, 1088:2112])
    S.dma("sp", stg2, stg2.ap, None, w_uq.rearrange("(c p) n -> p c n", p=128))
    for fc in range(4):
        ts(Wuq.ap[:, fc, :], stg2.ap[:, fc, :], gq.ap[:, fc:fc + 1], None, ALU.mult, None, [stg2, gq], [Wuq])
    for fc in range(4):
        src = Wuq.ap[:, fc, :].rearrange("p (h c) -> p h c", c=192)
        cp(Wuqsw.ap[:, fc, :, 0:32], src[:, :, 160:192], [Wuq], [Wuqsw])
        cp(Wuqsw.ap[:, fc, :, 32:64], src[:, :, 128:160], [Wuq], [Wuqsw])
    S.barrier()
    R2.off = mark
    xb2 = R2.alloc("xb2", [128, ns, D], BF16)
    xT2 = R2.alloc("xT2", [128, 16, TT], BF16)
    cqn, emit_cq = proj_rms(R2, Wqc, 0, xT2, TT, "q")
    cosq, sinq, emit_ropeq = rope_tables(R2, pos_q, 0, TT, "q")
    T1q = R2.alloc("T1q", [64, TT], F32); T2q = R2.alloc("T2q", [64, TT], F32)
    selb = R2.alloc("selb", [128, 4, 128], F32)
    S.dma("sp", selb, selb.ap, None, sel_in)
    for T in range(NQ // TT):
        t0 = T * TT
        S.dma("pool", xb2, xb2.ap, None, xq[t0:t0 + TT, :].rearrange("(s p) d -> p s d", p=128))
        transpose_in(xb2, xT2, ns)
        emit_cq()
        emit_ropeq(t0)
        for h in range(8):
            pf = PFR.next()
            for fc in range(4):
                mm(pf.ap[:, 0:TT], Wuq.ap[:, fc, h * 192:h * 192 + 128], cqn.ap[:, fc, :], fc == 0, fc == 3, [Wuq, cqn], [pf])
            evac(QN.ap[:, h, t0:t0 + TT], pf.ap[:, 0:TT], [pf], [QN])
            pk = PFR.next(); pks = PFR.next()
            for fc in range(4):
                mm(pk.ap[0:64, 0:TT], Wuq.ap[:, fc, h * 192 + 128:h * 192 + 192], cqn.ap[:, fc, :], fc == 0, fc == 3, [Wuq, cqn], [pk])
            for fc in range(4):
                mm(pks.ap[0:64, 0:TT], Wuqsw.ap[:, fc, h, :], cqn.ap[:, fc, :], fc == 0, fc == 3, [Wuqsw, cqn], [pks])
            tt(T1q.ap, pk.ap[0:64, 0:TT], cosq.ap, ALU.mult, [pk, cosq], [T1q])
            tt(T2q.ap, pks.ap[0:64, 0:TT], sinq.ap, ALU.mult, [pks, sinq], [T2q])
            tt(QR.ap[:, h, t0:t0 + TT], T1q.ap, T2q.ap, ALU.add, [T1q, T2q], [QR])
            pf = PFR.next()
            for ck in range(16):
                mm(pf.ap[:, 0:TT], Wqf.ap[:, ck, h * 128:(h + 1) * 128], xT2.ap[:, ck, :], ck == 0, ck == 15, [Wqf, xT2], [pf])
            evac(FQ.ap[:, h, t0:t0 + TT], pf.ap[:, 0:TT], [pf], [FQ])
    for i in range(8):
        for h in range(8):
            pf = PFR.next()
            for m in range(4):
                mm(pf.ap[0:1, 0:128], negF_tok.ap[:, 4 * i + m, h:h + 1], selb.ap[:, m, :], m == 0, m == 3, [negF_tok, selb], [pf])
            ts(Fq_row.ap[0:1, h, i * 128:(i + 1) * 128], pf.ap[0:1, 0:128], -1.0 / SC_FOX, None, ALU.mult, None, [pf], [Fq_row])
    dbg("QN", QN, [128, 8, NQ], BF16)
    dbg("QR", QR, [64, 8, NQ], BF16)
    dbg("FQ", FQ, [128, 8, NQ], BF16)
    dbg("Fqrow", Fq_row, [1, 8, NQ], BF16)
    if stop_after <= 2:
        S.finish(final_bufs)
        return nc

    S.barrier()
    OT = Region(arena, 158, 190).alloc("OT", [128, 16, NQ], BF16)
    R3 = Region(arena, 88, 158)
    KT = RR([R3.alloc("KT%d" % i, [128, NT], BF16) for i in range(2)])
    KR = R3.alloc("KR", [64, NT], BF16)
    VV = RR([R3.alloc("V%d" % i, [128, 32, 129], BF16) for i in range(2)])
    PT = RR([R3.alloc("PT%d" % i, [128, 512], BF16) for i in range(4)])
    OTl = RR([R3.alloc("Otl%d" % i, [128, 128], BF16) for i in range(2)])
    rec = RR([R3.alloc("rec%d" % i, [128, 1], F32) for i in range(2)])
    for v in VV.items:
        S.op("dve", lambda e, v=v: e.memset(v.ap[:, :, 128:129], 1.0), w=[v])
    S.dma("sp", KR, KR.ap, D_krT, krT_d)
    PS = RR(PF[0:4]); PO = RR(PF[4:6])
    NHD = int(os.environ.get("NHD", "16"))
    for hd in range(NHD):
        mla = hd < 8; h = hd % 8
        kt = KT.next(); v = VV.next()
        S.dma("sp", kt, kt.ap, (D_kTm if mla else D_kTf), (kTm_d if mla else kTf_d)[h])
        S.dma("sp", v, v.ap[:, :, 0:128], (D_vm if mla else D_vf), (vm_d if mla else vf_d)[h])
        for i in range(8):
            po = PO.next()
            q0 = i * 128
            nkb = 4 * i + 4
            for g in range(i + 1):
                ps = PS.next(); pt = PT.next()
                diag = (g == i)
                for m in range(4):
                    kb = 4 * g + m
                    out = ps.ap[:, m * 128:(m + 1) * 128]
                    if mla:
                        mm(out, kt.ap[:, kb * 128:(kb + 1) * 128], QN.ap[:, h, q0:q0 + 128], True, False, [kt, QN], [ps])
                        mm(out, KR.ap[0:64, kb * 128:(kb + 1) * 128], QR.ap[0:64, h, q0:q0 + 128], False, not diag, [KR, QR], [ps])
                    else:
                        mm(out, kt.ap[:, kb * 128:(kb + 1) * 128], FQ.ap[:, h, q0:q0 + 128], True, False, [kt, FQ], [ps])
                        mm(out, ones_bf.ap[0:1, 0:128], Fq_row.ap[0:1, h, q0:q0 + 128], False, not diag, [ones_bf, Fq_row], [ps])
                    if diag:
                        mm(out, ident.ap, negmask.ap[:, m, :], False, True, [ident, negmask], [ps])
                if mla:
                    act(pt.ap, ps.ap, AF.Exp, [ps], [pt], scale=SC_MLA)
                else:
                    for m in range(4):
                        act(pt.ap[:, m * 128:(m + 1) * 128], ps.ap[:, m * 128:(m + 1) * 128], AF.Exp, [ps, negF_tok], [pt],
                            bias=negF_tok.ap[:, 4 * g + m, h:h + 1], scale=SC_FOX)
                for m in range(4):
                    kb = 4 * g + m
                    mm(po.ap[:, 0:129], pt.ap[:, m * 128:(m + 1) * 128], v.ap[:, kb, :], kb == 0, kb == nkb - 1, [pt, v], [po])
            r_ = rec.next(); ot = OTl.next()
            S.op("dve", lambda e, r_=r_, po=po: e.reciprocal(r_.ap, po.ap[:, 128:129]), r=[po], w=[r_])
            ts(ot.ap, po.ap[:, 0:128], r_.ap[:, 0:1], None, ALU.mult, None, [po, r_], [ot])
            pb = PBR.next()
            tr(pb.ap[:, 0:128], ot.ap, ident.ap, [ot, ident], [pb])
            evac(OT.ap[:, hd, q0:q0 + 128], pb.ap[:, 0:128], [pb], [OT])
    dbg("OT", OT, [128, 16, NQ], BF16)
    if stop_after <= 3:
        S.finish(final_bufs)
        return nc

    S.barrier()
    H1 = Region(arena, 8, 72).alloc("H1", [128, 8, D], F32)
    R4 = Region(arena, 72, 158)
    WoutC = RR([R4.alloc("WoutC%d" % i, [128, 16, 512], BF16) for i in range(2)])
    xqt = RR([R4.alloc("xqt%d" % i, [128, D], F32) for i in range(2)])
    Gbc = R4.alloc("Gbc", [128, D], F32)
    Bbc = R4.alloc("Bbc", [128, D], F32)
    stats = R4.alloc("stats", [128, 4, 6], F32)
    mv = R4.alloc("mv", [128, 2], F32)
    rstd = R4.alloc("rstd", [128, 1], F32)
    w_out_v = w_out.rearrange("(c p) n -> p c n", p=128)

    def layer_norm_rows(Xap, Xbuf, Gb, Bb, stats, mv, rstd):
        for c4 in range(4):
            S.op("dve", lambda e, c4=c4, stats=stats: e.bn_stats(stats.ap[:, c4, :], Xap[:, c4 * 512:(c4 + 1) * 512]), r=[Xbuf], w=[stats])
        S.op("dve", lambda e, stats=stats, mv=mv: e.bn_aggr(mv.ap, stats.ap), r=[stats], w=[mv])
        act(rstd.ap, mv.ap[:, 1:2], AF.Sqrt, [mv, epsT], [rstd], bias=epsT.ap[:, 0:1])
        S.op("dve", lambda e, rstd=rstd: e.reciprocal(rstd.ap, rstd.ap), r=[rstd], w=[rstd])
        ts(Xap, Xap, mv.ap[:, 0:1], rstd.ap[:, 0:1], ALU.subtract, ALU.mult, [Xbuf, mv, rstd], [Xbuf])
        if Gb is not None:
            tt(Xap, Xap, Gb.ap, ALU.mult, [Xbuf, Gb], [Xbuf])
            tt(Xap, Xap, Bb.ap, ALU.add, [Xbuf, Bb], [Xbuf])

    S.dma("sp", Gbc, Gbc.ap, None, ln1g.partition_broadcast(128))
    S.dma("sp", Bbc, Bbc.ap, None, ln1b.partition_broadcast(128))
    for nf in range(4):
        wc = WoutC.next()
        for half in range(2):
            S.dma("pool", wc, wc.ap[:, half * 8:(half + 1) * 8, :], None, w_out_v[:, half * 8:(half + 1) * 8, nf * 512:(nf + 1) * 512])
        for i in range(8):
            if nf == 0:
                xt_ = xqt.next()
                S.dma("sp", xt_, xt_.ap, None, xq[i * 128:(i + 1) * 128, :])
                S.op("act", lambda e, xt_=xt_, i=i: e.mul(H1.ap[:, i, :], xt_.ap, ALPHA), r=[xt_], w=[H1])
            pf = PFR.next()
            for cc in range(16):
                mm(pf.ap, OT.ap[:, cc, i * 128:(i + 1) * 128], wc.ap[:, cc, :], cc == 0, cc == 15, [OT, wc], [pf])
            tt(H1.ap[:, i, nf * 512:(nf + 1) * 512], H1.ap[:, i, nf * 512:(nf + 1) * 512], pf.ap, ALU.add, [H1, pf], [H1])
    for i in range(8):
        layer_norm_rows(H1.ap[:, i, :], H1, Gbc, Bbc, stats, mv, rstd)
    dbg("H1", H1, [128, 8, D], F32)
    S.barrier()
    RB = Region(arena, 72, 136)
    H1B = RB.alloc("H1B", [128, 8, D], BF16)
    H1T = RB.alloc("H1T", [128, 16, NQ], BF16)
    for i in range(8):
        cp(H1B.ap[:, i, :], H1.ap[:, i, :], [H1], [H1B], eng=("act" if i % 2 else "dve"))
    for cc in range(16):
        for ig in range(2):
            pb = PBR.next()
            for k in range(4):
                i = ig * 4 + k
                tr(pb.ap[:, k * 128:(k + 1) * 128], H1B.ap[:, i, cc * 128:(cc + 1) * 128], ident.ap, [H1B, ident], [pb])
            evac(H1T.ap[:, cc, ig * 512:(ig + 1) * 512], pb.ap[:, 0:512], [pb], [H1T])
    if stop_after <= 4:
        S.finish(final_bufs)
        return nc

    R5 = Region(arena, 136, 190)
    WgC = RR([R5.alloc("WgC%d" % i, [128, 16, 512], BF16) for i in range(2)])
    Wp = R5.alloc("Wp", [128, 2, D], BF16)
    pT = R5.alloc("pT", [128, 2, NQ], BF16)
    pb16 = R5.alloc("pb16", [128, 8, 256], BF16)
    sig = RR([R5.alloc("sig%d" % i, [128, 512], F32) for i in range(2)])
    S.dma("pool", Wp, Wp.ap, None, ple_wproj.rearrange("(c p) n -> p c n", p=128))
    S.dma("pool", pb16, pb16.ap, None, pq.rearrange("(i p) d -> p i d", p=128))
    for c2 in range(2):
        for ig in range(2):
            pb = PBR.next()
            for k in range(4):
                i = ig * 4 + k
                tr(pb.ap[:, k * 128:(k + 1) * 128], pb16.ap[:, i, c2 * 128:(c2 + 1) * 128], ident.ap, [pb16, ident], [pb])
            evac(pT.ap[:, c2, ig * 512:(ig + 1) * 512], pb.ap[:, 0:512], [pb], [pT])
    wg_v = ple_wgate.rearrange("(c p) n -> p c n", p=128)
    for nf in range(4):
        wc = WgC.next()
        for half in range(2):
            S.dma("pool", wc, wc.ap[:, half * 8:(half + 1) * 8, :], None, wg_v[:, half * 8:(half + 1) * 8, nf * 512:(nf + 1) * 512])
        for i in range(8):
            pf = PFR.next()
            for cc in range(16):
                mm(pf.ap, H1T.ap[:, cc, i * 128:(i + 1) * 128], wc.ap[:, cc, :], cc == 0, cc == 15, [H1T, wc], [pf])
            sg = sig.next()
            act(sg.ap, pf.ap, AF.Sigmoid, [pf], [sg])
            pf2 = PFR.next()
            for c2 in range(2):
                mm(pf2.ap, pT.ap[:, c2, i * 128:(i + 1) * 128], Wp.ap[:, c2, nf * 512:(nf + 1) * 512], c2 == 0, c2 == 1, [pT, Wp], [pf2])
            tt(sg.ap, sg.ap, pf2.ap, ALU.mult, [sg, pf2], [sg])
            Hs = H1.ap[:, i, nf * 512:(nf + 1) * 512]
            stt(Hs, Hs, ALPHA, sg.ap, ALU.mult, ALU.add, [H1, sg], [H1])
    dbg("Y", H1, [128, 8, D], F32)
    if stop_after <= 5:
        S.finish(final_bufs)
        return nc

    S.barrier()
    RE = Region(arena, 182, 190)
    EI_all = RE.alloc("EI_all", [128, 8, 128], I32)
    GT_all = RE.alloc("GT_all", [128, 8, 128], F32)
    R5b = Region(arena, 136, 182)
    Wpq = R5b.alloc("Wpq", [128, 16, 1024], BF16)
    QPi = R5b.alloc("QPi", [128, 8, 128], BF16)
    KT12 = R5b.alloc("KT12", [128, 8, 128], BF16)
    k12 = R5b.alloc("k12", [128, 128], BF16)
    SCb = R5b.alloc("SCb", [128, 256], F32)
    tmpS = R5b.alloc("tmpS", [128, 128], F32)
    V12 = R5b.alloc("V12", [128, 32], F32)
    I12 = R5b.alloc("I12", [128, 32], U32)
    I12f = R5b.alloc("I12f", [128, 32], F32)
    cand = R5b.alloc("cand", [128, 16, 16], F32)
    cidx = R5b.alloc("cidx", [128, 16, 16], F32)
    tmp256 = R5b.alloc("tmp256", [128, 256], F32)
    junk256 = R5b.alloc("junk256", [128, 256], F32)
    iota_i = R5b.alloc("iota_i", [128, 256], I32)
    iota_f = R5b.alloc("iota_f", [128, 256], F32)
    SCv = R5b.alloc("SCv", [128, 16], F32)
    posu = R5b.alloc("posu", [128, 16], U32)
    posf2 = R5b.alloc("posf2", [128, 16], F32)
    EIf = R5b.alloc("EIf", [128, 16], F32)
    gexp = R5b.alloc("gexp", [128, 16], F32)
    negm = R5b.alloc("negm", [128, 1], F32)
    Zs = R5b.alloc("Zs", [128, 1], F32)
    pwq_v = peer_wq.rearrange("(c p) n -> p c n", p=128)
    for half in range(2):
        S.dma("pool", Wpq, Wpq.ap[:, half * 8:(half + 1) * 8, :], None, pwq_v[:, half * 8:(half + 1) * 8, :])
    S.op("pool", lambda e: e.iota(iota_i.ap, pattern=[[1, 256]], base=0, channel_multiplier=0), w=[iota_i])
    cp(iota_f.ap, iota_i.ap, [iota_i], [iota_f])
    for h in range(8):
        S.dma("pool", k12, k12.ap[:, 0:64], None, keys1[h])
        S.dma("pool", k12, k12.ap[:, 64:128], None, keys2[h])
        pb = PBR.next()
        tr(pb.ap[:, 0:128], k12.ap, ident.ap, [k12, ident], [pb])
        evac(KT12.ap[:, h, :], pb.ap[:, 0:128], [pb], [KT12])
    cand_f = cand.ap.rearrange("p a b -> p (a b)")
    cidx_f = cidx.ap.rearrange("p a b -> p (a b)")

    def top16(vals_ap, vbuf, out_v, out_i, obufs, scratch, n):
        S.op("dve", lambda e: e.max(out_v[:, 0:8], vals_ap), r=[vbuf], w=[obufs[0]])
        S.op("dve", lambda e: e.max_index(out_i[:, 0:8], out_v[:, 0:8], vals_ap), r=[vbuf, obufs[0]], w=[obufs[1]])
        S.op("dve", lambda e: e.match_replace(scratch.ap[:, 0:n], out_v[:, 0:8], vals_ap, NEG), r=[vbuf, obufs[0]], w=[scratch])
        S.op("dve", lambda e: e.max(out_v[:, 8:16], scratch.ap[:, 0:n]), r=[scratch], w=[obufs[0]])
        S.op("dve", lambda e: e.max_index(out_i[:, 8:16], out_v[:, 8:16], scratch.ap[:, 0:n]), r=[scratch, obufs[0]], w=[obufs[1]])

    NTI = int(os.environ.get("NTI", "8"))
    for i in range(NTI):
        for h in range(8):
            pf = PFR.next()
            for ck in range(16):
                mm(pf.ap[:, 0:128], Wpq.ap[:, ck, h * 128:(h + 1) * 128], H1T.ap[:, ck, i * 128:(i + 1) * 128], ck == 0, ck == 15, [Wpq, H1T], [pf])
            evac(QPi.ap[:, h, :], pf.ap[:, 0:128], [pf], [QPi])
        for h in range(8):
            pf = PFR.next(); pf2 = PFR.next()
            mm(pf.ap[:, 0:128], QPi.ap[0:64, h, :], KT12.ap[0:64, h, :], True, True, [QPi, KT12], [pf])
            mm(pf2.ap[:, 0:128], QPi.ap[64:128, h, :], KT12.ap[64:128, h, :], True, True, [QPi, KT12], [pf2])
            cp(SCb.ap[:, 0:128], pf.ap[:, 0:128], [pf], [SCb])
            cp(SCb.ap[:, 128:256], pf2.ap[:, 0:128], [pf2], [SCb])
            for half in range(2):
                top16(SCb.ap[:, half * 128:(half + 1) * 128], SCb, V12.ap[:, half * 16:(half + 1) * 16], I12.ap[:, half * 16:(half + 1) * 16],
                      (V12, I12), tmpS, 128)
            cp(I12f.ap, I12.ap, [I12], [I12f])
            v1b = V12.ap[:, 0:16].unsqueeze(2).to_broadcast([128, 16, 16])
            v2b = V12.ap[:, 16:32].unsqueeze(1).to_broadcast([128, 16, 16])
            i1b = I12f.ap[:, 0:16].unsqueeze(2).to_broadcast([128, 16, 16])
            i2b = I12f.ap[:, 16:32].unsqueeze(1).to_broadcast([128, 16, 16])
            tt(cand.ap, v1b, v2b, ALU.add, [V12], [cand])
            stt(cidx.ap, i1b, 128.0, i2b, ALU.mult, ALU.add, [I12f], [cidx])
            top16(cand_f, cand, SCv.ap, posu.ap, (SCv, posu), tmp256, 256)
            cp(posf2.ap, posu.ap, [posu], [posf2])
            S.op("dve", lambda e: e.memset(EIf.ap, 0.0), w=[EIf])
            for k in range(16):
                stt(junk256.ap, iota_f.ap, posf2.ap[:, k:k + 1], cidx_f, ALU.is_equal, ALU.mult, [iota_f, posf2, cidx], [junk256, EIf],
                    accum=EIf.ap[:, k:k + 1])
            cp(EI_all.ap[:, i, h * 16:(h + 1) * 16], EIf.ap, [EIf], [EI_all])
            ts(negm.ap, SCv.ap[:, 0:1], -1.0, None, ALU.mult, None, [SCv], [negm])
            S.op("dve", lambda e: e.memset(Zs.ap, 0.0), w=[Zs])
            act(gexp.ap, SCv.ap, AF.Exp, [SCv, negm], [gexp, Zs], bias=negm.ap[:, 0:1], accum=Zs.ap[:, 0:1])
            S.op("dve", lambda e: e.reciprocal(Zs.ap, Zs.ap), r=[Zs], w=[Zs])
            ts(GT_all.ap[:, i, h * 16:(h + 1) * 16], gexp.ap, Zs.ap[:, 0:1], None, ALU.mult, None, [gexp, Zs], [GT_all])
    dbg("EI", EI_all, [128, 8, 128], I32)
    dbg("GT", GT_all, [128, 8, 128], F32)
    if stop_after <= 6:
        S.finish(final_bufs)
        return nc

    S.barrier()
    R5c = Region(arena, 104, 182)
    UVG = RR([R5c.alloc("UVG%d" % k, [128, 4, 2 * D], BF16) for k in range(2)])
    UV_flat = UVb_d.rearrange("e t d -> e (t d)")
    Adot = R5c.alloc("Adot", [128, 128], F32)
    AW = R5c.alloc("AW", [128, 128], F32)
    DG = RR([R5c.alloc("DG%d" % k, [128, 128], BF16) for k in range(4)])
    stats2 = R5c.alloc("stats2", [128, 4, 6], F32)
    mv2 = R5c.alloc("mv2", [128, 2], F32)
    rstd2 = R5c.alloc("rstd2", [128, 1], F32)
    Gh = R5c.alloc("Gh", [128, 1024], F32)
    Bh = R5c.alloc("Bh", [128, 1024], F32)
    NSL = int(os.environ.get("NSL", "128"))
    for i in range(NTI):
        S.op("dve", lambda e: e.memset(Adot.ap, 0.0), w=[Adot])
        for ch in range(NSL // 4):
            uvg = UVG.next()
            for c in range(4):
                slot = ch * 4 + c
                S.op("pool", lambda e, uvg=uvg, c=c, slot=slot, i=i: e.indirect_dma_start(
                    out=uvg.ap[:, c, :], out_offset=None, in_=UV_flat,
                    in_offset=bass.IndirectOffsetOnAxis(ap=EI_all.ap[:, i, slot:slot + 1], axis=0)), r=[EI_all, D_UV], w=[uvg], dma=True)
            for c in range(4):
                slot = ch * 4 + c
                stt(uvg.ap[:, c, 0:D], uvg.ap[:, c, 0:D], 1.0, H1B.ap[:, i, :], ALU.mult, ALU.mult, [uvg, H1B], [uvg, Adot], accum=Adot.ap[:, slot:slot + 1])
            sl = slice(ch * 4, ch * 4 + 4)
            act(AW.ap[:, sl], Adot.ap[:, sl], AF.Gelu, [Adot], [AW])
            tt(AW.ap[:, sl], AW.ap[:, sl], GT_all.ap[:, i, sl], ALU.mult, [AW, GT_all], [AW])
            for c in range(4):
                slot = ch * 4 + c
                dg = DG.next()
                ts(dg.ap, ident.ap, AW.ap[:, slot:slot + 1], None, ALU.mult, None, [ident, AW], [dg])
                for nf in range(4):
                    mm(PF[nf].ap, dg.ap, uvg.ap[:, c, D + nf * 512:D + (nf + 1) * 512], slot == 0, slot == NSL - 1, [dg, uvg], [PF[nf]])
        for nf in range(4):
            Hs = H1.ap[:, i, nf * 512:(nf + 1) * 512]
            tt(Hs, Hs, PF[nf].ap, ALU.add, [H1, PF[nf]], [H1])
        layer_norm_rows(H1.ap[:, i, :], H1, None, None, stats2, mv2, rstd2)
        for hf in range(2):
            S.dma("sp", Gh, Gh.ap, None, ln2g[:, hf * 1024:(hf + 1) * 1024].partition_broadcast(128))
            S.dma("sp", Bh, Bh.ap, None, ln2b[:, hf * 1024:(hf + 1) * 1024].partition_broadcast(128))
            Xh = H1.ap[:, i, hf * 1024:(hf + 1) * 1024]
            tt(Xh, Xh, Gh.ap, ALU.mult, [H1, Gh], [H1])
            tt(Xh, Xh, Bh.ap, ALU.add, [H1, Bh], [H1])
        S.dma("sp", D_out, out_d[i * 128:(i + 1) * 128, :], H1, H1.ap[:, i, :])
    S.finish(final_bufs)
    return nc


def own_rows(j):
    return np.concatenate([np.arange((4 * i + j) * 128, (4 * i + j + 1) * 128) for i in range(8)])


def core_inputs(inp, c):
    b, j = c // 4, c % 4
    own = own_rows(j)
    f = lambda a: np.ascontiguousarray(a, dtype=np.float32)
    x = inp["x"][b]
    negmask = np.zeros((128, 4, 128), np.float32)
    sel = np.zeros((128, 4, 128), np.float32)
    kk = np.arange(128)[:, None]; qq = np.arange(128)[None, :]
    for m in range(4):
        if m == j:
            negmask[:, m, :] = np.where(kk <= qq, 0.0, NEG)
            sel[:, m, :] = np.eye(128, dtype=np.float32)
        elif m > j:
            negmask[:, m, :] = NEG
    inv_freq = (1.0 / (10000.0 ** (np.arange(0, 64, 2, dtype=np.float32) / 64))).astype(np.float32)
    invf = np.concatenate([inv_freq, inv_freq])[:, None]
    sgn = np.concatenate([-np.ones(32, np.float32), np.ones(32, np.float32)])[:, None]
    return {
        "xkv": f(x), "xq": f(x[own]), "pq": f(inp["p"][0, b][own]),
        "pos_kv": np.ascontiguousarray(inp["positions"][b][None, :], dtype=np.int32),
        "pos_q": np.ascontiguousarray(inp["positions"][b][own][None, :], dtype=np.int32),
        "w_in": f(inp["w_in"][0]), "w_uq": f(inp["w_uq"][0]), "w_ukv": f(inp["w_ukv"][0]),
        "w_out": f(inp["w_out"][0]), "peer_wq": f(inp["peer_wq"][0]),
        "keys1": f(inp["peer_keys1"][0]), "keys2": f(inp["peer_keys2"][0]),
        "peer_u": f(inp["peer_u"][0]), "peer_v": f(inp["peer_v"][0]),
        "ple_wgate": f(inp["ple_wgate"][0]), "ple_wproj": f(inp["ple_wproj"][0]),
        "gqT": f(inp["g_q_norm"][0].reshape(4, 128).T), "gkvT": f(inp["g_kv_norm"][0].reshape(4, 128).T),
        "bfg": f(inp["b_forget"][0][:, None]),
        "ln1g": f(inp["ln1_g"]), "ln1b": f(inp["ln1_b"]), "ln2g": f(inp["ln2_g"]), "ln2b": f(inp["ln2_b"]),
        "negmask": negmask, "sel": sel, "invf": f(invf), "sgn": f(sgn),
    }


_NC_CACHE = {}


def kernel(**inputs):
    inp = {k: np.asarray(v) for k, v in inputs.items()}
    if "nc" not in _NC_CACHE:
        _NC_CACHE["nc"] = build_nc()
    nc = _NC_CACHE["nc"]
    maps = [core_inputs(inp, c) for c in range(8)]
    res = run_bass_kernel_spmd(nc, maps, core_ids=list(range(8)))
    out = np.zeros((2, NT, D), np.float32)
    for c in range(8):
        b, j = c // 4, c % 4
        out[b, own_rows(j)] = np.asarray(res.results[c]["out"], dtype=np.float32)
    return out
```

```python
import numpy as np
from contextlib import ExitStack
import concourse.bass as bass
import concourse.mybir as mybir
from concourse.bass_utils import run_bass_kernel_spmd
from concourse.alu_op_type import AluOpType as ALU

F32 = mybir.dt.float32
BF16 = mybir.dt.bfloat16
I32 = mybir.dt.int32
U32 = mybir.dt.uint32
AF = mybir.ActivationFunctionType


class Buf:
    def __init__(self, name, ap):
        self.name = name
        self.ap = ap
        self.last_w = []
        self.reads = []
        self.dsem = None
        self.dcount = 0
        self.is_dram = False
        self.excl = False

    def __getitem__(self, k):
        return self.ap[k]


class Sched:
    ENG = ["pe", "dve", "act", "pool", "sp"]

    def __init__(self, nc):
        self.nc = nc
        self.stack = ExitStack()
        self.eobj = {"pe": nc.tensor, "dve": nc.vector, "act": nc.scalar, "pool": nc.gpsimd, "sp": nc.sync}
        self.q = {e: [] for e in self.ENG}
        self.cnt = {e: 0 for e in self.ENG}
        self.esem = {e: self.stack.enter_context(nc.semaphore("es_" + e)) for e in self.ENG}
        self.waited = {e: {} for e in self.ENG}
        self.nbuf = 0
        self.dma_tokens = []
        self.free_dsems = []

    def sbuf(self, name, shape, dtype, stack=None):
        t = (stack or self.stack).enter_context(self.nc.sbuf_tensor(name, list(shape), dtype))
        return Buf(name, t[:])

    def psum(self, name, shape, dtype, stack=None):
        t = (stack or self.stack).enter_context(self.nc.psum_tensor(name, list(shape), dtype))
        b = Buf(name, t[:])
        b.excl = True
        return b

    def dram(self, ap, name="dram"):
        b = Buf(name, ap)
        b.is_dram = True
        return b

    def _dsem(self, buf):
        if buf.dsem is None:
            self.nbuf += 1
            buf.dsem = self.stack.enter_context(self.nc.semaphore("ds%d" % self.nbuf))
        return buf.dsem

    def _deps(self, eng, r, w):
        deps = []
        for b in r:
            deps.extend(b.last_w)
        for b in w:
            deps.extend(b.last_w)
            deps.extend(b.reads)
        out = {}
        for (sem, val, e) in deps:
            if e == eng and eng in ("pe", "sp"):
                continue
            key = id(sem)
            if self.waited[eng].get(key, 0) >= val:
                continue
            if key not in out or out[key][1] < val:
                out[key] = (sem, val)
        for key, (sem, val) in out.items():
            self.waited[eng][key] = val
        return list(out.values())

    def _commit(self, tok, r, w, accumulate=False):
        for b in r:
            b.reads.append(tok)
        for b in w:
            if accumulate:
                b.last_w = [t for t in b.last_w if t[0] is not tok[0]] + [tok]
            else:
                b.last_w = [tok]
            b.reads = []

    def op(self, eng, fn, r=(), w=(), dma=False):
        r = list(r); w = list(w)
        for b in list(r):
            if b.excl:
                r.remove(b)
                if b not in w:
                    w.append(b)
        deps = self._deps(eng, r, w)
        if dma:
            wb = w[0]
            own = r[0] if (wb.is_dram and r) else wb
            sem = self._dsem(own)
            own.dcount += 16
            tok = (sem, own.dcount, "dma")
            self.q[eng].append((deps, fn, sem, 16))
            self.dma_tokens.append(tok)
            self._commit(tok, r, w, accumulate=wb.is_dram)
            return
        else:
            self.cnt[eng] += 1
            tok = (self.esem[eng], self.cnt[eng], eng)
            self.q[eng].append((deps, fn, self.esem[eng], 1))
        self._commit(tok, r, w)

    def dma(self, eng, wbuf, out_ap, rbuf, in_ap, **kw):
        r = [rbuf] if rbuf is not None else []
        self.op(eng, lambda e: e.dma_start(out=out_ap, in_=in_ap, **kw), r=r, w=[wbuf], dma=True)

    def barrier(self):
        toks = [(self.esem[e], self.cnt[e], e) for e in self.ENG if self.cnt[e] > 0]
        toks += self.dma_tokens
        self.dma_tokens = []
        for eng in self.ENG:
            out = {}
            for (sem, val, e) in toks:
                if e == eng:
                    continue
                key = id(sem)
                if self.waited[eng].get(key, 0) >= val:
                    continue
                if key not in out or out[key][1] < val:
                    out[key] = (sem, val)
            for key, (sem, val) in out.items():
                self.waited[eng][key] = val
            if out:
                self.q[eng].append((list(out.values()), None, None, 0))

    def finish(self, out_bufs):
        deps = []
        for b in out_bufs:
            for t in b.last_w:
                deps.append((t[0], t[1]))
        self.q["sp"].append((deps, None, None, 0))
        with self.nc.Block() as block:
            def mk(eng):
                def body(e):
                    for (deps, fn, sem, inc) in self.q[eng]:
                        for (s, v) in deps:
                            e.wait_ge(s, v)
                        if fn is not None:
                            ins = fn(e)
                            ins.then_inc(sem, inc)
                return body
            block.tensor(mk("pe"))
            block.vector(mk("dve"))
            block.scalar(mk("act"))
            block.gpsimd(mk("pool"))
            block.sync(mk("sp"))
        self.stack.close()


D = 2048
NT = 4096
NQ = 1024
TT = 256
ALPHA = 2.0 ** 0.25
EPS = 1e-6
PI = float(np.pi)
SC_MLA = 192.0 ** -0.5
SC_FOX = 128.0 ** -0.5
NEG = -1.0e30


class RR:
    def __init__(self, items):
        self.items = list(items); self.i = 0

    def next(self):
        x = self.items[self.i % len(self.items)]; self.i += 1
        return x


class Region:
    def __init__(self, arena, start_kb, end_kb):
        self.arena = arena; self.off = int(start_kb * 1024); self.end = int(end_kb * 1024)

    def alloc(self, name, shape, dtype):
        n = int(np.prod(shape[1:]))
        esz = 2 if dtype == BF16 else 4
        nb = (n * esz + 63) // 64 * 64
        assert self.off + nb <= self.end, (name, self.off, nb, self.end)
        o = self.off // 2
        ap = self.arena[0:shape[0], o:o + n * esz // 2]
        if dtype != BF16:
            ap = ap.bitcast(dtype)
        if len(shape) == 3:
            ap = ap.rearrange("p (a b) -> p a b", b=shape[2])
        elif len(shape) == 4:
            ap = ap.rearrange("p (a b c) -> p a b c", b=shape[2], c=shape[3])
        self.off += nb
        return Buf(name, ap)


import os
SECT = os.environ.get('SECT', 'mla,rope,fox,forget').split(',')
NTILES = int(os.environ.get('NTILES', '0'))


def build_nc(debug=False, stop_after=99):
    nc = bass.Bass("TRN2", target_bir_lowering=False)
    dbg_outs = {}

    def IN(name, shape, dtype=F32):
        return nc.dram_tensor(name, list(shape), dtype, kind="ExternalInput").ap()

    def SCR(name, shape, dtype):
        return nc.dram_tensor(name, list(shape), dtype, kind=("ExternalOutput" if debug else "Internal")).ap()

    xkv = IN("xkv", [NT, D]); xq = IN("xq", [NQ, D]); pq = IN("pq", [NQ, 256])
    pos_kv = IN("pos_kv", [1, NT], I32); pos_q = IN("pos_q", [1, NQ], I32)
    w_in = IN("w_in", [D, 4168]); w_uq = IN("w_uq", [512, 1536]); w_ukv = IN("w_ukv", [512, 2048])
    w_out = IN("w_out", [D, D]); peer_wq = IN("peer_wq", [D, 1024])
    keys1 = IN("keys1", [8, 128, 64]); keys2 = IN("keys2", [8, 128, 64])
    peer_u = IN("peer_u", [16384, D]); peer_v = IN("peer_v", [16384, D])
    ple_wgate = IN("ple_wgate", [D, D]); ple_wproj = IN("ple_wproj", [256, D])
    gqT = IN("gqT", [128, 4]); gkvT = IN("gkvT", [128, 4]); bfg = IN("bfg", [8, 1])
    ln1g = IN("ln1g", [1, D]); ln1b = IN("ln1b", [1, D]); ln2g = IN("ln2g", [1, D]); ln2b = IN("ln2b", [1, D])
    negmask_in = IN("negmask", [128, 4, 128]); sel_in = IN("sel", [128, 4, 128])
    invf_in = IN("invf", [64, 1]); sgn_in = IN("sgn", [64, 1])
    out_d = nc.dram_tensor("out", [NQ, D], F32, kind="ExternalOutput").ap()

    kTm_d = SCR("kTm_d", [8, 128, NT], BF16); krT_d = SCR("krT_d", [64, NT], BF16)
    vm_d = SCR("vm_d", [8, 128, 32, 128], BF16)
    kTf_d = SCR("kTf_d", [8, 128, NT], BF16); vf_d = SCR("vf_d", [8, 128, 32, 128], BF16)
    UVb_d = nc.dram_tensor("UVb_d", [16384, 2, D], BF16, kind="Internal").ap()

    S = Sched(nc)
    arena = S.stack.enter_context(nc.sbuf_tensor("arena", [128, 95 * 1024], BF16))
    D_kTm = S.dram(kTm_d); D_krT = S.dram(krT_d); D_vm = S.dram(vm_d); D_kTf = S.dram(kTf_d); D_vf = S.dram(vf_d)
    D_out = S.dram(out_d)
    D_UV = S.dram(UVb_d)
    final_bufs = [D_out]

    def dbg(name, buf, shape, dtype):
        if not debug:
            return
        t = nc.dram_tensor("dbg_" + name, list(shape), dtype, kind="ExternalOutput").ap()
        b = S.dram(t)
        S.dma("sp", b, t, buf, buf.ap)
        final_bufs.append(b)

    def mm(out, lhsT, rhs, start, stop, r, w):
        S.op("pe", lambda e: e.matmul(out, lhsT, rhs, start=start, stop=stop), r=r, w=w)

    def tr(out, in_, idn, r, w):
        S.op("pe", lambda e: e.transpose(out, in_, idn), r=r, w=w)

    def act(out, in_, func, r, w, bias=None, scale=None, accum=None):
        kw = {}
        if bias is not None: kw["bias"] = bias
        if scale is not None: kw["scale"] = scale
        if accum is not None: kw["accum_out"] = accum
        S.op("act", lambda e: e.activation(out, in_, func, **kw), r=r, w=w)

    def tt(out, a, b, op, r, w, eng="dve"):
        S.op(eng, lambda e: e.tensor_tensor(out, a, b, op), r=r, w=w)

    def ts(out, a, s1, s2, op0, op1, r, w, eng="dve", accum=None):
        if op1 is None:
            S.op(eng, lambda e: e.tensor_scalar(out, a, s1, None, op0), r=r, w=w)
        elif accum is not None:
            S.op(eng, lambda e: e.tensor_scalar(out, a, s1, s2, op0, op1, accum_out=accum), r=r, w=w)
        else:
            S.op(eng, lambda e: e.tensor_scalar(out, a, s1, s2, op0, op1), r=r, w=w)

    def stt(out, a, sc, b, op0, op1, r, w, accum=None):
        if accum is None:
            S.op("dve", lambda e: e.scalar_tensor_tensor(out=out, in0=a, scalar=sc, in1=b, op0=op0, op1=op1), r=r, w=w)
        else:
            S.op("dve", lambda e: e.scalar_tensor_tensor(out=out, in0=a, scalar=sc, in1=b, op0=op0, op1=op1, accum_out=accum), r=r, w=w)

    def cp(out, in_, r, w, eng="dve"):
        if eng == "act":
            S.op("act", lambda e: e.copy(out, in_), r=r, w=w)
        else:
            S.op(eng, lambda e: e.tensor_copy(out, in_), r=r, w=w)

    evac_i = [0]

    def evac(out, in_, r, w):
        evac_i[0] += 1
        cp(out, in_, r, w, eng=("act" if evac_i[0] % 2 else "dve"))

    PF = [S.psum("pf%d" % i, [128, 512], F32) for i in range(6)]
    PB = [S.psum("pb%d" % i, [128, 1024], BF16) for i in range(2)]
    PFR = RR(PF); PBR = RR(PB)

    RC = Region(arena, 0, 8)
    identf = RC.alloc("identf", [128, 128], F32)
    ident = RC.alloc("ident", [128, 128], BF16)
    ones_bf = RC.alloc("ones_bf", [128, 128], BF16)
    ones_f = RC.alloc("ones_f", [128, 256], F32)
    epsT = RC.alloc("epsT", [128, 1], F32)
    oneT = RC.alloc("oneT", [128, 1], F32)
    invf = RC.alloc("invf", [64, 1], F32)
    sgn = RC.alloc("sgn", [64, 1], F32)
    negb = RC.alloc("negb", [8, 1], F32)
    gkv = RC.alloc("gkv", [128, 4], F32)
    gq = RC.alloc("gq", [128, 4], F32)
    negF_tok = RC.alloc("negF_tok", [128, 32, 8], F32)
    negmask = RC.alloc("negmask", [128, 4, 128], BF16)
    EI = RC.alloc("EI", [128, 128], I32)
    GT = RC.alloc("GT", [128, 128], F32)

    S.op("pool", lambda e: e.memset(identf.ap, 0.0), w=[identf])
    S.op("pool", lambda e: e.affine_select(out=identf.ap, in_=identf.ap, pattern=[[-1, 128]], compare_op=ALU.not_equal,
                                           fill=1.0, base=0, channel_multiplier=1), r=[identf], w=[identf])
    cp(ident.ap, identf.ap, [identf], [ident])
    S.op("dve", lambda e: e.memset(ones_bf.ap, 1.0), w=[ones_bf])
    S.op("dve", lambda e: e.memset(ones_f.ap, 1.0), w=[ones_f])
    S.op("dve", lambda e: e.memset(epsT.ap, EPS), w=[epsT])
    S.op("dve", lambda e: e.memset(oneT.ap, 1.0), w=[oneT])
    S.dma("sp", invf, invf.ap, None, invf_in)
    S.dma("sp", sgn, sgn.ap, None, sgn_in)
    S.dma("sp", negb, negb.ap, None, bfg)
    ts(negb.ap, negb.ap, -1.0, None, ALU.mult, None, [negb], [negb])
    S.dma("sp", gkv, gkv.ap, None, gkvT)
    S.dma("sp", gq, gq.ap, None, gqT)
    S.dma("pool", negmask, negmask.ap, None, negmask_in)

    w_in_v = w_in.rearrange("(c p) n -> p c n", p=128)

    def rope_tables(R, pos_ap, c0, n, tag):
        posi = R.alloc("posi" + tag, [64, n], I32)
        posf = R.alloc("posf" + tag, [64, n], F32)
        ang = R.alloc("ang" + tag, [64, n], F32)
        tq = R.alloc("tq" + tag, [64, n], F32)
        ki = R.alloc("ki" + tag, [64, n], I32)
        cosb = R.alloc("cos" + tag, [64, n], F32)
        sinb = R.alloc("sin" + tag, [64, n], F32)

        def emit(c0):
            S.dma("sp", posi, posi.ap, None, pos_ap[0:1, c0:c0 + n].partition_broadcast(64))
            cp(posf.ap, posi.ap, [posi], [posf])
            for (phase, dst) in ((0.0, sinb), (PI / 2, cosb)):
                ts(ang.ap, posf.ap, invf.ap[:, 0:1], phase, ALU.mult, ALU.add, [posf, invf], [ang])
                ts(tq.ap, ang.ap, 1.0 / (2 * PI), None, ALU.mult, None, [ang], [tq])
                cp(ki.ap, tq.ap, [tq], [ki])
                cp(tq.ap, ki.ap, [ki], [tq])
                stt(ang.ap, tq.ap, -2 * PI, ang.ap, ALU.mult, ALU.add, [tq, ang], [ang])
                ts(tq.ap, ang.ap, PI, -2 * PI, ALU.is_gt, ALU.mult, [ang], [tq])
                tt(ang.ap, ang.ap, tq.ap, ALU.add, [ang, tq], [ang])
                ts(tq.ap, ang.ap, -PI, 2 * PI, ALU.is_lt, ALU.mult, [ang], [tq])
                tt(ang.ap, ang.ap, tq.ap, ALU.add, [ang, tq], [ang])
                ts(ang.ap, ang.ap, PI, -PI, ALU.min, ALU.max, [ang], [ang])
                act(dst.ap, ang.ap, AF.Sin, [ang], [dst])
            ts(sinb.ap, sinb.ap, sgn.ap[:, 0:1], None, ALU.mult, None, [sinb, sgn], [sinb])
        return cosb, sinb, emit

    def transpose_in(xb, xT, ns):
        for ck in range(16):
            pb = PBR.next()
            for s_ in range(ns):
                tr(pb.ap[:, s_ * 128:(s_ + 1) * 128], xb.ap[:, s_, ck * 128:(ck + 1) * 128], ident.ap, [xb, ident], [pb])
            evac(xT.ap[:, ck, :], pb.ap[:, 0:ns * 128], [pb], [xT])

    def proj_rms(R, Wt, col0, xT, n, tag):
        raw = R.alloc("raw" + tag, [128, 4, n], F32)
        sq = R.alloc("sq" + tag, [128, 4, n], BF16)
        nrm = R.alloc("nrm" + tag, [128, 4, n], BF16)
        Rs = R.alloc("Rs" + tag, [128, n], F32)

        def emit():
            for fc in range(4):
                pf = PFR.next()
                for ck in range(16):
                    mm(pf.ap[:, 0:n], Wt.ap[:, ck, col0 + fc * 128:col0 + (fc + 1) * 128], xT.ap[:, ck, :], ck == 0, ck == 15, [Wt, xT], [pf])
                act(sq.ap[:, fc, :], pf.ap[:, 0:n], AF.Square, [pf], [sq])
                cp(raw.ap[:, fc, :], pf.ap[:, 0:n], [pf], [raw])
            pf = PFR.next()
            for fc in range(4):
                mm(pf.ap[:, 0:n], ones_bf.ap, sq.ap[:, fc, :], fc == 0, fc == 3, [ones_bf, sq], [pf])
            act(Rs.ap, pf.ap[:, 0:n], AF.Sqrt, [pf, epsT], [Rs], bias=epsT.ap[:, 0:1], scale=1.0 / 512)
            S.op("dve", lambda e: e.reciprocal(Rs.ap, Rs.ap), r=[Rs], w=[Rs])
            for fc in range(4):
                tt(nrm.ap[:, fc, :], raw.ap[:, fc, :], Rs.ap, ALU.mult, [raw, Rs], [nrm])
        return nrm, emit

    R1 = Region(arena, 8, 190)
    Frow = R1.alloc("Frow", [8, NT], F32)
    Wm = R1.alloc("Wm", [128, 16, 576], BF16)
    Wf = R1.alloc("Wf", [128, 16, 2056], BF16)
    Wukv = R1.alloc("Wukv", [128, 4, 2048], BF16)
    Wkrsw = R1.alloc("Wkrsw", [128, 16, 64], BF16)
    mark = R1.off
    stg = R1.alloc("stg", [128, 4, 2048], F32)
    S.dma("pool", Wm, Wm.ap, None, w_in_v[:, :, 512:1088])
    S.dma("pool", Wf, Wf.ap[:, :, 0:1028], None, w_in_v[:, :, 2112:3140])
    S.dma("pool", Wf, Wf.ap[:, :, 1028:2056], None, w_in_v[:, :, 3140:4168])
    S.dma("sp", stg, stg.ap, None, w_ukv.rearrange("(c p) n -> p c n", p=128))
    for fc in range(4):
        ts(Wukv.ap[:, fc, :], stg.ap[:, fc, :], gkv.ap[:, fc:fc + 1], None, ALU.mult, None, [stg, gkv], [Wukv])
    cp(Wkrsw.ap[:, :, 0:32], Wm.ap[:, :, 544:576], [Wm], [Wkrsw])
    cp(Wkrsw.ap[:, :, 32:64], Wm.ap[:, :, 512:544], [Wm], [Wkrsw])
    S.barrier()
    R1.off = mark
    ns = TT // 128
    xb = R1.alloc("xb", [128, ns, D], BF16)
    xT = R1.alloc("xT", [128, 16, TT], BF16)
    ckvn, emit_ckv = proj_rms(R1, Wm, 0, xT, TT, "kv")
    kn_st = RR([R1.alloc("kn_st%d" % i, [128, 8, TT], BF16) for i in range(2)])
    fk_st = RR([R1.alloc("fk_st%d" % i, [128, 8, TT], BF16) for i in range(2)])
    v_st = R1.alloc("v_st", [128, ns, 1024], BF16)
    fv_st = R1.alloc("fv_st", [128, ns, 1024], BF16)
    cosb, sinb, emit_rope = rope_tables(R1, pos_kv, 0, TT, "kv")
    T1 = R1.alloc("T1", [64, TT], F32); T2 = R1.alloc("T2", [64, TT], F32)
    kr_st = R1.alloc("kr_st", [64, TT], BF16)
    exb = R1.alloc("exb", [8, TT], F32); lnb = R1.alloc("lnb", [8, TT], F32)

    def sec_load(T, t0):
        S.dma("pool", xb, xb.ap, None, xkv[t0:t0 + TT, :].rearrange("(s p) d -> p s d", p=128))
        transpose_in(xb, xT, ns)
    MLAK = int(os.environ.get('MLAK', '9'))

    def sec_mla(T, t0):
        emit_ckv()
        if MLAK < 2: return
        kst = kn_st.next()
        for h in range(8):
            pf = PFR.next()
            for fc in range(4):
                mm(pf.ap[:, 0:TT], Wukv.ap[:, fc, h * 256:h * 256 + 128], ckvn.ap[:, fc, :], fc == 0, fc == 3, [Wukv, ckvn], [pf])
            evac(kst.ap[:, h, :], pf.ap[:, 0:TT], [pf], [kst])
        S.dma("sp", D_kTm, kTm_d.rearrange("h d t -> d h t")[:, :, t0:t0 + TT], kst, kst.ap)
        if MLAK < 3: return
        for s_ in range(ns):
            for hh in range(2):
                pf = PFR.next()
                for fc in range(4):
                    rhs = Wukv.ap[:, fc, :].rearrange("p (h c) -> p h c", c=256)[:, hh * 4:(hh + 1) * 4, 128:256]
                    mm(pf.ap, ckvn.ap[:, fc, s_ * 128:(s_ + 1) * 128], rhs, fc == 0, fc == 3, [Wukv, ckvn], [pf])
                evac(v_st.ap[:, s_, hh * 512:(hh + 1) * 512], pf.ap, [pf], [v_st])
            blk = T * ns + s_
            S.dma("sp", D_vm, vm_d.rearrange("h p b d -> p b h d")[:, blk, :, :], v_st,
                  v_st.ap[:, s_, :].rearrange("p (h d) -> p h d", d=128))
    def sec_rope(T, t0):
        pk = PFR.next(); pks = PFR.next()
        for ck in range(16):
            mm(pk.ap[0:64, 0:TT], Wm.ap[:, ck, 512:576], xT.ap[:, ck, :], ck == 0, ck == 15, [Wm, xT], [pk])
        for ck in range(16):
            mm(pks.ap[0:64, 0:TT], Wkrsw.ap[:, ck, :], xT.ap[:, ck, :], ck == 0, ck == 15, [Wkrsw, xT], [pks])
        emit_rope(t0)
        tt(T1.ap, pk.ap[0:64, 0:TT], cosb.ap, ALU.mult, [pk, cosb], [T1])
        tt(T2.ap, pks.ap[0:64, 0:TT], sinb.ap, ALU.mult, [pks, sinb], [T2])
        tt(kr_st.ap, T1.ap, T2.ap, ALU.add, [T1, T2], [kr_st])
        S.dma("sp", D_krT, krT_d[:, t0:t0 + TT], kr_st, kr_st.ap)
    def sec_fox(T, t0):
        fst = fk_st.next()
        for h in range(8):
            pf = PFR.next()
            for ck in range(16):
                mm(pf.ap[:, 0:TT], Wf.ap[:, ck, h * 128:(h + 1) * 128], xT.ap[:, ck, :], ck == 0, ck == 15, [Wf, xT], [pf])
            evac(fst.ap[:, h, :], pf.ap[:, 0:TT], [pf], [fst])
        S.dma("sp", D_kTf, kTf_d.rearrange("h d t -> d h t")[:, :, t0:t0 + TT], fst, fst.ap)
        for s_ in range(ns):
            for hh in range(2):
                pf = PFR.next()
                for ck in range(16):
                    mm(pf.ap, xT.ap[:, ck, s_ * 128:(s_ + 1) * 128], Wf.ap[:, ck, 1024 + hh * 512:1024 + (hh + 1) * 512], ck == 0, ck == 15, [Wf, xT], [pf])
                evac(fv_st.ap[:, s_, hh * 512:(hh + 1) * 512], pf.ap, [pf], [fv_st])
            blk = T * ns + s_
            S.dma("sp", D_vf, vf_d.rearrange("h p b d -> p b h d")[:, blk, :, :], fv_st,
                  fv_st.ap[:, s_, :].rearrange("p (h d) -> p h d", d=128))
    def sec_forget(T, t0):
        pff = PFR.next()
        for ck in range(16):
            mm(pff.ap[0:8, 0:TT], Wf.ap[:, ck, 2048:2056], xT.ap[:, ck, :], ck == 0, ck == 15, [Wf, xT], [pff])
        act(exb.ap, pff.ap[0:8, 0:TT], AF.Exp, [pff, negb], [exb], bias=negb.ap[:, 0:1], scale=-1.0)
        act(lnb.ap, exb.ap, AF.Ln, [exb, oneT], [lnb], bias=oneT.ap[0:8, 0:1])
        init = 0.0 if T == 0 else Frow.ap[:, t0 - 1:t0]
        S.op("dve", lambda e, init=init, t0=t0: e.tensor_tensor_scan(Frow.ap[:, t0:t0 + TT], ones_f.ap[0:8, 0:TT], lnb.ap, init, ALU.mult, ALU.subtract),
             r=[Frow, ones_f, lnb], w=[Frow])
        for s_ in range(ns):
            blk = T * ns + s_
            pf = PFR.next()
            tr(pf.ap[:, 0:8], Frow.ap[0:8, blk * 128:(blk + 1) * 128], identf.ap[0:8, 0:8], [Frow, identf], [pf])
            ts(negF_tok.ap[:, blk, :], pf.ap[:, 0:8], -1.0, None, ALU.mult, None, [pf], [negF_tok])

    for T in range(NTILES or (NT // TT)):
        t0 = T * TT
        sec_load(T, t0)
        r0 = T * 1024
        S.dma("pool", D_UV, UVb_d[r0:r0 + 1024, 0, :], None, peer_u[r0:r0 + 1024, :])
        S.dma("pool", D_UV, UVb_d[r0:r0 + 1024, 1, :], None, peer_v[r0:r0 + 1024, :])
        if 'mla' in SECT: sec_mla(T, t0)
        if 'rope' in SECT: sec_rope(T, t0)
        if 'fox' in SECT: sec_fox(T, t0)
        if 'forget' in SECT: sec_forget(T, t0)
    dbg("negF", negF_tok, [128, 32, 8], F32)
    dbg("Frow", Frow, [8, NT], F32)
    if stop_after <= 1:
        final_bufs.extend([D_kTm, D_krT, D_vm, D_kTf, D_vf])
        S.finish(final_bufs)
        return nc

    S.barrier()
    RP = Region(arena, 24, 88)
    QN = RP.alloc("QN", [128, 8, NQ], BF16)
    QR = RP.alloc("QR", [64, 8, NQ], BF16)
    FQ = RP.alloc("FQ", [128, 8, NQ], BF16)
    Fq_row = RP.alloc("Fq_row", [1, 8, NQ], BF16)
    R2 = Region(arena, 88, 190)
    Wqc = R2.alloc("Wqc", [128, 16, 512], BF16)
    Wqf = R2.alloc("Wqf", [128, 16, 1024], BF16)
    Wuq = R2.alloc("Wuq", [128, 4, 1536], BF16)
    Wuqsw = R2.alloc("Wuqsw", [128, 4, 8, 64], BF16)
    mark = R2.off
    stg2 = R2.alloc("stg2", [128, 4, 1536], F32)
    S.dma("pool", Wqc, Wqc.ap, None, w_in_v[:, :, 0:512])
    S.dma("pool", Wqf, Wqf.ap, None, w_in_v[:, :, 1088:2112])
    S.dma("sp", stg2, stg2.ap, None, w_uq.rearrange("(c p) n -> p c n", p=128))
    for fc in range(4):
        ts(Wuq.ap[:, fc, :], stg2.ap[:, fc, :], gq.ap[:, fc:fc + 1], None, ALU.mult, None, [stg2, gq], [Wuq])
    for fc in range(4):
        src = Wuq.ap[:, fc, :].rearrange("p (h c) -> p h c", c=192)
        cp(Wuqsw.ap[:, fc, :, 0:32], src[:, :, 160:192], [Wuq], [Wuqsw])
        cp(Wuqsw.ap[:, fc, :, 32:64], src[:, :, 128:160], [Wuq], [Wuqsw])
    S.barrier()
    R2.off = mark
    xb2 = R2.alloc("xb2", [128, ns, D], BF16)
    xT2 = R2.alloc("xT2", [128, 16, TT], BF16)
    cqn, emit_cq = proj_rms(R2, Wqc, 0, xT2, TT, "q")
    cosq, sinq, emit_ropeq = rope_tables(R2, pos_q, 0, TT, "q")
    T1q = R2.alloc("T1q", [64, TT], F32); T2q = R2.alloc("T2q", [64, TT], F32)
    selb = R2.alloc("selb", [128, 4, 128], F32)
    S.dma("sp", selb, selb.ap, None, sel_in)
    for T in range(NQ // TT):
        t0 = T * TT
        S.dma("pool", xb2, xb2.ap, None, xq[t0:t0 + TT, :].rearrange("(s p) d -> p s d", p=128))
        transpose_in(xb2, xT2, ns)
        emit_cq()
        emit_ropeq(t0)
        for h in range(8):
            pf = PFR.next()
            for fc in range(4):
                mm(pf.ap[:, 0:TT], Wuq.ap[:, fc, h * 192:h * 192 + 128], cqn.ap[:, fc, :], fc == 0, fc == 3, [Wuq, cqn], [pf])
            evac(QN.ap[:, h, t0:t0 + TT], pf.ap[:, 0:TT], [pf], [QN])
            pk = PFR.next(); pks = PFR.next()
            for fc in range(4):
                mm(pk.ap[0:64, 0:TT], Wuq.ap[:, fc, h * 192 + 128:h * 192 + 192], cqn.ap[:, fc, :], fc == 0, fc == 3, [Wuq, cqn], [pk])
            for fc in range(4):
                mm(pks.ap[0:64, 0:TT], Wuqsw.ap[:, fc, h, :], cqn.ap[:, fc, :], fc == 0, fc == 3, [Wuqsw, cqn], [pks])
            tt(T1q.ap, pk.ap[0:64, 0:TT], cosq.ap, ALU.mult, [pk, cosq], [T1q])
            tt(T2q.ap, pks.ap[0:64, 0:TT], sinq.ap, ALU.mult, [pks, sinq], [T2q])
            tt(QR.ap[:, h, t0:t0 + TT], T1q.ap, T2q.ap, ALU.add, [T1q, T2q], [QR])
            pf = PFR.next()
            for ck in range(16):
                mm(pf.ap[:, 0:TT], Wqf.ap[:, ck, h * 128:(h + 1) * 128], xT2.ap[:, ck, :], ck == 0, ck == 15, [Wqf, xT2], [pf])
            evac(FQ.ap[:, h, t0:t0 + TT], pf.ap[:, 0:TT], [pf], [FQ])
    for i in range(8):
        for h in range(8):
            pf = PFR.next()
            for m in range(4):
                mm(pf.ap[0:1, 0:128], negF_tok.ap[:, 4 * i + m, h:h + 1], selb.ap[:, m, :], m == 0, m == 3, [negF_tok, selb], [pf])
            ts(Fq_row.ap[0:1, h, i * 128:(i + 1) * 128], pf.ap[0:1, 0:128], -1.0 / SC_FOX, None, ALU.mult, None, [pf], [Fq_row])
    dbg("QN", QN, [128, 8, NQ], BF16)
    dbg("QR", QR, [64, 8, NQ], BF16)
    dbg("FQ", FQ, [128, 8, NQ], BF16)
    dbg("Fqrow", Fq_row, [1, 8, NQ], BF16)
    if stop_after <= 2:
        S.finish(final_bufs)
        return nc

    S.barrier()
    OT = Region(arena, 158, 190).alloc("OT", [128, 16, NQ], BF16)
    R3 = Region(arena, 88, 158)
    KT = RR([R3.alloc("KT%d" % i, [128, NT], BF16) for i in range(2)])
    KR = R3.alloc("KR", [64, NT], BF16)
    VV = RR([R3.alloc("V%d" % i, [128, 32, 129], BF16) for i in range(2)])
    PT = RR([R3.alloc("PT%d" % i, [128, 512], BF16) for i in range(4)])
    recb = RR([R3.alloc("recb%d" % i, [128, 512], F32) for i in range(2)])
    for v in VV.items:
        S.op("dve", lambda e, v=v: e.memset(v.ap[:, :, 128:129], 1.0), w=[v])
    S.dma("sp", KR, KR.ap, D_krT, krT_d)
    OACC = [PF[0], PF[1]]; DEN = [PF[2], PF[3]]; SAB = [PF[4], PF[5]]
    NHD = int(os.environ.get("NHD", "16"))
    for hd in range(NHD):
        mla = hd < 8; h = hd % 8
        kt = KT.next(); v = VV.next()
        S.dma("sp", kt, kt.ap, (D_kTm if mla else D_kTf), (kTm_d if mla else kTf_d)[h])
        S.dma("sp", v, v.ap[:, :, 0:128], (D_vm if mla else D_vf), (vm_d if mla else vf_d)[h])
        for kb in range(32):
            g = kb // 4; m = kb % 4
            parts = []
            if g < 4:
                parts.append((0, g * 128, 512))
                parts.append((1, 512, 1024))
            else:
                parts.append((1, g * 128, 1024))
            kcols = slice(kb * 128, (kb + 1) * 128)
            pts = []
            for (bk, c0, c1) in parts:
                n = c1 - c0
                ps = SAB[bk]
                out = ps.ap[:, 0:n]
                has_diag = (c0 == g * 128)
                if mla:
                    mm(out, kt.ap[:, kcols], QN.ap[:, h, c0:c1], True, False, [kt, QN], [ps])
                    mm(out, KR.ap[0:64, kcols], QR.ap[0:64, h, c0:c1], False, not has_diag, [KR, QR], [ps])
                else:
                    mm(out, kt.ap[:, kcols], FQ.ap[:, h, c0:c1], True, False, [kt, FQ], [ps])
                    mm(out, ones_bf.ap[0:1, 0:128], Fq_row.ap[0:1, h, c0:c1], False, not has_diag, [ones_bf, Fq_row], [ps])
                if has_diag:
                    mm(ps.ap[:, 0:128], ident.ap, negmask.ap[:, m, :], False, True, [ident, negmask], [ps])
                pt = PT.next()
                if mla:
                    act(pt.ap[:, 0:n], out, AF.Exp, [ps], [pt], scale=SC_MLA)
                else:
                    act(pt.ap[:, 0:n], out, AF.Exp, [ps, negF_tok], [pt], bias=negF_tok.ap[:, kb, h:h + 1], scale=SC_FOX)
                pts.append((bk, c0, c1, pt))
            for (bk, c0, c1, pt) in pts:
                n = c1 - c0
                o0 = c0 - bk * 512
                last = (kb == 31)
                mm(OACC[bk].ap[:, o0:o0 + n], v.ap[:, kb, 0:128], pt.ap[:, 0:n], kb == 0, last, [v, pt], [OACC[bk]])
                mm(DEN[bk].ap[:, o0:o0 + n], ones_bf.ap, pt.ap[:, 0:n], kb == 0, last, [ones_bf, pt], [DEN[bk]])
        for bk in range(2):
            rb = recb.next()
            S.op("dve", lambda e, rb=rb, bk=bk: e.reciprocal(rb.ap, DEN[bk].ap), r=[DEN[bk]], w=[rb])
            tt(OT.ap[:, hd, bk * 512:(bk + 1) * 512], OACC[bk].ap, rb.ap, ALU.mult, [OACC[bk], rb], [OT])
    dbg("OT", OT, [128, 16, NQ], BF16)
    if stop_after <= 3:
        S.finish(final_bufs)
        return nc

    S.barrier()
    H1 = Region(arena, 8, 72).alloc("H1", [128, 8, D], F32)
    R4 = Region(arena, 72, 158)
    WoutC = RR([R4.alloc("WoutC%d" % i, [128, 16, 512], BF16) for i in range(2)])
    xqt = RR([R4.alloc("xqt%d" % i, [128, D], F32) for i in range(2)])
    Gbc = R4.alloc("Gbc", [128, D], F32)
    Bbc = R4.alloc("Bbc", [128, D], F32)
    stats = R4.alloc("stats", [128, 4, 6], F32)
    mv = R4.alloc("mv", [128, 2], F32)
    rstd = R4.alloc("rstd", [128, 1], F32)
    w_out_v = w_out.rearrange("(c p) n -> p c n", p=128)

    def layer_norm_rows(Xap, Xbuf, Gb, Bb, stats, mv, rstd):
        for c4 in range(4):
            S.op("dve", lambda e, c4=c4, stats=stats: e.bn_stats(stats.ap[:, c4, :], Xap[:, c4 * 512:(c4 + 1) * 512]), r=[Xbuf], w=[stats])
        S.op("dve", lambda e, stats=stats, mv=mv: e.bn_aggr(mv.ap, stats.ap), r=[stats], w=[mv])
        act(rstd.ap, mv.ap[:, 1:2], AF.Sqrt, [mv, epsT], [rstd], bias=epsT.ap[:, 0:1])
        S.op("dve", lambda e, rstd=rstd: e.reciprocal(rstd.ap, rstd.ap), r=[rstd], w=[rstd])
        ts(Xap, Xap, mv.ap[:, 0:1], rstd.ap[:, 0:1], ALU.subtract, ALU.mult, [Xbuf, mv, rstd], [Xbuf])
        if Gb is not None:
            tt(Xap, Xap, Gb.ap, ALU.mult, [Xbuf, Gb], [Xbuf])
            tt(Xap, Xap, Bb.ap, ALU.add, [Xbuf, Bb], [Xbuf])

    S.dma("sp", Gbc, Gbc.ap, None, ln1g.partition_broadcast(128))
    S.dma("sp", Bbc, Bbc.ap, None, ln1b.partition_broadcast(128))
    for nf in range(4):
        wc = WoutC.next()
        for half in range(2):
            S.dma("pool", wc, wc.ap[:, half * 8:(half + 1) * 8, :], None, w_out_v[:, half * 8:(half + 1) * 8, nf * 512:(nf + 1) * 512])
        for i in range(8):
            if nf == 0:
                xt_ = xqt.next()
                S.dma("sp", xt_, xt_.ap, None, xq[i * 128:(i + 1) * 128, :])
                S.op("act", lambda e, xt_=xt_, i=i: e.mul(H1.ap[:, i, :], xt_.ap, ALPHA), r=[xt_], w=[H1])
            pf = PFR.next()
            for cc in range(16):
                mm(pf.ap, OT.ap[:, cc, i * 128:(i + 1) * 128], wc.ap[:, cc, :], cc == 0, cc == 15, [OT, wc], [pf])
            tt(H1.ap[:, i, nf * 512:(nf + 1) * 512], H1.ap[:, i, nf * 512:(nf + 1) * 512], pf.ap, ALU.add, [H1, pf], [H1])
    for i in range(8):
        layer_norm_rows(H1.ap[:, i, :], H1, Gbc, Bbc, stats, mv, rstd)
    dbg("H1", H1, [128, 8, D], F32)
    S.barrier()
    RB = Region(arena, 72, 136)
    H1B = RB.alloc("H1B", [128, 8, D], BF16)
    H1T = RB.alloc("H1T", [128, 16, NQ], BF16)
    for i in range(8):
        cp(H1B.ap[:, i, :], H1.ap[:, i, :], [H1], [H1B], eng=("act" if i % 2 else "dve"))
    for cc in range(16):
        for ig in range(2):
            pb = PBR.next()
            for k in range(4):
                i = ig * 4 + k
                tr(pb.ap[:, k * 128:(k + 1) * 128], H1B.ap[:, i, cc * 128:(cc + 1) * 128], ident.ap, [H1B, ident], [pb])
            evac(H1T.ap[:, cc, ig * 512:(ig + 1) * 512], pb.ap[:, 0:512], [pb], [H1T])
    if stop_after <= 4:
        S.finish(final_bufs)
        return nc

    R5 = Region(arena, 136, 190)
    WgC = RR([R5.alloc("WgC%d" % i, [128, 16, 512], BF16) for i in range(2)])
    Wp = R5.alloc("Wp", [128, 2, D], BF16)
    pT = R5.alloc("pT", [128, 2, NQ], BF16)
    pb16 = R5.alloc("pb16", [128, 8, 256], BF16)
    sig = RR([R5.alloc("sig%d" % i, [128, 512], F32) for i in range(2)])
    S.dma("pool", Wp, Wp.ap, None, ple_wproj.rearrange("(c p) n -> p c n", p=128))
    S.dma("pool", pb16, pb16.ap, None, pq.rearrange("(i p) d -> p i d", p=128))
    for c2 in range(2):
        for ig in range(2):
            pb = PBR.next()
            for k in range(4):
                i = ig * 4 + k
                tr(pb.ap[:, k * 128:(k + 1) * 128], pb16.ap[:, i, c2 * 128:(c2 + 1) * 128], ident.ap, [pb16, ident], [pb])
            evac(pT.ap[:, c2, ig * 512:(ig + 1) * 512], pb.ap[:, 0:512], [pb], [pT])
    wg_v = ple_wgate.rearrange("(c p) n -> p c n", p=128)
    for nf in range(4):
        wc = WgC.next()
        for half in range(2):
            S.dma("pool", wc, wc.ap[:, half * 8:(half + 1) * 8, :], None, wg_v[:, half * 8:(half + 1) * 8, nf * 512:(nf + 1) * 512])
        for i in range(8):
            pf = PFR.next()
            for cc in range(16):
                mm(pf.ap, H1T.ap[:, cc, i * 128:(i + 1) * 128], wc.ap[:, cc, :], cc == 0, cc == 15, [H1T, wc], [pf])
            sg = sig.next()
            act(sg.ap, pf.ap, AF.Sigmoid, [pf], [sg])
            pf2 = PFR.next()
            for c2 in range(2):
                mm(pf2.ap, pT.ap[:, c2, i * 128:(i + 1) * 128], Wp.ap[:, c2, nf * 512:(nf + 1) * 512], c2 == 0, c2 == 1, [pT, Wp], [pf2])
            tt(sg.ap, sg.ap, pf2.ap, ALU.mult, [sg, pf2], [sg])
            Hs = H1.ap[:, i, nf * 512:(nf + 1) * 512]
            stt(Hs, Hs, ALPHA, sg.ap, ALU.mult, ALU.add, [H1, sg], [H1])
    dbg("Y", H1, [128, 8, D], F32)
    if stop_after <= 5:
        S.finish(final_bufs)
        return nc

    S.barrier()
    KT12 = RC.alloc("KT12", [128, 8, 128], BF16)
    Rpre = Region(arena, 136, 190)
    Wpq = Rpre.alloc("Wpq", [128, 16, 1024], BF16)
    QP_all = Rpre.alloc("QP_all", [128, 8, NQ], BF16)
    k12 = Rpre.alloc("k12", [128, 128], BF16)
    pwq_v = peer_wq.rearrange("(c p) n -> p c n", p=128)
    for half in range(2):
        S.dma("pool", Wpq, Wpq.ap[:, half * 8:(half + 1) * 8, :], None, pwq_v[:, half * 8:(half + 1) * 8, :])
    for h in range(8):
        S.dma("pool", k12, k12.ap[:, 0:64], None, keys1[h])
        S.dma("pool", k12, k12.ap[:, 64:128], None, keys2[h])
        pb = PBR.next()
        tr(pb.ap[:, 0:128], k12.ap, ident.ap, [k12, ident], [pb])
        evac(KT12.ap[:, h, :], pb.ap[:, 0:128], [pb], [KT12])
    for h in range(8):
        for half in range(2):
            pf = PFR.next()
            for ck in range(16):
                mm(pf.ap, Wpq.ap[:, ck, h * 128:(h + 1) * 128], H1T.ap[:, ck, half * 512:(half + 1) * 512], ck == 0, ck == 15, [Wpq, H1T], [pf])
            evac(QP_all.ap[:, h, half * 512:(half + 1) * 512], pf.ap, [pf], [QP_all])

    S.barrier()
    NTI = int(os.environ.get("NTI", "8"))
    RE = Region(arena, 104, 112)
    EI_t = [RE.alloc("EI_t%d" % i, [128, 128], I32) for i in range(8)]
    GT_t = [RE.alloc("GT_t%d" % i, [128, 128], F32) for i in range(8)]
    R5b = Region(arena, 112, 120)
    SCb = R5b.alloc("SCb", [128, 256], F32)
    tmpS = R5b.alloc("tmpS", [128, 128], F32)
    V12 = R5b.alloc("V12", [128, 32], F32)
    I12 = R5b.alloc("I12", [128, 32], U32)
    I12f = R5b.alloc("I12f", [128, 32], F32)
    cand = R5b.alloc("cand", [128, 16, 16], F32)
    cidx = R5b.alloc("cidx", [128, 16, 16], F32)
    tmp256 = R5b.alloc("tmp256", [128, 256], F32)
    junk256 = R5b.alloc("junk256", [128, 256], F32)
    iota_f = R5b.alloc("iota_f", [128, 256], F32)
    SCv = R5b.alloc("SCv", [128, 16], F32)
    posu = R5b.alloc("posu", [128, 16], U32)
    posf2 = R5b.alloc("posf2", [128, 16], F32)
    EIf = R5b.alloc("EIf", [128, 16], F32)
    gexp = R5b.alloc("gexp", [128, 16], F32)
    negm = R5b.alloc("negm", [128, 1], F32)
    Zs = R5b.alloc("Zs", [128, 1], F32)
    iota_i_ap = tmp256.ap.bitcast(I32)
    S.op("pool", lambda e: e.iota(iota_i_ap, pattern=[[1, 256]], base=0, channel_multiplier=0), w=[tmp256])
    cp(iota_f.ap, iota_i_ap, [tmp256], [iota_f])
    cand_f = cand.ap.rearrange("p a b -> p (a b)")
    cidx_f = cidx.ap.rearrange("p a b -> p (a b)")

    def top16(vals_ap, vbuf, out_v, out_i, obufs, scratch, n):
        S.op("dve", lambda e: e.max(out_v[:, 0:8], vals_ap), r=[vbuf], w=[obufs[0]])
        S.op("dve", lambda e: e.max_index(out_i[:, 0:8], out_v[:, 0:8], vals_ap), r=[vbuf, obufs[0]], w=[obufs[1]])
        S.op("dve", lambda e: e.match_replace(scratch.ap[:, 0:n], out_v[:, 0:8], vals_ap, NEG), r=[vbuf, obufs[0]], w=[scratch])
        S.op("dve", lambda e: e.max(out_v[:, 8:16], scratch.ap[:, 0:n]), r=[scratch], w=[obufs[0]])
        S.op("dve", lambda e: e.max_index(out_i[:, 8:16], out_v[:, 8:16], scratch.ap[:, 0:n]), r=[scratch, obufs[0]], w=[obufs[1]])

    def routing_head(i, h):
        pf = PF[4]; pf2 = PF[5]
        qs = slice(i * 128, (i + 1) * 128)
        mm(pf.ap[:, 0:128], QP_all.ap[0:64, h, qs], KT12.ap[0:64, h, :], True, True, [QP_all, KT12], [pf])
        mm(pf2.ap[:, 0:128], QP_all.ap[64:128, h, qs], KT12.ap[64:128, h, :], True, True, [QP_all, KT12], [pf2])
        cp(SCb.ap[:, 0:128], pf.ap[:, 0:128], [pf], [SCb])
        cp(SCb.ap[:, 128:256], pf2.ap[:, 0:128], [pf2], [SCb])
        for half in range(2):
            top16(SCb.ap[:, half * 128:(half + 1) * 128], SCb, V12.ap[:, half * 16:(half + 1) * 16], I12.ap[:, half * 16:(half + 1) * 16],
                  (V12, I12), tmpS, 128)
        cp(I12f.ap, I12.ap, [I12], [I12f])
        v1b = V12.ap[:, 0:16].unsqueeze(2).to_broadcast([128, 16, 16])
        v2b = V12.ap[:, 16:32].unsqueeze(1).to_broadcast([128, 16, 16])
        i1b = I12f.ap[:, 0:16].unsqueeze(2).to_broadcast([128, 16, 16])
        i2b = I12f.ap[:, 16:32].unsqueeze(1).to_broadcast([128, 16, 16])
        tt(cand.ap, v1b, v2b, ALU.add, [V12], [cand])
        stt(cidx.ap, i1b, 128.0, i2b, ALU.mult, ALU.add, [I12f], [cidx])
        top16(cand_f, cand, SCv.ap, posu.ap, (SCv, posu), tmp256, 256)
        cp(posf2.ap, posu.ap, [posu], [posf2])
        S.op("dve", lambda e: e.memset(EIf.ap, 0.0), w=[EIf])
        for k in range(16):
            stt(junk256.ap, iota_f.ap, posf2.ap[:, k:k + 1], cidx_f, ALU.is_equal, ALU.mult, [iota_f, posf2, cidx], [junk256, EIf],
                accum=EIf.ap[:, k:k + 1])
        cp(EI_t[i].ap[:, h * 16:(h + 1) * 16], EIf.ap, [EIf], [EI_t[i]])
        ts(negm.ap, SCv.ap[:, 0:1], -1.0, None, ALU.mult, None, [SCv], [negm])
        S.op("dve", lambda e: e.memset(Zs.ap, 0.0), w=[Zs])
        act(gexp.ap, SCv.ap, AF.Exp, [SCv, negm], [gexp, Zs], bias=negm.ap[:, 0:1], accum=Zs.ap[:, 0:1])
        S.op("dve", lambda e: e.reciprocal(Zs.ap, Zs.ap), r=[Zs], w=[Zs])
        ts(GT_t[i].ap[:, h * 16:(h + 1) * 16], gexp.ap, Zs.ap[:, 0:1], None, ALU.mult, None, [gexp, Zs], [GT_t[i]])

    CHS = 2
    R5c = Region(arena, 120, 168)
    UVG = RR([R5c.alloc("UVG%d" % k, [128, CHS, 2 * D], BF16) for k in range(3)])
    UV_flat = UVb_d.rearrange("e t d -> e (t d)")
    R5s = Region(arena, 184, 190)
    Adot = R5s.alloc("Adot", [128, 128], F32)
    AW = R5s.alloc("AW", [128, 128], F32)
    DG = RR([R5s.alloc("DG%d" % k, [128, 128], BF16) for k in range(2)])
    stats2 = R5s.alloc("stats2", [128, 4, 6], F32)
    mv2 = R5s.alloc("mv2", [128, 2], F32)
    rstd2 = R5s.alloc("rstd2", [128, 1], F32)
    Gh = R5s.alloc("Gh", [128, 512], F32)
    Bh = R5s.alloc("Bh", [128, 512], F32)
    NSL = int(os.environ.get("NSL", "128"))
    for h in range(8):
        routing_head(0, h)
    for i in range(NTI):
        S.op("dve", lambda e: e.memset(Adot.ap, 0.0), w=[Adot])
        nch = NSL // CHS
        for ch in range(nch):
            uvg = UVG.next()
            for c in range(CHS):
                slot = ch * CHS + c
                S.op("pool", lambda e, uvg=uvg, c=c, slot=slot, i=i: e.indirect_dma_start(
                    out=uvg.ap[:, c, :], out_offset=None, in_=UV_flat,
                    in_offset=bass.IndirectOffsetOnAxis(ap=EI_t[i].ap[:, slot:slot + 1], axis=0)), r=[EI_t[i], D_UV], w=[uvg], dma=True)
            for c in range(CHS):
                slot = ch * CHS + c
                stt(uvg.ap[:, c, 0:D], uvg.ap[:, c, 0:D], 1.0, H1B.ap[:, i, :], ALU.mult, ALU.mult, [uvg, H1B], [uvg, Adot], accum=Adot.ap[:, slot:slot + 1])
            sl = slice(ch * CHS, ch * CHS + CHS)
            act(AW.ap[:, sl], Adot.ap[:, sl], AF.Gelu, [Adot], [AW])
            tt(AW.ap[:, sl], AW.ap[:, sl], GT_t[i].ap[:, sl], ALU.mult, [AW, GT_t[i]], [AW])
            for c in range(CHS):
                slot = ch * CHS + c
                dg = DG.next()
                ts(dg.ap, ident.ap, AW.ap[:, slot:slot + 1], None, ALU.mult, None, [ident, AW], [dg])
                for nf in range(4):
                    mm(PF[nf].ap, dg.ap, uvg.ap[:, c, D + nf * 512:D + (nf + 1) * 512], slot == 0, slot == NSL - 1, [dg, uvg], [PF[nf]])
            if (ch % 8 == 7) and (i + 1 < NTI) and (ch // 8 < 8):
                routing_head(i + 1, ch // 8)
        for nf in range(4):
            Hs = H1.ap[:, i, nf * 512:(nf + 1) * 512]
            tt(Hs, Hs, PF[nf].ap, ALU.add, [H1, PF[nf]], [H1])
        layer_norm_rows(H1.ap[:, i, :], H1, None, None, stats2, mv2, rstd2)
        for qf in range(4):
            S.dma("sp", Gh, Gh.ap, None, ln2g[:, qf * 512:(qf + 1) * 512].partition_broadcast(128))
            S.dma("sp", Bh, Bh.ap, None, ln2b[:, qf * 512:(qf + 1) * 512].partition_broadcast(128))
            Xh = H1.ap[:, i, qf * 512:(qf + 1) * 512]
            tt(Xh, Xh, Gh.ap, ALU.mult, [H1, Gh], [H1])
            tt(Xh, Xh, Bh.ap, ALU.add, [H1, Bh], [H1])
        S.dma("sp", D_out, out_d[i * 128:(i + 1) * 128, :], H1, H1.ap[:, i, :])
    S.finish(final_bufs)
    return nc


def own_rows(j):
    return np.concatenate([np.arange((4 * i + j) * 128, (4 * i + j + 1) * 128) for i in range(8)])


def core_inputs(inp, c):
    b, j = c // 4, c % 4
    own = own_rows(j)
    f = lambda a: np.ascontiguousarray(a, dtype=np.float32)
    x = inp["x"][b]
    negmask = np.zeros((128, 4, 128), np.float32)
    sel = np.zeros((128, 4, 128), np.float32)
    kk = np.arange(128)[:, None]; qq = np.arange(128)[None, :]
    for m in range(4):
        if m == j:
            negmask[:, m, :] = np.where(kk <= qq, 0.0, NEG)
            sel[:, m, :] = np.eye(128, dtype=np.float32)
        elif m > j:
            negmask[:, m, :] = NEG
    inv_freq = (1.0 / (10000.0 ** (np.arange(0, 64, 2, dtype=np.float32) / 64))).astype(np.float32)
    invf = np.concatenate([inv_freq, inv_freq])[:, None]
    sgn = np.concatenate([-np.ones(32, np.float32), np.ones(32, np.float32)])[:, None]
    return {
        "xkv": f(x), "xq": f(x[own]), "pq": f(inp["p"][0, b][own]),
        "pos_kv": np.ascontiguousarray(inp["positions"][b][None, :], dtype=np.int32),
        "pos_q": np.ascontiguousarray(inp["positions"][b][own][None, :], dtype=np.int32),
        "w_in": f(inp["w_in"][0]), "w_uq": f(inp["w_uq"][0]), "w_ukv": f(inp["w_ukv"][0]),
        "w_out": f(inp["w_out"][0]), "peer_wq": f(inp["peer_wq"][0]),
        "keys1": f(inp["peer_keys1"][0]), "keys2": f(inp["peer_keys2"][0]),
        "peer_u": f(inp["peer_u"][0]), "peer_v": f(inp["peer_v"][0]),
        "ple_wgate": f(inp["ple_wgate"][0]), "ple_wproj": f(inp["ple_wproj"][0]),
        "gqT": f(inp["g_q_norm"][0].reshape(4, 128).T), "gkvT": f(inp["g_kv_norm"][0].reshape(4, 128).T),
        "bfg": f(inp["b_forget"][0][:, None]),
        "ln1g": f(inp["ln1_g"]), "ln1b": f(inp["ln1_b"]), "ln2g": f(inp["ln2_g"]), "ln2b": f(inp["ln2_b"]),
        "negmask": negmask, "sel": sel, "invf": f(invf), "sgn": f(sgn),
    }


_NC_CACHE = {}


def kernel(**inputs):
    inp = {k: np.asarray(v) for k, v in inputs.items()}
    if "nc" not in _NC_CACHE:
        _NC_CACHE["nc"] = build_nc()
    nc = _NC_CACHE["nc"]
    maps = [core_inputs(inp, c) for c in range(8)]
    res = run_bass_kernel_spmd(nc, maps, core_ids=list(range(8)))
    out = np.zeros((2, NT, D), np.float32)
    for c in range(8):
        b, j = c // 4, c % 4
        out[b, own_rows(j)] = np.asarray(res.results[c]["out"], dtype=np.float32)
    return out
```

```python
import numpy as np
from contextlib import ExitStack
import concourse.bass as bass
import concourse.mybir as mybir
from concourse.bass_utils import run_bass_kernel_spmd
from concourse.alu_op_type import AluOpType as ALU

F32 = mybir.dt.float32
BF16 = mybir.dt.bfloat16
I32 = mybir.dt.int32
U32 = mybir.dt.uint32
AF = mybir.ActivationFunctionType


class Buf:
    def __init__(self, name, ap):
        self.name = name
        self.ap = ap
        self.last_w = []
        self.reads = []
        self.dsem = None
        self.dcount = 0
        self.is_dram = False
        self.excl = False

    def __getitem__(self, k):
        return self.ap[k]


class Sched:
    ENG = ["pe", "dve", "act", "pool", "sp"]

    def __init__(self, nc):
        self.nc = nc
        self.stack = ExitStack()
        self.eobj = {"pe": nc.tensor, "dve": nc.vector, "act": nc.scalar, "pool": nc.gpsimd, "sp": nc.sync}
        self.q = {e: [] for e in self.ENG}
        self.cnt = {e: 0 for e in self.ENG}
        self.esem = {e: self.stack.enter_context(nc.semaphore("es_" + e)) for e in self.ENG}
        self.waited = {e: {} for e in self.ENG}
        self.nbuf = 0
        self.dma_tokens = []
        self.free_dsems = []

    def sbuf(self, name, shape, dtype, stack=None):
        t = (stack or self.stack).enter_context(self.nc.sbuf_tensor(name, list(shape), dtype))
        return Buf(name, t[:])

    def psum(self, name, shape, dtype, stack=None):
        t = (stack or self.stack).enter_context(self.nc.psum_tensor(name, list(shape), dtype))
        b = Buf(name, t[:])
        b.excl = True
        return b

    def dram(self, ap, name="dram"):
        b = Buf(name, ap)
        b.is_dram = True
        return b

    def _dsem(self, buf):
        if buf.dsem is None:
            self.nbuf += 1
            buf.dsem = self.stack.enter_context(self.nc.semaphore("ds%d" % self.nbuf))
        return buf.dsem

    def _deps(self, eng, r, w):
        deps = []
        for b in r:
            deps.extend(b.last_w)
        for b in w:
            deps.extend(b.last_w)
            deps.extend(b.reads)
        out = {}
        for (sem, val, e) in deps:
            if e == eng and eng in ("pe", "sp"):
                continue
            key = id(sem)
            if self.waited[eng].get(key, 0) >= val:
                continue
            if key not in out or out[key][1] < val:
                out[key] = (sem, val)
        for key, (sem, val) in out.items():
            self.waited[eng][key] = val
        return list(out.values())

    def _commit(self, tok, r, w, accumulate=False):
        for b in r:
            b.reads.append(tok)
        for b in w:
            if accumulate:
                b.last_w = [t for t in b.last_w if t[0] is not tok[0]] + [tok]
            else:
                b.last_w = [tok]
            b.reads = []

    def op(self, eng, fn, r=(), w=(), dma=False):
        r = list(r); w = list(w)
        for b in list(r):
            if b.excl:
                r.remove(b)
                if b not in w:
                    w.append(b)
        deps = self._deps(eng, r, w)
        if dma:
            wb = w[0]
            own = r[0] if (wb.is_dram and r) else wb
            sem = self._dsem(own)
            own.dcount += 16
            tok = (sem, own.dcount, "dma")
            self.q[eng].append((deps, fn, sem, 16))
            self.dma_tokens.append(tok)
            self._commit(tok, r, w, accumulate=wb.is_dram)
            return
        else:
            self.cnt[eng] += 1
            tok = (self.esem[eng], self.cnt[eng], eng)
            self.q[eng].append((deps, fn, self.esem[eng], 1))
        self._commit(tok, r, w)

    def dma(self, eng, wbuf, out_ap, rbuf, in_ap, **kw):
        r = [rbuf] if rbuf is not None else []
        self.op(eng, lambda e: e.dma_start(out=out_ap, in_=in_ap, **kw), r=r, w=[wbuf], dma=True)

    def barrier(self):
        toks = [(self.esem[e], self.cnt[e], e) for e in self.ENG if self.cnt[e] > 0]
        toks += self.dma_tokens
        self.dma_tokens = []
        for eng in self.ENG:
            out = {}
            for (sem, val, e) in toks:
                if e == eng:
                    continue
                key = id(sem)
                if self.waited[eng].get(key, 0) >= val:
                    continue
                if key not in out or out[key][1] < val:
                    out[key] = (sem, val)
            for key, (sem, val) in out.items():
                self.waited[eng][key] = val
            if out:
                self.q[eng].append((list(out.values()), None, None, 0))

    def finish(self, out_bufs):
        deps = []
        for b in out_bufs:
            for t in b.last_w:
                deps.append((t[0], t[1]))
        self.q["sp"].append((deps, None, None, 0))
        with self.nc.Block() as block:
            def mk(eng):
                def body(e):
                    for (deps, fn, sem, inc) in self.q[eng]:
                        for (s, v) in deps:
                            e.wait_ge(s, v)
                        if fn is not None:
                            ins = fn(e)
                            ins.then_inc(sem, inc)
                return body
            block.tensor(mk("pe"))
            block.vector(mk("dve"))
            block.scalar(mk("act"))
            block.gpsimd(mk("pool"))
            block.sync(mk("sp"))
        self.stack.close()


D = 2048
NT = 4096
NQ = 1024
TT = 256
ALPHA = 2.0 ** 0.25
EPS = 1e-6
PI = float(np.pi)
SC_MLA = 192.0 ** -0.5
SC_FOX = 128.0 ** -0.5
NEG = -1.0e30


class RR:
    def __init__(self, items):
        self.items = list(items); self.i = 0

    def next(self):
        x = self.items[self.i % len(self.items)]; self.i += 1
        return x


class Region:
    def __init__(self, arena, start_kb, end_kb):
        self.arena = arena; self.off = int(start_kb * 1024); self.end = int(end_kb * 1024)

    def alloc(self, name, shape, dtype):
        n = int(np.prod(shape[1:]))
        esz = 2 if dtype == BF16 else 4
        nb = (n * esz + 63) // 64 * 64
        assert self.off + nb <= self.end, (name, self.off, nb, self.end)
        o = self.off // 2
        ap = self.arena[0:shape[0], o:o + n * esz // 2]
        if dtype != BF16:
            ap = ap.bitcast(dtype)
        if len(shape) == 3:
            ap = ap.rearrange("p (a b) -> p a b", b=shape[2])
        elif len(shape) == 4:
            ap = ap.rearrange("p (a b c) -> p a b c", b=shape[2], c=shape[3])
        self.off += nb
        return Buf(name, ap)


import os
SECT = os.environ.get('SECT', 'mla,rope,fox,forget').split(',')
NTILES = int(os.environ.get('NTILES', '0'))


def build_nc(debug=False, stop_after=99):
    nc = bass.Bass("TRN2", target_bir_lowering=False)
    dbg_outs = {}

    def IN(name, shape, dtype=F32):
        return nc.dram_tensor(name, list(shape), dtype, kind="ExternalInput").ap()

    def SCR(name, shape, dtype):
        return nc.dram_tensor(name, list(shape), dtype, kind=("ExternalOutput" if debug else "Internal")).ap()

    xkv = IN("xkv", [NT, D]); xq = IN("xq", [NQ, D]); pq = IN("pq", [NQ, 256])
    pos_kv = IN("pos_kv", [1, NT], I32); pos_q = IN("pos_q", [1, NQ], I32)
    w_in = IN("w_in", [D, 4168]); w_uq = IN("w_uq", [512, 1536]); w_ukv = IN("w_ukv", [512, 2048])
    w_out = IN("w_out", [D, D]); peer_wq = IN("peer_wq", [D, 1024])
    keys1 = IN("keys1", [8, 128, 64]); keys2 = IN("keys2", [8, 128, 64])
    peer_u = IN("peer_u", [16384, D]); peer_v = IN("peer_v", [16384, D])
    ple_wgate = IN("ple_wgate", [D, D]); ple_wproj = IN("ple_wproj", [256, D])
    gqT = IN("gqT", [128, 4]); gkvT = IN("gkvT", [128, 4]); bfg = IN("bfg", [8, 1])
    ln1g = IN("ln1g", [1, D]); ln1b = IN("ln1b", [1, D]); ln2g = IN("ln2g", [1, D]); ln2b = IN("ln2b", [1, D])
    negmask_in = IN("negmask", [128, 4, 128]); sel_in = IN("sel", [128, 4, 128])
    invf_in = IN("invf", [64, 1]); sgn_in = IN("sgn", [64, 1])
    out_d = nc.dram_tensor("out", [NQ, D], F32, kind="ExternalOutput").ap()

    kTm_d = SCR("kTm_d", [8, 128, NT], BF16); krT_d = SCR("krT_d", [64, NT], BF16)
    vm_d = SCR("vm_d", [8, 128, 32, 128], BF16)
    kTf_d = SCR("kTf_d", [8, 128, NT], BF16); vf_d = SCR("vf_d", [8, 128, 32, 128], BF16)
    UVb_d = nc.dram_tensor("UVb_d", [16384, 2, D], BF16, kind="Internal").ap()

    S = Sched(nc)
    arena = S.stack.enter_context(nc.sbuf_tensor("arena", [128, 95 * 1024], BF16))
    D_kTm = S.dram(kTm_d); D_krT = S.dram(krT_d); D_vm = S.dram(vm_d); D_kTf = S.dram(kTf_d); D_vf = S.dram(vf_d)
    D_out = S.dram(out_d)
    D_UV = S.dram(UVb_d)
    final_bufs = [D_out]

    def dbg(name, buf, shape, dtype):
        if not debug:
            return
        t = nc.dram_tensor("dbg_" + name, list(shape), dtype, kind="ExternalOutput").ap()
        b = S.dram(t)
        S.dma("sp", b, t, buf, buf.ap)
        final_bufs.append(b)

    def mm(out, lhsT, rhs, start, stop, r, w):
        S.op("pe", lambda e: e.matmul(out, lhsT, rhs, start=start, stop=stop), r=r, w=w)

    def tr(out, in_, idn, r, w):
        S.op("pe", lambda e: e.transpose(out, in_, idn), r=r, w=w)

    def act(out, in_, func, r, w, bias=None, scale=None, accum=None):
        kw = {}
        if bias is not None: kw["bias"] = bias
        if scale is not None: kw["scale"] = scale
        if accum is not None: kw["accum_out"] = accum
        S.op("act", lambda e: e.activation(out, in_, func, **kw), r=r, w=w)

    def tt(out, a, b, op, r, w, eng="dve"):
        S.op(eng, lambda e: e.tensor_tensor(out, a, b, op), r=r, w=w)

    def ts(out, a, s1, s2, op0, op1, r, w, eng="dve", accum=None):
        if op1 is None:
            S.op(eng, lambda e: e.tensor_scalar(out, a, s1, None, op0), r=r, w=w)
        elif accum is not None:
            S.op(eng, lambda e: e.tensor_scalar(out, a, s1, s2, op0, op1, accum_out=accum), r=r, w=w)
        else:
            S.op(eng, lambda e: e.tensor_scalar(out, a, s1, s2, op0, op1), r=r, w=w)

    def stt(out, a, sc, b, op0, op1, r, w, accum=None):
        if accum is None:
            S.op("dve", lambda e: e.scalar_tensor_tensor(out=out, in0=a, scalar=sc, in1=b, op0=op0, op1=op1), r=r, w=w)
        else:
            S.op("dve", lambda e: e.scalar_tensor_tensor(out=out, in0=a, scalar=sc, in1=b, op0=op0, op1=op1, accum_out=accum), r=r, w=w)

    def cp(out, in_, r, w, eng="dve"):
        if eng == "act":
            S.op("act", lambda e: e.copy(out, in_), r=r, w=w)
        else:
            S.op(eng, lambda e: e.tensor_copy(out, in_), r=r, w=w)

    evac_i = [0]

    def evac(out, in_, r, w):
        evac_i[0] += 1
        cp(out, in_, r, w, eng=("act" if evac_i[0] % 2 else "dve"))

    PF = [S.psum("pf%d" % i, [128, 512], F32) for i in range(6)]
    PB = [S.psum("pb%d" % i, [128, 1024], BF16) for i in range(2)]
    PFR = RR(PF); PBR = RR(PB)

    RC = Region(arena, 0, 8)
    identf = RC.alloc("identf", [128, 128], F32)
    ident = RC.alloc("ident", [128, 128], BF16)
    ones_bf = RC.alloc("ones_bf", [128, 128], BF16)
    ones_f = RC.alloc("ones_f", [128, 256], F32)
    epsT = RC.alloc("epsT", [128, 1], F32)
    oneT = RC.alloc("oneT", [128, 1], F32)
    invf = RC.alloc("invf", [64, 1], F32)
    sgn = RC.alloc("sgn", [64, 1], F32)
    negb = RC.alloc("negb", [8, 1], F32)
    gkv = RC.alloc("gkv", [128, 4], F32)
    gq = RC.alloc("gq", [128, 4], F32)
    negF_tok = RC.alloc("negF_tok", [128, 32, 8], F32)
    negmask = RC.alloc("negmask", [128, 4, 128], BF16)
    EI = RC.alloc("EI", [128, 128], I32)
    GT = RC.alloc("GT", [128, 128], F32)

    S.op("pool", lambda e: e.memset(identf.ap, 0.0), w=[identf])
    S.op("pool", lambda e: e.affine_select(out=identf.ap, in_=identf.ap, pattern=[[-1, 128]], compare_op=ALU.not_equal,
                                           fill=1.0, base=0, channel_multiplier=1), r=[identf], w=[identf])
    cp(ident.ap, identf.ap, [identf], [ident])
    S.op("dve", lambda e: e.memset(ones_bf.ap, 1.0), w=[ones_bf])
    S.op("dve", lambda e: e.memset(ones_f.ap, 1.0), w=[ones_f])
    S.op("dve", lambda e: e.memset(epsT.ap, EPS), w=[epsT])
    S.op("dve", lambda e: e.memset(oneT.ap, 1.0), w=[oneT])
    S.dma("sp", invf, invf.ap, None, invf_in)
    S.dma("sp", sgn, sgn.ap, None, sgn_in)
    S.dma("sp", negb, negb.ap, None, bfg)
    ts(negb.ap, negb.ap, -1.0, None, ALU.mult, None, [negb], [negb])
    S.dma("sp", gkv, gkv.ap, None, gkvT)
    S.dma("sp", gq, gq.ap, None, gqT)
    S.dma("pool", negmask, negmask.ap, None, negmask_in)

    w_in_v = w_in.rearrange("(c p) n -> p c n", p=128)

    def rope_tables(R, pos_ap, c0, n, tag):
        posi = R.alloc("posi" + tag, [64, n], I32)
        posf = R.alloc("posf" + tag, [64, n], F32)
        ang = R.alloc("ang" + tag, [64, n], F32)
        tq = R.alloc("tq" + tag, [64, n], F32)
        ki = R.alloc("ki" + tag, [64, n], I32)
        cosb = R.alloc("cos" + tag, [64, n], F32)
        sinb = R.alloc("sin" + tag, [64, n], F32)

        def emit(c0):
            S.dma("sp", posi, posi.ap, None, pos_ap[0:1, c0:c0 + n].partition_broadcast(64))
            cp(posf.ap, posi.ap, [posi], [posf])
            for (phase, dst) in ((0.0, sinb), (PI / 2, cosb)):
                ts(ang.ap, posf.ap, invf.ap[:, 0:1], phase, ALU.mult, ALU.add, [posf, invf], [ang])
                ts(tq.ap, ang.ap, 1.0 / (2 * PI), None, ALU.mult, None, [ang], [tq])
                cp(ki.ap, tq.ap, [tq], [ki])
                cp(tq.ap, ki.ap, [ki], [tq])
                stt(ang.ap, tq.ap, -2 * PI, ang.ap, ALU.mult, ALU.add, [tq, ang], [ang])
                ts(tq.ap, ang.ap, PI, -2 * PI, ALU.is_gt, ALU.mult, [ang], [tq])
                tt(ang.ap, ang.ap, tq.ap, ALU.add, [ang, tq], [ang])
                ts(tq.ap, ang.ap, -PI, 2 * PI, ALU.is_lt, ALU.mult, [ang], [tq])
                tt(ang.ap, ang.ap, tq.ap, ALU.add, [ang, tq], [ang])
                ts(ang.ap, ang.ap, PI, -PI, ALU.min, ALU.max, [ang], [ang])
                act(dst.ap, ang.ap, AF.Sin, [ang], [dst])
            ts(sinb.ap, sinb.ap, sgn.ap[:, 0:1], None, ALU.mult, None, [sinb, sgn], [sinb])
        return cosb, sinb, emit

    def transpose_in(xb, xT, ns):
        for ck in range(16):
            pb = PBR.next()
            for s_ in range(ns):
                tr(pb.ap[:, s_ * 128:(s_ + 1) * 128], xb.ap[:, s_, ck * 128:(ck + 1) * 128], ident.ap, [xb, ident], [pb])
            evac(xT.ap[:, ck, :], pb.ap[:, 0:ns * 128], [pb], [xT])

    def proj_rms(R, Wt, col0, xT, n, tag):
        raw = R.alloc("raw" + tag, [128, 4, n], F32)
        sq = R.alloc("sq" + tag, [128, 4, n], BF16)
        nrm = R.alloc("nrm" + tag, [128, 4, n], BF16)
        Rs = R.alloc("Rs" + tag, [128, n], F32)

        def emit():
            for fc in range(4):
                pf = PFR.next()
                for ck in range(16):
                    mm(pf.ap[:, 0:n], Wt.ap[:, ck, col0 + fc * 128:col0 + (fc + 1) * 128], xT.ap[:, ck, :], ck == 0, ck == 15, [Wt, xT], [pf])
                act(sq.ap[:, fc, :], pf.ap[:, 0:n], AF.Square, [pf], [sq])
                cp(raw.ap[:, fc, :], pf.ap[:, 0:n], [pf], [raw])
            pf = PFR.next()
            for fc in range(4):
                mm(pf.ap[:, 0:n], ones_bf.ap, sq.ap[:, fc, :], fc == 0, fc == 3, [ones_bf, sq], [pf])
            act(Rs.ap, pf.ap[:, 0:n], AF.Sqrt, [pf, epsT], [Rs], bias=epsT.ap[:, 0:1], scale=1.0 / 512)
            S.op("dve", lambda e: e.reciprocal(Rs.ap, Rs.ap), r=[Rs], w=[Rs])
            for fc in range(4):
                tt(nrm.ap[:, fc, :], raw.ap[:, fc, :], Rs.ap, ALU.mult, [raw, Rs], [nrm])
        return nrm, emit

    R1 = Region(arena, 8, 190)
    Frow = R1.alloc("Frow", [8, NT], F32)
    Wm = R1.alloc("Wm", [128, 16, 576], BF16)
    Wf = R1.alloc("Wf", [128, 16, 2056], BF16)
    Wukv = R1.alloc("Wukv", [128, 4, 2048], BF16)
    Wkrsw = R1.alloc("Wkrsw", [128, 16, 64], BF16)
    mark = R1.off
    stg = R1.alloc("stg", [128, 4, 2048], F32)
    S.dma("pool", Wm, Wm.ap, None, w_in_v[:, :, 512:1088])
    S.dma("pool", Wf, Wf.ap[:, :, 0:1028], None, w_in_v[:, :, 2112:3140])
    S.dma("pool", Wf, Wf.ap[:, :, 1028:2056], None, w_in_v[:, :, 3140:4168])
    S.dma("sp", stg, stg.ap, None, w_ukv.rearrange("(c p) n -> p c n", p=128))
    for fc in range(4):
        ts(Wukv.ap[:, fc, :], stg.ap[:, fc, :], gkv.ap[:, fc:fc + 1], None, ALU.mult, None, [stg, gkv], [Wukv])
    cp(Wkrsw.ap[:, :, 0:32], Wm.ap[:, :, 544:576], [Wm], [Wkrsw])
    cp(Wkrsw.ap[:, :, 32:64], Wm.ap[:, :, 512:544], [Wm], [Wkrsw])
    S.barrier()
    R1.off = mark
    ns = TT // 128
    xb = R1.alloc("xb", [128, ns, D], BF16)
    xT = R1.alloc("xT", [128, 16, TT], BF16)
    ckvn, emit_ckv = proj_rms(R1, Wm, 0, xT, TT, "kv")
    kn_st = RR([R1.alloc("kn_st%d" % i, [128, 8, TT], BF16) for i in range(2)])
    fk_st = RR([R1.alloc("fk_st%d" % i, [128, 8, TT], BF16) for i in range(2)])
    v_st = R1.alloc("v_st", [128, ns, 1024], BF16)
    fv_st = R1.alloc("fv_st", [128, ns, 1024], BF16)
    cosb, sinb, emit_rope = rope_tables(R1, pos_kv, 0, TT, "kv")
    T1 = R1.alloc("T1", [64, TT], F32); T2 = R1.alloc("T2", [64, TT], F32)
    kr_st = R1.alloc("kr_st", [64, TT], BF16)
    exb = R1.alloc("exb", [8, TT], F32); lnb = R1.alloc("lnb", [8, TT], F32)

    def sec_load(T, t0):
        S.dma("pool", xb, xb.ap, None, xkv[t0:t0 + TT, :].rearrange("(s p) d -> p s d", p=128))
        transpose_in(xb, xT, ns)
    MLAK = int(os.environ.get('MLAK', '9'))

    def sec_mla(T, t0):
        emit_ckv()
        if MLAK < 2: return
        kst = kn_st.next()
        for h in range(8):
            pf = PFR.next()
            for fc in range(4):
                mm(pf.ap[:, 0:TT], Wukv.ap[:, fc, h * 256:h * 256 + 128], ckvn.ap[:, fc, :], fc == 0, fc == 3, [Wukv, ckvn], [pf])
            evac(kst.ap[:, h, :], pf.ap[:, 0:TT], [pf], [kst])
        S.dma("sp", D_kTm, kTm_d.rearrange("h d t -> d h t")[:, :, t0:t0 + TT], kst, kst.ap)
        if MLAK < 3: return
        for s_ in range(ns):
            for hh in range(2):
                pf = PFR.next()
                for fc in range(4):
                    rhs = Wukv.ap[:, fc, :].rearrange("p (h c) -> p h c", c=256)[:, hh * 4:(hh + 1) * 4, 128:256]
                    mm(pf.ap, ckvn.ap[:, fc, s_ * 128:(s_ + 1) * 128], rhs, fc == 0, fc == 3, [Wukv, ckvn], [pf])
                evac(v_st.ap[:, s_, hh * 512:(hh + 1) * 512], pf.ap, [pf], [v_st])
            blk = T * ns + s_
            S.dma("sp", D_vm, vm_d.rearrange("h p b d -> p b h d")[:, blk, :, :], v_st,
                  v_st.ap[:, s_, :].rearrange("p (h d) -> p h d", d=128))
    def sec_rope(T, t0):
        pk = PFR.next(); pks = PFR.next()
        for ck in range(16):
            mm(pk.ap[0:64, 0:TT], Wm.ap[:, ck, 512:576], xT.ap[:, ck, :], ck == 0, ck == 15, [Wm, xT], [pk])
        for ck in range(16):
            mm(pks.ap[0:64, 0:TT], Wkrsw.ap[:, ck, :], xT.ap[:, ck, :], ck == 0, ck == 15, [Wkrsw, xT], [pks])
        emit_rope(t0)
        tt(T1.ap, pk.ap[0:64, 0:TT], cosb.ap, ALU.mult, [pk, cosb], [T1])
        tt(T2.ap, pks.ap[0:64, 0:TT], sinb.ap, ALU.mult, [pks, sinb], [T2])
        tt(kr_st.ap, T1.ap, T2.ap, ALU.add, [T1, T2], [kr_st])
        S.dma("sp", D_krT, krT_d[:, t0:t0 + TT], kr_st, kr_st.ap)
    def sec_fox(T, t0):
        fst = fk_st.next()
        for h in range(8):
            pf = PFR.next()
            for ck in range(16):
                mm(pf.ap[:, 0:TT], Wf.ap[:, ck, h * 128:(h + 1) * 128], xT.ap[:, ck, :], ck == 0, ck == 15, [Wf, xT], [pf])
            evac(fst.ap[:, h, :], pf.ap[:, 0:TT], [pf], [fst])
        S.dma("sp", D_kTf, kTf_d.rearrange("h d t -> d h t")[:, :, t0:t0 + TT], fst, fst.ap)
        for s_ in range(ns):
            for hh in range(2):
                pf = PFR.next()
                for ck in range(16):
                    mm(pf.ap, xT.ap[:, ck, s_ * 128:(s_ + 1) * 128], Wf.ap[:, ck, 1024 + hh * 512:1024 + (hh + 1) * 512], ck == 0, ck == 15, [Wf, xT], [pf])
                evac(fv_st.ap[:, s_, hh * 512:(hh + 1) * 512], pf.ap, [pf], [fv_st])
            blk = T * ns + s_
            S.dma("sp", D_vf, vf_d.rearrange("h p b d -> p b h d")[:, blk, :, :], fv_st,
                  fv_st.ap[:, s_, :].rearrange("p (h d) -> p h d", d=128))
    def sec_forget(T, t0):
        pff = PFR.next()
        for ck in range(16):
            mm(pff.ap[0:8, 0:TT], Wf.ap[:, ck, 2048:2056], xT.ap[:, ck, :], ck == 0, ck == 15, [Wf, xT], [pff])
        act(exb.ap, pff.ap[0:8, 0:TT], AF.Exp, [pff, negb], [exb], bias=negb.ap[:, 0:1], scale=-1.0)
        act(lnb.ap, exb.ap, AF.Ln, [exb, oneT], [lnb], bias=oneT.ap[0:8, 0:1])
        init = 0.0 if T == 0 else Frow.ap[:, t0 - 1:t0]
        S.op("dve", lambda e, init=init, t0=t0: e.tensor_tensor_scan(Frow.ap[:, t0:t0 + TT], ones_f.ap[0:8, 0:TT], lnb.ap, init, ALU.mult, ALU.subtract),
             r=[Frow, ones_f, lnb], w=[Frow])
        for s_ in range(ns):
            blk = T * ns + s_
            pf = PFR.next()
            tr(pf.ap[:, 0:8], Frow.ap[0:8, blk * 128:(blk + 1) * 128], identf.ap[0:8, 0:8], [Frow, identf], [pf])
            ts(negF_tok.ap[:, blk, :], pf.ap[:, 0:8], -1.0, None, ALU.mult, None, [pf], [negF_tok])

    for T in range(NTILES or (NT // TT)):
        t0 = T * TT
        sec_load(T, t0)
        r0 = T * 1024
        S.dma("pool", D_UV, UVb_d[r0:r0 + 1024, 0, :], None, peer_u[r0:r0 + 1024, :])
        S.dma("pool", D_UV, UVb_d[r0:r0 + 1024, 1, :], None, peer_v[r0:r0 + 1024, :])
        if 'mla' in SECT: sec_mla(T, t0)
        if 'rope' in SECT: sec_rope(T, t0)
        if 'fox' in SECT: sec_fox(T, t0)
        if 'forget' in SECT: sec_forget(T, t0)
    dbg("negF", negF_tok, [128, 32, 8], F32)
    dbg("Frow", Frow, [8, NT], F32)
    if stop_after <= 1:
        final_bufs.extend([D_kTm, D_krT, D_vm, D_kTf, D_vf])
        S.finish(final_bufs)
        return nc

    S.barrier()
    RP = Region(arena, 24, 88)
    QN = RP.alloc("QN", [128, 8, NQ], BF16)
    QR = RP.alloc("QR", [64, 8, NQ], BF16)
    FQ = RP.alloc("FQ", [128, 8, NQ], BF16)
    Fq_row = RP.alloc("Fq_row", [1, 8, NQ], BF16)
    R2 = Region(arena, 88, 190)
    Wqc = R2.alloc("Wqc", [128, 16, 512], BF16)
    Wqf = R2.alloc("Wqf", [128, 16, 1024], BF16)
    Wuq = R2.alloc("Wuq", [128, 4, 1536], BF16)
    Wuqsw = R2.alloc("Wuqsw", [128, 4, 8, 64], BF16)
    mark = R2.off
    stg2 = R2.alloc("stg2", [128, 4, 1536], F32)
    S.dma("pool", Wqc, Wqc.ap, None, w_in_v[:, :, 0:512])
    S.dma("pool", Wqf, Wqf.ap, None, w_in_v[:, :, 1088:2112])
    S.dma("sp", stg2, stg2.ap, None, w_uq.rearrange("(c p) n -> p c n", p=128))
    for fc in range(4):
        ts(Wuq.ap[:, fc, :], stg2.ap[:, fc, :], gq.ap[:, fc:fc + 1], None, ALU.mult, None, [stg2, gq], [Wuq])
    for fc in range(4):
        src = Wuq.ap[:, fc, :].rearrange("p (h c) -> p h c", c=192)
        cp(Wuqsw.ap[:, fc, :, 0:32], src[:, :, 160:192], [Wuq], [Wuqsw])
        cp(Wuqsw.ap[:, fc, :, 32:64], src[:, :, 128:160], [Wuq], [Wuqsw])
    S.barrier()
    R2.off = mark
    xb2 = R2.alloc("xb2", [128, ns, D], BF16)
    xT2 = R2.alloc("xT2", [128, 16, TT], BF16)
    cqn, emit_cq = proj_rms(R2, Wqc, 0, xT2, TT, "q")
    cosq, sinq, emit_ropeq = rope_tables(R2, pos_q, 0, TT, "q")
    T1q = R2.alloc("T1q", [64, TT], F32); T2q = R2.alloc("T2q", [64, TT], F32)
    selb = R2.alloc("selb", [128, 4, 128], F32)
    S.dma("sp", selb, selb.ap, None, sel_in)
    for T in range(NQ // TT):
        t0 = T * TT
        S.dma("pool", xb2, xb2.ap, None, xq[t0:t0 + TT, :].rearrange("(s p) d -> p s d", p=128))
        transpose_in(xb2, xT2, ns)
        emit_cq()
        emit_ropeq(t0)
        for h in range(8):
            pf = PFR.next()
            for fc in range(4):
                mm(pf.ap[:, 0:TT], Wuq.ap[:, fc, h * 192:h * 192 + 128], cqn.ap[:, fc, :], fc == 0, fc == 3, [Wuq, cqn], [pf])
            evac(QN.ap[:, h, t0:t0 + TT], pf.ap[:, 0:TT], [pf], [QN])
            pk = PFR.next(); pks = PFR.next()
            for fc in range(4):
                mm(pk.ap[0:64, 0:TT], Wuq.ap[:, fc, h * 192 + 128:h * 192 + 192], cqn.ap[:, fc, :], fc == 0, fc == 3, [Wuq, cqn], [pk])
            for fc in range(4):
                mm(pks.ap[0:64, 0:TT], Wuqsw.ap[:, fc, h, :], cqn.ap[:, fc, :], fc == 0, fc == 3, [Wuqsw, cqn], [pks])
            tt(T1q.ap, pk.ap[0:64, 0:TT], cosq.ap, ALU.mult, [pk, cosq], [T1q])
            tt(T2q.ap, pks.ap[0:64, 0:TT], sinq.ap, ALU.mult, [pks, sinq], [T2q])
            tt(QR.ap[:, h, t0:t0 + TT], T1q.ap, T2q.ap, ALU.add, [T1q, T2q], [QR])
            pf = PFR.next()
            for ck in range(16):
                mm(pf.ap[:, 0:TT], Wqf.ap[:, ck, h * 128:(h + 1) * 128], xT2.ap[:, ck, :], ck == 0, ck == 15, [Wqf, xT2], [pf])
            evac(FQ.ap[:, h, t0:t0 + TT], pf.ap[:, 0:TT], [pf], [FQ])
    for i in range(8):
        for h in range(8):
            pf = PFR.next()
            for m in range(4):
                mm(pf.ap[0:1, 0:128], negF_tok.ap[:, 4 * i + m, h:h + 1], selb.ap[:, m, :], m == 0, m == 3, [negF_tok, selb], [pf])
            ts(Fq_row.ap[0:1, h, i * 128:(i + 1) * 128], pf.ap[0:1, 0:128], -1.0 / SC_FOX, None, ALU.mult, None, [pf], [Fq_row])
    dbg("QN", QN, [128, 8, NQ], BF16)
    dbg("QR", QR, [64, 8, NQ], BF16)
    dbg("FQ", FQ, [128, 8, NQ], BF16)
    dbg("Fqrow", Fq_row, [1, 8, NQ], BF16)
    if stop_after <= 2:
        S.finish(final_bufs)
        return nc

    S.barrier()
    OT = Region(arena, 158, 190).alloc("OT", [128, 16, NQ], BF16)
    R3 = Region(arena, 88, 158)
    KT = RR([R3.alloc("KT%d" % i, [128, NT], BF16) for i in range(2)])
    KR = R3.alloc("KR", [64, NT], BF16)
    VV = RR([R3.alloc("V%d" % i, [128, 32, 129], BF16) for i in range(2)])
    PT = RR([R3.alloc("PT%d" % i, [128, 512], BF16) for i in range(4)])
    recb = RR([R3.alloc("recb%d" % i, [128, 512], F32) for i in range(2)])
    for v in VV.items:
        S.op("dve", lambda e, v=v: e.memset(v.ap[:, :, 128:129], 1.0), w=[v])
    S.dma("sp", KR, KR.ap, D_krT, krT_d)
    OACC = [PF[0], PF[1]]; DEN = [PF[2], PF[3]]; SAB = [PF[4], PF[5]]
    NHD = int(os.environ.get("NHD", "16"))
    for hd in range(NHD):
        mla = hd < 8; h = hd % 8
        kt = KT.next(); v = VV.next()
        S.dma("sp", kt, kt.ap, (D_kTm if mla else D_kTf), (kTm_d if mla else kTf_d)[h])
        S.dma("sp", v, v.ap[:, :, 0:128], (D_vm if mla else D_vf), (vm_d if mla else vf_d)[h])
        for kb in range(32):
            g = kb // 4; m = kb % 4
            parts = []
            if g < 4:
                parts.append((0, g * 128, 512))
                parts.append((1, 512, 1024))
            else:
                parts.append((1, g * 128, 1024))
            kcols = slice(kb * 128, (kb + 1) * 128)
            pts = []
            for (bk, c0, c1) in parts:
                n = c1 - c0
                ps = SAB[bk]
                out = ps.ap[:, 0:n]
                has_diag = (c0 == g * 128)
                if mla:
                    mm(out, kt.ap[:, kcols], QN.ap[:, h, c0:c1], True, False, [kt, QN], [ps])
                    mm(out, KR.ap[0:64, kcols], QR.ap[0:64, h, c0:c1], False, not has_diag, [KR, QR], [ps])
                else:
                    mm(out, kt.ap[:, kcols], FQ.ap[:, h, c0:c1], True, False, [kt, FQ], [ps])
                    mm(out, ones_bf.ap[0:1, 0:128], Fq_row.ap[0:1, h, c0:c1], False, not has_diag, [ones_bf, Fq_row], [ps])
                if has_diag:
                    mm(ps.ap[:, 0:128], ident.ap, negmask.ap[:, m, :], False, True, [ident, negmask], [ps])
                pt = PT.next()
                if mla:
                    act(pt.ap[:, 0:n], out, AF.Exp, [ps], [pt], scale=SC_MLA)
                else:
                    act(pt.ap[:, 0:n], out, AF.Exp, [ps, negF_tok], [pt], bias=negF_tok.ap[:, kb, h:h + 1], scale=SC_FOX)
                pts.append((bk, c0, c1, pt))
            for (bk, c0, c1, pt) in pts:
                n = c1 - c0
                o0 = c0 - bk * 512
                last = (kb == 31)
                mm(OACC[bk].ap[:, o0:o0 + n], v.ap[:, kb, 0:128], pt.ap[:, 0:n], kb == 0, last, [v, pt], [OACC[bk]])
                mm(DEN[bk].ap[:, o0:o0 + n], ones_bf.ap, pt.ap[:, 0:n], kb == 0, last, [ones_bf, pt], [DEN[bk]])
        for bk in range(2):
            rb = recb.next()
            S.op("dve", lambda e, rb=rb, bk=bk: e.reciprocal(rb.ap, DEN[bk].ap), r=[DEN[bk]], w=[rb])
            tt(OT.ap[:, hd, bk * 512:(bk + 1) * 512], OACC[bk].ap, rb.ap, ALU.mult, [OACC[bk], rb], [OT])
    dbg("OT", OT, [128, 16, NQ], BF16)
    if stop_after <= 3:
        S.finish(final_bufs)
        return nc

    S.barrier()
    H1 = Region(arena, 8, 72).alloc("H1", [128, 8, D], F32)
    R4 = Region(arena, 72, 158)
    WoutC = RR([R4.alloc("WoutC%d" % i, [128, 16, 512], BF16) for i in range(2)])
    xqt = RR([R4.alloc("xqt%d" % i, [128, D], F32) for i in range(2)])
    Gbc = R4.alloc("Gbc", [128, D], F32)
    Bbc = R4.alloc("Bbc", [128, D], F32)
    stats = R4.alloc("stats", [128, 4, 6], F32)
    mv = R4.alloc("mv", [128, 2], F32)
    rstd = R4.alloc("rstd", [128, 1], F32)
    w_out_v = w_out.rearrange("(c p) n -> p c n", p=128)

    def layer_norm_rows(Xap, Xbuf, Gb, Bb, stats, mv, rstd):
        for c4 in range(4):
            S.op("dve", lambda e, c4=c4, stats=stats: e.bn_stats(stats.ap[:, c4, :], Xap[:, c4 * 512:(c4 + 1) * 512]), r=[Xbuf], w=[stats])
        S.op("dve", lambda e, stats=stats, mv=mv: e.bn_aggr(mv.ap, stats.ap), r=[stats], w=[mv])
        act(rstd.ap, mv.ap[:, 1:2], AF.Sqrt, [mv, epsT], [rstd], bias=epsT.ap[:, 0:1])
        S.op("dve", lambda e, rstd=rstd: e.reciprocal(rstd.ap, rstd.ap), r=[rstd], w=[rstd])
        ts(Xap, Xap, mv.ap[:, 0:1], rstd.ap[:, 0:1], ALU.subtract, ALU.mult, [Xbuf, mv, rstd], [Xbuf])
        if Gb is not None:
            tt(Xap, Xap, Gb.ap, ALU.mult, [Xbuf, Gb], [Xbuf])
            tt(Xap, Xap, Bb.ap, ALU.add, [Xbuf, Bb], [Xbuf])

    S.dma("sp", Gbc, Gbc.ap, None, ln1g.partition_broadcast(128))
    S.dma("sp", Bbc, Bbc.ap, None, ln1b.partition_broadcast(128))
    for nf in range(4):
        wc = WoutC.next()
        for half in range(2):
            S.dma("pool", wc, wc.ap[:, half * 8:(half + 1) * 8, :], None, w_out_v[:, half * 8:(half + 1) * 8, nf * 512:(nf + 1) * 512])
        for i in range(8):
            if nf == 0:
                xt_ = xqt.next()
                S.dma("sp", xt_, xt_.ap, None, xq[i * 128:(i + 1) * 128, :])
                S.op("act", lambda e, xt_=xt_, i=i: e.mul(H1.ap[:, i, :], xt_.ap, ALPHA), r=[xt_], w=[H1])
            pf = PFR.next()
            for cc in range(16):
                mm(pf.ap, OT.ap[:, cc, i * 128:(i + 1) * 128], wc.ap[:, cc, :], cc == 0, cc == 15, [OT, wc], [pf])
            tt(H1.ap[:, i, nf * 512:(nf + 1) * 512], H1.ap[:, i, nf * 512:(nf + 1) * 512], pf.ap, ALU.add, [H1, pf], [H1])
    for i in range(8):
        layer_norm_rows(H1.ap[:, i, :], H1, Gbc, Bbc, stats, mv, rstd)
    dbg("H1", H1, [128, 8, D], F32)
    S.barrier()
    RB = Region(arena, 72, 136)
    H1B = RB.alloc("H1B", [128, 8, D], BF16)
    H1T = RB.alloc("H1T", [128, 16, NQ], BF16)
    for i in range(8):
        cp(H1B.ap[:, i, :], H1.ap[:, i, :], [H1], [H1B], eng=("act" if i % 2 else "dve"))
    for cc in range(16):
        for ig in range(2):
            pb = PBR.next()
            for k in range(4):
                i = ig * 4 + k
                tr(pb.ap[:, k * 128:(k + 1) * 128], H1B.ap[:, i, cc * 128:(cc + 1) * 128], ident.ap, [H1B, ident], [pb])
            evac(H1T.ap[:, cc, ig * 512:(ig + 1) * 512], pb.ap[:, 0:512], [pb], [H1T])
    if stop_after <= 4:
        S.finish(final_bufs)
        return nc

    R5 = Region(arena, 136, 190)
    WgC = RR([R5.alloc("WgC%d" % i, [128, 16, 512], BF16) for i in range(2)])
    Wp = R5.alloc("Wp", [128, 2, D], BF16)
    pT = R5.alloc("pT", [128, 2, NQ], BF16)
    pb16 = R5.alloc("pb16", [128, 8, 256], BF16)
    sig = RR([R5.alloc("sig%d" % i, [128, 512], F32) for i in range(2)])
    S.dma("pool", Wp, Wp.ap, None, ple_wproj.rearrange("(c p) n -> p c n", p=128))
    S.dma("pool", pb16, pb16.ap, None, pq.rearrange("(i p) d -> p i d", p=128))
    for c2 in range(2):
        for ig in range(2):
            pb = PBR.next()
            for k in range(4):
                i = ig * 4 + k
                tr(pb.ap[:, k * 128:(k + 1) * 128], pb16.ap[:, i, c2 * 128:(c2 + 1) * 128], ident.ap, [pb16, ident], [pb])
            evac(pT.ap[:, c2, ig * 512:(ig + 1) * 512], pb.ap[:, 0:512], [pb], [pT])
    wg_v = ple_wgate.rearrange("(c p) n -> p c n", p=128)
    for nf in range(4):
        wc = WgC.next()
        for half in range(2):
            S.dma("pool", wc, wc.ap[:, half * 8:(half + 1) * 8, :], None, wg_v[:, half * 8:(half + 1) * 8, nf * 512:(nf + 1) * 512])
        for i in range(8):
            pf = PFR.next()
            for cc in range(16):
                mm(pf.ap, H1T.ap[:, cc, i * 128:(i + 1) * 128], wc.ap[:, cc, :], cc == 0, cc == 15, [H1T, wc], [pf])
            sg = sig.next()
            act(sg.ap, pf.ap, AF.Sigmoid, [pf], [sg])
            pf2 = PFR.next()
            for c2 in range(2):
                mm(pf2.ap, pT.ap[:, c2, i * 128:(i + 1) * 128], Wp.ap[:, c2, nf * 512:(nf + 1) * 512], c2 == 0, c2 == 1, [pT, Wp], [pf2])
            tt(sg.ap, sg.ap, pf2.ap, ALU.mult, [sg, pf2], [sg])
            Hs = H1.ap[:, i, nf * 512:(nf + 1) * 512]
            stt(Hs, Hs, ALPHA, sg.ap, ALU.mult, ALU.add, [H1, sg], [H1])
    dbg("Y", H1, [128, 8, D], F32)
    if stop_after <= 5:
        S.finish(final_bufs)
        return nc

    S.barrier()
    KT12 = RC.alloc("KT12", [128, 8, 128], BF16)
    Rpre = Region(arena, 136, 190)
    Wpq = Rpre.alloc("Wpq", [128, 16, 1024], BF16)
    QP_all = Rpre.alloc("QP_all", [128, 8, NQ], BF16)
    k12 = Rpre.alloc("k12", [128, 128], BF16)
    pwq_v = peer_wq.rearrange("(c p) n -> p c n", p=128)
    for half in range(2):
        S.dma("pool", Wpq, Wpq.ap[:, half * 8:(half + 1) * 8, :], None, pwq_v[:, half * 8:(half + 1) * 8, :])
    for h in range(8):
        S.dma("pool", k12, k12.ap[:, 0:64], None, keys1[h])
        S.dma("pool", k12, k12.ap[:, 64:128], None, keys2[h])
        pb = PBR.next()
        tr(pb.ap[:, 0:128], k12.ap, ident.ap, [k12, ident], [pb])
        evac(KT12.ap[:, h, :], pb.ap[:, 0:128], [pb], [KT12])
    for h in range(8):
        for half in range(2):
            pf = PFR.next()
            for ck in range(16):
                mm(pf.ap, Wpq.ap[:, ck, h * 128:(h + 1) * 128], H1T.ap[:, ck, half * 512:(half + 1) * 512], ck == 0, ck == 15, [Wpq, H1T], [pf])
            evac(QP_all.ap[:, h, half * 512:(half + 1) * 512], pf.ap, [pf], [QP_all])

    S.barrier()
    NTI = int(os.environ.get("NTI", "8"))
    RE = Region(arena, 104, 112)
    EI_t = [RE.alloc("EI_t%d" % i, [128, 128], I32) for i in range(8)]
    GT_t = [RE.alloc("GT_t%d" % i, [128, 128], F32) for i in range(8)]
    R5b = Region(arena, 112, 120)
    SCb = R5b.alloc("SCb", [128, 256], F32)
    tmpS = R5b.alloc("tmpS", [128, 128], F32)
    V12 = R5b.alloc("V12", [128, 32], F32)
    I12 = R5b.alloc("I12", [128, 32], U32)
    I12f = R5b.alloc("I12f", [128, 32], F32)
    cand = R5b.alloc("cand", [128, 16, 16], F32)
    cidx = R5b.alloc("cidx", [128, 16, 16], F32)
    tmp256 = R5b.alloc("tmp256", [128, 256], F32)
    junk256 = R5b.alloc("junk256", [128, 256], F32)
    iota_f = R5b.alloc("iota_f", [128, 256], F32)
    SCv = R5b.alloc("SCv", [128, 16], F32)
    posu = R5b.alloc("posu", [128, 16], U32)
    posf2 = R5b.alloc("posf2", [128, 16], F32)
    EIf = R5b.alloc("EIf", [128, 16], F32)
    gexp = R5b.alloc("gexp", [128, 16], F32)
    negm = R5b.alloc("negm", [128, 1], F32)
    Zs = R5b.alloc("Zs", [128, 1], F32)
    iota_i_ap = tmp256.ap.bitcast(I32)
    S.op("pool", lambda e: e.iota(iota_i_ap, pattern=[[1, 256]], base=0, channel_multiplier=0), w=[tmp256])
    cp(iota_f.ap, iota_i_ap, [tmp256], [iota_f])
    cand_f = cand.ap.rearrange("p a b -> p (a b)")
    cidx_f = cidx.ap.rearrange("p a b -> p (a b)")

    def top16(vals_ap, vbuf, out_v, out_i, obufs, scratch, n):
        S.op("dve", lambda e: e.max(out_v[:, 0:8], vals_ap), r=[vbuf], w=[obufs[0]])
        S.op("dve", lambda e: e.max_index(out_i[:, 0:8], out_v[:, 0:8], vals_ap), r=[vbuf, obufs[0]], w=[obufs[1]])
        S.op("dve", lambda e: e.match_replace(scratch.ap[:, 0:n], out_v[:, 0:8], vals_ap, NEG), r=[vbuf, obufs[0]], w=[scratch])
        S.op("dve", lambda e: e.max(out_v[:, 8:16], scratch.ap[:, 0:n]), r=[scratch], w=[obufs[0]])
        S.op("dve", lambda e: e.max_index(out_i[:, 8:16], out_v[:, 8:16], scratch.ap[:, 0:n]), r=[scratch, obufs[0]], w=[obufs[1]])

    def routing_head(i, h):
        pf = PF[4]; pf2 = PF[5]
        qs = slice(i * 128, (i + 1) * 128)
        mm(pf.ap[:, 0:128], QP_all.ap[0:64, h, qs], KT12.ap[0:64, h, :], True, True, [QP_all, KT12], [pf])
        mm(pf2.ap[:, 0:128], QP_all.ap[64:128, h, qs], KT12.ap[64:128, h, :], True, True, [QP_all, KT12], [pf2])
        cp(SCb.ap[:, 0:128], pf.ap[:, 0:128], [pf], [SCb])
        cp(SCb.ap[:, 128:256], pf2.ap[:, 0:128], [pf2], [SCb])
        for half in range(2):
            top16(SCb.ap[:, half * 128:(half + 1) * 128], SCb, V12.ap[:, half * 16:(half + 1) * 16], I12.ap[:, half * 16:(half + 1) * 16],
                  (V12, I12), tmpS, 128)
        cp(I12f.ap, I12.ap, [I12], [I12f])
        v1b = V12.ap[:, 0:16].unsqueeze(2).to_broadcast([128, 16, 16])
        v2b = V12.ap[:, 16:32].unsqueeze(1).to_broadcast([128, 16, 16])
        i1b = I12f.ap[:, 0:16].unsqueeze(2).to_broadcast([128, 16, 16])
        i2b = I12f.ap[:, 16:32].unsqueeze(1).to_broadcast([128, 16, 16])
        tt(cand.ap, v1b, v2b, ALU.add, [V12], [cand])
        stt(cidx.ap, i1b, 128.0, i2b, ALU.mult, ALU.add, [I12f], [cidx])
        top16(cand_f, cand, SCv.ap, posu.ap, (SCv, posu), tmp256, 256)
        cp(posf2.ap, posu.ap, [posu], [posf2])
        S.op("dve", lambda e: e.memset(EIf.ap, 0.0), w=[EIf])
        for k in range(16):
            stt(junk256.ap, iota_f.ap, posf2.ap[:, k:k + 1], cidx_f, ALU.is_equal, ALU.mult, [iota_f, posf2, cidx], [junk256, EIf],
                accum=EIf.ap[:, k:k + 1])
        cp(EI_t[i].ap[:, h * 16:(h + 1) * 16], EIf.ap, [EIf], [EI_t[i]])
        ts(negm.ap, SCv.ap[:, 0:1], -1.0, None, ALU.mult, None, [SCv], [negm])
        S.op("dve", lambda e: e.memset(Zs.ap, 0.0), w=[Zs])
        act(gexp.ap, SCv.ap, AF.Exp, [SCv, negm], [gexp, Zs], bias=negm.ap[:, 0:1], accum=Zs.ap[:, 0:1])
        S.op("dve", lambda e: e.reciprocal(Zs.ap, Zs.ap), r=[Zs], w=[Zs])
        ts(GT_t[i].ap[:, h * 16:(h + 1) * 16], gexp.ap, Zs.ap[:, 0:1], None, ALU.mult, None, [gexp, Zs], [GT_t[i]])

    R5c = Region(arena, 120, 168)
    UVG = RR([R5c.alloc("UVG%d" % k, [128, 2 * D], BF16) for k in range(6)])
    UV_flat = UVb_d.rearrange("e t d -> e (t d)")
    R5s = Region(arena, 184, 190)
    Adot = RR([R5s.alloc("Adot%d" % k, [128, 1], F32) for k in range(8)])
    AW = RR([R5s.alloc("AW%d" % k, [128, 1], F32) for k in range(8)])
    DG = RR([R5s.alloc("DG%d" % k, [128, 128], BF16) for k in range(3)])
    stats2 = R5s.alloc("stats2", [128, 4, 6], F32)
    mv2 = R5s.alloc("mv2", [128, 2], F32)
    rstd2 = R5s.alloc("rstd2", [128, 1], F32)
    Gh = R5s.alloc("Gh", [128, 512], F32)
    Bh = R5s.alloc("Bh", [128, 512], F32)
    NSL = int(os.environ.get("NSL", "128"))
    for h in range(8):
        routing_head(0, h)
    for i in range(NTI):
        for slot in range(NSL):
            uvg = UVG.next(); ad = Adot.next(); aw = AW.next(); dg = DG.next()
            S.op("pool", lambda e, uvg=uvg, slot=slot, i=i: e.indirect_dma_start(
                out=uvg.ap, out_offset=None, in_=UV_flat,
                in_offset=bass.IndirectOffsetOnAxis(ap=EI_t[i].ap[:, slot:slot + 1], axis=0)), r=[EI_t[i], D_UV], w=[uvg], dma=True)
            S.op("dve", lambda e, ad=ad: e.memset(ad.ap, 0.0), w=[ad])
            stt(uvg.ap[:, 0:D], uvg.ap[:, 0:D], 1.0, H1B.ap[:, i, :], ALU.mult, ALU.mult, [uvg, H1B], [uvg, ad], accum=ad.ap[:, 0:1])
            act(aw.ap, ad.ap, AF.Gelu, [ad], [aw])
            stt(dg.ap, ident.ap, aw.ap[:, 0:1], GT_t[i].ap[:, slot:slot + 1].to_broadcast([128, 128]), ALU.mult, ALU.mult, [ident, aw, GT_t[i]], [dg])
            for nf in range(4):
                mm(PF[nf].ap, dg.ap, uvg.ap[:, D + nf * 512:D + (nf + 1) * 512], slot == 0, slot == NSL - 1, [dg, uvg], [PF[nf]])
            if (slot % 16 == 15) and (i + 1 < NTI):
                routing_head(i + 1, slot // 16)
        for nf in range(4):
            Hs = H1.ap[:, i, nf * 512:(nf + 1) * 512]
            tt(Hs, Hs, PF[nf].ap, ALU.add, [H1, PF[nf]], [H1])
        layer_norm_rows(H1.ap[:, i, :], H1, None, None, stats2, mv2, rstd2)
        for qf in range(4):
            S.dma("sp", Gh, Gh.ap, None, ln2g[:, qf * 512:(qf + 1) * 512].partition_broadcast(128))
            S.dma("sp", Bh, Bh.ap, None, ln2b[:, qf * 512:(qf + 1) * 512].partition_broadcast(128))
            Xh = H1.ap[:, i, qf * 512:(qf + 1) * 512]
            tt(Xh, Xh, Gh.ap, ALU.mult, [H1, Gh], [H1])
            tt(Xh, Xh, Bh.ap, ALU.add, [H1, Bh], [H1])
        S.dma("sp", D_out, out_d[i * 128:(i + 1) * 128, :], H1, H1.ap[:, i, :])
    S.finish(final_bufs)
    return nc


def own_rows(j):
    return np.concatenate([np.arange((4 * i + j) * 128, (4 * i + j + 1) * 128) for i in range(8)])


def core_inputs(inp, c):
    b, j = c // 4, c % 4
    own = own_rows(j)
    f = lambda a: np.ascontiguousarray(a, dtype=np.float32)
    x = inp["x"][b]
    negmask = np.zeros((128, 4, 128), np.float32)
    sel = np.zeros((128, 4, 128), np.float32)
    kk = np.arange(128)[:, None]; qq = np.arange(128)[None, :]
    for m in range(4):
        if m == j:
            negmask[:, m, :] = np.where(kk <= qq, 0.0, NEG)
            sel[:, m, :] = np.eye(128, dtype=np.float32)
        elif m > j:
            negmask[:, m, :] = NEG
    inv_freq = (1.0 / (10000.0 ** (np.arange(0, 64, 2, dtype=np.float32) / 64))).astype(np.float32)
    invf = np.concatenate([inv_freq, inv_freq])[:, None]
    sgn = np.concatenate([-np.ones(32, np.float32), np.ones(32, np.float32)])[:, None]
    return {
        "xkv": f(x), "xq": f(x[own]), "pq": f(inp["p"][0, b][own]),
        "pos_kv": np.ascontiguousarray(inp["positions"][b][None, :], dtype=np.int32),
        "pos_q": np.ascontiguousarray(inp["positions"][b][own][None, :], dtype=np.int32),
        "w_in": f(inp["w_in"][0]), "w_uq": f(inp["w_uq"][0]), "w_ukv": f(inp["w_ukv"][0]),
        "w_out": f(inp["w_out"][0]), "peer_wq": f(inp["peer_wq"][0]),
        "keys1": f(inp["peer_keys1"][0]), "keys2": f(inp["peer_keys2"][0]),
        "peer_u": f(inp["peer_u"][0]), "peer_v": f(inp["peer_v"][0]),
        "ple_wgate": f(inp["ple_wgate"][0]), "ple_wproj": f(inp["ple_wproj"][0]),
        "gqT": f(inp["g_q_norm"][0].reshape(4, 128).T), "gkvT": f(inp["g_kv_norm"][0].reshape(4, 128).T),
        "bfg": f(inp["b_forget"][0][:, None]),
        "ln1g": f(inp["ln1_g"]), "ln1b": f(inp["ln1_b"]), "ln2g": f(inp["ln2_g"]), "ln2b": f(inp["ln2_b"]),
        "negmask": negmask, "sel": sel, "invf": f(invf), "sgn": f(sgn),
    }


_NC_CACHE = {}


def kernel(**inputs):
    inp = {k: np.asarray(v) for k, v in inputs.items()}
    if "nc" not in _NC_CACHE:
        _NC_CACHE["nc"] = build_nc()
    nc = _NC_CACHE["nc"]
    maps = [core_inputs(inp, c) for c in range(8)]
    res = run_bass_kernel_spmd(nc, maps, core_ids=list(range(8)))
    out = np.zeros((2, NT, D), np.float32)
    for c in range(8):
        b, j = c // 4, c % 4
        out[b, own_rows(j)] = np.asarray(res.results[c]["out"], dtype=np.float32)
    return out
```

```python
import numpy as np
from contextlib import ExitStack
import concourse.bass as bass
import concourse.mybir as mybir
from concourse.bass_utils import run_bass_kernel_spmd
from concourse.alu_op_type import AluOpType as ALU

F32 = mybir.dt.float32
BF16 = mybir.dt.bfloat16
I32 = mybir.dt.int32
U32 = mybir.dt.uint32
AF = mybir.ActivationFunctionType


class Buf:
    def __init__(self, name, ap):
        self.name = name
        self.ap = ap
        self.last_w = []
        self.reads = []
        self.dsem = None
        self.dcount = 0
        self.is_dram = False
        self.excl = False

    def __getitem__(self, k):
        return self.ap[k]


class Sched:
    ENG = ["pe", "dve", "act", "pool", "sp"]

    def __init__(self, nc):
        self.nc = nc
        self.stack = ExitStack()
        self.eobj = {"pe": nc.tensor, "dve": nc.vector, "act": nc.scalar, "pool": nc.gpsimd, "sp": nc.sync}
        self.q = {e: [] for e in self.ENG}
        self.cnt = {e: 0 for e in self.ENG}
        self.esem = {e: self.stack.enter_context(nc.semaphore("es_" + e)) for e in self.ENG}
        self.waited = {e: {} for e in self.ENG}
        self.nbuf = 0
        self.dma_tokens = []
        self.free_dsems = []

    def sbuf(self, name, shape, dtype, stack=None):
        t = (stack or self.stack).enter_context(self.nc.sbuf_tensor(name, list(shape), dtype))
        return Buf(name, t[:])

    def psum(self, name, shape, dtype, stack=None):
        t = (stack or self.stack).enter_context(self.nc.psum_tensor(name, list(shape), dtype))
        b = Buf(name, t[:])
        b.excl = True
        return b

    def dram(self, ap, name="dram"):
        b = Buf(name, ap)
        b.is_dram = True
        return b

    def _dsem(self, buf):
        if buf.dsem is None:
            self.nbuf += 1
            buf.dsem = self.stack.enter_context(self.nc.semaphore("ds%d" % self.nbuf))
        return buf.dsem

    def _deps(self, eng, r, w):
        deps = []
        for b in r:
            deps.extend(b.last_w)
        for b in w:
            deps.extend(b.last_w)
            deps.extend(b.reads)
        out = {}
        for (sem, val, e) in deps:
            if e == eng and eng in ("pe", "sp"):
                continue
            key = id(sem)
            if self.waited[eng].get(key, 0) >= val:
                continue
            if key not in out or out[key][1] < val:
                out[key] = (sem, val)
        for key, (sem, val) in out.items():
            self.waited[eng][key] = val
        return list(out.values())

    def _commit(self, tok, r, w, accumulate=False):
        for b in r:
            b.reads.append(tok)
        for b in w:
            if accumulate:
                b.last_w = [t for t in b.last_w if t[0] is not tok[0]] + [tok]
            else:
                b.last_w = [tok]
            b.reads = []

    def op(self, eng, fn, r=(), w=(), dma=False):
        r = list(r); w = list(w)
        for b in list(r):
            if b.excl:
                r.remove(b)
                if b not in w:
                    w.append(b)
        deps = self._deps(eng, r, w)
        if dma:
            wb = w[0]
            own = r[0] if (wb.is_dram and r) else wb
            sem = self._dsem(own)
            own.dcount += 16
            tok = (sem, own.dcount, "dma")
            self.q[eng].append((deps, fn, sem, 16))
            self.dma_tokens.append(tok)
            self._commit(tok, r, w, accumulate=wb.is_dram)
            return
        else:
            self.cnt[eng] += 1
            tok = (self.esem[eng], self.cnt[eng], eng)
            self.q[eng].append((deps, fn, self.esem[eng], 1))
        self._commit(tok, r, w)

    def dma(self, eng, wbuf, out_ap, rbuf, in_ap, **kw):
        r = [rbuf] if rbuf is not None else []
        self.op(eng, lambda e: e.dma_start(out=out_ap, in_=in_ap, **kw), r=r, w=[wbuf], dma=True)

    def barrier(self):
        toks = [(self.esem[e], self.cnt[e], e) for e in self.ENG if self.cnt[e] > 0]
        toks += self.dma_tokens
        self.dma_tokens = []
        for eng in self.ENG:
            out = {}
            for (sem, val, e) in toks:
                if e == eng:
                    continue
                key = id(sem)
                if self.waited[eng].get(key, 0) >= val:
                    continue
                if key not in out or out[key][1] < val:
                    out[key] = (sem, val)
            for key, (sem, val) in out.items():
                self.waited[eng][key] = val
            if out:
                self.q[eng].append((list(out.values()), None, None, 0))

    def finish(self, out_bufs):
        deps = []
        for b in out_bufs:
            for t in b.last_w:
                deps.append((t[0], t[1]))
        self.q["sp"].append((deps, None, None, 0))
        with self.nc.Block() as block:
            def mk(eng):
                def body(e):
                    for (deps, fn, sem, inc) in self.q[eng]:
                        for (s, v) in deps:
                            e.wait_ge(s, v)
                        if fn is not None:
                            ins = fn(e)
                            ins.then_inc(sem, inc)
                return body
            block.tensor(mk("pe"))
            block.vector(mk("dve"))
            block.scalar(mk("act"))
            block.gpsimd(mk("pool"))
            block.sync(mk("sp"))
        self.stack.close()


D = 2048
NT = 4096
NQ = 1024
TT = 256
ALPHA = 2.0 ** 0.25
EPS = 1e-6
PI = float(np.pi)
SC_MLA = 192.0 ** -0.5
SC_FOX = 128.0 ** -0.5
NEG = -1.0e30


class RR:
    def __init__(self, items):
        self.items = list(items); self.i = 0

    def next(self):
        x = self.items[self.i % len(self.items)]; self.i += 1
        return x


class Region:
    def __init__(self, arena, start_kb, end_kb):
        self.arena = arena; self.off = int(start_kb * 1024); self.end = int(end_kb * 1024)

    def alloc(self, name, shape, dtype):
        n = int(np.prod(shape[1:]))
        esz = 2 if dtype == BF16 else 4
        nb = (n * esz + 63) // 64 * 64
        assert self.off + nb <= self.end, (name, self.off, nb, self.end)
        o = self.off // 2
        ap = self.arena[0:shape[0], o:o + n * esz // 2]
        if dtype != BF16:
            ap = ap.bitcast(dtype)
        if len(shape) == 3:
            ap = ap.rearrange("p (a b) -> p a b", b=shape[2])
        elif len(shape) == 4:
            ap = ap.rearrange("p (a b c) -> p a b c", b=shape[2], c=shape[3])
        self.off += nb
        return Buf(name, ap)


import os
SECT = os.environ.get('SECT', 'mla,rope,fox,forget').split(',')
NTILES = int(os.environ.get('NTILES', '0'))


def build_nc(debug=False, stop_after=99):
    nc = bass.Bass("TRN2", target_bir_lowering=False)
    dbg_outs = {}

    def IN(name, shape, dtype=F32):
        return nc.dram_tensor(name, list(shape), dtype, kind="ExternalInput").ap()

    def SCR(name, shape, dtype):
        return nc.dram_tensor(name, list(shape), dtype, kind=("ExternalOutput" if debug else "Internal")).ap()

    xkv = IN("xkv", [NT, D]); xq = IN("xq", [NQ, D]); pq = IN("pq", [NQ, 256])
    pos_kv = IN("pos_kv", [1, NT], I32); pos_q = IN("pos_q", [1, NQ], I32)
    w_in = IN("w_in", [D, 4168]); w_uq = IN("w_uq", [512, 1536]); w_ukv = IN("w_ukv", [512, 2048])
    w_out = IN("w_out", [D, D]); peer_wq = IN("peer_wq", [D, 1024])
    keys1 = IN("keys1", [8, 128, 64]); keys2 = IN("keys2", [8, 128, 64])
    peer_u = IN("peer_u", [16384, D]); peer_v = IN("peer_v", [16384, D])
    ple_wgate = IN("ple_wgate", [D, D]); ple_wproj = IN("ple_wproj", [256, D])
    gqT = IN("gqT", [128, 4]); gkvT = IN("gkvT", [128, 4]); bfg = IN("bfg", [8, 1])
    ln1g = IN("ln1g", [1, D]); ln1b = IN("ln1b", [1, D]); ln2g = IN("ln2g", [1, D]); ln2b = IN("ln2b", [1, D])
    negmask_in = IN("negmask", [128, 4, 128]); sel_in = IN("sel", [128, 4, 128])
    invf_in = IN("invf", [64, 1]); sgn_in = IN("sgn", [64, 1])
    out_d = nc.dram_tensor("out", [NQ, D], F32, kind="ExternalOutput").ap()

    kTm_d = SCR("kTm_d", [8, 128, NT], BF16); krT_d = SCR("krT_d", [64, NT], BF16)
    vm_d = SCR("vm_d", [8, 128, 32, 128], BF16)
    kTf_d = SCR("kTf_d", [8, 128, NT], BF16); vf_d = SCR("vf_d", [8, 128, 32, 128], BF16)
    UVb_d = nc.dram_tensor("UVb_d", [16384, 2, D], BF16, kind="Internal").ap()

    S = Sched(nc)
    arena = S.stack.enter_context(nc.sbuf_tensor("arena", [128, 95 * 1024], BF16))
    D_kTm = S.dram(kTm_d); D_krT = S.dram(krT_d); D_vm = S.dram(vm_d); D_kTf = S.dram(kTf_d); D_vf = S.dram(vf_d)
    D_out = S.dram(out_d)
    D_UV = S.dram(UVb_d)
    final_bufs = [D_out]

    def dbg(name, buf, shape, dtype):
        if not debug:
            return
        t = nc.dram_tensor("dbg_" + name, list(shape), dtype, kind="ExternalOutput").ap()
        b = S.dram(t)
        S.dma("sp", b, t, buf, buf.ap)
        final_bufs.append(b)

    def mm(out, lhsT, rhs, start, stop, r, w):
        S.op("pe", lambda e: e.matmul(out, lhsT, rhs, start=start, stop=stop), r=r, w=w)

    def tr(out, in_, idn, r, w):
        S.op("pe", lambda e: e.transpose(out, in_, idn), r=r, w=w)

    def act(out, in_, func, r, w, bias=None, scale=None, accum=None):
        kw = {}
        if bias is not None: kw["bias"] = bias
        if scale is not None: kw["scale"] = scale
        if accum is not None: kw["accum_out"] = accum
        S.op("act", lambda e: e.activation(out, in_, func, **kw), r=r, w=w)

    def tt(out, a, b, op, r, w, eng="dve"):
        S.op(eng, lambda e: e.tensor_tensor(out, a, b, op), r=r, w=w)

    def ts(out, a, s1, s2, op0, op1, r, w, eng="dve", accum=None):
        if op1 is None:
            S.op(eng, lambda e: e.tensor_scalar(out, a, s1, None, op0), r=r, w=w)
        elif accum is not None:
            S.op(eng, lambda e: e.tensor_scalar(out, a, s1, s2, op0, op1, accum_out=accum), r=r, w=w)
        else:
            S.op(eng, lambda e: e.tensor_scalar(out, a, s1, s2, op0, op1), r=r, w=w)

    def stt(out, a, sc, b, op0, op1, r, w, accum=None):
        if accum is None:
            S.op("dve", lambda e: e.scalar_tensor_tensor(out=out, in0=a, scalar=sc, in1=b, op0=op0, op1=op1), r=r, w=w)
        else:
            S.op("dve", lambda e: e.scalar_tensor_tensor(out=out, in0=a, scalar=sc, in1=b, op0=op0, op1=op1, accum_out=accum), r=r, w=w)

    def cp(out, in_, r, w, eng="dve"):
        if eng == "act":
            S.op("act", lambda e: e.copy(out, in_), r=r, w=w)
        else:
            S.op(eng, lambda e: e.tensor_copy(out, in_), r=r, w=w)

    evac_i = [0]

    def evac(out, in_, r, w):
        evac_i[0] += 1
        cp(out, in_, r, w, eng=("act" if evac_i[0] % 2 else "dve"))

    PF = [S.psum("pf%d" % i, [128, 512], F32) for i in range(6)]
    PB = [S.psum("pb%d" % i, [128, 1024], BF16) for i in range(2)]
    PFR = RR(PF); PBR = RR(PB)

    RC = Region(arena, 0, 8)
    identf = RC.alloc("identf", [128, 128], F32)
    ident = RC.alloc("ident", [128, 128], BF16)
    ones_bf = RC.alloc("ones_bf", [128, 128], BF16)
    ones_f = RC.alloc("ones_f", [128, 256], F32)
    epsT = RC.alloc("epsT", [128, 1], F32)
    oneT = RC.alloc("oneT", [128, 1], F32)
    invf = RC.alloc("invf", [64, 1], F32)
    sgn = RC.alloc("sgn", [64, 1], F32)
    negb = RC.alloc("negb", [8, 1], F32)
    gkv = RC.alloc("gkv", [128, 4], F32)
    gq = RC.alloc("gq", [128, 4], F32)
    negF_tok = RC.alloc("negF_tok", [128, 32, 8], F32)
    negmask = RC.alloc("negmask", [128, 4, 128], BF16)
    EI = RC.alloc("EI", [128, 128], I32)
    GT = RC.alloc("GT", [128, 128], F32)

    S.op("pool", lambda e: e.memset(identf.ap, 0.0), w=[identf])
    S.op("pool", lambda e: e.affine_select(out=identf.ap, in_=identf.ap, pattern=[[-1, 128]], compare_op=ALU.not_equal,
                                           fill=1.0, base=0, channel_multiplier=1), r=[identf], w=[identf])
    cp(ident.ap, identf.ap, [identf], [ident])
    S.op("dve", lambda e: e.memset(ones_bf.ap, 1.0), w=[ones_bf])
    S.op("dve", lambda e: e.memset(ones_f.ap, 1.0), w=[ones_f])
    S.op("dve", lambda e: e.memset(epsT.ap, EPS), w=[epsT])
    S.op("dve", lambda e: e.memset(oneT.ap, 1.0), w=[oneT])
    S.dma("sp", invf, invf.ap, None, invf_in)
    S.dma("sp", sgn, sgn.ap, None, sgn_in)
    S.dma("sp", negb, negb.ap, None, bfg)
    ts(negb.ap, negb.ap, -1.0, None, ALU.mult, None, [negb], [negb])
    S.dma("sp", gkv, gkv.ap, None, gkvT)
    S.dma("sp", gq, gq.ap, None, gqT)
    S.dma("pool", negmask, negmask.ap, None, negmask_in)

    w_in_v = w_in.rearrange("(c p) n -> p c n", p=128)

    def rope_tables(R, pos_ap, c0, n, tag):
        posi = R.alloc("posi" + tag, [64, n], I32)
        posf = R.alloc("posf" + tag, [64, n], F32)
        ang = R.alloc("ang" + tag, [64, n], F32)
        tq = R.alloc("tq" + tag, [64, n], F32)
        ki = R.alloc("ki" + tag, [64, n], I32)
        cosb = R.alloc("cos" + tag, [64, n], F32)
        sinb = R.alloc("sin" + tag, [64, n], F32)

        def emit(c0):
            S.dma("sp", posi, posi.ap, None, pos_ap[0:1, c0:c0 + n].partition_broadcast(64))
            cp(posf.ap, posi.ap, [posi], [posf])
            for (phase, dst) in ((0.0, sinb), (PI / 2, cosb)):
                ts(ang.ap, posf.ap, invf.ap[:, 0:1], phase, ALU.mult, ALU.add, [posf, invf], [ang])
                ts(tq.ap, ang.ap, 1.0 / (2 * PI), None, ALU.mult, None, [ang], [tq])
                cp(ki.ap, tq.ap, [tq], [ki])
                cp(tq.ap, ki.ap, [ki], [tq])
                stt(ang.ap, tq.ap, -2 * PI, ang.ap, ALU.mult, ALU.add, [tq, ang], [ang])
                ts(tq.ap, ang.ap, PI, -2 * PI, ALU.is_gt, ALU.mult, [ang], [tq])
                tt(ang.ap, ang.ap, tq.ap, ALU.add, [ang, tq], [ang])
                ts(tq.ap, ang.ap, -PI, 2 * PI, ALU.is_lt, ALU.mult, [ang], [tq])
                tt(ang.ap, ang.ap, tq.ap, ALU.add, [ang, tq], [ang])
                ts(ang.ap, ang.ap, PI, -PI, ALU.min, ALU.max, [ang], [ang])
                act(dst.ap, ang.ap, AF.Sin, [ang], [dst])
            ts(sinb.ap, sinb.ap, sgn.ap[:, 0:1], None, ALU.mult, None, [sinb, sgn], [sinb])
        return cosb, sinb, emit

    def transpose_in(xb, xT, ns):
        for ck in range(16):
            pb = PBR.next()
            for s_ in range(ns):
                tr(pb.ap[:, s_ * 128:(s_ + 1) * 128], xb.ap[:, s_, ck * 128:(ck + 1) * 128], ident.ap, [xb, ident], [pb])
            evac(xT.ap[:, ck, :], pb.ap[:, 0:ns * 128], [pb], [xT])

    def proj_rms(R, Wt, col0, xT, n, tag):
        raw = R.alloc("raw" + tag, [128, 4, n], F32)
        sq = R.alloc("sq" + tag, [128, 4, n], BF16)
        nrm = R.alloc("nrm" + tag, [128, 4, n], BF16)
        Rs = R.alloc("Rs" + tag, [128, n], F32)

        def emit():
            for fc in range(4):
                pf = PFR.next()
                for ck in range(16):
                    mm(pf.ap[:, 0:n], Wt.ap[:, ck, col0 + fc * 128:col0 + (fc + 1) * 128], xT.ap[:, ck, :], ck == 0, ck == 15, [Wt, xT], [pf])
                act(sq.ap[:, fc, :], pf.ap[:, 0:n], AF.Square, [pf], [sq])
                cp(raw.ap[:, fc, :], pf.ap[:, 0:n], [pf], [raw])
            pf = PFR.next()
            for fc in range(4):
                mm(pf.ap[:, 0:n], ones_bf.ap, sq.ap[:, fc, :], fc == 0, fc == 3, [ones_bf, sq], [pf])
            act(Rs.ap, pf.ap[:, 0:n], AF.Sqrt, [pf, epsT], [Rs], bias=epsT.ap[:, 0:1], scale=1.0 / 512)
            S.op("dve", lambda e: e.reciprocal(Rs.ap, Rs.ap), r=[Rs], w=[Rs])
            for fc in range(4):
                tt(nrm.ap[:, fc, :], raw.ap[:, fc, :], Rs.ap, ALU.mult, [raw, Rs], [nrm])
        return nrm, emit

    R1 = Region(arena, 8, 190)
    Frow = R1.alloc("Frow", [8, NT], F32)
    Wm = R1.alloc("Wm", [128, 16, 576], BF16)
    Wf = R1.alloc("Wf", [128, 16, 2056], BF16)
    Wukv = R1.alloc("Wukv", [128, 4, 2048], BF16)
    Wkrsw = R1.alloc("Wkrsw", [128, 16, 64], BF16)
    mark = R1.off
    stg = R1.alloc("stg", [128, 4, 2048], F32)
    S.dma("pool", Wm, Wm.ap, None, w_in_v[:, :, 512:1088])
    S.dma("pool", Wf, Wf.ap[:, :, 0:1028], None, w_in_v[:, :, 2112:3140])
    S.dma("pool", Wf, Wf.ap[:, :, 1028:2056], None, w_in_v[:, :, 3140:4168])
    S.dma("sp", stg, stg.ap, None, w_ukv.rearrange("(c p) n -> p c n", p=128))
    for fc in range(4):
        ts(Wukv.ap[:, fc, :], stg.ap[:, fc, :], gkv.ap[:, fc:fc + 1], None, ALU.mult, None, [stg, gkv], [Wukv])
    cp(Wkrsw.ap[:, :, 0:32], Wm.ap[:, :, 544:576], [Wm], [Wkrsw])
    cp(Wkrsw.ap[:, :, 32:64], Wm.ap[:, :, 512:544], [Wm], [Wkrsw])
    S.barrier()
    R1.off = mark
    ns = TT // 128
    xb = R1.alloc("xb", [128, ns, D], BF16)
    xT = R1.alloc("xT", [128, 16, TT], BF16)
    ckvn, emit_ckv = proj_rms(R1, Wm, 0, xT, TT, "kv")
    kn_st = RR([R1.alloc("kn_st%d" % i, [128, 8, TT], BF16) for i in range(2)])
    fk_st = RR([R1.alloc("fk_st%d" % i, [128, 8, TT], BF16) for i in range(2)])
    v_st = R1.alloc("v_st", [128, ns, 1024], BF16)
    fv_st = R1.alloc("fv_st", [128, ns, 1024], BF16)
    cosb, sinb, emit_rope = rope_tables(R1, pos_kv, 0, TT, "kv")
    T1 = R1.alloc("T1", [64, TT], F32); T2 = R1.alloc("T2", [64, TT], F32)
    kr_st = R1.alloc("kr_st", [64, TT], BF16)
    exb = R1.alloc("exb", [8, TT], F32); lnb = R1.alloc("lnb", [8, TT], F32)

    def sec_load(T, t0):
        S.dma("pool", xb, xb.ap, None, xkv[t0:t0 + TT, :].rearrange("(s p) d -> p s d", p=128))
        transpose_in(xb, xT, ns)
    MLAK = int(os.environ.get('MLAK', '9'))

    def sec_mla(T, t0):
        emit_ckv()
        if MLAK < 2: return
        kst = kn_st.next()
        for h in range(8):
            pf = PFR.next()
            for fc in range(4):
                mm(pf.ap[:, 0:TT], Wukv.ap[:, fc, h * 256:h * 256 + 128], ckvn.ap[:, fc, :], fc == 0, fc == 3, [Wukv, ckvn], [pf])
            evac(kst.ap[:, h, :], pf.ap[:, 0:TT], [pf], [kst])
        S.dma("sp", D_kTm, kTm_d.rearrange("h d t -> d h t")[:, :, t0:t0 + TT], kst, kst.ap)
        if MLAK < 3: return
        for s_ in range(ns):
            for hh in range(2):
                pf = PFR.next()
                for fc in range(4):
                    rhs = Wukv.ap[:, fc, :].rearrange("p (h c) -> p h c", c=256)[:, hh * 4:(hh + 1) * 4, 128:256]
                    mm(pf.ap, ckvn.ap[:, fc, s_ * 128:(s_ + 1) * 128], rhs, fc == 0, fc == 3, [Wukv, ckvn], [pf])
                evac(v_st.ap[:, s_, hh * 512:(hh + 1) * 512], pf.ap, [pf], [v_st])
            blk = T * ns + s_
            S.dma("sp", D_vm, vm_d.rearrange("h p b d -> p b h d")[:, blk, :, :], v_st,
                  v_st.ap[:, s_, :].rearrange("p (h d) -> p h d", d=128))
    def sec_rope(T, t0):
        pk = PFR.next(); pks = PFR.next()
        for ck in range(16):
            mm(pk.ap[0:64, 0:TT], Wm.ap[:, ck, 512:576], xT.ap[:, ck, :], ck == 0, ck == 15, [Wm, xT], [pk])
        for ck in range(16):
            mm(pks.ap[0:64, 0:TT], Wkrsw.ap[:, ck, :], xT.ap[:, ck, :], ck == 0, ck == 15, [Wkrsw, xT], [pks])
        emit_rope(t0)
        tt(T1.ap, pk.ap[0:64, 0:TT], cosb.ap, ALU.mult, [pk, cosb], [T1])
        tt(T2.ap, pks.ap[0:64, 0:TT], sinb.ap, ALU.mult, [pks, sinb], [T2])
        tt(kr_st.ap, T1.ap, T2.ap, ALU.add, [T1, T2], [kr_st])
        S.dma("sp", D_krT, krT_d[:, t0:t0 + TT], kr_st, kr_st.ap)
    def sec_fox(T, t0):
        fst = fk_st.next()
        for h in range(8):
            pf = PFR.next()
            for ck in range(16):
                mm(pf.ap[:, 0:TT], Wf.ap[:, ck, h * 128:(h + 1) * 128], xT.ap[:, ck, :], ck == 0, ck == 15, [Wf, xT], [pf])
            evac(fst.ap[:, h, :], pf.ap[:, 0:TT], [pf], [fst])
        S.dma("sp", D_kTf, kTf_d.rearrange("h d t -> d h t")[:, :, t0:t0 + TT], fst, fst.ap)
        for s_ in range(ns):
            for hh in range(2):
                pf = PFR.next()
                for ck in range(16):
                    mm(pf.ap, xT.ap[:, ck, s_ * 128:(s_ + 1) * 128], Wf.ap[:, ck, 1024 + hh * 512:1024 + (hh + 1) * 512], ck == 0, ck == 15, [Wf, xT], [pf])
                evac(fv_st.ap[:, s_, hh * 512:(hh + 1) * 512], pf.ap, [pf], [fv_st])
            blk = T * ns + s_
            S.dma("sp", D_vf, vf_d.rearrange("h p b d -> p b h d")[:, blk, :, :], fv_st,
                  fv_st.ap[:, s_, :].rearrange("p (h d) -> p h d", d=128))
    def sec_forget(T, t0):
        pff = PFR.next()
        for ck in range(16):
            mm(pff.ap[0:8, 0:TT], Wf.ap[:, ck, 2048:2056], xT.ap[:, ck, :], ck == 0, ck == 15, [Wf, xT], [pff])
        act(exb.ap, pff.ap[0:8, 0:TT], AF.Exp, [pff, negb], [exb], bias=negb.ap[:, 0:1], scale=-1.0)
        act(lnb.ap, exb.ap, AF.Ln, [exb, oneT], [lnb], bias=oneT.ap[0:8, 0:1])
        init = 0.0 if T == 0 else Frow.ap[:, t0 - 1:t0]
        S.op("dve", lambda e, init=init, t0=t0: e.tensor_tensor_scan(Frow.ap[:, t0:t0 + TT], ones_f.ap[0:8, 0:TT], lnb.ap, init, ALU.mult, ALU.subtract),
             r=[Frow, ones_f, lnb], w=[Frow])
        for s_ in range(ns):
            blk = T * ns + s_
            pf = PFR.next()
            tr(pf.ap[:, 0:8], Frow.ap[0:8, blk * 128:(blk + 1) * 128], identf.ap[0:8, 0:8], [Frow, identf], [pf])
            ts(negF_tok.ap[:, blk, :], pf.ap[:, 0:8], -1.0, None, ALU.mult, None, [pf], [negF_tok])

    for T in range(NTILES or (NT // TT)):
        t0 = T * TT
        sec_load(T, t0)
        r0 = T * 1024
        S.dma("pool", D_UV, UVb_d[r0:r0 + 1024, 0, :], None, peer_u[r0:r0 + 1024, :])
        if 'mla' in SECT: sec_mla(T, t0)
        if 'rope' in SECT: sec_rope(T, t0)
        if 'fox' in SECT: sec_fox(T, t0)
        if 'forget' in SECT: sec_forget(T, t0)
    dbg("negF", negF_tok, [128, 32, 8], F32)
    dbg("Frow", Frow, [8, NT], F32)
    if stop_after <= 1:
        final_bufs.extend([D_kTm, D_krT, D_vm, D_kTf, D_vf])
        S.finish(final_bufs)
        return nc

    S.barrier()
    RP = Region(arena, 24, 88)
    QN = RP.alloc("QN", [128, 8, NQ], BF16)
    QR = RP.alloc("QR", [64, 8, NQ], BF16)
    FQ = RP.alloc("FQ", [128, 8, NQ], BF16)
    Fq_row = RP.alloc("Fq_row", [1, 8, NQ], BF16)
    R2 = Region(arena, 88, 190)
    Wqc = R2.alloc("Wqc", [128, 16, 512], BF16)
    Wqf = R2.alloc("Wqf", [128, 16, 1024], BF16)
    Wuq = R2.alloc("Wuq", [128, 4, 1536], BF16)
    Wuqsw = R2.alloc("Wuqsw", [128, 4, 8, 64], BF16)
    mark = R2.off
    stg2 = R2.alloc("stg2", [128, 4, 1536], F32)
    S.dma("pool", Wqc, Wqc.ap, None, w_in_v[:, :, 0:512])
    S.dma("pool", Wqf, Wqf.ap, None, w_in_v[:, :, 1088:2112])
    S.dma("sp", stg2, stg2.ap, None, w_uq.rearrange("(c p) n -> p c n", p=128))
    for fc in range(4):
        ts(Wuq.ap[:, fc, :], stg2.ap[:, fc, :], gq.ap[:, fc:fc + 1], None, ALU.mult, None, [stg2, gq], [Wuq])
    for fc in range(4):
        src = Wuq.ap[:, fc, :].rearrange("p (h c) -> p h c", c=192)
        cp(Wuqsw.ap[:, fc, :, 0:32], src[:, :, 160:192], [Wuq], [Wuqsw])
        cp(Wuqsw.ap[:, fc, :, 32:64], src[:, :, 128:160], [Wuq], [Wuqsw])
    S.barrier()
    R2.off = mark
    xb2 = R2.alloc("xb2", [128, ns, D], BF16)
    xT2 = R2.alloc("xT2", [128, 16, TT], BF16)
    cqn, emit_cq = proj_rms(R2, Wqc, 0, xT2, TT, "q")
    cosq, sinq, emit_ropeq = rope_tables(R2, pos_q, 0, TT, "q")
    T1q = R2.alloc("T1q", [64, TT], F32); T2q = R2.alloc("T2q", [64, TT], F32)
    selb = R2.alloc("selb", [128, 4, 128], F32)
    S.dma("sp", selb, selb.ap, None, sel_in)
    for T in range(NQ // TT):
        t0 = T * TT
        S.dma("pool", xb2, xb2.ap, None, xq[t0:t0 + TT, :].rearrange("(s p) d -> p s d", p=128))
        transpose_in(xb2, xT2, ns)
        emit_cq()
        emit_ropeq(t0)
        for h in range(8):
            pf = PFR.next()
            for fc in range(4):
                mm(pf.ap[:, 0:TT], Wuq.ap[:, fc, h * 192:h * 192 + 128], cqn.ap[:, fc, :], fc == 0, fc == 3, [Wuq, cqn], [pf])
            evac(QN.ap[:, h, t0:t0 + TT], pf.ap[:, 0:TT], [pf], [QN])
            pk = PFR.next(); pks = PFR.next()
            for fc in range(4):
                mm(pk.ap[0:64, 0:TT], Wuq.ap[:, fc, h * 192 + 128:h * 192 + 192], cqn.ap[:, fc, :], fc == 0, fc == 3, [Wuq, cqn], [pk])
            for fc in range(4):
                mm(pks.ap[0:64, 0:TT], Wuqsw.ap[:, fc, h, :], cqn.ap[:, fc, :], fc == 0, fc == 3, [Wuqsw, cqn], [pks])
            tt(T1q.ap, pk.ap[0:64, 0:TT], cosq.ap, ALU.mult, [pk, cosq], [T1q])
            tt(T2q.ap, pks.ap[0:64, 0:TT], sinq.ap, ALU.mult, [pks, sinq], [T2q])
            tt(QR.ap[:, h, t0:t0 + TT], T1q.ap, T2q.ap, ALU.add, [T1q, T2q], [QR])
            pf = PFR.next()
            for ck in range(16):
                mm(pf.ap[:, 0:TT], Wqf.ap[:, ck, h * 128:(h + 1) * 128], xT2.ap[:, ck, :], ck == 0, ck == 15, [Wqf, xT2], [pf])
            evac(FQ.ap[:, h, t0:t0 + TT], pf.ap[:, 0:TT], [pf], [FQ])
    for i in range(8):
        for h in range(8):
            pf = PFR.next()
            for m in range(4):
                mm(pf.ap[0:1, 0:128], negF_tok.ap[:, 4 * i + m, h:h + 1], selb.ap[:, m, :], m == 0, m == 3, [negF_tok, selb], [pf])
            ts(Fq_row.ap[0:1, h, i * 128:(i + 1) * 128], pf.ap[0:1, 0:128], -1.0 / SC_FOX, None, ALU.mult, None, [pf], [Fq_row])
    dbg("QN", QN, [128, 8, NQ], BF16)
    dbg("QR", QR, [64, 8, NQ], BF16)
    dbg("FQ", FQ, [128, 8, NQ], BF16)
    dbg("Fqrow", Fq_row, [1, 8, NQ], BF16)
    if stop_after <= 2:
        S.finish(final_bufs)
        return nc

    S.barrier()
    OT = Region(arena, 158, 190).alloc("OT", [128, 16, NQ], BF16)
    R3 = Region(arena, 88, 158)
    KT = RR([R3.alloc("KT%d" % i, [128, NT], BF16) for i in range(2)])
    KR = R3.alloc("KR", [64, NT], BF16)
    VV = RR([R3.alloc("V%d" % i, [128, 32, 129], BF16) for i in range(2)])
    PT = RR([R3.alloc("PT%d" % i, [128, 512], BF16) for i in range(4)])
    recb = RR([R3.alloc("recb%d" % i, [128, 512], F32) for i in range(2)])
    for v in VV.items:
        S.op("dve", lambda e, v=v: e.memset(v.ap[:, :, 128:129], 1.0), w=[v])
    S.dma("sp", KR, KR.ap, D_krT, krT_d)
    OACC = [PF[0], PF[1]]; DEN = [PF[2], PF[3]]
    SB1 = []
    for k in range(2):
        b_ = Buf("sb1_%d" % k, PB[k].ap.bitcast(F32))
        b_.excl = True
        SB1.append(b_)
    SSET = [[PF[4], PF[5]], SB1]
    NHD = int(os.environ.get("NHD", "16"))
    for hd in range(NHD):
        mla = hd < 8; h = hd % 8
        kt = KT.next(); v = VV.next()
        S.dma("sp", kt, kt.ap, (D_kTm if mla else D_kTf), (kTm_d if mla else kTf_d)[h])
        S.dma("sp", v, v.ap[:, :, 0:128], (D_vm if mla else D_vf), (vm_d if mla else vf_d)[h])
        S.dma("pool", D_UV, UVb_d[hd * 1024:(hd + 1) * 1024, 1, :], None, peer_v[hd * 1024:(hd + 1) * 1024, :])
        def emit_scores(kb):
            g = kb // 4; m = kb % 4
            parts = []
            if g < 4:
                parts.append((0, g * 128, 512))
                parts.append((1, 512, 1024))
            else:
                parts.append((1, g * 128, 1024))
            kcols = slice(kb * 128, (kb + 1) * 128)
            pts = []
            for (bk, c0, c1) in parts:
                n = c1 - c0
                ps = SSET[kb % 2][bk]
                out = ps.ap[:, 0:n]
                has_diag = (c0 == g * 128)
                if mla:
                    mm(out, kt.ap[:, kcols], QN.ap[:, h, c0:c1], True, False, [kt, QN], [ps])
                    mm(out, KR.ap[0:64, kcols], QR.ap[0:64, h, c0:c1], False, not has_diag, [KR, QR], [ps])
                else:
                    mm(out, kt.ap[:, kcols], FQ.ap[:, h, c0:c1], True, False, [kt, FQ], [ps])
                    mm(out, ones_bf.ap[0:1, 0:128], Fq_row.ap[0:1, h, c0:c1], False, not has_diag, [ones_bf, Fq_row], [ps])
                if has_diag:
                    mm(ps.ap[:, 0:128], ident.ap, negmask.ap[:, m, :], False, True, [ident, negmask], [ps])
                pt = PT.next()
                if mla:
                    act(pt.ap[:, 0:n], out, AF.Exp, [ps], [pt], scale=SC_MLA)
                else:
                    act(pt.ap[:, 0:n], out, AF.Exp, [ps, negF_tok], [pt], bias=negF_tok.ap[:, kb, h:h + 1], scale=SC_FOX)
                pts.append((bk, c0, c1, pt))
            return pts

        def emit_pv(kb, pts):
            for (bk, c0, c1, pt) in pts:
                n = c1 - c0
                o0 = c0 - bk * 512
                last = (kb == 31)
                mm(OACC[bk].ap[:, o0:o0 + n], v.ap[:, kb, 0:128], pt.ap[:, 0:n], kb == 0, last, [v, pt], [OACC[bk]])
                mm(DEN[bk].ap[:, o0:o0 + n], ones_bf.ap, pt.ap[:, 0:n], kb == 0, last, [ones_bf, pt], [DEN[bk]])

        nxt = emit_scores(0)
        for kb in range(32):
            cur = nxt
            if kb + 1 < 32:
                nxt = emit_scores(kb + 1)
            emit_pv(kb, cur)
        for bk in range(2):
            rb = recb.next()
            S.op("dve", lambda e, rb=rb, bk=bk: e.reciprocal(rb.ap, DEN[bk].ap), r=[DEN[bk]], w=[rb])
            tt(OT.ap[:, hd, bk * 512:(bk + 1) * 512], OACC[bk].ap, rb.ap, ALU.mult, [OACC[bk], rb], [OT])
    dbg("OT", OT, [128, 16, NQ], BF16)
    if stop_after <= 3:
        S.finish(final_bufs)
        return nc

    S.barrier()
    H1 = Region(arena, 8, 72).alloc("H1", [128, 8, D], F32)
    R4 = Region(arena, 72, 158)
    WoutC = RR([R4.alloc("WoutC%d" % i, [128, 16, 512], BF16) for i in range(2)])
    xqt = RR([R4.alloc("xqt%d" % i, [128, D], F32) for i in range(2)])
    Gbc = R4.alloc("Gbc", [128, D], F32)
    Bbc = R4.alloc("Bbc", [128, D], F32)
    stats = R4.alloc("stats", [128, 4, 6], F32)
    mv = R4.alloc("mv", [128, 2], F32)
    rstd = R4.alloc("rstd", [128, 1], F32)
    w_out_v = w_out.rearrange("(c p) n -> p c n", p=128)

    def layer_norm_rows(Xap, Xbuf, Gb, Bb, stats, mv, rstd):
        for c4 in range(4):
            S.op("dve", lambda e, c4=c4, stats=stats: e.bn_stats(stats.ap[:, c4, :], Xap[:, c4 * 512:(c4 + 1) * 512]), r=[Xbuf], w=[stats])
        S.op("dve", lambda e, stats=stats, mv=mv: e.bn_aggr(mv.ap, stats.ap), r=[stats], w=[mv])
        act(rstd.ap, mv.ap[:, 1:2], AF.Sqrt, [mv, epsT], [rstd], bias=epsT.ap[:, 0:1])
        S.op("dve", lambda e, rstd=rstd: e.reciprocal(rstd.ap, rstd.ap), r=[rstd], w=[rstd])
        ts(Xap, Xap, mv.ap[:, 0:1], rstd.ap[:, 0:1], ALU.subtract, ALU.mult, [Xbuf, mv, rstd], [Xbuf])
        if Gb is not None:
            tt(Xap, Xap, Gb.ap, ALU.mult, [Xbuf, Gb], [Xbuf])
            tt(Xap, Xap, Bb.ap, ALU.add, [Xbuf, Bb], [Xbuf])

    S.dma("sp", Gbc, Gbc.ap, None, ln1g.partition_broadcast(128))
    S.dma("sp", Bbc, Bbc.ap, None, ln1b.partition_broadcast(128))
    for nf in range(4):
        wc = WoutC.next()
        for half in range(2):
            S.dma("pool", wc, wc.ap[:, half * 8:(half + 1) * 8, :], None, w_out_v[:, half * 8:(half + 1) * 8, nf * 512:(nf + 1) * 512])
        for i in range(8):
            if nf == 0:
                xt_ = xqt.next()
                S.dma("sp", xt_, xt_.ap, None, xq[i * 128:(i + 1) * 128, :])
                S.op("act", lambda e, xt_=xt_, i=i: e.mul(H1.ap[:, i, :], xt_.ap, ALPHA), r=[xt_], w=[H1])
            pf = PFR.next()
            for cc in range(16):
                mm(pf.ap, OT.ap[:, cc, i * 128:(i + 1) * 128], wc.ap[:, cc, :], cc == 0, cc == 15, [OT, wc], [pf])
            tt(H1.ap[:, i, nf * 512:(nf + 1) * 512], H1.ap[:, i, nf * 512:(nf + 1) * 512], pf.ap, ALU.add, [H1, pf], [H1])
    for i in range(8):
        layer_norm_rows(H1.ap[:, i, :], H1, Gbc, Bbc, stats, mv, rstd)
    dbg("H1", H1, [128, 8, D], F32)
    S.barrier()
    RB = Region(arena, 72, 136)
    H1B = RB.alloc("H1B", [128, 8, D], BF16)
    H1T = RB.alloc("H1T", [128, 16, NQ], BF16)
    for i in range(8):
        cp(H1B.ap[:, i, :], H1.ap[:, i, :], [H1], [H1B], eng=("act" if i % 2 else "dve"))
    for cc in range(16):
        for ig in range(2):
            pb = PBR.next()
            for k in range(4):
                i = ig * 4 + k
                tr(pb.ap[:, k * 128:(k + 1) * 128], H1B.ap[:, i, cc * 128:(cc + 1) * 128], ident.ap, [H1B, ident], [pb])
            evac(H1T.ap[:, cc, ig * 512:(ig + 1) * 512], pb.ap[:, 0:512], [pb], [H1T])
    if stop_after <= 4:
        S.finish(final_bufs)
        return nc

    R5 = Region(arena, 136, 190)
    WgC = RR([R5.alloc("WgC%d" % i, [128, 16, 512], BF16) for i in range(2)])
    Wp = R5.alloc("Wp", [128, 2, D], BF16)
    pT = R5.alloc("pT", [128, 2, NQ], BF16)
    pb16 = R5.alloc("pb16", [128, 8, 256], BF16)
    sig = RR([R5.alloc("sig%d" % i, [128, 512], F32) for i in range(2)])
    S.dma("pool", Wp, Wp.ap, None, ple_wproj.rearrange("(c p) n -> p c n", p=128))
    S.dma("pool", pb16, pb16.ap, None, pq.rearrange("(i p) d -> p i d", p=128))
    for c2 in range(2):
        for ig in range(2):
            pb = PBR.next()
            for k in range(4):
                i = ig * 4 + k
                tr(pb.ap[:, k * 128:(k + 1) * 128], pb16.ap[:, i, c2 * 128:(c2 + 1) * 128], ident.ap, [pb16, ident], [pb])
            evac(pT.ap[:, c2, ig * 512:(ig + 1) * 512], pb.ap[:, 0:512], [pb], [pT])
    wg_v = ple_wgate.rearrange("(c p) n -> p c n", p=128)
    for nf in range(4):
        wc = WgC.next()
        for half in range(2):
            S.dma("pool", wc, wc.ap[:, half * 8:(half + 1) * 8, :], None, wg_v[:, half * 8:(half + 1) * 8, nf * 512:(nf + 1) * 512])
        for i in range(8):
            pf = PFR.next()
            for cc in range(16):
                mm(pf.ap, H1T.ap[:, cc, i * 128:(i + 1) * 128], wc.ap[:, cc, :], cc == 0, cc == 15, [H1T, wc], [pf])
            sg = sig.next()
            act(sg.ap, pf.ap, AF.Sigmoid, [pf], [sg])
            pf2 = PFR.next()
            for c2 in range(2):
                mm(pf2.ap, pT.ap[:, c2, i * 128:(i + 1) * 128], Wp.ap[:, c2, nf * 512:(nf + 1) * 512], c2 == 0, c2 == 1, [pT, Wp], [pf2])
            tt(sg.ap, sg.ap, pf2.ap, ALU.mult, [sg, pf2], [sg])
            Hs = H1.ap[:, i, nf * 512:(nf + 1) * 512]
            stt(Hs, Hs, ALPHA, sg.ap, ALU.mult, ALU.add, [H1, sg], [H1])
    dbg("Y", H1, [128, 8, D], F32)
    if stop_after <= 5:
        S.finish(final_bufs)
        return nc

    S.barrier()
    KT12 = RC.alloc("KT12", [128, 8, 128], BF16)
    Rpre = Region(arena, 136, 190)
    Wpq = Rpre.alloc("Wpq", [128, 16, 1024], BF16)
    QP_all = Rpre.alloc("QP_all", [128, 8, NQ], BF16)
    k12 = Rpre.alloc("k12", [128, 128], BF16)
    pwq_v = peer_wq.rearrange("(c p) n -> p c n", p=128)
    for half in range(2):
        S.dma("pool", Wpq, Wpq.ap[:, half * 8:(half + 1) * 8, :], None, pwq_v[:, half * 8:(half + 1) * 8, :])
    for h in range(8):
        S.dma("pool", k12, k12.ap[:, 0:64], None, keys1[h])
        S.dma("pool", k12, k12.ap[:, 64:128], None, keys2[h])
        pb = PBR.next()
        tr(pb.ap[:, 0:128], k12.ap, ident.ap, [k12, ident], [pb])
        evac(KT12.ap[:, h, :], pb.ap[:, 0:128], [pb], [KT12])
    for h in range(8):
        for half in range(2):
            pf = PFR.next()
            for ck in range(16):
                mm(pf.ap, Wpq.ap[:, ck, h * 128:(h + 1) * 128], H1T.ap[:, ck, half * 512:(half + 1) * 512], ck == 0, ck == 15, [Wpq, H1T], [pf])
            evac(QP_all.ap[:, h, half * 512:(half + 1) * 512], pf.ap, [pf], [QP_all])

    S.barrier()
    NTI = int(os.environ.get("NTI", "8"))
    RE = Region(arena, 104, 112)
    EI_t = [RE.alloc("EI_t%d" % i, [128, 128], I32) for i in range(8)]
    GT_t = [RE.alloc("GT_t%d" % i, [128, 128], F32) for i in range(8)]
    R5b = Region(arena, 112, 120)
    SCb = R5b.alloc("SCb", [128, 256], F32)
    tmpS = R5b.alloc("tmpS", [128, 128], F32)
    V12 = R5b.alloc("V12", [128, 32], F32)
    I12 = R5b.alloc("I12", [128, 32], U32)
    I12f = R5b.alloc("I12f", [128, 32], F32)
    cand = R5b.alloc("cand", [128, 16, 16], F32)
    cidx = R5b.alloc("cidx", [128, 16, 16], F32)
    tmp256 = R5b.alloc("tmp256", [128, 256], F32)
    junk256 = R5b.alloc("junk256", [128, 256], F32)
    iota_f = R5b.alloc("iota_f", [128, 256], F32)
    SCv = R5b.alloc("SCv", [128, 16], F32)
    posu = R5b.alloc("posu", [128, 16], U32)
    posf2 = R5b.alloc("posf2", [128, 16], F32)
    EIf = R5b.alloc("EIf", [128, 16], F32)
    gexp = R5b.alloc("gexp", [128, 16], F32)
    negm = R5b.alloc("negm", [128, 1], F32)
    Zs = R5b.alloc("Zs", [128, 1], F32)
    iota_i_ap = tmp256.ap.bitcast(I32)
    S.op("pool", lambda e: e.iota(iota_i_ap, pattern=[[1, 256]], base=0, channel_multiplier=0), w=[tmp256])
    cp(iota_f.ap, iota_i_ap, [tmp256], [iota_f])
    cand_f = cand.ap.rearrange("p a b -> p (a b)")
    cidx_f = cidx.ap.rearrange("p a b -> p (a b)")

    def top16(vals_ap, vbuf, out_v, out_i, obufs, scratch, n):
        S.op("dve", lambda e: e.max(out_v[:, 0:8], vals_ap), r=[vbuf], w=[obufs[0]])
        S.op("dve", lambda e: e.max_index(out_i[:, 0:8], out_v[:, 0:8], vals_ap), r=[vbuf, obufs[0]], w=[obufs[1]])
        S.op("dve", lambda e: e.match_replace(scratch.ap[:, 0:n], out_v[:, 0:8], vals_ap, NEG), r=[vbuf, obufs[0]], w=[scratch])
        S.op("dve", lambda e: e.max(out_v[:, 8:16], scratch.ap[:, 0:n]), r=[scratch], w=[obufs[0]])
        S.op("dve", lambda e: e.max_index(out_i[:, 8:16], out_v[:, 8:16], scratch.ap[:, 0:n]), r=[scratch, obufs[0]], w=[obufs[1]])

    def routing_head(i, h):
        pf = PF[4]; pf2 = PF[5]
        qs = slice(i * 128, (i + 1) * 128)
        mm(pf.ap[:, 0:128], QP_all.ap[0:64, h, qs], KT12.ap[0:64, h, :], True, True, [QP_all, KT12], [pf])
        mm(pf2.ap[:, 0:128], QP_all.ap[64:128, h, qs], KT12.ap[64:128, h, :], True, True, [QP_all, KT12], [pf2])
        cp(SCb.ap[:, 0:128], pf.ap[:, 0:128], [pf], [SCb])
        cp(SCb.ap[:, 128:256], pf2.ap[:, 0:128], [pf2], [SCb])
        for half in range(2):
            top16(SCb.ap[:, half * 128:(half + 1) * 128], SCb, V12.ap[:, half * 16:(half + 1) * 16], I12.ap[:, half * 16:(half + 1) * 16],
                  (V12, I12), tmpS, 128)
        cp(I12f.ap, I12.ap, [I12], [I12f])
        v1b = V12.ap[:, 0:16].unsqueeze(2).to_broadcast([128, 16, 16])
        v2b = V12.ap[:, 16:32].unsqueeze(1).to_broadcast([128, 16, 16])
        i1b = I12f.ap[:, 0:16].unsqueeze(2).to_broadcast([128, 16, 16])
        i2b = I12f.ap[:, 16:32].unsqueeze(1).to_broadcast([128, 16, 16])
        tt(cand.ap, v1b, v2b, ALU.add, [V12], [cand])
        stt(cidx.ap, i1b, 128.0, i2b, ALU.mult, ALU.add, [I12f], [cidx])
        top16(cand_f, cand, SCv.ap, posu.ap, (SCv, posu), tmp256, 256)
        cp(posf2.ap, posu.ap, [posu], [posf2])
        for k in range(16):
            stt(junk256.ap, iota_f.ap, posf2.ap[:, k:k + 1], cidx_f, ALU.is_equal, ALU.mult, [iota_f, posf2, cidx], [junk256, EIf],
                accum=EIf.ap[:, k:k + 1])
        cp(EI_t[i].ap[:, h * 16:(h + 1) * 16], EIf.ap, [EIf], [EI_t[i]])
        ts(negm.ap, SCv.ap[:, 0:1], -1.0, None, ALU.mult, None, [SCv], [negm])
        act(gexp.ap, SCv.ap, AF.Exp, [SCv, negm], [gexp, Zs], bias=negm.ap[:, 0:1], accum=Zs.ap[:, 0:1])
        S.op("dve", lambda e: e.reciprocal(Zs.ap, Zs.ap), r=[Zs], w=[Zs])
        ts(GT_t[i].ap[:, h * 16:(h + 1) * 16], gexp.ap, Zs.ap[:, 0:1], None, ALU.mult, None, [gexp, Zs], [GT_t[i]])

    R5c = Region(arena, 120, 168)
    NUB = 5
    UVG = [R5c.alloc("UVG%d" % k, [128, 2 * D], BF16) for k in range(NUB)]
    PRD = R5c.alloc("PRD", [128, D], F32)
    UV_flat = UVb_d.rearrange("e t d -> e (t d)")
    R5s = Region(arena, 184, 190)
    Adot = RR([R5s.alloc("Adot%d" % k, [128, 1], F32) for k in range(8)])
    AW = RR([R5s.alloc("AW%d" % k, [128, 1], F32) for k in range(8)])
    DG = RR([R5s.alloc("DG%d" % k, [128, 128], BF16) for k in range(3)])
    stats2 = R5s.alloc("stats2", [128, 4, 6], F32)
    mv2 = R5s.alloc("mv2", [128, 2], F32)
    rstd2 = R5s.alloc("rstd2", [128, 1], F32)
    Gh = R5s.alloc("Gh", [128, 512], F32)
    Bh = R5s.alloc("Bh", [128, 512], F32)
    NSL = int(os.environ.get("NSL", "128"))
    LOOK = NUB - 1
    POOL_EVERY = int(os.environ.get("POOL_EVERY", "0"))
    for h in range(8):
        routing_head(0, h)

    def emit_gather(gidx):
        i, slot = divmod(gidx, NSL)
        uvg = UVG[gidx % NUB]
        S.op("pool", lambda e: e.indirect_dma_start(
            out=uvg.ap, out_offset=None, in_=UV_flat,
            in_offset=bass.IndirectOffsetOnAxis(ap=EI_t[i].ap[:, slot:slot + 1], axis=0)), r=[EI_t[i], D_UV], w=[uvg], dma=True)

    for gidx in range(min(LOOK, NTI * NSL)):
        emit_gather(gidx)
    for i in range(NTI):
        for slot in range(NSL):
            gidx = i * NSL + slot
            if gidx + LOOK < NTI * NSL:
                emit_gather(gidx + LOOK)
            uvg = UVG[gidx % NUB]; ad = Adot.next(); aw = AW.next(); dg = DG.next()
            if POOL_EVERY and (slot % POOL_EVERY == POOL_EVERY - 1):
                tt(PRD.ap, uvg.ap[:, 0:D], H1B.ap[:, i, :], ALU.mult, [uvg, H1B], [PRD], eng="pool")
                act(PRD.ap, PRD.ap, AF.Copy, [PRD], [PRD, ad], accum=ad.ap[:, 0:1])
            else:
                stt(uvg.ap[:, 0:D], uvg.ap[:, 0:D], 1.0, H1B.ap[:, i, :], ALU.mult, ALU.mult, [uvg, H1B], [uvg, ad], accum=ad.ap[:, 0:1])
            act(aw.ap, ad.ap, AF.Gelu, [ad], [aw])
            ts(dg.ap, ident.ap, aw.ap[:, 0:1], GT_t[i].ap[:, slot:slot + 1], ALU.mult, ALU.mult, [ident, aw, GT_t[i]], [dg])
            for nf in range(4):
                mm(PF[nf].ap, dg.ap, uvg.ap[:, D + nf * 512:D + (nf + 1) * 512], slot == 0, slot == NSL - 1, [dg, uvg], [PF[nf]])
            if (slot % 14 == 13) and (slot // 14 < 8) and (i + 1 < NTI):
                routing_head(i + 1, slot // 14)
        for nf in range(4):
            Hs = H1.ap[:, i, nf * 512:(nf + 1) * 512]
            tt(Hs, Hs, PF[nf].ap, ALU.add, [H1, PF[nf]], [H1])
        layer_norm_rows(H1.ap[:, i, :], H1, None, None, stats2, mv2, rstd2)
        for qf in range(4):
            S.dma("sp", Gh, Gh.ap, None, ln2g[:, qf * 512:(qf + 1) * 512].partition_broadcast(128))
            S.dma("sp", Bh, Bh.ap, None, ln2b[:, qf * 512:(qf + 1) * 512].partition_broadcast(128))
            Xh = H1.ap[:, i, qf * 512:(qf + 1) * 512]
            tt(Xh, Xh, Gh.ap, ALU.mult, [H1, Gh], [H1])
            tt(Xh, Xh, Bh.ap, ALU.add, [H1, Bh], [H1])
        S.dma("sp", D_out, out_d[i * 128:(i + 1) * 128, :], H1, H1.ap[:, i, :])
    S.finish(final_bufs)
    return nc


def own_rows(j):
    return np.concatenate([np.arange((4 * i + j) * 128, (4 * i + j + 1) * 128) for i in range(8)])


def core_inputs(inp, c):
    b, j = c // 4, c % 4
    own = own_rows(j)
    f = lambda a: np.ascontiguousarray(a, dtype=np.float32)
    x = inp["x"][b]
    negmask = np.zeros((128, 4, 128), np.float32)
    sel = np.zeros((128, 4, 128), np.float32)
    kk = np.arange(128)[:, None]; qq = np.arange(128)[None, :]
    for m in range(4):
        if m == j:
            negmask[:, m, :] = np.where(kk <= qq, 0.0, NEG)
            sel[:, m, :] = np.eye(128, dtype=np.float32)
        elif m > j:
            negmask[:, m, :] = NEG
    inv_freq = (1.0 / (10000.0 ** (np.arange(0, 64, 2, dtype=np.float32) / 64))).astype(np.float32)
    invf = np.concatenate([inv_freq, inv_freq])[:, None]
    sgn = np.concatenate([-np.ones(32, np.float32), np.ones(32, np.float32)])[:, None]
    return {
        "xkv": f(x), "xq": f(x[own]), "pq": f(inp["p"][0, b][own]),
        "pos_kv": np.ascontiguousarray(inp["positions"][b][None, :], dtype=np.int32),
        "pos_q": np.ascontiguousarray(inp["positions"][b][own][None, :], dtype=np.int32),
        "w_in": f(inp["w_in"][0]), "w_uq": f(inp["w_uq"][0]), "w_ukv": f(inp["w_ukv"][0]),
        "w_out": f(inp["w_out"][0]), "peer_wq": f(inp["peer_wq"][0]),
        "keys1": f(inp["peer_keys1"][0]), "keys2": f(inp["peer_keys2"][0]),
        "peer_u": f(inp["peer_u"][0]), "peer_v": f(inp["peer_v"][0]),
        "ple_wgate": f(inp["ple_wgate"][0]), "ple_wproj": f(inp["ple_wproj"][0]),
        "gqT": f(inp["g_q_norm"][0].reshape(4, 128).T), "gkvT": f(inp["g_kv_norm"][0].reshape(4, 128).T),
        "bfg": f(inp["b_forget"][0][:, None]),
        "ln1g": f(inp["ln1_g"]), "ln1b": f(inp["ln1_b"]), "ln2g": f(inp["ln2_g"]), "ln2b": f(inp["ln2_b"]),
        "negmask": negmask, "sel": sel, "invf": f(invf), "sgn": f(sgn),
    }


_NC_CACHE = {}


def kernel(**inputs):
    inp = {k: np.asarray(v) for k, v in inputs.items()}
    if "nc" not in _NC_CACHE:
        _NC_CACHE["nc"] = build_nc()
    nc = _NC_CACHE["nc"]
    maps = [core_inputs(inp, c) for c in range(8)]
    res = run_bass_kernel_spmd(nc, maps, core_ids=list(range(8)))
    out = np.zeros((2, NT, D), np.float32)
    for c in range(8):
        b, j = c // 4, c % 4
        out[b, own_rows(j)] = np.asarray(res.results[c]["out"], dtype=np.float32)
    return out
```

```python
import numpy as np
from contextlib import ExitStack
import concourse.bass as bass
import concourse.mybir as mybir
from concourse.bass_utils import run_bass_kernel_spmd
from concourse.alu_op_type import AluOpType as ALU

F32 = mybir.dt.float32
BF16 = mybir.dt.bfloat16
I32 = mybir.dt.int32
U32 = mybir.dt.uint32
AF = mybir.ActivationFunctionType


class Buf:
    def __init__(self, name, ap):
        self.name = name
        self.ap = ap
        self.last_w = []
        self.reads = []
        self.dsem = None
        self.dcount = 0
        self.is_dram = False
        self.excl = False

    def __getitem__(self, k):
        return self.ap[k]


class Sched:
    ENG = ["pe", "dve", "act", "pool", "sp"]

    def __init__(self, nc):
        self.nc = nc
        self.stack = ExitStack()
        self.eobj = {"pe": nc.tensor, "dve": nc.vector, "act": nc.scalar, "pool": nc.gpsimd, "sp": nc.sync}
        self.q = {e: [] for e in self.ENG}
        self.cnt = {e: 0 for e in self.ENG}
        self.esem = {e: self.stack.enter_context(nc.semaphore("es_" + e)) for e in self.ENG}
        self.waited = {e: {} for e in self.ENG}
        self.nbuf = 0
        self.dma_tokens = []
        self.free_dsems = []

    def sbuf(self, name, shape, dtype, stack=None):
        t = (stack or self.stack).enter_context(self.nc.sbuf_tensor(name, list(shape), dtype))
        return Buf(name, t[:])

    def psum(self, name, shape, dtype, stack=None):
        t = (stack or self.stack).enter_context(self.nc.psum_tensor(name, list(shape), dtype))
        b = Buf(name, t[:])
        b.excl = True
        return b

    def dram(self, ap, name="dram"):
        b = Buf(name, ap)
        b.is_dram = True
        return b

    def _dsem(self, buf):
        if buf.dsem is None:
            self.nbuf += 1
            buf.dsem = self.stack.enter_context(self.nc.semaphore("ds%d" % self.nbuf))
        return buf.dsem

    def _deps(self, eng, r, w):
        deps = []
        for b in r:
            deps.extend(b.last_w)
        for b in w:
            deps.extend(b.last_w)
            deps.extend(b.reads)
        out = {}
        for (sem, val, e) in deps:
            if e == eng and eng in ("pe", "sp"):
                continue
            key = id(sem)
            if self.waited[eng].get(key, 0) >= val:
                continue
            if key not in out or out[key][1] < val:
                out[key] = (sem, val)
        for key, (sem, val) in out.items():
            self.waited[eng][key] = val
        return list(out.values())

    def _commit(self, tok, r, w, accumulate=False):
        for b in r:
            b.reads.append(tok)
        for b in w:
            if accumulate:
                b.last_w = [t for t in b.last_w if t[0] is not tok[0]] + [tok]
            else:
                b.last_w = [tok]
            b.reads = []

    def op(self, eng, fn, r=(), w=(), dma=False):
        r = list(r); w = list(w)
        for b in list(r):
            if b.excl:
                r.remove(b)
                if b not in w:
                    w.append(b)
        deps = self._deps(eng, r, w)
        if dma:
            wb = w[0]
            own = r[0] if (wb.is_dram and r) else wb
            sem = self._dsem(own)
            own.dcount += 16
            tok = (sem, own.dcount, "dma")
            self.q[eng].append((deps, fn, sem, 16))
            self.dma_tokens.append(tok)
            self._commit(tok, r, w, accumulate=wb.is_dram)
            return
        else:
            self.cnt[eng] += 1
            tok = (self.esem[eng], self.cnt[eng], eng)
            self.q[eng].append((deps, fn, self.esem[eng], 1))
        self._commit(tok, r, w)

    def dma(self, eng, wbuf, out_ap, rbuf, in_ap, **kw):
        r = [rbuf] if rbuf is not None else []
        self.op(eng, lambda e: e.dma_start(out=out_ap, in_=in_ap, **kw), r=r, w=[wbuf], dma=True)

    def barrier(self):
        toks = [(self.esem[e], self.cnt[e], e) for e in self.ENG if self.cnt[e] > 0]
        toks += self.dma_tokens
        self.dma_tokens = []
        for eng in self.ENG:
            out = {}
            for (sem, val, e) in toks:
                if e == eng:
                    continue
                key = id(sem)
                if self.waited[eng].get(key, 0) >= val:
                    continue
                if key not in out or out[key][1] < val:
                    out[key] = (sem, val)
            for key, (sem, val) in out.items():
                self.waited[eng][key] = val
            if out:
                self.q[eng].append((list(out.values()), None, None, 0))

    def finish(self, out_bufs):
        deps = []
        for b in out_bufs:
            for t in b.last_w:
                deps.append((t[0], t[1]))
        self.q["sp"].append((deps, None, None, 0))
        with self.nc.Block() as block:
            def mk(eng):
                def body(e):
                    for (deps, fn, sem, inc) in self.q[eng]:
                        for (s, v) in deps:
                            e.wait_ge(s, v)
                        if fn is not None:
                            ins = fn(e)
                            ins.then_inc(sem, inc)
                return body
            block.tensor(mk("pe"))
            block.vector(mk("dve"))
            block.scalar(mk("act"))
            block.gpsimd(mk("pool"))
            block.sync(mk("sp"))
        self.stack.close()


D = 2048
NT = 4096
NQ = 1024
TT = 256
ALPHA = 2.0 ** 0.25
EPS = 1e-6
PI = float(np.pi)
SC_MLA = 192.0 ** -0.5
SC_FOX = 128.0 ** -0.5
NEG = -1.0e30


class RR:
    def __init__(self, items):
        self.items = list(items); self.i = 0

    def next(self):
        x = self.items[self.i % len(self.items)]; self.i += 1
        return x


class Region:
    def __init__(self, arena, start_kb, end_kb):
        self.arena = arena; self.off = int(start_kb * 1024); self.end = int(end_kb * 1024)

    def alloc(self, name, shape, dtype):
        n = int(np.prod(shape[1:]))
        esz = 2 if dtype == BF16 else 4
        nb = (n * esz + 63) // 64 * 64
        assert self.off + nb <= self.end, (name, self.off, nb, self.end)
        o = self.off // 2
        ap = self.arena[0:shape[0], o:o + n * esz // 2]
        if dtype != BF16:
            ap = ap.bitcast(dtype)
        if len(shape) == 3:
            ap = ap.rearrange("p (a b) -> p a b", b=shape[2])
        elif len(shape) == 4:
            ap = ap.rearrange("p (a b c) -> p a b c", b=shape[2], c=shape[3])
        self.off += nb
        return Buf(name, ap)


import os
SECT = os.environ.get('SECT', 'mla,rope,fox,forget').split(',')
NTILES = int(os.environ.get('NTILES', '0'))


def build_nc(debug=False, stop_after=99):
    nc = bass.Bass("TRN2", target_bir_lowering=False)
    dbg_outs = {}

    def IN(name, shape, dtype=F32):
        return nc.dram_tensor(name, list(shape), dtype, kind="ExternalInput").ap()

    def SCR(name, shape, dtype):
        return nc.dram_tensor(name, list(shape), dtype, kind=("ExternalOutput" if debug else "Internal")).ap()

    xkv = IN("xkv", [NT, D]); xq = IN("xq", [NQ, D]); pq = IN("pq", [NQ, 256])
    pos_kv = IN("pos_kv", [1, NT], I32); pos_q = IN("pos_q", [1, NQ], I32)
    w_in = IN("w_in", [D, 4168]); w_uq = IN("w_uq", [512, 1536]); w_ukv = IN("w_ukv", [512, 2048])
    w_out = IN("w_out", [D, D]); peer_wq = IN("peer_wq", [D, 1024])
    keys1 = IN("keys1", [8, 128, 64]); keys2 = IN("keys2", [8, 128, 64])
    peer_u = IN("peer_u", [16384, D]); peer_v = IN("peer_v", [16384, D])
    ple_wgate = IN("ple_wgate", [D, D]); ple_wproj = IN("ple_wproj", [256, D])
    gqT = IN("gqT", [128, 4]); gkvT = IN("gkvT", [128, 4]); bfg = IN("bfg", [8, 1])
    ln1g = IN("ln1g", [1, D]); ln1b = IN("ln1b", [1, D]); ln2g = IN("ln2g", [1, D]); ln2b = IN("ln2b", [1, D])
    negmask_in = IN("negmask", [128, 4, 128]); sel_in = IN("sel", [128, 4, 128])
    invf_in = IN("invf", [64, 1]); sgn_in = IN("sgn", [64, 1])
    out_d = nc.dram_tensor("out", [NQ, D], F32, kind="ExternalOutput").ap()

    kTm_d = SCR("kTm_d", [8, 128, NT], BF16); krT_d = SCR("krT_d", [64, NT], BF16)
    vm_d = SCR("vm_d", [8, 128, 32, 128], BF16)
    kTf_d = SCR("kTf_d", [8, 128, NT], BF16); vf_d = SCR("vf_d", [8, 128, 32, 128], BF16)
    UVb_d = nc.dram_tensor("UVb_d", [16384, 2, D], BF16, kind="Internal").ap()

    S = Sched(nc)
    arena = S.stack.enter_context(nc.sbuf_tensor("arena", [128, 95 * 1024], BF16))
    D_kTm = S.dram(kTm_d); D_krT = S.dram(krT_d); D_vm = S.dram(vm_d); D_kTf = S.dram(kTf_d); D_vf = S.dram(vf_d)
    D_out = S.dram(out_d)
    D_UV = S.dram(UVb_d)
    final_bufs = [D_out]

    def dbg(name, buf, shape, dtype):
        if not debug:
            return
        t = nc.dram_tensor("dbg_" + name, list(shape), dtype, kind="ExternalOutput").ap()
        b = S.dram(t)
        S.dma("sp", b, t, buf, buf.ap)
        final_bufs.append(b)

    def mm(out, lhsT, rhs, start, stop, r, w):
        S.op("pe", lambda e: e.matmul(out, lhsT, rhs, start=start, stop=stop), r=r, w=w)

    def tr(out, in_, idn, r, w):
        S.op("pe", lambda e: e.transpose(out, in_, idn), r=r, w=w)

    def act(out, in_, func, r, w, bias=None, scale=None, accum=None):
        kw = {}
        if bias is not None: kw["bias"] = bias
        if scale is not None: kw["scale"] = scale
        if accum is not None: kw["accum_out"] = accum
        S.op("act", lambda e: e.activation(out, in_, func, **kw), r=r, w=w)

    def tt(out, a, b, op, r, w, eng="dve"):
        S.op(eng, lambda e: e.tensor_tensor(out, a, b, op), r=r, w=w)

    def ts(out, a, s1, s2, op0, op1, r, w, eng="dve", accum=None):
        if op1 is None:
            S.op(eng, lambda e: e.tensor_scalar(out, a, s1, None, op0), r=r, w=w)
        elif accum is not None:
            S.op(eng, lambda e: e.tensor_scalar(out, a, s1, s2, op0, op1, accum_out=accum), r=r, w=w)
        else:
            S.op(eng, lambda e: e.tensor_scalar(out, a, s1, s2, op0, op1), r=r, w=w)

    def stt(out, a, sc, b, op0, op1, r, w, accum=None):
        if accum is None:
            S.op("dve", lambda e: e.scalar_tensor_tensor(out=out, in0=a, scalar=sc, in1=b, op0=op0, op1=op1), r=r, w=w)
        else:
            S.op("dve", lambda e: e.scalar_tensor_tensor(out=out, in0=a, scalar=sc, in1=b, op0=op0, op1=op1, accum_out=accum), r=r, w=w)

    def cp(out, in_, r, w, eng="dve"):
        if eng == "act":
            S.op("act", lambda e: e.copy(out, in_), r=r, w=w)
        else:
            S.op(eng, lambda e: e.tensor_copy(out, in_), r=r, w=w)

    evac_i = [0]

    def evac(out, in_, r, w):
        evac_i[0] += 1
        cp(out, in_, r, w, eng=("act" if evac_i[0] % 2 else "dve"))

    PF = [S.psum("pf%d" % i, [128, 512], F32) for i in range(6)]
    PB = [S.psum("pb%d" % i, [128, 1024], BF16) for i in range(2)]
    PFR = RR(PF); PBR = RR(PB)

    RC = Region(arena, 0, 8)
    identf = RC.alloc("identf", [128, 128], F32)
    ident = RC.alloc("ident", [128, 128], BF16)
    ones_bf = RC.alloc("ones_bf", [128, 128], BF16)
    ones_f = RC.alloc("ones_f", [128, 256], F32)
    epsT = RC.alloc("epsT", [128, 1], F32)
    oneT = RC.alloc("oneT", [128, 1], F32)
    invf = RC.alloc("invf", [64, 1], F32)
    sgn = RC.alloc("sgn", [64, 1], F32)
    negb = RC.alloc("negb", [8, 1], F32)
    gkv = RC.alloc("gkv", [128, 4], F32)
    gq = RC.alloc("gq", [128, 4], F32)
    negF_tok = RC.alloc("negF_tok", [128, 32, 8], F32)
    negmask = RC.alloc("negmask", [128, 4, 128], BF16)
    EI = RC.alloc("EI", [128, 128], I32)
    GT = RC.alloc("GT", [128, 128], F32)

    S.op("pool", lambda e: e.memset(identf.ap, 0.0), w=[identf])
    S.op("pool", lambda e: e.affine_select(out=identf.ap, in_=identf.ap, pattern=[[-1, 128]], compare_op=ALU.not_equal,
                                           fill=1.0, base=0, channel_multiplier=1), r=[identf], w=[identf])
    cp(ident.ap, identf.ap, [identf], [ident])
    S.op("dve", lambda e: e.memset(ones_bf.ap, 1.0), w=[ones_bf])
    S.op("dve", lambda e: e.memset(ones_f.ap, 1.0), w=[ones_f])
    S.op("dve", lambda e: e.memset(epsT.ap, EPS), w=[epsT])
    S.op("dve", lambda e: e.memset(oneT.ap, 1.0), w=[oneT])
    S.dma("sp", invf, invf.ap, None, invf_in)
    S.dma("sp", sgn, sgn.ap, None, sgn_in)
    S.dma("sp", negb, negb.ap, None, bfg)
    ts(negb.ap, negb.ap, -1.0, None, ALU.mult, None, [negb], [negb])
    S.dma("sp", gkv, gkv.ap, None, gkvT)
    S.dma("sp", gq, gq.ap, None, gqT)
    S.dma("pool", negmask, negmask.ap, None, negmask_in)

    w_in_v = w_in.rearrange("(c p) n -> p c n", p=128)

    def rope_tables(R, pos_ap, c0, n, tag):
        posi = R.alloc("posi" + tag, [64, n], I32)
        posf = R.alloc("posf" + tag, [64, n], F32)
        ang = R.alloc("ang" + tag, [64, n], F32)
        tq = R.alloc("tq" + tag, [64, n], F32)
        ki = R.alloc("ki" + tag, [64, n], I32)
        cosb = R.alloc("cos" + tag, [64, n], F32)
        sinb = R.alloc("sin" + tag, [64, n], F32)

        def emit(c0):
            S.dma("sp", posi, posi.ap, None, pos_ap[0:1, c0:c0 + n].partition_broadcast(64))
            cp(posf.ap, posi.ap, [posi], [posf])
            for (phase, dst) in ((0.0, sinb), (PI / 2, cosb)):
                ts(ang.ap, posf.ap, invf.ap[:, 0:1], phase, ALU.mult, ALU.add, [posf, invf], [ang])
                ts(tq.ap, ang.ap, 1.0 / (2 * PI), None, ALU.mult, None, [ang], [tq])
                cp(ki.ap, tq.ap, [tq], [ki])
                cp(tq.ap, ki.ap, [ki], [tq])
                stt(ang.ap, tq.ap, -2 * PI, ang.ap, ALU.mult, ALU.add, [tq, ang], [ang])
                ts(tq.ap, ang.ap, PI, -2 * PI, ALU.is_gt, ALU.mult, [ang], [tq])
                tt(ang.ap, ang.ap, tq.ap, ALU.add, [ang, tq], [ang])
                ts(tq.ap, ang.ap, -PI, 2 * PI, ALU.is_lt, ALU.mult, [ang], [tq])
                tt(ang.ap, ang.ap, tq.ap, ALU.add, [ang, tq], [ang])
                ts(ang.ap, ang.ap, PI, -PI, ALU.min, ALU.max, [ang], [ang])
                act(dst.ap, ang.ap, AF.Sin, [ang], [dst])
            ts(sinb.ap, sinb.ap, sgn.ap[:, 0:1], None, ALU.mult, None, [sinb, sgn], [sinb])
        return cosb, sinb, emit

    def transpose_in(xb, xT, ns):
        for ck in range(16):
            pb = PBR.next()
            for s_ in range(ns):
                tr(pb.ap[:, s_ * 128:(s_ + 1) * 128], xb.ap[:, s_, ck * 128:(ck + 1) * 128], ident.ap, [xb, ident], [pb])
            evac(xT.ap[:, ck, :], pb.ap[:, 0:ns * 128], [pb], [xT])

    def proj_rms(R, Wt, col0, xT, n, tag):
        raw = R.alloc("raw" + tag, [128, 4, n], F32)
        sq = R.alloc("sq" + tag, [128, 4, n], BF16)
        nrm = R.alloc("nrm" + tag, [128, 4, n], BF16)
        Rs = R.alloc("Rs" + tag, [128, n], F32)

        def emit():
            for fc in range(4):
                pf = PFR.next()
                for ck in range(16):
                    mm(pf.ap[:, 0:n], Wt.ap[:, ck, col0 + fc * 128:col0 + (fc + 1) * 128], xT.ap[:, ck, :], ck == 0, ck == 15, [Wt, xT], [pf])
                act(sq.ap[:, fc, :], pf.ap[:, 0:n], AF.Square, [pf], [sq])
                cp(raw.ap[:, fc, :], pf.ap[:, 0:n], [pf], [raw])
            pf = PFR.next()
            for fc in range(4):
                mm(pf.ap[:, 0:n], ones_bf.ap, sq.ap[:, fc, :], fc == 0, fc == 3, [ones_bf, sq], [pf])
            act(Rs.ap, pf.ap[:, 0:n], AF.Sqrt, [pf, epsT], [Rs], bias=epsT.ap[:, 0:1], scale=1.0 / 512)
            S.op("dve", lambda e: e.reciprocal(Rs.ap, Rs.ap), r=[Rs], w=[Rs])
            for fc in range(4):
                tt(nrm.ap[:, fc, :], raw.ap[:, fc, :], Rs.ap, ALU.mult, [raw, Rs], [nrm])
        return nrm, emit

    R1 = Region(arena, 8, 190)
    Frow = R1.alloc("Frow", [8, NT], F32)
    Wm = R1.alloc("Wm", [128, 16, 576], BF16)
    Wf = R1.alloc("Wf", [128, 16, 2056], BF16)
    Wukv = R1.alloc("Wukv", [128, 4, 2048], BF16)
    Wkrsw = R1.alloc("Wkrsw", [128, 16, 64], BF16)
    mark = R1.off
    stg = R1.alloc("stg", [128, 4, 2048], F32)
    S.dma("pool", Wm, Wm.ap, None, w_in_v[:, :, 512:1088])
    S.dma("pool", Wf, Wf.ap[:, :, 0:1028], None, w_in_v[:, :, 2112:3140])
    S.dma("pool", Wf, Wf.ap[:, :, 1028:2056], None, w_in_v[:, :, 3140:4168])
    S.dma("sp", stg, stg.ap, None, w_ukv.rearrange("(c p) n -> p c n", p=128))
    for fc in range(4):
        ts(Wukv.ap[:, fc, :], stg.ap[:, fc, :], gkv.ap[:, fc:fc + 1], None, ALU.mult, None, [stg, gkv], [Wukv])
    cp(Wkrsw.ap[:, :, 0:32], Wm.ap[:, :, 544:576], [Wm], [Wkrsw])
    cp(Wkrsw.ap[:, :, 32:64], Wm.ap[:, :, 512:544], [Wm], [Wkrsw])
    S.barrier()
    R1.off = mark
    ns = TT // 128
    xb = R1.alloc("xb", [128, ns, D], BF16)
    xT = R1.alloc("xT", [128, 16, TT], BF16)
    ckvn, emit_ckv = proj_rms(R1, Wm, 0, xT, TT, "kv")
    kn_st = RR([R1.alloc("kn_st%d" % i, [128, 8, TT], BF16) for i in range(2)])
    fk_st = RR([R1.alloc("fk_st%d" % i, [128, 8, TT], BF16) for i in range(2)])
    v_st = R1.alloc("v_st", [128, ns, 1024], BF16)
    fv_st = R1.alloc("fv_st", [128, ns, 1024], BF16)
    cosb, sinb, emit_rope = rope_tables(R1, pos_kv, 0, TT, "kv")
    T1 = R1.alloc("T1", [64, TT], F32); T2 = R1.alloc("T2", [64, TT], F32)
    kr_st = R1.alloc("kr_st", [64, TT], BF16)
    exb = R1.alloc("exb", [8, TT], F32); lnb = R1.alloc("lnb", [8, TT], F32)

    def sec_load(T, t0):
        S.dma("pool", xb, xb.ap, None, xkv[t0:t0 + TT, :].rearrange("(s p) d -> p s d", p=128))
        transpose_in(xb, xT, ns)
    MLAK = int(os.environ.get('MLAK', '9'))

    def sec_mla(T, t0):
        emit_ckv()
        if MLAK < 2: return
        kst = kn_st.next()
        for h in range(8):
            pf = PFR.next()
            for fc in range(4):
                mm(pf.ap[:, 0:TT], Wukv.ap[:, fc, h * 256:h * 256 + 128], ckvn.ap[:, fc, :], fc == 0, fc == 3, [Wukv, ckvn], [pf])
            evac(kst.ap[:, h, :], pf.ap[:, 0:TT], [pf], [kst])
        S.dma("sp", D_kTm, kTm_d.rearrange("h d t -> d h t")[:, :, t0:t0 + TT], kst, kst.ap)
        if MLAK < 3: return
        for s_ in range(ns):
            for hh in range(2):
                pf = PFR.next()
                for fc in range(4):
                    rhs = Wukv.ap[:, fc, :].rearrange("p (h c) -> p h c", c=256)[:, hh * 4:(hh + 1) * 4, 128:256]
                    mm(pf.ap, ckvn.ap[:, fc, s_ * 128:(s_ + 1) * 128], rhs, fc == 0, fc == 3, [Wukv, ckvn], [pf])
                evac(v_st.ap[:, s_, hh * 512:(hh + 1) * 512], pf.ap, [pf], [v_st])
            blk = T * ns + s_
            S.dma("sp", D_vm, vm_d.rearrange("h p b d -> p b h d")[:, blk, :, :], v_st,
                  v_st.ap[:, s_, :].rearrange("p (h d) -> p h d", d=128))
    def sec_rope(T, t0):
        pk = PFR.next(); pks = PFR.next()
        for ck in range(16):
            mm(pk.ap[0:64, 0:TT], Wm.ap[:, ck, 512:576], xT.ap[:, ck, :], ck == 0, ck == 15, [Wm, xT], [pk])
        for ck in range(16):
            mm(pks.ap[0:64, 0:TT], Wkrsw.ap[:, ck, :], xT.ap[:, ck, :], ck == 0, ck == 15, [Wkrsw, xT], [pks])
        emit_rope(t0)
        tt(T1.ap, pk.ap[0:64, 0:TT], cosb.ap, ALU.mult, [pk, cosb], [T1])
        tt(T2.ap, pks.ap[0:64, 0:TT], sinb.ap, ALU.mult, [pks, sinb], [T2])
        tt(kr_st.ap, T1.ap, T2.ap, ALU.add, [T1, T2], [kr_st])
        S.dma("sp", D_krT, krT_d[:, t0:t0 + TT], kr_st, kr_st.ap)
    def sec_fox(T, t0):
        fst = fk_st.next()
        for h in range(8):
            pf = PFR.next()
            for ck in range(16):
                mm(pf.ap[:, 0:TT], Wf.ap[:, ck, h * 128:(h + 1) * 128], xT.ap[:, ck, :], ck == 0, ck == 15, [Wf, xT], [pf])
            evac(fst.ap[:, h, :], pf.ap[:, 0:TT], [pf], [fst])
        S.dma("sp", D_kTf, kTf_d.rearrange("h d t -> d h t")[:, :, t0:t0 + TT], fst, fst.ap)
        for s_ in range(ns):
            for hh in range(2):
                pf = PFR.next()
                for ck in range(16):
                    mm(pf.ap, xT.ap[:, ck, s_ * 128:(s_ + 1) * 128], Wf.ap[:, ck, 1024 + hh * 512:1024 + (hh + 1) * 512], ck == 0, ck == 15, [Wf, xT], [pf])
                evac(fv_st.ap[:, s_, hh * 512:(hh + 1) * 512], pf.ap, [pf], [fv_st])
            blk = T * ns + s_
            S.dma("sp", D_vf, vf_d.rearrange("h p b d -> p b h d")[:, blk, :, :], fv_st,
                  fv_st.ap[:, s_, :].rearrange("p (h d) -> p h d", d=128))
    def sec_forget(T, t0):
        pff = PFR.next()
        for ck in range(16):
            mm(pff.ap[0:8, 0:TT], Wf.ap[:, ck, 2048:2056], xT.ap[:, ck, :], ck == 0, ck == 15, [Wf, xT], [pff])
        act(exb.ap, pff.ap[0:8, 0:TT], AF.Exp, [pff, negb], [exb], bias=negb.ap[:, 0:1], scale=-1.0)
        act(lnb.ap, exb.ap, AF.Ln, [exb, oneT], [lnb], bias=oneT.ap[0:8, 0:1])
        init = 0.0 if T == 0 else Frow.ap[:, t0 - 1:t0]
        S.op("dve", lambda e, init=init, t0=t0: e.tensor_tensor_scan(Frow.ap[:, t0:t0 + TT], ones_f.ap[0:8, 0:TT], lnb.ap, init, ALU.mult, ALU.subtract),
             r=[Frow, ones_f, lnb], w=[Frow])
        for s_ in range(ns):
            blk = T * ns + s_
            pf = PFR.next()
            tr(pf.ap[:, 0:8], Frow.ap[0:8, blk * 128:(blk + 1) * 128], identf.ap[0:8, 0:8], [Frow, identf], [pf])
            ts(negF_tok.ap[:, blk, :], pf.ap[:, 0:8], -1.0, None, ALU.mult, None, [pf], [negF_tok])

    for T in range(NTILES or (NT // TT)):
        t0 = T * TT
        sec_load(T, t0)
        r0 = T * 1024
        S.dma("pool", D_UV, UVb_d[r0:r0 + 1024, 0, :], None, peer_u[r0:r0 + 1024, :])
        if 'mla' in SECT: sec_mla(T, t0)
        if 'rope' in SECT: sec_rope(T, t0)
        if 'fox' in SECT: sec_fox(T, t0)
        if 'forget' in SECT: sec_forget(T, t0)
    dbg("negF", negF_tok, [128, 32, 8], F32)
    dbg("Frow", Frow, [8, NT], F32)
    if stop_after <= 1:
        final_bufs.extend([D_kTm, D_krT, D_vm, D_kTf, D_vf])
        S.finish(final_bufs)
        return nc

    S.barrier()
    RP = Region(arena, 24, 88)
    QN = RP.alloc("QN", [128, 8, NQ], BF16)
    QR = RP.alloc("QR", [64, 8, NQ], BF16)
    FQ = RP.alloc("FQ", [128, 8, NQ], BF16)
    Fq_row = RP.alloc("Fq_row", [1, 8, NQ], BF16)
    R2 = Region(arena, 88, 190)
    Wqc = R2.alloc("Wqc", [128, 16, 512], BF16)
    Wqf = R2.alloc("Wqf", [128, 16, 1024], BF16)
    Wuq = R2.alloc("Wuq", [128, 4, 1536], BF16)
    Wuqsw = R2.alloc("Wuqsw", [128, 4, 8, 64], BF16)
    mark = R2.off
    stg2 = R2.alloc("stg2", [128, 4, 1536], F32)
    S.dma("pool", Wqc, Wqc.ap, None, w_in_v[:, :, 0:512])
    S.dma("pool", Wqf, Wqf.ap, None, w_in_v[:, :, 1088:2112])
    S.dma("sp", stg2, stg2.ap, None, w_uq.rearrange("(c p) n -> p c n", p=128))
    for fc in range(4):
        ts(Wuq.ap[:, fc, :], stg2.ap[:, fc, :], gq.ap[:, fc:fc + 1], None, ALU.mult, None, [stg2, gq], [Wuq])
    for fc in range(4):
        src = Wuq.ap[:, fc, :].rearrange("p (h c) -> p h c", c=192)
        cp(Wuqsw.ap[:, fc, :, 0:32], src[:, :, 160:192], [Wuq], [Wuqsw])
        cp(Wuqsw.ap[:, fc, :, 32:64], src[:, :, 128:160], [Wuq], [Wuqsw])
    S.barrier()
    R2.off = mark
    xb2 = R2.alloc("xb2", [128, ns, D], BF16)
    xT2 = R2.alloc("xT2", [128, 16, TT], BF16)
    cqn, emit_cq = proj_rms(R2, Wqc, 0, xT2, TT, "q")
    cosq, sinq, emit_ropeq = rope_tables(R2, pos_q, 0, TT, "q")
    T1q = R2.alloc("T1q", [64, TT], F32); T2q = R2.alloc("T2q", [64, TT], F32)
    selb = R2.alloc("selb", [128, 4, 128], F32)
    S.dma("sp", selb, selb.ap, None, sel_in)
    for T in range(NQ // TT):
        t0 = T * TT
        S.dma("pool", xb2, xb2.ap, None, xq[t0:t0 + TT, :].rearrange("(s p) d -> p s d", p=128))
        transpose_in(xb2, xT2, ns)
        emit_cq()
        emit_ropeq(t0)
        for h in range(8):
            pf = PFR.next()
            for fc in range(4):
                mm(pf.ap[:, 0:TT], Wuq.ap[:, fc, h * 192:h * 192 + 128], cqn.ap[:, fc, :], fc == 0, fc == 3, [Wuq, cqn], [pf])
            evac(QN.ap[:, h, t0:t0 + TT], pf.ap[:, 0:TT], [pf], [QN])
            pk = PFR.next(); pks = PFR.next()
            for fc in range(4):
                mm(pk.ap[0:64, 0:TT], Wuq.ap[:, fc, h * 192 + 128:h * 192 + 192], cqn.ap[:, fc, :], fc == 0, fc == 3, [Wuq, cqn], [pk])
            for fc in range(4):
                mm(pks.ap[0:64, 0:TT], Wuqsw.ap[:, fc, h, :], cqn.ap[:, fc, :], fc == 0, fc == 3, [Wuqsw, cqn], [pks])
            tt(T1q.ap, pk.ap[0:64, 0:TT], cosq.ap, ALU.mult, [pk, cosq], [T1q])
            tt(T2q.ap, pks.ap[0:64, 0:TT], sinq.ap, ALU.mult, [pks, sinq], [T2q])
            tt(QR.ap[:, h, t0:t0 + TT], T1q.ap, T2q.ap, ALU.add, [T1q, T2q], [QR])
            pf = PFR.next()
            for ck in range(16):
                mm(pf.ap[:, 0:TT], Wqf.ap[:, ck, h * 128:(h + 1) * 128], xT2.ap[:, ck, :], ck == 0, ck == 15, [Wqf, xT2], [pf])
            evac(FQ.ap[:, h, t0:t0 + TT], pf.ap[:, 0:TT], [pf], [FQ])
    for i in range(8):
        for h in range(8):
            pf = PFR.next()
            for m in range(4):
                mm(pf.ap[0:1, 0:128], negF_tok.ap[:, 4 * i + m, h:h + 1], selb.ap[:, m, :], m == 0, m == 3, [negF_tok, selb], [pf])
            ts(Fq_row.ap[0:1, h, i * 128:(i + 1) * 128], pf.ap[0:1, 0:128], -1.0 / SC_FOX, None, ALU.mult, None, [pf], [Fq_row])
    dbg("QN", QN, [128, 8, NQ], BF16)
    dbg("QR", QR, [64, 8, NQ], BF16)
    dbg("FQ", FQ, [128, 8, NQ], BF16)
    dbg("Fqrow", Fq_row, [1, 8, NQ], BF16)
    if stop_after <= 2:
        S.finish(final_bufs)
        return nc

    S.barrier()
    OT = Region(arena, 158, 190).alloc("OT", [128, 16, NQ], BF16)
    R3 = Region(arena, 88, 158)
    KT = RR([R3.alloc("KT%d" % i, [128, NT], BF16) for i in range(2)])
    KR = R3.alloc("KR", [64, NT], BF16)
    VV = RR([R3.alloc("V%d" % i, [128, 32, 129], BF16) for i in range(2)])
    PT = RR([R3.alloc("PT%d" % i, [128, 512], BF16) for i in range(4)])
    recb = RR([R3.alloc("recb%d" % i, [128, 512], F32) for i in range(2)])
    for v in VV.items:
        S.op("dve", lambda e, v=v: e.memset(v.ap[:, :, 128:129], 1.0), w=[v])
    S.dma("sp", KR, KR.ap, D_krT, krT_d)
    OACC = [PF[0], PF[1]]; DEN = [PF[2], PF[3]]
    SB1 = []
    for k in range(2):
        b_ = Buf("sb1_%d" % k, PB[k].ap.bitcast(F32))
        b_.excl = True
        SB1.append(b_)
    SSET = [[PF[4], PF[5]], SB1]
    NHD = int(os.environ.get("NHD", "16"))
    for hd in range(NHD):
        mla = hd < 8; h = hd % 8
        kt = KT.next(); v = VV.next()
        S.dma("sp", kt, kt.ap, (D_kTm if mla else D_kTf), (kTm_d if mla else kTf_d)[h])
        S.dma("sp", v, v.ap[:, :, 0:128], (D_vm if mla else D_vf), (vm_d if mla else vf_d)[h])
        S.dma("pool", D_UV, UVb_d[hd * 1024:(hd + 1) * 1024, 1, :], None, peer_v[hd * 1024:(hd + 1) * 1024, :])
        def emit_scores(kb):
            g = kb // 4; m = kb % 4
            parts = []
            if g < 4:
                parts.append((0, g * 128, 512))
                parts.append((1, 512, 1024))
            else:
                parts.append((1, g * 128, 1024))
            kcols = slice(kb * 128, (kb + 1) * 128)
            pts = []
            for (bk, c0, c1) in parts:
                n = c1 - c0
                ps = SSET[kb % 2][bk]
                out = ps.ap[:, 0:n]
                has_diag = (c0 == g * 128)
                if mla:
                    mm(out, kt.ap[:, kcols], QN.ap[:, h, c0:c1], True, False, [kt, QN], [ps])
                    mm(out, KR.ap[0:64, kcols], QR.ap[0:64, h, c0:c1], False, not has_diag, [KR, QR], [ps])
                else:
                    mm(out, kt.ap[:, kcols], FQ.ap[:, h, c0:c1], True, False, [kt, FQ], [ps])
                    mm(out, ones_bf.ap[0:1, 0:128], Fq_row.ap[0:1, h, c0:c1], False, not has_diag, [ones_bf, Fq_row], [ps])
                if has_diag:
                    mm(ps.ap[:, 0:128], ident.ap, negmask.ap[:, m, :], False, True, [ident, negmask], [ps])
                pt = PT.next()
                if mla:
                    act(pt.ap[:, 0:n], out, AF.Exp, [ps], [pt], scale=SC_MLA)
                else:
                    act(pt.ap[:, 0:n], out, AF.Exp, [ps, negF_tok], [pt], bias=negF_tok.ap[:, kb, h:h + 1], scale=SC_FOX)
                pts.append((bk, c0, c1, pt))
            return pts

        def emit_pv(kb, pts):
            for (bk, c0, c1, pt) in pts:
                n = c1 - c0
                o0 = c0 - bk * 512
                last = (kb == (15 if bk == 0 else 31))
                mm(OACC[bk].ap[:, o0:o0 + n], v.ap[:, kb, 0:128], pt.ap[:, 0:n], kb == 0, last, [v, pt], [OACC[bk]])
                mm(DEN[bk].ap[:, o0:o0 + n], ones_bf.ap, pt.ap[:, 0:n], kb == 0, last, [ones_bf, pt], [DEN[bk]])

        nxt = emit_scores(0)
        for kb in range(32):
            cur = nxt
            if kb + 1 < 32:
                nxt = emit_scores(kb + 1)
            emit_pv(kb, cur)
        for bk in range(2):
            rb = recb.next()
            S.op("dve", lambda e, rb=rb, bk=bk: e.reciprocal(rb.ap, DEN[bk].ap), r=[DEN[bk]], w=[rb])
            tt(OT.ap[:, hd, bk * 512:(bk + 1) * 512], OACC[bk].ap, rb.ap, ALU.mult, [OACC[bk], rb], [OT])
    dbg("OT", OT, [128, 16, NQ], BF16)
    if stop_after <= 3:
        S.finish(final_bufs)
        return nc

    S.barrier()
    H1 = Region(arena, 8, 72).alloc("H1", [128, 8, D], F32)
    R4 = Region(arena, 72, 158)
    WoutC = RR([R4.alloc("WoutC%d" % i, [128, 16, 512], BF16) for i in range(2)])
    xqt = RR([R4.alloc("xqt%d" % i, [128, D], F32) for i in range(2)])
    Gbc = R4.alloc("Gbc", [128, D], F32)
    Bbc = R4.alloc("Bbc", [128, D], F32)
    stats = R4.alloc("stats", [128, 4, 6], F32)
    mv = R4.alloc("mv", [128, 2], F32)
    rstd = R4.alloc("rstd", [128, 1], F32)
    w_out_v = w_out.rearrange("(c p) n -> p c n", p=128)

    def layer_norm_rows(Xap, Xbuf, Gb, Bb, stats, mv, rstd):
        for c4 in range(4):
            S.op("dve", lambda e, c4=c4, stats=stats: e.bn_stats(stats.ap[:, c4, :], Xap[:, c4 * 512:(c4 + 1) * 512]), r=[Xbuf], w=[stats])
        S.op("dve", lambda e, stats=stats, mv=mv: e.bn_aggr(mv.ap, stats.ap), r=[stats], w=[mv])
        act(rstd.ap, mv.ap[:, 1:2], AF.Sqrt, [mv, epsT], [rstd], bias=epsT.ap[:, 0:1])
        S.op("dve", lambda e, rstd=rstd: e.reciprocal(rstd.ap, rstd.ap), r=[rstd], w=[rstd])
        ts(Xap, Xap, mv.ap[:, 0:1], rstd.ap[:, 0:1], ALU.subtract, ALU.mult, [Xbuf, mv, rstd], [Xbuf])
        if Gb is not None:
            tt(Xap, Xap, Gb.ap, ALU.mult, [Xbuf, Gb], [Xbuf])
            tt(Xap, Xap, Bb.ap, ALU.add, [Xbuf, Bb], [Xbuf])

    S.dma("sp", Gbc, Gbc.ap, None, ln1g.partition_broadcast(128))
    S.dma("sp", Bbc, Bbc.ap, None, ln1b.partition_broadcast(128))
    for nf in range(4):
        wc = WoutC.next()
        for half in range(2):
            S.dma("pool", wc, wc.ap[:, half * 8:(half + 1) * 8, :], None, w_out_v[:, half * 8:(half + 1) * 8, nf * 512:(nf + 1) * 512])
        for i in range(8):
            if nf == 0:
                xt_ = xqt.next()
                S.dma("sp", xt_, xt_.ap, None, xq[i * 128:(i + 1) * 128, :])
                S.op("act", lambda e, xt_=xt_, i=i: e.mul(H1.ap[:, i, :], xt_.ap, ALPHA), r=[xt_], w=[H1])
            pf = PFR.next()
            for cc in range(16):
                mm(pf.ap, OT.ap[:, cc, i * 128:(i + 1) * 128], wc.ap[:, cc, :], cc == 0, cc == 15, [OT, wc], [pf])
            tt(H1.ap[:, i, nf * 512:(nf + 1) * 512], H1.ap[:, i, nf * 512:(nf + 1) * 512], pf.ap, ALU.add, [H1, pf], [H1])
    for i in range(8):
        layer_norm_rows(H1.ap[:, i, :], H1, Gbc, Bbc, stats, mv, rstd)
    dbg("H1", H1, [128, 8, D], F32)
    S.barrier()
    RB = Region(arena, 72, 136)
    H1B = RB.alloc("H1B", [128, 8, D], BF16)
    H1T = RB.alloc("H1T", [128, 16, NQ], BF16)
    for i in range(8):
        cp(H1B.ap[:, i, :], H1.ap[:, i, :], [H1], [H1B], eng=("act" if i % 2 else "dve"))
    for cc in range(16):
        for ig in range(2):
            pb = PBR.next()
            for k in range(4):
                i = ig * 4 + k
                tr(pb.ap[:, k * 128:(k + 1) * 128], H1B.ap[:, i, cc * 128:(cc + 1) * 128], ident.ap, [H1B, ident], [pb])
            evac(H1T.ap[:, cc, ig * 512:(ig + 1) * 512], pb.ap[:, 0:512], [pb], [H1T])
    if stop_after <= 4:
        S.finish(final_bufs)
        return nc

    R5 = Region(arena, 136, 190)
    WgC = RR([R5.alloc("WgC%d" % i, [128, 16, 512], BF16) for i in range(2)])
    Wp = R5.alloc("Wp", [128, 2, D], BF16)
    pT = R5.alloc("pT", [128, 2, NQ], BF16)
    pb16 = R5.alloc("pb16", [128, 8, 256], BF16)
    sig = RR([R5.alloc("sig%d" % i, [128, 512], F32) for i in range(2)])
    S.dma("pool", Wp, Wp.ap, None, ple_wproj.rearrange("(c p) n -> p c n", p=128))
    S.dma("pool", pb16, pb16.ap, None, pq.rearrange("(i p) d -> p i d", p=128))
    for c2 in range(2):
        for ig in range(2):
            pb = PBR.next()
            for k in range(4):
                i = ig * 4 + k
                tr(pb.ap[:, k * 128:(k + 1) * 128], pb16.ap[:, i, c2 * 128:(c2 + 1) * 128], ident.ap, [pb16, ident], [pb])
            evac(pT.ap[:, c2, ig * 512:(ig + 1) * 512], pb.ap[:, 0:512], [pb], [pT])
    wg_v = ple_wgate.rearrange("(c p) n -> p c n", p=128)
    for nf in range(4):
        wc = WgC.next()
        for half in range(2):
            S.dma("pool", wc, wc.ap[:, half * 8:(half + 1) * 8, :], None, wg_v[:, half * 8:(half + 1) * 8, nf * 512:(nf + 1) * 512])
        for i in range(8):
            pf = PFR.next()
            for cc in range(16):
                mm(pf.ap, H1T.ap[:, cc, i * 128:(i + 1) * 128], wc.ap[:, cc, :], cc == 0, cc == 15, [H1T, wc], [pf])
            sg = sig.next()
            act(sg.ap, pf.ap, AF.Sigmoid, [pf], [sg])
            pf2 = PFR.next()
            for c2 in range(2):
                mm(pf2.ap, pT.ap[:, c2, i * 128:(i + 1) * 128], Wp.ap[:, c2, nf * 512:(nf + 1) * 512], c2 == 0, c2 == 1, [pT, Wp], [pf2])
            tt(sg.ap, sg.ap, pf2.ap, ALU.mult, [sg, pf2], [sg])
            Hs = H1.ap[:, i, nf * 512:(nf + 1) * 512]
            stt(Hs, Hs, ALPHA, sg.ap, ALU.mult, ALU.add, [H1, sg], [H1])
    dbg("Y", H1, [128, 8, D], F32)
    if stop_after <= 5:
        S.finish(final_bufs)
        return nc

    S.barrier()
    KT12 = RC.alloc("KT12", [128, 8, 128], BF16)
    Rpre = Region(arena, 136, 190)
    Wpq = Rpre.alloc("Wpq", [128, 16, 1024], BF16)
    QP_all = Rpre.alloc("QP_all", [128, 8, NQ], BF16)
    k12 = Rpre.alloc("k12", [128, 128], BF16)
    pwq_v = peer_wq.rearrange("(c p) n -> p c n", p=128)
    for half in range(2):
        S.dma("pool", Wpq, Wpq.ap[:, half * 8:(half + 1) * 8, :], None, pwq_v[:, half * 8:(half + 1) * 8, :])
    for h in range(8):
        S.dma("pool", k12, k12.ap[:, 0:64], None, keys1[h])
        S.dma("pool", k12, k12.ap[:, 64:128], None, keys2[h])
        pb = PBR.next()
        tr(pb.ap[:, 0:128], k12.ap, ident.ap, [k12, ident], [pb])
        evac(KT12.ap[:, h, :], pb.ap[:, 0:128], [pb], [KT12])
    for h in range(8):
        for half in range(2):
            pf = PFR.next()
            for ck in range(16):
                mm(pf.ap, Wpq.ap[:, ck, h * 128:(h + 1) * 128], H1T.ap[:, ck, half * 512:(half + 1) * 512], ck == 0, ck == 15, [Wpq, H1T], [pf])
            evac(QP_all.ap[:, h, half * 512:(half + 1) * 512], pf.ap, [pf], [QP_all])

    S.barrier()
    NTI = int(os.environ.get("NTI", "8"))
    RE = Region(arena, 104, 112)
    EI_t = [RE.alloc("EI_t%d" % i, [128, 128], I32) for i in range(8)]
    GT_t = [RE.alloc("GT_t%d" % i, [128, 128], F32) for i in range(8)]
    R5b = Region(arena, 112, 120)
    SCb = R5b.alloc("SCb", [128, 256], F32)
    tmpS = R5b.alloc("tmpS", [128, 128], F32)
    V12 = R5b.alloc("V12", [128, 32], F32)
    I12 = R5b.alloc("I12", [128, 32], U32)
    I12f = R5b.alloc("I12f", [128, 32], F32)
    cand = R5b.alloc("cand", [128, 16, 16], F32)
    cidx = R5b.alloc("cidx", [128, 16, 16], F32)
    tmp256 = R5b.alloc("tmp256", [128, 256], F32)
    junk256 = R5b.alloc("junk256", [128, 256], F32)
    iota_f = R5b.alloc("iota_f", [128, 256], F32)
    SCv = R5b.alloc("SCv", [128, 16], F32)
    posu = R5b.alloc("posu", [128, 16], U32)
    posf2 = R5b.alloc("posf2", [128, 16], F32)
    EIf = R5b.alloc("EIf", [128, 16], F32)
    gexp = R5b.alloc("gexp", [128, 16], F32)
    negm = R5b.alloc("negm", [128, 1], F32)
    Zs = R5b.alloc("Zs", [128, 1], F32)
    iota_i_ap = tmp256.ap.bitcast(I32)
    S.op("pool", lambda e: e.iota(iota_i_ap, pattern=[[1, 256]], base=0, channel_multiplier=0), w=[tmp256])
    cp(iota_f.ap, iota_i_ap, [tmp256], [iota_f])
    cand_f = cand.ap.rearrange("p a b -> p (a b)")
    cidx_f = cidx.ap.rearrange("p a b -> p (a b)")

    def top16(vals_ap, vbuf, out_v, out_i, obufs, scratch, n):
        S.op("dve", lambda e: e.max(out_v[:, 0:8], vals_ap), r=[vbuf], w=[obufs[0]])
        S.op("dve", lambda e: e.max_index(out_i[:, 0:8], out_v[:, 0:8], vals_ap), r=[vbuf, obufs[0]], w=[obufs[1]])
        S.op("dve", lambda e: e.match_replace(scratch.ap[:, 0:n], out_v[:, 0:8], vals_ap, NEG), r=[vbuf, obufs[0]], w=[scratch])
        S.op("dve", lambda e: e.max(out_v[:, 8:16], scratch.ap[:, 0:n]), r=[scratch], w=[obufs[0]])
        S.op("dve", lambda e: e.max_index(out_i[:, 8:16], out_v[:, 8:16], scratch.ap[:, 0:n]), r=[scratch, obufs[0]], w=[obufs[1]])

    def routing_head(i, h):
        pf = PF[4]; pf2 = PF[5]
        qs = slice(i * 128, (i + 1) * 128)
        mm(pf.ap[:, 0:128], QP_all.ap[0:64, h, qs], KT12.ap[0:64, h, :], True, True, [QP_all, KT12], [pf])
        mm(pf2.ap[:, 0:128], QP_all.ap[64:128, h, qs], KT12.ap[64:128, h, :], True, True, [QP_all, KT12], [pf2])
        cp(SCb.ap[:, 0:128], pf.ap[:, 0:128], [pf], [SCb])
        cp(SCb.ap[:, 128:256], pf2.ap[:, 0:128], [pf2], [SCb])
        for half in range(2):
            top16(SCb.ap[:, half * 128:(half + 1) * 128], SCb, V12.ap[:, half * 16:(half + 1) * 16], I12.ap[:, half * 16:(half + 1) * 16],
                  (V12, I12), tmpS, 128)
        cp(I12f.ap, I12.ap, [I12], [I12f])
        v1b = V12.ap[:, 0:16].unsqueeze(2).to_broadcast([128, 16, 16])
        v2b = V12.ap[:, 16:32].unsqueeze(1).to_broadcast([128, 16, 16])
        i1b = I12f.ap[:, 0:16].unsqueeze(2).to_broadcast([128, 16, 16])
        i2b = I12f.ap[:, 16:32].unsqueeze(1).to_broadcast([128, 16, 16])
        tt(cand.ap, v1b, v2b, ALU.add, [V12], [cand])
        stt(cidx.ap, i1b, 128.0, i2b, ALU.mult, ALU.add, [I12f], [cidx])
        top16(cand_f, cand, SCv.ap, posu.ap, (SCv, posu), tmp256, 256)
        cp(posf2.ap, posu.ap, [posu], [posf2])
        for k in range(16):
            stt(junk256.ap, iota_f.ap, posf2.ap[:, k:k + 1], cidx_f, ALU.is_equal, ALU.mult, [iota_f, posf2, cidx], [junk256, EIf],
                accum=EIf.ap[:, k:k + 1])
        cp(EI_t[i].ap[:, h * 16:(h + 1) * 16], EIf.ap, [EIf], [EI_t[i]])
        ts(negm.ap, SCv.ap[:, 0:1], -1.0, None, ALU.mult, None, [SCv], [negm])
        act(gexp.ap, SCv.ap, AF.Exp, [SCv, negm], [gexp, Zs], bias=negm.ap[:, 0:1], accum=Zs.ap[:, 0:1])
        S.op("dve", lambda e: e.reciprocal(Zs.ap, Zs.ap), r=[Zs], w=[Zs])
        ts(GT_t[i].ap[:, h * 16:(h + 1) * 16], gexp.ap, Zs.ap[:, 0:1], None, ALU.mult, None, [gexp, Zs], [GT_t[i]])

    R5c = Region(arena, 120, 168)
    NUB = 5
    UVG = [R5c.alloc("UVG%d" % k, [128, 2 * D], BF16) for k in range(NUB)]
    PRD = R5c.alloc("PRD", [128, D], F32)
    UV_flat = UVb_d.rearrange("e t d -> e (t d)")
    R5s = Region(arena, 184, 190)
    Adot = RR([R5s.alloc("Adot%d" % k, [128, 1], F32) for k in range(8)])
    AW = RR([R5s.alloc("AW%d" % k, [128, 1], F32) for k in range(8)])
    DG = RR([R5s.alloc("DG%d" % k, [128, 128], BF16) for k in range(3)])
    stats2 = R5s.alloc("stats2", [128, 4, 6], F32)
    mv2 = R5s.alloc("mv2", [128, 2], F32)
    rstd2 = R5s.alloc("rstd2", [128, 1], F32)
    Gh = R5s.alloc("Gh", [128, 512], F32)
    Bh = R5s.alloc("Bh", [128, 512], F32)
    NSL = int(os.environ.get("NSL", "128"))
    LOOK = NUB - 1
    POOL_EVERY = int(os.environ.get("POOL_EVERY", "0"))
    for h in range(8):
        routing_head(0, h)

    def emit_gather(gidx):
        i, slot = divmod(gidx, NSL)
        uvg = UVG[gidx % NUB]
        S.op("pool", lambda e: e.indirect_dma_start(
            out=uvg.ap, out_offset=None, in_=UV_flat,
            in_offset=bass.IndirectOffsetOnAxis(ap=EI_t[i].ap[:, slot:slot + 1], axis=0)), r=[EI_t[i], D_UV], w=[uvg], dma=True)

    for gidx in range(min(LOOK, NTI * NSL)):
        emit_gather(gidx)
    for i in range(NTI):
        for slot in range(NSL):
            gidx = i * NSL + slot
            if gidx + LOOK < NTI * NSL:
                emit_gather(gidx + LOOK)
            uvg = UVG[gidx % NUB]; ad = Adot.next(); aw = AW.next(); dg = DG.next()
            if POOL_EVERY and (slot % POOL_EVERY == POOL_EVERY - 1):
                tt(PRD.ap, uvg.ap[:, 0:D], H1B.ap[:, i, :], ALU.mult, [uvg, H1B], [PRD], eng="pool")
                act(PRD.ap, PRD.ap, AF.Copy, [PRD], [PRD, ad], accum=ad.ap[:, 0:1])
            else:
                stt(uvg.ap[:, 0:D], uvg.ap[:, 0:D], 1.0, H1B.ap[:, i, :], ALU.mult, ALU.mult, [uvg, H1B], [uvg, ad], accum=ad.ap[:, 0:1])
            act(aw.ap, ad.ap, AF.Gelu, [ad], [aw])
            ts(dg.ap, ident.ap, aw.ap[:, 0:1], GT_t[i].ap[:, slot:slot + 1], ALU.mult, ALU.mult, [ident, aw, GT_t[i]], [dg])
            for nf in range(4):
                mm(PF[nf].ap, dg.ap, uvg.ap[:, D + nf * 512:D + (nf + 1) * 512], slot == 0, slot == NSL - 1, [dg, uvg], [PF[nf]])
            if (slot % 14 == 13) and (slot // 14 < 8) and (i + 1 < NTI):
                routing_head(i + 1, slot // 14)
        for nf in range(4):
            Hs = H1.ap[:, i, nf * 512:(nf + 1) * 512]
            tt(Hs, Hs, PF[nf].ap, ALU.add, [H1, PF[nf]], [H1])
        layer_norm_rows(H1.ap[:, i, :], H1, None, None, stats2, mv2, rstd2)
        for qf in range(4):
            S.dma("sp", Gh, Gh.ap, None, ln2g[:, qf * 512:(qf + 1) * 512].partition_broadcast(128))
            S.dma("sp", Bh, Bh.ap, None, ln2b[:, qf * 512:(qf + 1) * 512].partition_broadcast(128))
            Xh = H1.ap[:, i, qf * 512:(qf + 1) * 512]
            tt(Xh, Xh, Gh.ap, ALU.mult, [H1, Gh], [H1])
            tt(Xh, Xh, Bh.ap, ALU.add, [H1, Bh], [H1])
        S.dma("sp", D_out, out_d[i * 128:(i + 1) * 128, :], H1, H1.ap[:, i, :])
    S.finish(final_bufs)
    return nc


def own_rows(j):
    return np.concatenate([np.arange((4 * i + j) * 128, (4 * i + j + 1) * 128) for i in range(8)])


def core_inputs(inp, c):
    b, j = c // 4, c % 4
    own = own_rows(j)
    f = lambda a: np.ascontiguousarray(a, dtype=np.float32)
    x = inp["x"][b]
    negmask = np.zeros((128, 4, 128), np.float32)
    sel = np.zeros((128, 4, 128), np.float32)
    kk = np.arange(128)[:, None]; qq = np.arange(128)[None, :]
    for m in range(4):
        if m == j:
            negmask[:, m, :] = np.where(kk <= qq, 0.0, NEG)
            sel[:, m, :] = np.eye(128, dtype=np.float32)
        elif m > j:
            negmask[:, m, :] = NEG
    inv_freq = (1.0 / (10000.0 ** (np.arange(0, 64, 2, dtype=np.float32) / 64))).astype(np.float32)
    invf = np.concatenate([inv_freq, inv_freq])[:, None]
    sgn = np.concatenate([-np.ones(32, np.float32), np.ones(32, np.float32)])[:, None]
    return {
        "xkv": f(x), "xq": f(x[own]), "pq": f(inp["p"][0, b][own]),
        "pos_kv": np.ascontiguousarray(inp["positions"][b][None, :], dtype=np.int32),
        "pos_q": np.ascontiguousarray(inp["positions"][b][own][None, :], dtype=np.int32),
        "w_in": f(inp["w_in"][0]), "w_uq": f(inp["w_uq"][0]), "w_ukv": f(inp["w_ukv"][0]),
        "w_out": f(inp["w_out"][0]), "peer_wq": f(inp["peer_wq"][0]),
        "keys1": f(inp["peer_keys1"][0]), "keys2": f(inp["peer_keys2"][0]),
        "peer_u": f(inp["peer_u"][0]), "peer_v": f(inp["peer_v"][0]),
        "ple_wgate": f(inp["ple_wgate"][0]), "ple_wproj": f(inp["ple_wproj"][0]),
        "gqT": f(inp["g_q_norm"][0].reshape(4, 128).T), "gkvT": f(inp["g_kv_norm"][0].reshape(4, 128).T),
        "bfg": f(inp["b_forget"][0][:, None]),
        "ln1g": f(inp["ln1_g"]), "ln1b": f(inp["ln1_b"]), "ln2g": f(inp["ln2_g"]), "ln2b": f(inp["ln2_b"]),
        "negmask": negmask, "sel": sel, "invf": f(invf), "sgn": f(sgn),
    }


_NC_CACHE = {}


def kernel(**inputs):
    inp = {k: np.asarray(v) for k, v in inputs.items()}
    if "nc" not in _NC_CACHE:
        _NC_CACHE["nc"] = build_nc()
    nc = _NC_CACHE["nc"]
    maps = [core_inputs(inp, c) for c in range(8)]
    res = run_bass_kernel_spmd(nc, maps, core_ids=list(range(8)))
    out = np.zeros((2, NT, D), np.float32)
    for c in range(8):
        b, j = c // 4, c % 4
        out[b, own_rows(j)] = np.asarray(res.results[c]["out"], dtype=np.float32)
    return out
```

```python
import numpy as np
from contextlib import ExitStack
import concourse.bass as bass
import concourse.mybir as mybir
from concourse.bass_utils import run_bass_kernel_spmd
from concourse.alu_op_type import AluOpType as ALU

F32 = mybir.dt.float32
BF16 = mybir.dt.bfloat16
I32 = mybir.dt.int32
U32 = mybir.dt.uint32
AF = mybir.ActivationFunctionType


class Buf:
    def __init__(self, name, ap):
        self.name = name
        self.ap = ap
        self.last_w = []
        self.reads = []
        self.dsem = None
        self.dcount = 0
        self.is_dram = False
        self.excl = False

    def __getitem__(self, k):
        return self.ap[k]


class Sched:
    ENG = ["pe", "dve", "act", "pool", "sp"]

    def __init__(self, nc):
        self.nc = nc
        self.stack = ExitStack()
        self.eobj = {"pe": nc.tensor, "dve": nc.vector, "act": nc.scalar, "pool": nc.gpsimd, "sp": nc.sync}
        self.q = {e: [] for e in self.ENG}
        self.cnt = {e: 0 for e in self.ENG}
        self.esem = {e: self.stack.enter_context(nc.semaphore("es_" + e)) for e in self.ENG}
        self.waited = {e: {} for e in self.ENG}
        self.nbuf = 0
        self.dma_tokens = []
        self.free_dsems = []

    def sbuf(self, name, shape, dtype, stack=None):
        t = (stack or self.stack).enter_context(self.nc.sbuf_tensor(name, list(shape), dtype))
        return Buf(name, t[:])

    def psum(self, name, shape, dtype, stack=None):
        t = (stack or self.stack).enter_context(self.nc.psum_tensor(name, list(shape), dtype))
        b = Buf(name, t[:])
        b.excl = True
        return b

    def dram(self, ap, name="dram"):
        b = Buf(name, ap)
        b.is_dram = True
        return b

    def _dsem(self, buf):
        if buf.dsem is None:
            self.nbuf += 1
            buf.dsem = self.stack.enter_context(self.nc.semaphore("ds%d" % self.nbuf))
        return buf.dsem

    def _deps(self, eng, r, w):
        deps = []
        for b in r:
            deps.extend(b.last_w)
        for b in w:
            deps.extend(b.last_w)
            deps.extend(b.reads)
        out = {}
        for (sem, val, e) in deps:
            if e == eng and eng in ("pe", "sp"):
                continue
            key = id(sem)
            if self.waited[eng].get(key, 0) >= val:
                continue
            if key not in out or out[key][1] < val:
                out[key] = (sem, val)
        for key, (sem, val) in out.items():
            self.waited[eng][key] = val
        return list(out.values())

    def _commit(self, tok, r, w, accumulate=False):
        for b in r:
            b.reads.append(tok)
        for b in w:
            if accumulate:
                b.last_w = [t for t in b.last_w if t[0] is not tok[0]] + [tok]
            else:
                b.last_w = [tok]
            b.reads = []

    def op(self, eng, fn, r=(), w=(), dma=False):
        r = list(r); w = list(w)
        for b in list(r):
            if b.excl:
                r.remove(b)
                if b not in w:
                    w.append(b)
        deps = self._deps(eng, r, w)
        if dma:
            wb = w[0]
            own = r[0] if (wb.is_dram and r) else wb
            sem = self._dsem(own)
            own.dcount += 16
            tok = (sem, own.dcount, "dma")
            self.q[eng].append((deps, fn, sem, 16))
            self.dma_tokens.append(tok)
            self._commit(tok, r, w, accumulate=wb.is_dram)
            return
        else:
            self.cnt[eng] += 1
            tok = (self.esem[eng], self.cnt[eng], eng)
            self.q[eng].append((deps, fn, self.esem[eng], 1))
        self._commit(tok, r, w)

    def dma(self, eng, wbuf, out_ap, rbuf, in_ap, **kw):
        r = [rbuf] if rbuf is not None else []
        self.op(eng, lambda e: e.dma_start(out=out_ap, in_=in_ap, **kw), r=r, w=[wbuf], dma=True)

    def barrier(self):
        toks = [(self.esem[e], self.cnt[e], e) for e in self.ENG if self.cnt[e] > 0]
        toks += self.dma_tokens
        self.dma_tokens = []
        for eng in self.ENG:
            out = {}
            for (sem, val, e) in toks:
                if e == eng:
                    continue
                key = id(sem)
                if self.waited[eng].get(key, 0) >= val:
                    continue
                if key not in out or out[key][1] < val:
                    out[key] = (sem, val)
            for key, (sem, val) in out.items():
                self.waited[eng][key] = val
            if out:
                self.q[eng].append((list(out.values()), None, None, 0))

    def finish(self, out_bufs):
        deps = []
        for b in out_bufs:
            for t in b.last_w:
                deps.append((t[0], t[1]))
        self.q["sp"].append((deps, None, None, 0))
        with self.nc.Block() as block:
            def mk(eng):
                def body(e):
                    for (deps, fn, sem, inc) in self.q[eng]:
                        for (s, v) in deps:
                            e.wait_ge(s, v)
                        if fn is not None:
                            ins = fn(e)
                            ins.then_inc(sem, inc)
                return body
            block.tensor(mk("pe"))
            block.vector(mk("dve"))
            block.scalar(mk("act"))
            block.gpsimd(mk("pool"))
            block.sync(mk("sp"))
        self.stack.close()


D = 2048
NT = 4096
NQ = 1024
TT = 256
ALPHA = 2.0 ** 0.25
EPS = 1e-6
PI = float(np.pi)
SC_MLA = 192.0 ** -0.5
SC_FOX = 128.0 ** -0.5
NEG = -1.0e30


class RR:
    def __init__(self, items):
        self.items = list(items); self.i = 0

    def next(self):
        x = self.items[self.i % len(self.items)]; self.i += 1
        return x


class Region:
    def __init__(self, arena, start_kb, end_kb):
        self.arena = arena; self.off = int(start_kb * 1024); self.end = int(end_kb * 1024)

    def alloc(self, name, shape, dtype):
        n = int(np.prod(shape[1:]))
        esz = 2 if dtype == BF16 else 4
        nb = (n * esz + 63) // 64 * 64
        assert self.off + nb <= self.end, (name, self.off, nb, self.end)
        o = self.off // 2
        ap = self.arena[0:shape[0], o:o + n * esz // 2]
        if dtype != BF16:
            ap = ap.bitcast(dtype)
        if len(shape) == 3:
            ap = ap.rearrange("p (a b) -> p a b", b=shape[2])
        elif len(shape) == 4:
            ap = ap.rearrange("p (a b c) -> p a b c", b=shape[2], c=shape[3])
        self.off += nb
        return Buf(name, ap)


import os
SECT = os.environ.get('SECT', 'mla,rope,fox,forget').split(',')
NTILES = int(os.environ.get('NTILES', '0'))


def build_nc(debug=False, stop_after=99):
    nc = bass.Bass("TRN2", target_bir_lowering=False)
    dbg_outs = {}

    def IN(name, shape, dtype=F32):
        return nc.dram_tensor(name, list(shape), dtype, kind="ExternalInput").ap()

    def SCR(name, shape, dtype):
        return nc.dram_tensor(name, list(shape), dtype, kind=("ExternalOutput" if debug else "Internal")).ap()

    xkv = IN("xkv", [NT, D]); xq = IN("xq", [NQ, D]); pq = IN("pq", [NQ, 256])
    pos_kv = IN("pos_kv", [1, NT], I32); pos_q = IN("pos_q", [1, NQ], I32)
    w_in = IN("w_in", [D, 4168]); w_uq = IN("w_uq", [512, 1536]); w_ukv = IN("w_ukv", [512, 2048])
    w_out = IN("w_out", [D, D]); peer_wq = IN("peer_wq", [D, 1024])
    keys1 = IN("keys1", [8, 128, 64]); keys2 = IN("keys2", [8, 128, 64])
    peer_u = IN("peer_u", [16384, D]); peer_v = IN("peer_v", [16384, D])
    ple_wgate = IN("ple_wgate", [D, D]); ple_wproj = IN("ple_wproj", [256, D])
    gqT = IN("gqT", [128, 4]); gkvT = IN("gkvT", [128, 4]); bfg = IN("bfg", [8, 1])
    ln1g = IN("ln1g", [1, D]); ln1b = IN("ln1b", [1, D]); ln2g = IN("ln2g", [1, D]); ln2b = IN("ln2b", [1, D])
    negmask_in = IN("negmask", [128, 4, 128]); sel_in = IN("sel", [128, 4, 128])
    invf_in = IN("invf", [64, 1]); sgn_in = IN("sgn", [64, 1])
    out_d = nc.dram_tensor("out", [NQ, D], F32, kind="ExternalOutput").ap()

    kTm_d = SCR("kTm_d", [8, 128, NT], BF16); krT_d = SCR("krT_d", [64, NT], BF16)
    vm_d = SCR("vm_d", [8, 128, 32, 128], BF16)
    kTf_d = SCR("kTf_d", [8, 128, NT], BF16); vf_d = SCR("vf_d", [8, 128, 32, 128], BF16)
    UVb_d = nc.dram_tensor("UVb_d", [16384, 2, D], BF16, kind="Internal").ap()

    S = Sched(nc)
    arena = S.stack.enter_context(nc.sbuf_tensor("arena", [128, 95 * 1024], BF16))
    D_kTm = S.dram(kTm_d); D_krT = S.dram(krT_d); D_vm = S.dram(vm_d); D_kTf = S.dram(kTf_d); D_vf = S.dram(vf_d)
    D_out = S.dram(out_d)
    D_UV = S.dram(UVb_d)
    final_bufs = [D_out]

    def dbg(name, buf, shape, dtype):
        if not debug:
            return
        t = nc.dram_tensor("dbg_" + name, list(shape), dtype, kind="ExternalOutput").ap()
        b = S.dram(t)
        S.dma("sp", b, t, buf, buf.ap)
        final_bufs.append(b)

    def mm(out, lhsT, rhs, start, stop, r, w):
        S.op("pe", lambda e: e.matmul(out, lhsT, rhs, start=start, stop=stop), r=r, w=w)

    def tr(out, in_, idn, r, w):
        S.op("pe", lambda e: e.transpose(out, in_, idn), r=r, w=w)

    def act(out, in_, func, r, w, bias=None, scale=None, accum=None):
        kw = {}
        if bias is not None: kw["bias"] = bias
        if scale is not None: kw["scale"] = scale
        if accum is not None: kw["accum_out"] = accum
        S.op("act", lambda e: e.activation(out, in_, func, **kw), r=r, w=w)

    def tt(out, a, b, op, r, w, eng="dve"):
        S.op(eng, lambda e: e.tensor_tensor(out, a, b, op), r=r, w=w)

    def ts(out, a, s1, s2, op0, op1, r, w, eng="dve", accum=None):
        if op1 is None:
            S.op(eng, lambda e: e.tensor_scalar(out, a, s1, None, op0), r=r, w=w)
        elif accum is not None:
            S.op(eng, lambda e: e.tensor_scalar(out, a, s1, s2, op0, op1, accum_out=accum), r=r, w=w)
        else:
            S.op(eng, lambda e: e.tensor_scalar(out, a, s1, s2, op0, op1), r=r, w=w)

    def stt(out, a, sc, b, op0, op1, r, w, accum=None):
        if accum is None:
            S.op("dve", lambda e: e.scalar_tensor_tensor(out=out, in0=a, scalar=sc, in1=b, op0=op0, op1=op1), r=r, w=w)
        else:
            S.op("dve", lambda e: e.scalar_tensor_tensor(out=out, in0=a, scalar=sc, in1=b, op0=op0, op1=op1, accum_out=accum), r=r, w=w)

    def cp(out, in_, r, w, eng="dve"):
        if eng == "act":
            S.op("act", lambda e: e.copy(out, in_), r=r, w=w)
        else:
            S.op(eng, lambda e: e.tensor_copy(out, in_), r=r, w=w)

    evac_i = [0]

    def evac(out, in_, r, w):
        evac_i[0] += 1
        cp(out, in_, r, w, eng=("act" if evac_i[0] % 2 else "dve"))

    PF = [S.psum("pf%d" % i, [128, 512], F32) for i in range(6)]
    PB = [S.psum("pb%d" % i, [128, 1024], BF16) for i in range(2)]
    PFR = RR(PF); PBR = RR(PB)

    RC = Region(arena, 0, 8)
    identf = RC.alloc("identf", [128, 128], F32)
    ident = RC.alloc("ident", [128, 128], BF16)
    ones_bf = RC.alloc("ones_bf", [128, 128], BF16)
    ones_f = RC.alloc("ones_f", [128, 256], F32)
    epsT = RC.alloc("epsT", [128, 1], F32)
    oneT = RC.alloc("oneT", [128, 1], F32)
    invf = RC.alloc("invf", [64, 1], F32)
    sgn = RC.alloc("sgn", [64, 1], F32)
    negb = RC.alloc("negb", [8, 1], F32)
    gkv = RC.alloc("gkv", [128, 4], F32)
    gq = RC.alloc("gq", [128, 4], F32)
    negF_tok = RC.alloc("negF_tok", [128, 32, 8], F32)
    negmask = RC.alloc("negmask", [128, 4, 128], BF16)
    EI = RC.alloc("EI", [128, 128], I32)
    GT = RC.alloc("GT", [128, 128], F32)

    S.op("pool", lambda e: e.memset(identf.ap, 0.0), w=[identf])
    S.op("pool", lambda e: e.affine_select(out=identf.ap, in_=identf.ap, pattern=[[-1, 128]], compare_op=ALU.not_equal,
                                           fill=1.0, base=0, channel_multiplier=1), r=[identf], w=[identf])
    cp(ident.ap, identf.ap, [identf], [ident])
    S.op("dve", lambda e: e.memset(ones_bf.ap, 1.0), w=[ones_bf])
    S.op("dve", lambda e: e.memset(ones_f.ap, 1.0), w=[ones_f])
    S.op("dve", lambda e: e.memset(epsT.ap, EPS), w=[epsT])
    S.op("dve", lambda e: e.memset(oneT.ap, 1.0), w=[oneT])
    S.dma("sp", invf, invf.ap, None, invf_in)
    S.dma("sp", sgn, sgn.ap, None, sgn_in)
    S.dma("sp", negb, negb.ap, None, bfg)
    ts(negb.ap, negb.ap, -1.0, None, ALU.mult, None, [negb], [negb])
    S.dma("sp", gkv, gkv.ap, None, gkvT)
    S.dma("sp", gq, gq.ap, None, gqT)
    S.dma("pool", negmask, negmask.ap, None, negmask_in)

    w_in_v = w_in.rearrange("(c p) n -> p c n", p=128)

    def rope_tables(R, pos_ap, c0, n, tag):
        posi = R.alloc("posi" + tag, [64, n], I32)
        posf = R.alloc("posf" + tag, [64, n], F32)
        ang = R.alloc("ang" + tag, [64, n], F32)
        tq = R.alloc("tq" + tag, [64, n], F32)
        ki = R.alloc("ki" + tag, [64, n], I32)
        cosb = R.alloc("cos" + tag, [64, n], F32)
        sinb = R.alloc("sin" + tag, [64, n], F32)

        def emit(c0):
            S.dma("sp", posi, posi.ap, None, pos_ap[0:1, c0:c0 + n].partition_broadcast(64))
            cp(posf.ap, posi.ap, [posi], [posf])
            for (phase, dst) in ((0.0, sinb), (PI / 2, cosb)):
                ts(ang.ap, posf.ap, invf.ap[:, 0:1], phase, ALU.mult, ALU.add, [posf, invf], [ang])
                ts(tq.ap, ang.ap, 1.0 / (2 * PI), None, ALU.mult, None, [ang], [tq])
                cp(ki.ap, tq.ap, [tq], [ki])
                cp(tq.ap, ki.ap, [ki], [tq])
                stt(ang.ap, tq.ap, -2 * PI, ang.ap, ALU.mult, ALU.add, [tq, ang], [ang])
                ts(tq.ap, ang.ap, PI, -2 * PI, ALU.is_gt, ALU.mult, [ang], [tq])
                tt(ang.ap, ang.ap, tq.ap, ALU.add, [ang, tq], [ang])
                ts(tq.ap, ang.ap, -PI, 2 * PI, ALU.is_lt, ALU.mult, [ang], [tq])
                tt(ang.ap, ang.ap, tq.ap, ALU.add, [ang, tq], [ang])
                ts(ang.ap, ang.ap, PI, -PI, ALU.min, ALU.max, [ang], [ang])
                act(dst.ap, ang.ap, AF.Sin, [ang], [dst])
            ts(sinb.ap, sinb.ap, sgn.ap[:, 0:1], None, ALU.mult, None, [sinb, sgn], [sinb])
        return cosb, sinb, emit

    def transpose_in(xb, xT, ns):
        for ck in range(16):
            pb = PBR.next()
            for s_ in range(ns):
                tr(pb.ap[:, s_ * 128:(s_ + 1) * 128], xb.ap[:, s_, ck * 128:(ck + 1) * 128], ident.ap, [xb, ident], [pb])
            evac(xT.ap[:, ck, :], pb.ap[:, 0:ns * 128], [pb], [xT])

    def proj_rms(R, Wt, col0, xT, n, tag):
        raw = R.alloc("raw" + tag, [128, 4, n], F32)
        sq = R.alloc("sq" + tag, [128, 4, n], BF16)
        nrm = R.alloc("nrm" + tag, [128, 4, n], BF16)
        Rs = R.alloc("Rs" + tag, [128, n], F32)

        def emit():
            for fc in range(4):
                pf = PFR.next()
                for ck in range(16):
                    mm(pf.ap[:, 0:n], Wt.ap[:, ck, col0 + fc * 128:col0 + (fc + 1) * 128], xT.ap[:, ck, :], ck == 0, ck == 15, [Wt, xT], [pf])
                act(sq.ap[:, fc, :], pf.ap[:, 0:n], AF.Square, [pf], [sq])
                cp(raw.ap[:, fc, :], pf.ap[:, 0:n], [pf], [raw])
            pf = PFR.next()
            for fc in range(4):
                mm(pf.ap[:, 0:n], ones_bf.ap, sq.ap[:, fc, :], fc == 0, fc == 3, [ones_bf, sq], [pf])
            act(Rs.ap, pf.ap[:, 0:n], AF.Sqrt, [pf, epsT], [Rs], bias=epsT.ap[:, 0:1], scale=1.0 / 512)
            S.op("dve", lambda e: e.reciprocal(Rs.ap, Rs.ap), r=[Rs], w=[Rs])
            for fc in range(4):
                tt(nrm.ap[:, fc, :], raw.ap[:, fc, :], Rs.ap, ALU.mult, [raw, Rs], [nrm])
        return nrm, emit

    R1 = Region(arena, 8, 190)
    Frow = R1.alloc("Frow", [8, NT], F32)
    Wm = R1.alloc("Wm", [128, 16, 576], BF16)
    Wf = R1.alloc("Wf", [128, 16, 2056], BF16)
    Wukv = R1.alloc("Wukv", [128, 4, 2048], BF16)
    Wkrsw = R1.alloc("Wkrsw", [128, 16, 64], BF16)
    mark = R1.off
    stg = R1.alloc("stg", [128, 4, 2048], F32)
    S.dma("pool", Wm, Wm.ap, None, w_in_v[:, :, 512:1088])
    S.dma("pool", Wf, Wf.ap[:, :, 0:1028], None, w_in_v[:, :, 2112:3140])
    S.dma("pool", Wf, Wf.ap[:, :, 1028:2056], None, w_in_v[:, :, 3140:4168])
    S.dma("sp", stg, stg.ap, None, w_ukv.rearrange("(c p) n -> p c n", p=128))
    for fc in range(4):
        ts(Wukv.ap[:, fc, :], stg.ap[:, fc, :], gkv.ap[:, fc:fc + 1], None, ALU.mult, None, [stg, gkv], [Wukv])
    cp(Wkrsw.ap[:, :, 0:32], Wm.ap[:, :, 544:576], [Wm], [Wkrsw])
    cp(Wkrsw.ap[:, :, 32:64], Wm.ap[:, :, 512:544], [Wm], [Wkrsw])
    S.barrier()
    R1.off = mark
    ns = TT // 128
    xb = R1.alloc("xb", [128, ns, D], BF16)
    xT = R1.alloc("xT", [128, 16, TT], BF16)
    ckvn, emit_ckv = proj_rms(R1, Wm, 0, xT, TT, "kv")
    kn_st = RR([R1.alloc("kn_st%d" % i, [128, 8, TT], BF16) for i in range(2)])
    fk_st = RR([R1.alloc("fk_st%d" % i, [128, 8, TT], BF16) for i in range(2)])
    v_st = R1.alloc("v_st", [128, ns, 1024], BF16)
    fv_st = R1.alloc("fv_st", [128, ns, 1024], BF16)
    cosb, sinb, emit_rope = rope_tables(R1, pos_kv, 0, TT, "kv")
    T1 = R1.alloc("T1", [64, TT], F32); T2 = R1.alloc("T2", [64, TT], F32)
    kr_st = R1.alloc("kr_st", [64, TT], BF16)
    exb = R1.alloc("exb", [8, TT], F32); lnb = R1.alloc("lnb", [8, TT], F32)

    def sec_load(T, t0):
        S.dma("pool", xb, xb.ap, None, xkv[t0:t0 + TT, :].rearrange("(s p) d -> p s d", p=128))
        transpose_in(xb, xT, ns)
    MLAK = int(os.environ.get('MLAK', '9'))

    def sec_mla(T, t0):
        emit_ckv()
        if MLAK < 2: return
        kst = kn_st.next()
        for h in range(8):
            pf = PFR.next()
            for fc in range(4):
                mm(pf.ap[:, 0:TT], Wukv.ap[:, fc, h * 256:h * 256 + 128], ckvn.ap[:, fc, :], fc == 0, fc == 3, [Wukv, ckvn], [pf])
            evac(kst.ap[:, h, :], pf.ap[:, 0:TT], [pf], [kst])
        S.dma("sp", D_kTm, kTm_d.rearrange("h d t -> d h t")[:, :, t0:t0 + TT], kst, kst.ap)
        if MLAK < 3: return
        for s_ in range(ns):
            for hh in range(2):
                pf = PFR.next()
                for fc in range(4):
                    rhs = Wukv.ap[:, fc, :].rearrange("p (h c) -> p h c", c=256)[:, hh * 4:(hh + 1) * 4, 128:256]
                    mm(pf.ap, ckvn.ap[:, fc, s_ * 128:(s_ + 1) * 128], rhs, fc == 0, fc == 3, [Wukv, ckvn], [pf])
                evac(v_st.ap[:, s_, hh * 512:(hh + 1) * 512], pf.ap, [pf], [v_st])
            blk = T * ns + s_
            S.dma("sp", D_vm, vm_d.rearrange("h p b d -> p b h d")[:, blk, :, :], v_st,
                  v_st.ap[:, s_, :].rearrange("p (h d) -> p h d", d=128))
    def sec_rope(T, t0):
        pk = PFR.next(); pks = PFR.next()
        for ck in range(16):
            mm(pk.ap[0:64, 0:TT], Wm.ap[:, ck, 512:576], xT.ap[:, ck, :], ck == 0, ck == 15, [Wm, xT], [pk])
        for ck in range(16):
            mm(pks.ap[0:64, 0:TT], Wkrsw.ap[:, ck, :], xT.ap[:, ck, :], ck == 0, ck == 15, [Wkrsw, xT], [pks])
        emit_rope(t0)
        tt(T1.ap, pk.ap[0:64, 0:TT], cosb.ap, ALU.mult, [pk, cosb], [T1])
        tt(T2.ap, pks.ap[0:64, 0:TT], sinb.ap, ALU.mult, [pks, sinb], [T2])
        tt(kr_st.ap, T1.ap, T2.ap, ALU.add, [T1, T2], [kr_st])
        S.dma("sp", D_krT, krT_d[:, t0:t0 + TT], kr_st, kr_st.ap)
    def sec_fox(T, t0):
        fst = fk_st.next()
        for h in range(8):
            pf = PFR.next()
            for ck in range(16):
                mm(pf.ap[:, 0:TT], Wf.ap[:, ck, h * 128:(h + 1) * 128], xT.ap[:, ck, :], ck == 0, ck == 15, [Wf, xT], [pf])
            evac(fst.ap[:, h, :], pf.ap[:, 0:TT], [pf], [fst])
        S.dma("sp", D_kTf, kTf_d.rearrange("h d t -> d h t")[:, :, t0:t0 + TT], fst, fst.ap)
        for s_ in range(ns):
            for hh in range(2):
                pf = PFR.next()
                for ck in range(16):
                    mm(pf.ap, xT.ap[:, ck, s_ * 128:(s_ + 1) * 128], Wf.ap[:, ck, 1024 + hh * 512:1024 + (hh + 1) * 512], ck == 0, ck == 15, [Wf, xT], [pf])
                evac(fv_st.ap[:, s_, hh * 512:(hh + 1) * 512], pf.ap, [pf], [fv_st])
            blk = T * ns + s_
            S.dma("sp", D_vf, vf_d.rearrange("h p b d -> p b h d")[:, blk, :, :], fv_st,
                  fv_st.ap[:, s_, :].rearrange("p (h d) -> p h d", d=128))
    def sec_forget(T, t0):
        pff = PFR.next()
        for ck in range(16):
            mm(pff.ap[0:8, 0:TT], Wf.ap[:, ck, 2048:2056], xT.ap[:, ck, :], ck == 0, ck == 15, [Wf, xT], [pff])
        act(exb.ap, pff.ap[0:8, 0:TT], AF.Exp, [pff, negb], [exb], bias=negb.ap[:, 0:1], scale=-1.0)
        act(lnb.ap, exb.ap, AF.Ln, [exb, oneT], [lnb], bias=oneT.ap[0:8, 0:1])
        init = 0.0 if T == 0 else Frow.ap[:, t0 - 1:t0]
        S.op("dve", lambda e, init=init, t0=t0: e.tensor_tensor_scan(Frow.ap[:, t0:t0 + TT], ones_f.ap[0:8, 0:TT], lnb.ap, init, ALU.mult, ALU.subtract),
             r=[Frow, ones_f, lnb], w=[Frow])
        for s_ in range(ns):
            blk = T * ns + s_
            pf = PFR.next()
            tr(pf.ap[:, 0:8], Frow.ap[0:8, blk * 128:(blk + 1) * 128], identf.ap[0:8, 0:8], [Frow, identf], [pf])
            ts(negF_tok.ap[:, blk, :], pf.ap[:, 0:8], -1.0, None, ALU.mult, None, [pf], [negF_tok])

    for T in range(NTILES or (NT // TT)):
        t0 = T * TT
        sec_load(T, t0)
        r0 = T * 1024
        S.dma("pool", D_UV, UVb_d[r0:r0 + 1024, 0, :], None, peer_u[r0:r0 + 1024, :])
        if 'mla' in SECT: sec_mla(T, t0)
        if 'rope' in SECT: sec_rope(T, t0)
        if 'fox' in SECT: sec_fox(T, t0)
        if 'forget' in SECT: sec_forget(T, t0)
    dbg("negF", negF_tok, [128, 32, 8], F32)
    dbg("Frow", Frow, [8, NT], F32)
    if stop_after <= 1:
        final_bufs.extend([D_kTm, D_krT, D_vm, D_kTf, D_vf])
        S.finish(final_bufs)
        return nc

    S.barrier()
    RP = Region(arena, 24, 88)
    QN = RP.alloc("QN", [128, 8, NQ], BF16)
    QR = RP.alloc("QR", [64, 8, NQ], BF16)
    FQ = RP.alloc("FQ", [128, 8, NQ], BF16)
    Fq_row = RP.alloc("Fq_row", [1, 8, NQ], BF16)
    R2 = Region(arena, 88, 190)
    Wqc = R2.alloc("Wqc", [128, 16, 512], BF16)
    Wqf = R2.alloc("Wqf", [128, 16, 1024], BF16)
    Wuq = R2.alloc("Wuq", [128, 4, 1536], BF16)
    Wuqsw = R2.alloc("Wuqsw", [128, 4, 8, 64], BF16)
    mark = R2.off
    stg2 = R2.alloc("stg2", [128, 4, 1536], F32)
    S.dma("pool", Wqc, Wqc.ap, None, w_in_v[:, :, 0:512])
    S.dma("pool", Wqf, Wqf.ap, None, w_in_v[:, :, 1088:2112])
    S.dma("sp", stg2, stg2.ap, None, w_uq.rearrange("(c p) n -> p c n", p=128))
    for fc in range(4):
        ts(Wuq.ap[:, fc, :], stg2.ap[:, fc, :], gq.ap[:, fc:fc + 1], None, ALU.mult, None, [stg2, gq], [Wuq])
    for fc in range(4):
        src = Wuq.ap[:, fc, :].rearrange("p (h c) -> p h c", c=192)
        cp(Wuqsw.ap[:, fc, :, 0:32], src[:, :, 160:192], [Wuq], [Wuqsw])
        cp(Wuqsw.ap[:, fc, :, 32:64], src[:, :, 128:160], [Wuq], [Wuqsw])
    S.barrier()
    R2.off = mark
    xb2 = R2.alloc("xb2", [128, ns, D], BF16)
    xT2 = R2.alloc("xT2", [128, 16, TT], BF16)
    cqn, emit_cq = proj_rms(R2, Wqc, 0, xT2, TT, "q")
    cosq, sinq, emit_ropeq = rope_tables(R2, pos_q, 0, TT, "q")
    T1q = R2.alloc("T1q", [64, TT], F32); T2q = R2.alloc("T2q", [64, TT], F32)
    selb = R2.alloc("selb", [128, 4, 128], F32)
    S.dma("sp", selb, selb.ap, None, sel_in)
    for T in range(NQ // TT):
        t0 = T * TT
        S.dma("pool", xb2, xb2.ap, None, xq[t0:t0 + TT, :].rearrange("(s p) d -> p s d", p=128))
        transpose_in(xb2, xT2, ns)
        emit_cq()
        emit_ropeq(t0)
        for h in range(8):
            pf = PFR.next()
            for fc in range(4):
                mm(pf.ap[:, 0:TT], Wuq.ap[:, fc, h * 192:h * 192 + 128], cqn.ap[:, fc, :], fc == 0, fc == 3, [Wuq, cqn], [pf])
            evac(QN.ap[:, h, t0:t0 + TT], pf.ap[:, 0:TT], [pf], [QN])
            pk = PFR.next(); pks = PFR.next()
            for fc in range(4):
                mm(pk.ap[0:64, 0:TT], Wuq.ap[:, fc, h * 192 + 128:h * 192 + 192], cqn.ap[:, fc, :], fc == 0, fc == 3, [Wuq, cqn], [pk])
            for fc in range(4):
                mm(pks.ap[0:64, 0:TT], Wuqsw.ap[:, fc, h, :], cqn.ap[:, fc, :], fc == 0, fc == 3, [Wuqsw, cqn], [pks])
            tt(T1q.ap, pk.ap[0:64, 0:TT], cosq.ap, ALU.mult, [pk, cosq], [T1q])
            tt(T2q.ap, pks.ap[0:64, 0:TT], sinq.ap, ALU.mult, [pks, sinq], [T2q])
            tt(QR.ap[:, h, t0:t0 + TT], T1q.ap, T2q.ap, ALU.add, [T1q, T2q], [QR])
            pf = PFR.next()
            for ck in range(16):
                mm(pf.ap[:, 0:TT], Wqf.ap[:, ck, h * 128:(h + 1) * 128], xT2.ap[:, ck, :], ck == 0, ck == 15, [Wqf, xT2], [pf])
            evac(FQ.ap[:, h, t0:t0 + TT], pf.ap[:, 0:TT], [pf], [FQ])
    for i in range(8):
        for h in range(8):
            pf = PFR.next()
            for m in range(4):
                mm(pf.ap[0:1, 0:128], negF_tok.ap[:, 4 * i + m, h:h + 1], selb.ap[:, m, :], m == 0, m == 3, [negF_tok, selb], [pf])
            ts(Fq_row.ap[0:1, h, i * 128:(i + 1) * 128], pf.ap[0:1, 0:128], -1.0 / SC_FOX, None, ALU.mult, None, [pf], [Fq_row])
    dbg("QN", QN, [128, 8, NQ], BF16)
    dbg("QR", QR, [64, 8, NQ], BF16)
    dbg("FQ", FQ, [128, 8, NQ], BF16)
    dbg("Fqrow", Fq_row, [1, 8, NQ], BF16)
    if stop_after <= 2:
        S.finish(final_bufs)
        return nc

    S.barrier()
    OT = Region(arena, 158, 190).alloc("OT", [128, 16, NQ], BF16)
    R3 = Region(arena, 88, 158)
    KT = RR([R3.alloc("KT%d" % i, [128, NT], BF16) for i in range(2)])
    KR = R3.alloc("KR", [64, NT], BF16)
    VV = RR([R3.alloc("V%d" % i, [128, 32, 129], BF16) for i in range(2)])
    PT = RR([R3.alloc("PT%d" % i, [128, 512], BF16) for i in range(4)])
    recb = RR([R3.alloc("recb%d" % i, [128, 512], F32) for i in range(2)])
    for v in VV.items:
        S.op("dve", lambda e, v=v: e.memset(v.ap[:, :, 128:129], 1.0), w=[v])
    S.dma("sp", KR, KR.ap, D_krT, krT_d)
    OACC = [PF[0], PF[1]]; DEN = [PF[2], PF[3]]
    SB1 = []
    for k in range(2):
        b_ = Buf("sb1_%d" % k, PB[k].ap.bitcast(F32))
        b_.excl = True
        SB1.append(b_)
    SSET = [[PF[4], PF[5]], SB1]
    NHD = int(os.environ.get("NHD", "16"))
    for hd in range(NHD):
        mla = hd < 8; h = hd % 8
        kt = KT.next(); v = VV.next()
        S.dma("sp", kt, kt.ap, (D_kTm if mla else D_kTf), (kTm_d if mla else kTf_d)[h])
        S.dma("sp", v, v.ap[:, :, 0:128], (D_vm if mla else D_vf), (vm_d if mla else vf_d)[h])
        S.dma("pool", D_UV, UVb_d[hd * 1024:(hd + 1) * 1024, 1, :], None, peer_v[hd * 1024:(hd + 1) * 1024, :])
        def emit_scores(kb):
            g = kb // 4; m = kb % 4
            parts = []
            if g < 4:
                parts.append((0, g * 128, 512))
                parts.append((1, 512, 1024))
            else:
                parts.append((1, g * 128, 1024))
            kcols = slice(kb * 128, (kb + 1) * 128)
            pts = []
            for (bk, c0, c1) in parts:
                n = c1 - c0
                ps = SSET[kb % 2][bk]
                out = ps.ap[:, 0:n]
                has_diag = (c0 == g * 128)
                if mla:
                    mm(out, kt.ap[:, kcols], QN.ap[:, h, c0:c1], True, False, [kt, QN], [ps])
                    mm(out, KR.ap[0:64, kcols], QR.ap[0:64, h, c0:c1], False, not has_diag, [KR, QR], [ps])
                else:
                    mm(out, kt.ap[:, kcols], FQ.ap[:, h, c0:c1], True, False, [kt, FQ], [ps])
                    mm(out, ones_bf.ap[0:1, 0:128], Fq_row.ap[0:1, h, c0:c1], False, not has_diag, [ones_bf, Fq_row], [ps])
                if has_diag:
                    mm(ps.ap[:, 0:128], ident.ap, negmask.ap[:, m, :], False, True, [ident, negmask], [ps])
                pt = PT.next()
                if mla:
                    act(pt.ap[:, 0:n], out, AF.Exp, [ps], [pt], scale=SC_MLA)
                else:
                    act(pt.ap[:, 0:n], out, AF.Exp, [ps, negF_tok], [pt], bias=negF_tok.ap[:, kb, h:h + 1], scale=SC_FOX)
                pts.append((bk, c0, c1, pt))
            return pts

        def emit_pv(kb, pts):
            for (bk, c0, c1, pt) in pts:
                n = c1 - c0
                o0 = c0 - bk * 512
                last = (kb == (15 if bk == 0 else 31))
                mm(OACC[bk].ap[:, o0:o0 + n], v.ap[:, kb, 0:128], pt.ap[:, 0:n], kb == 0, last, [v, pt], [OACC[bk]])
                mm(DEN[bk].ap[:, o0:o0 + n], ones_bf.ap, pt.ap[:, 0:n], kb == 0, last, [ones_bf, pt], [DEN[bk]])

        nxt = emit_scores(0)
        for kb in range(32):
            cur = nxt
            if kb + 1 < 32:
                nxt = emit_scores(kb + 1)
            emit_pv(kb, cur)
        for bk in range(2):
            rb = recb.next()
            S.op("dve", lambda e, rb=rb, bk=bk: e.reciprocal(rb.ap, DEN[bk].ap), r=[DEN[bk]], w=[rb])
            tt(OT.ap[:, hd, bk * 512:(bk + 1) * 512], OACC[bk].ap, rb.ap, ALU.mult, [OACC[bk], rb], [OT])
    dbg("OT", OT, [128, 16, NQ], BF16)
    if stop_after <= 3:
        S.finish(final_bufs)
        return nc

    S.barrier()
    H1 = Region(arena, 8, 72).alloc("H1", [128, 8, D], F32)
    R4 = Region(arena, 72, 158)
    WoutC = RR([R4.alloc("WoutC%d" % i, [128, 16, 512], BF16) for i in range(2)])
    xqt = RR([R4.alloc("xqt%d" % i, [128, D], F32) for i in range(2)])
    Gbc = R4.alloc("Gbc", [128, D], F32)
    Bbc = R4.alloc("Bbc", [128, D], F32)
    stats = R4.alloc("stats", [128, 4, 6], F32)
    mv = R4.alloc("mv", [128, 2], F32)
    rstd = R4.alloc("rstd", [128, 1], F32)
    w_out_v = w_out.rearrange("(c p) n -> p c n", p=128)

    def layer_norm_rows(Xap, Xbuf, Gb, Bb, stats, mv, rstd):
        for c4 in range(4):
            S.op("dve", lambda e, c4=c4, stats=stats: e.bn_stats(stats.ap[:, c4, :], Xap[:, c4 * 512:(c4 + 1) * 512]), r=[Xbuf], w=[stats])
        S.op("dve", lambda e, stats=stats, mv=mv: e.bn_aggr(mv.ap, stats.ap), r=[stats], w=[mv])
        act(rstd.ap, mv.ap[:, 1:2], AF.Sqrt, [mv, epsT], [rstd], bias=epsT.ap[:, 0:1])
        S.op("dve", lambda e, rstd=rstd: e.reciprocal(rstd.ap, rstd.ap), r=[rstd], w=[rstd])
        ts(Xap, Xap, mv.ap[:, 0:1], rstd.ap[:, 0:1], ALU.subtract, ALU.mult, [Xbuf, mv, rstd], [Xbuf])
        if Gb is not None:
            tt(Xap, Xap, Gb.ap, ALU.mult, [Xbuf, Gb], [Xbuf])
            tt(Xap, Xap, Bb.ap, ALU.add, [Xbuf, Bb], [Xbuf])

    S.dma("sp", Gbc, Gbc.ap, None, ln1g.partition_broadcast(128))
    S.dma("sp", Bbc, Bbc.ap, None, ln1b.partition_broadcast(128))
    for nf in range(4):
        wc = WoutC.next()
        for half in range(2):
            S.dma("pool", wc, wc.ap[:, half * 8:(half + 1) * 8, :], None, w_out_v[:, half * 8:(half + 1) * 8, nf * 512:(nf + 1) * 512])
        for i in range(8):
            if nf == 0:
                xt_ = xqt.next()
                S.dma("sp", xt_, xt_.ap, None, xq[i * 128:(i + 1) * 128, :])
                S.op("act", lambda e, xt_=xt_, i=i: e.mul(H1.ap[:, i, :], xt_.ap, ALPHA), r=[xt_], w=[H1])
            pf = PFR.next()
            for cc in range(16):
                mm(pf.ap, OT.ap[:, cc, i * 128:(i + 1) * 128], wc.ap[:, cc, :], cc == 0, cc == 15, [OT, wc], [pf])
            tt(H1.ap[:, i, nf * 512:(nf + 1) * 512], H1.ap[:, i, nf * 512:(nf + 1) * 512], pf.ap, ALU.add, [H1, pf], [H1])
    for i in range(8):
        layer_norm_rows(H1.ap[:, i, :], H1, Gbc, Bbc, stats, mv, rstd)
    dbg("H1", H1, [128, 8, D], F32)
    S.barrier()
    RB = Region(arena, 72, 136)
    H1B = RB.alloc("H1B", [128, 8, D], BF16)
    H1T = RB.alloc("H1T", [128, 16, NQ], BF16)
    for i in range(8):
        cp(H1B.ap[:, i, :], H1.ap[:, i, :], [H1], [H1B], eng=("act" if i % 2 else "dve"))
    for cc in range(16):
        for ig in range(2):
            pb = PBR.next()
            for k in range(4):
                i = ig * 4 + k
                tr(pb.ap[:, k * 128:(k + 1) * 128], H1B.ap[:, i, cc * 128:(cc + 1) * 128], ident.ap, [H1B, ident], [pb])
            evac(H1T.ap[:, cc, ig * 512:(ig + 1) * 512], pb.ap[:, 0:512], [pb], [H1T])
    if stop_after <= 4:
        S.finish(final_bufs)
        return nc

    R5 = Region(arena, 136, 190)
    WgC = RR([R5.alloc("WgC%d" % i, [128, 16, 512], BF16) for i in range(2)])
    Wp = R5.alloc("Wp", [128, 2, D], BF16)
    pT = R5.alloc("pT", [128, 2, NQ], BF16)
    pb16 = R5.alloc("pb16", [128, 8, 256], BF16)
    sig = RR([R5.alloc("sig%d" % i, [128, 512], F32) for i in range(2)])
    S.dma("pool", Wp, Wp.ap, None, ple_wproj.rearrange("(c p) n -> p c n", p=128))
    S.dma("pool", pb16, pb16.ap, None, pq.rearrange("(i p) d -> p i d", p=128))
    for c2 in range(2):
        for ig in range(2):
            pb = PBR.next()
            for k in range(4):
                i = ig * 4 + k
                tr(pb.ap[:, k * 128:(k + 1) * 128], pb16.ap[:, i, c2 * 128:(c2 + 1) * 128], ident.ap, [pb16, ident], [pb])
            evac(pT.ap[:, c2, ig * 512:(ig + 1) * 512], pb.ap[:, 0:512], [pb], [pT])
    wg_v = ple_wgate.rearrange("(c p) n -> p c n", p=128)
    for nf in range(4):
        wc = WgC.next()
        for half in range(2):
            S.dma("pool", wc, wc.ap[:, half * 8:(half + 1) * 8, :], None, wg_v[:, half * 8:(half + 1) * 8, nf * 512:(nf + 1) * 512])
        for i in range(8):
            pf = PFR.next()
            for cc in range(16):
                mm(pf.ap, H1T.ap[:, cc, i * 128:(i + 1) * 128], wc.ap[:, cc, :], cc == 0, cc == 15, [H1T, wc], [pf])
            sg = sig.next()
            act(sg.ap, pf.ap, AF.Sigmoid, [pf], [sg])
            pf2 = PFR.next()
            for c2 in range(2):
                mm(pf2.ap, pT.ap[:, c2, i * 128:(i + 1) * 128], Wp.ap[:, c2, nf * 512:(nf + 1) * 512], c2 == 0, c2 == 1, [pT, Wp], [pf2])
            tt(sg.ap, sg.ap, pf2.ap, ALU.mult, [sg, pf2], [sg])
            Hs = H1.ap[:, i, nf * 512:(nf + 1) * 512]
            stt(Hs, Hs, ALPHA, sg.ap, ALU.mult, ALU.add, [H1, sg], [H1])
    dbg("Y", H1, [128, 8, D], F32)
    if stop_after <= 5:
        S.finish(final_bufs)
        return nc

    S.barrier()
    KT12 = RC.alloc("KT12", [128, 8, 128], BF16)
    Rpre = Region(arena, 136, 190)
    Wpq = Rpre.alloc("Wpq", [128, 16, 1024], BF16)
    QP_all = Rpre.alloc("QP_all", [128, 8, NQ], BF16)
    k12 = Rpre.alloc("k12", [128, 128], BF16)
    pwq_v = peer_wq.rearrange("(c p) n -> p c n", p=128)
    for half in range(2):
        S.dma("pool", Wpq, Wpq.ap[:, half * 8:(half + 1) * 8, :], None, pwq_v[:, half * 8:(half + 1) * 8, :])
    for h in range(8):
        S.dma("pool", k12, k12.ap[:, 0:64], None, keys1[h])
        S.dma("pool", k12, k12.ap[:, 64:128], None, keys2[h])
        pb = PBR.next()
        tr(pb.ap[:, 0:128], k12.ap, ident.ap, [k12, ident], [pb])
        evac(KT12.ap[:, h, :], pb.ap[:, 0:128], [pb], [KT12])
    for h in range(8):
        for half in range(2):
            pf = PFR.next()
            for ck in range(16):
                mm(pf.ap, Wpq.ap[:, ck, h * 128:(h + 1) * 128], H1T.ap[:, ck, half * 512:(half + 1) * 512], ck == 0, ck == 15, [Wpq, H1T], [pf])
            evac(QP_all.ap[:, h, half * 512:(half + 1) * 512], pf.ap, [pf], [QP_all])

    S.barrier()
    NTI = int(os.environ.get("NTI", "8"))
    RE = Region(arena, 104, 112)
    EI_t = [RE.alloc("EI_t%d" % i, [128, 128], I32) for i in range(8)]
    GT_t = [RE.alloc("GT_t%d" % i, [128, 128], F32) for i in range(8)]
    R5b = Region(arena, 112, 120)
    SCb = R5b.alloc("SCb", [128, 256], F32)
    tmpS = R5b.alloc("tmpS", [128, 128], F32)
    V12 = R5b.alloc("V12", [128, 32], F32)
    I12 = R5b.alloc("I12", [128, 32], U32)
    I12f = R5b.alloc("I12f", [128, 32], F32)
    cand = R5b.alloc("cand", [128, 16, 16], F32)
    cidx = R5b.alloc("cidx", [128, 16, 16], F32)
    tmp256 = R5b.alloc("tmp256", [128, 256], F32)
    junk256 = R5b.alloc("junk256", [128, 256], F32)
    iota_f = R5b.alloc("iota_f", [128, 256], F32)
    SCv = R5b.alloc("SCv", [128, 16], F32)
    posu = R5b.alloc("posu", [128, 16], U32)
    posf2 = R5b.alloc("posf2", [128, 16], F32)
    EIf = R5b.alloc("EIf", [128, 16], F32)
    gexp = R5b.alloc("gexp", [128, 16], F32)
    negm = R5b.alloc("negm", [128, 1], F32)
    Zs = R5b.alloc("Zs", [128, 1], F32)
    iota_i_ap = tmp256.ap.bitcast(I32)
    S.op("pool", lambda e: e.iota(iota_i_ap, pattern=[[1, 256]], base=0, channel_multiplier=0), w=[tmp256])
    cp(iota_f.ap, iota_i_ap, [tmp256], [iota_f])
    cand_f = cand.ap.rearrange("p a b -> p (a b)")
    cidx_f = cidx.ap.rearrange("p a b -> p (a b)")

    def top16(vals_ap, vbuf, out_v, out_i, obufs, scratch, n):
        S.op("dve", lambda e: e.max(out_v[:, 0:8], vals_ap), r=[vbuf], w=[obufs[0]])
        S.op("dve", lambda e: e.max_index(out_i[:, 0:8], out_v[:, 0:8], vals_ap), r=[vbuf, obufs[0]], w=[obufs[1]])
        S.op("dve", lambda e: e.match_replace(scratch.ap[:, 0:n], out_v[:, 0:8], vals_ap, NEG), r=[vbuf, obufs[0]], w=[scratch])
        S.op("dve", lambda e: e.max(out_v[:, 8:16], scratch.ap[:, 0:n]), r=[scratch], w=[obufs[0]])
        S.op("dve", lambda e: e.max_index(out_i[:, 8:16], out_v[:, 8:16], scratch.ap[:, 0:n]), r=[scratch, obufs[0]], w=[obufs[1]])

    def routing_head(i, h):
        pf = PF[4]; pf2 = PF[5]
        qs = slice(i * 128, (i + 1) * 128)
        mm(pf.ap[:, 0:128], QP_all.ap[0:64, h, qs], KT12.ap[0:64, h, :], True, True, [QP_all, KT12], [pf])
        mm(pf2.ap[:, 0:128], QP_all.ap[64:128, h, qs], KT12.ap[64:128, h, :], True, True, [QP_all, KT12], [pf2])
        cp(SCb.ap[:, 0:128], pf.ap[:, 0:128], [pf], [SCb])
        cp(SCb.ap[:, 128:256], pf2.ap[:, 0:128], [pf2], [SCb])
        for half in range(2):
            top16(SCb.ap[:, half * 128:(half + 1) * 128], SCb, V12.ap[:, half * 16:(half + 1) * 16], I12.ap[:, half * 16:(half + 1) * 16],
                  (V12, I12), tmpS, 128)
        cp(I12f.ap, I12.ap, [I12], [I12f])
        v1b = V12.ap[:, 0:16].unsqueeze(2).to_broadcast([128, 16, 16])
        v2b = V12.ap[:, 16:32].unsqueeze(1).to_broadcast([128, 16, 16])
        i1b = I12f.ap[:, 0:16].unsqueeze(2).to_broadcast([128, 16, 16])
        i2b = I12f.ap[:, 16:32].unsqueeze(1).to_broadcast([128, 16, 16])
        tt(cand.ap, v1b, v2b, ALU.add, [V12], [cand])
        stt(cidx.ap, i1b, 128.0, i2b, ALU.mult, ALU.add, [I12f], [cidx])
        top16(cand_f, cand, SCv.ap, posu.ap, (SCv, posu), tmp256, 256)
        cp(posf2.ap, posu.ap, [posu], [posf2])
        for k in range(16):
            stt(junk256.ap, iota_f.ap, posf2.ap[:, k:k + 1], cidx_f, ALU.is_equal, ALU.mult, [iota_f, posf2, cidx], [junk256, EIf],
                accum=EIf.ap[:, k:k + 1])
        cp(EI_t[i].ap[:, h * 16:(h + 1) * 16], EIf.ap, [EIf], [EI_t[i]])
        ts(negm.ap, SCv.ap[:, 0:1], -1.0, None, ALU.mult, None, [SCv], [negm])
        act(gexp.ap, SCv.ap, AF.Exp, [SCv, negm], [gexp, Zs], bias=negm.ap[:, 0:1], accum=Zs.ap[:, 0:1])
        S.op("dve", lambda e: e.reciprocal(Zs.ap, Zs.ap), r=[Zs], w=[Zs])
        ts(GT_t[i].ap[:, h * 16:(h + 1) * 16], gexp.ap, Zs.ap[:, 0:1], None, ALU.mult, None, [gexp, Zs], [GT_t[i]])

    R5c = Region(arena, 120, 168)
    NUB = 5
    UVG = [R5c.alloc("UVG%d" % k, [128, 2 * D], BF16) for k in range(NUB)]
    PRD = R5c.alloc("PRD", [128, D], F32)
    UV_flat = UVb_d.rearrange("e t d -> e (t d)")
    R5s = Region(arena, 184, 190)
    Adot = RR([R5s.alloc("Adot%d" % k, [128, 1], F32) for k in range(8)])
    AW = RR([R5s.alloc("AW%d" % k, [128, 1], F32) for k in range(8)])
    DG = RR([R5s.alloc("DG%d" % k, [128, 128], BF16) for k in range(3)])
    stats2 = R5s.alloc("stats2", [128, 4, 6], F32)
    mv2 = R5s.alloc("mv2", [128, 2], F32)
    rstd2 = R5s.alloc("rstd2", [128, 1], F32)
    Gh = R5s.alloc("Gh", [128, 512], F32)
    Bh = R5s.alloc("Bh", [128, 512], F32)
    NSL = int(os.environ.get("NSL", "128"))
    LOOK = NUB - 1
    POOL_EVERY = int(os.environ.get("POOL_EVERY", "0"))
    for h in range(8):
        routing_head(0, h)

    def emit_gather(gidx):
        i, slot = divmod(gidx, NSL)
        uvg = UVG[gidx % NUB]
        S.op("pool", lambda e: e.indirect_dma_start(
            out=uvg.ap, out_offset=None, in_=UV_flat,
            in_offset=bass.IndirectOffsetOnAxis(ap=EI_t[i].ap[:, slot:slot + 1], axis=0)), r=[EI_t[i], D_UV], w=[uvg], dma=True)

    for gidx in range(min(LOOK, NTI * NSL)):
        emit_gather(gidx)
    for i in range(NTI):
        for slot in range(NSL):
            gidx = i * NSL + slot
            if gidx + LOOK < NTI * NSL:
                emit_gather(gidx + LOOK)
            uvg = UVG[gidx % NUB]; ad = Adot.next(); aw = AW.next(); dg = DG.next()
            if POOL_EVERY and (slot % POOL_EVERY == POOL_EVERY - 1):
                tt(PRD.ap, uvg.ap[:, 0:D], H1B.ap[:, i, :], ALU.mult, [uvg, H1B], [PRD], eng="pool")
                act(PRD.ap, PRD.ap, AF.Copy, [PRD], [PRD, ad], accum=ad.ap[:, 0:1])
            else:
                stt(uvg.ap[:, 0:D], uvg.ap[:, 0:D], 1.0, H1B.ap[:, i, :], ALU.mult, ALU.mult, [uvg, H1B], [uvg, ad], accum=ad.ap[:, 0:1])
            act(aw.ap, ad.ap, AF.Gelu, [ad], [aw])
            act(aw.ap, aw.ap, AF.Copy, [aw, GT_t[i]], [aw], scale=GT_t[i].ap[:, slot:slot + 1])
            act(dg.ap, ident.ap, AF.Copy, [ident, aw], [dg], scale=aw.ap[:, 0:1])
            for nf in range(4):
                mm(PF[nf].ap, dg.ap, uvg.ap[:, D + nf * 512:D + (nf + 1) * 512], slot == 0, slot == NSL - 1, [dg, uvg], [PF[nf]])
            if (slot % 14 == 13) and (slot // 14 < 8) and (i + 1 < NTI):
                routing_head(i + 1, slot // 14)
        for nf in range(4):
            Hs = H1.ap[:, i, nf * 512:(nf + 1) * 512]
            tt(Hs, Hs, PF[nf].ap, ALU.add, [H1, PF[nf]], [H1])
        layer_norm_rows(H1.ap[:, i, :], H1, None, None, stats2, mv2, rstd2)
        for qf in range(4):
            S.dma("sp", Gh, Gh.ap, None, ln2g[:, qf * 512:(qf + 1) * 512].partition_broadcast(128))
            S.dma("sp", Bh, Bh.ap, None, ln2b[:, qf * 512:(qf + 1) * 512].partition_broadcast(128))
            Xh = H1.ap[:, i, qf * 512:(qf + 1) * 512]
            tt(Xh, Xh, Gh.ap, ALU.mult, [H1, Gh], [H1])
            tt(Xh, Xh, Bh.ap, ALU.add, [H1, Bh], [H1])
        S.dma("sp", D_out, out_d[i * 128:(i + 1) * 128, :], H1, H1.ap[:, i, :])
    S.finish(final_bufs)
    return nc


def own_rows(j):
    return np.concatenate([np.arange((4 * i + j) * 128, (4 * i + j + 1) * 128) for i in range(8)])


def core_inputs(inp, c):
    b, j = c // 4, c % 4
    own = own_rows(j)
    f = lambda a: np.ascontiguousarray(a, dtype=np.float32)
    x = inp["x"][b]
    negmask = np.zeros((128, 4, 128), np.float32)
    sel = np.zeros((128, 4, 128), np.float32)
    kk = np.arange(128)[:, None]; qq = np.arange(128)[None, :]
    for m in range(4):
        if m == j:
            negmask[:, m, :] = np.where(kk <= qq, 0.0, NEG)
            sel[:, m, :] = np.eye(128, dtype=np.float32)
        elif m > j:
            negmask[:, m, :] = NEG
    inv_freq = (1.0 / (10000.0 ** (np.arange(0, 64, 2, dtype=np.float32) / 64))).astype(np.float32)
    invf = np.concatenate([inv_freq, inv_freq])[:, None]
    sgn = np.concatenate([-np.ones(32, np.float32), np.ones(32, np.float32)])[:, None]
    return {
        "xkv": f(x), "xq": f(x[own]), "pq": f(inp["p"][0, b][own]),
        "pos_kv": np.ascontiguousarray(inp["positions"][b][None, :], dtype=np.int32),
        "pos_q": np.ascontiguousarray(inp["positions"][b][own][None, :], dtype=np.int32),
        "w_in": f(inp["w_in"][0]), "w_uq": f(inp["w_uq"][0]), "w_ukv": f(inp["w_ukv"][0]),
        "w_out": f(inp["w_out"][0]), "peer_wq": f(inp["peer_wq"][0]),
        "keys1": f(inp["peer_keys1"][0]), "keys2": f(inp["peer_keys2"][0]),
        "peer_u": f(inp["peer_u"][0]), "peer_v": f(inp["peer_v"][0]),
        "ple_wgate": f(inp["ple_wgate"][0]), "ple_wproj": f(inp["ple_wproj"][0]),
        "gqT": f(inp["g_q_norm"][0].reshape(4, 128).T), "gkvT": f(inp["g_kv_norm"][0].reshape(4, 128).T),
        "bfg": f(inp["b_forget"][0][:, None]),
        "ln1g": f(inp["ln1_g"]), "ln1b": f(inp["ln1_b"]), "ln2g": f(inp["ln2_g"]), "ln2b": f(inp["ln2_b"]),
        "negmask": negmask, "sel": sel, "invf": f(invf), "sgn": f(sgn),
    }


_NC_CACHE = {}


def kernel(**inputs):
    inp = {k: np.asarray(v) for k, v in inputs.items()}
    if "nc" not in _NC_CACHE:
        _NC_CACHE["nc"] = build_nc()
    nc = _NC_CACHE["nc"]
    maps = [core_inputs(inp, c) for c in range(8)]
    res = run_bass_kernel_spmd(nc, maps, core_ids=list(range(8)))
    out = np.zeros((2, NT, D), np.float32)
    for c in range(8):
        b, j = c // 4, c % 4
        out[b, own_rows(j)] = np.asarray(res.results[c]["out"], dtype=np.float32)
    return out
```

```python
import numpy as np
from contextlib import ExitStack
import concourse.bass as bass
import concourse.mybir as mybir
from concourse.bass_utils import run_bass_kernel_spmd
from concourse.alu_op_type import AluOpType as ALU

F32 = mybir.dt.float32
BF16 = mybir.dt.bfloat16
I32 = mybir.dt.int32
U32 = mybir.dt.uint32
AF = mybir.ActivationFunctionType


class Buf:
    def __init__(self, name, ap):
        self.name = name
        self.ap = ap
        self.last_w = []
        self.reads = []
        self.dsem = None
        self.dcount = 0
        self.is_dram = False
        self.excl = False

    def __getitem__(self, k):
        return self.ap[k]


class Sched:
    ENG = ["pe", "dve", "act", "pool", "sp"]

    def __init__(self, nc):
        self.nc = nc
        self.stack = ExitStack()
        self.eobj = {"pe": nc.tensor, "dve": nc.vector, "act": nc.scalar, "pool": nc.gpsimd, "sp": nc.sync}
        self.q = {e: [] for e in self.ENG}
        self.cnt = {e: 0 for e in self.ENG}
        self.esem = {e: self.stack.enter_context(nc.semaphore("es_" + e)) for e in self.ENG}
        self.waited = {e: {} for e in self.ENG}
        self.nbuf = 0
        self.dma_tokens = []
        self.free_dsems = []

    def sbuf(self, name, shape, dtype, stack=None):
        t = (stack or self.stack).enter_context(self.nc.sbuf_tensor(name, list(shape), dtype))
        return Buf(name, t[:])

    def psum(self, name, shape, dtype, stack=None):
        t = (stack or self.stack).enter_context(self.nc.psum_tensor(name, list(shape), dtype))
        b = Buf(name, t[:])
        b.excl = True
        return b

    def dram(self, ap, name="dram"):
        b = Buf(name, ap)
        b.is_dram = True
        return b

    def _dsem(self, buf):
        if buf.dsem is None:
            self.nbuf += 1
            buf.dsem = self.stack.enter_context(self.nc.semaphore("ds%d" % self.nbuf))
        return buf.dsem

    def _deps(self, eng, r, w):
        deps = []
        for b in r:
            deps.extend(b.last_w)
        for b in w:
            deps.extend(b.last_w)
            deps.extend(b.reads)
        out = {}
        for (sem, val, e) in deps:
            if e == eng and eng in ("pe", "sp"):
                continue
            key = id(sem)
            if self.waited[eng].get(key, 0) >= val:
                continue
            if key not in out or out[key][1] < val:
                out[key] = (sem, val)
        for key, (sem, val) in out.items():
            self.waited[eng][key] = val
        return list(out.values())

    def _commit(self, tok, r, w, accumulate=False):
        for b in r:
            b.reads.append(tok)
        for b in w:
            if accumulate:
                b.last_w = [t for t in b.last_w if t[0] is not tok[0]] + [tok]
            else:
                b.last_w = [tok]
            b.reads = []

    def op(self, eng, fn, r=(), w=(), dma=False):
        r = list(r); w = list(w)
        for b in list(r):
            if b.excl:
                r.remove(b)
                if b not in w:
                    w.append(b)
        deps = self._deps(eng, r, w)
        if dma:
            wb = w[0]
            own = r[0] if (wb.is_dram and r) else wb
            sem = self._dsem(own)
            own.dcount += 16
            tok = (sem, own.dcount, "dma")
            self.q[eng].append((deps, fn, sem, 16))
            self.dma_tokens.append(tok)
            self._commit(tok, r, w, accumulate=wb.is_dram)
            return
        else:
            self.cnt[eng] += 1
            tok = (self.esem[eng], self.cnt[eng], eng)
            self.q[eng].append((deps, fn, self.esem[eng], 1))
        self._commit(tok, r, w)

    def dma(self, eng, wbuf, out_ap, rbuf, in_ap, **kw):
        r = [rbuf] if rbuf is not None else []
        self.op(eng, lambda e: e.dma_start(out=out_ap, in_=in_ap, **kw), r=r, w=[wbuf], dma=True)

    def barrier(self):
        toks = [(self.esem[e], self.cnt[e], e) for e in self.ENG if self.cnt[e] > 0]
        toks += self.dma_tokens
        self.dma_tokens = []
        for eng in self.ENG:
            out = {}
            for (sem, val, e) in toks:
                if e == eng:
                    continue
                key = id(sem)
                if self.waited[eng].get(key, 0) >= val:
                    continue
                if key not in out or out[key][1] < val:
                    out[key] = (sem, val)
            for key, (sem, val) in out.items():
                self.waited[eng][key] = val
            if out:
                self.q[eng].append((list(out.values()), None, None, 0))

    def finish(self, out_bufs):
        deps = []
        for b in out_bufs:
            for t in b.last_w:
                deps.append((t[0], t[1]))
        self.q["sp"].append((deps, None, None, 0))
        with self.nc.Block() as block:
            def mk(eng):
                def body(e):
                    for (deps, fn, sem, inc) in self.q[eng]:
                        for (s, v) in deps:
                            e.wait_ge(s, v)
                        if fn is not None:
                            ins = fn(e)
                            ins.then_inc(sem, inc)
                return body
            block.tensor(mk("pe"))
            block.vector(mk("dve"))
            block.scalar(mk("act"))
            block.gpsimd(mk("pool"))
            block.sync(mk("sp"))
        self.stack.close()


D = 2048
NT = 4096
NQ = 1024
TT = 256
ALPHA = 2.0 ** 0.25
EPS = 1e-6
PI = float(np.pi)
SC_MLA = 192.0 ** -0.5
SC_FOX = 128.0 ** -0.5
NEG = -1.0e30


class RR:
    def __init__(self, items):
        self.items = list(items); self.i = 0

    def next(self):
        x = self.items[self.i % len(self.items)]; self.i += 1
        return x


class Region:
    def __init__(self, arena, start_kb, end_kb):
        self.arena = arena; self.off = int(start_kb * 1024); self.end = int(end_kb * 1024)

    def alloc(self, name, shape, dtype):
        n = int(np.prod(shape[1:]))
        esz = 2 if dtype == BF16 else 4
        nb = (n * esz + 63) // 64 * 64
        assert self.off + nb <= self.end, (name, self.off, nb, self.end)
        o = self.off // 2
        ap = self.arena[0:shape[0], o:o + n * esz // 2]
        if dtype != BF16:
            ap = ap.bitcast(dtype)
        if len(shape) == 3:
            ap = ap.rearrange("p (a b) -> p a b", b=shape[2])
        elif len(shape) == 4:
            ap = ap.rearrange("p (a b c) -> p a b c", b=shape[2], c=shape[3])
        self.off += nb
        return Buf(name, ap)


import os
SECT = os.environ.get('SECT', 'mla,rope,fox,forget').split(',')
NTILES = int(os.environ.get('NTILES', '0'))


def build_nc(debug=False, stop_after=99):
    nc = bass.Bass("TRN2", target_bir_lowering=False)
    dbg_outs = {}

    def IN(name, shape, dtype=F32):
        return nc.dram_tensor(name, list(shape), dtype, kind="ExternalInput").ap()

    def SCR(name, shape, dtype):
        return nc.dram_tensor(name, list(shape), dtype, kind=("ExternalOutput" if debug else "Internal")).ap()

    xkv = IN("xkv", [NT, D]); xq = IN("xq", [NQ, D]); pq = IN("pq", [NQ, 256])
    pos_kv = IN("pos_kv", [1, NT], I32); pos_q = IN("pos_q", [1, NQ], I32)
    w_in = IN("w_in", [D, 4168]); w_uq = IN("w_uq", [512, 1536]); w_ukv = IN("w_ukv", [512, 2048])
    w_out = IN("w_out", [D, D]); peer_wq = IN("peer_wq", [D, 1024])
    keys1 = IN("keys1", [8, 128, 64]); keys2 = IN("keys2", [8, 128, 64])
    peer_u = IN("peer_u", [16384, D]); peer_v = IN("peer_v", [16384, D])
    ple_wgate = IN("ple_wgate", [D, D]); ple_wproj = IN("ple_wproj", [256, D])
    gqT = IN("gqT", [128, 4]); gkvT = IN("gkvT", [128, 4]); bfg = IN("bfg", [8, 1])
    ln1g = IN("ln1g", [1, D]); ln1b = IN("ln1b", [1, D]); ln2g = IN("ln2g", [1, D]); ln2b = IN("ln2b", [1, D])
    negmask_in = IN("negmask", [128, 4, 128]); sel_in = IN("sel", [128, 4, 128])
    invf_in = IN("invf", [64, 1]); sgn_in = IN("sgn", [64, 1])
    out_d = nc.dram_tensor("out", [NQ, D], F32, kind="ExternalOutput").ap()

    kTm_d = SCR("kTm_d", [8, 128, NT], BF16); krT_d = SCR("krT_d", [64, NT], BF16)
    vm_d = SCR("vm_d", [8, 128, 32, 128], BF16)
    kTf_d = SCR("kTf_d", [8, 128, NT], BF16); vf_d = SCR("vf_d", [8, 128, 32, 128], BF16)
    UVb_d = nc.dram_tensor("UVb_d", [16384, 2, D], BF16, kind="Internal").ap()

    S = Sched(nc)
    arena = S.stack.enter_context(nc.sbuf_tensor("arena", [128, 95 * 1024], BF16))
    D_kTm = S.dram(kTm_d); D_krT = S.dram(krT_d); D_vm = S.dram(vm_d); D_kTf = S.dram(kTf_d); D_vf = S.dram(vf_d)
    D_out = S.dram(out_d)
    D_UV = S.dram(UVb_d)
    final_bufs = [D_out]

    def dbg(name, buf, shape, dtype):
        if not debug:
            return
        t = nc.dram_tensor("dbg_" + name, list(shape), dtype, kind="ExternalOutput").ap()
        b = S.dram(t)
        S.dma("sp", b, t, buf, buf.ap)
        final_bufs.append(b)

    def mm(out, lhsT, rhs, start, stop, r, w):
        S.op("pe", lambda e: e.matmul(out, lhsT, rhs, start=start, stop=stop), r=r, w=w)

    def tr(out, in_, idn, r, w):
        S.op("pe", lambda e: e.transpose(out, in_, idn), r=r, w=w)

    def act(out, in_, func, r, w, bias=None, scale=None, accum=None):
        kw = {}
        if bias is not None: kw["bias"] = bias
        if scale is not None: kw["scale"] = scale
        if accum is not None: kw["accum_out"] = accum
        S.op("act", lambda e: e.activation(out, in_, func, **kw), r=r, w=w)

    def tt(out, a, b, op, r, w, eng="dve"):
        S.op(eng, lambda e: e.tensor_tensor(out, a, b, op), r=r, w=w)

    def ts(out, a, s1, s2, op0, op1, r, w, eng="dve", accum=None):
        if op1 is None:
            S.op(eng, lambda e: e.tensor_scalar(out, a, s1, None, op0), r=r, w=w)
        elif accum is not None:
            S.op(eng, lambda e: e.tensor_scalar(out, a, s1, s2, op0, op1, accum_out=accum), r=r, w=w)
        else:
            S.op(eng, lambda e: e.tensor_scalar(out, a, s1, s2, op0, op1), r=r, w=w)

    def stt(out, a, sc, b, op0, op1, r, w, accum=None):
        if accum is None:
            S.op("dve", lambda e: e.scalar_tensor_tensor(out=out, in0=a, scalar=sc, in1=b, op0=op0, op1=op1), r=r, w=w)
        else:
            S.op("dve", lambda e: e.scalar_tensor_tensor(out=out, in0=a, scalar=sc, in1=b, op0=op0, op1=op1, accum_out=accum), r=r, w=w)

    def cp(out, in_, r, w, eng="dve"):
        if eng == "act":
            S.op("act", lambda e: e.copy(out, in_), r=r, w=w)
        else:
            S.op(eng, lambda e: e.tensor_copy(out, in_), r=r, w=w)

    evac_i = [0]

    def evac(out, in_, r, w):
        evac_i[0] += 1
        cp(out, in_, r, w, eng=("act" if evac_i[0] % 2 else "dve"))

    PF = [S.psum("pf%d" % i, [128, 512], F32) for i in range(6)]
    PB = [S.psum("pb%d" % i, [128, 1024], BF16) for i in range(2)]
    PFR = RR(PF); PBR = RR(PB)

    RC = Region(arena, 0, 8)
    identf = RC.alloc("identf", [128, 128], F32)
    ident = RC.alloc("ident", [128, 128], BF16)
    ones_bf = RC.alloc("ones_bf", [128, 128], BF16)
    ones_f = RC.alloc("ones_f", [128, 256], F32)
    epsT = RC.alloc("epsT", [128, 1], F32)
    oneT = RC.alloc("oneT", [128, 1], F32)
    invf = RC.alloc("invf", [64, 1], F32)
    sgn = RC.alloc("sgn", [64, 1], F32)
    negb = RC.alloc("negb", [8, 1], F32)
    gkv = RC.alloc("gkv", [128, 4], F32)
    gq = RC.alloc("gq", [128, 4], F32)
    negF_tok = RC.alloc("negF_tok", [128, 32, 8], F32)
    negmask = RC.alloc("negmask", [128, 4, 128], BF16)
    EI = RC.alloc("EI", [128, 128], I32)
    GT = RC.alloc("GT", [128, 128], F32)

    S.op("pool", lambda e: e.memset(identf.ap, 0.0), w=[identf])
    S.op("pool", lambda e: e.affine_select(out=identf.ap, in_=identf.ap, pattern=[[-1, 128]], compare_op=ALU.not_equal,
                                           fill=1.0, base=0, channel_multiplier=1), r=[identf], w=[identf])
    cp(ident.ap, identf.ap, [identf], [ident])
    S.op("dve", lambda e: e.memset(ones_bf.ap, 1.0), w=[ones_bf])
    S.op("dve", lambda e: e.memset(ones_f.ap, 1.0), w=[ones_f])
    S.op("dve", lambda e: e.memset(epsT.ap, EPS), w=[epsT])
    S.op("dve", lambda e: e.memset(oneT.ap, 1.0), w=[oneT])
    S.dma("sp", invf, invf.ap, None, invf_in)
    S.dma("sp", sgn, sgn.ap, None, sgn_in)
    S.dma("sp", negb, negb.ap, None, bfg)
    ts(negb.ap, negb.ap, -1.0, None, ALU.mult, None, [negb], [negb])
    S.dma("sp", gkv, gkv.ap, None, gkvT)
    S.dma("sp", gq, gq.ap, None, gqT)
    S.dma("pool", negmask, negmask.ap, None, negmask_in)

    w_in_v = w_in.rearrange("(c p) n -> p c n", p=128)

    def rope_tables(R, pos_ap, c0, n, tag):
        posi = R.alloc("posi" + tag, [64, n], I32)
        posf = R.alloc("posf" + tag, [64, n], F32)
        ang = R.alloc("ang" + tag, [64, n], F32)
        tq = R.alloc("tq" + tag, [64, n], F32)
        ki = R.alloc("ki" + tag, [64, n], I32)
        cosb = R.alloc("cos" + tag, [64, n], F32)
        sinb = R.alloc("sin" + tag, [64, n], F32)

        def emit(c0):
            S.dma("sp", posi, posi.ap, None, pos_ap[0:1, c0:c0 + n].partition_broadcast(64))
            cp(posf.ap, posi.ap, [posi], [posf])
            for (phase, dst) in ((0.0, sinb), (PI / 2, cosb)):
                ts(ang.ap, posf.ap, invf.ap[:, 0:1], phase, ALU.mult, ALU.add, [posf, invf], [ang])
                ts(tq.ap, ang.ap, 1.0 / (2 * PI), None, ALU.mult, None, [ang], [tq])
                cp(ki.ap, tq.ap, [tq], [ki])
                cp(tq.ap, ki.ap, [ki], [tq])
                stt(ang.ap, tq.ap, -2 * PI, ang.ap, ALU.mult, ALU.add, [tq, ang], [ang])
                ts(tq.ap, ang.ap, PI, -2 * PI, ALU.is_gt, ALU.mult, [ang], [tq])
                tt(ang.ap, ang.ap, tq.ap, ALU.add, [ang, tq], [ang])
                ts(tq.ap, ang.ap, -PI, 2 * PI, ALU.is_lt, ALU.mult, [ang], [tq])
                tt(ang.ap, ang.ap, tq.ap, ALU.add, [ang, tq], [ang])
                ts(ang.ap, ang.ap, PI, -PI, ALU.min, ALU.max, [ang], [ang])
                act(dst.ap, ang.ap, AF.Sin, [ang], [dst])
            ts(sinb.ap, sinb.ap, sgn.ap[:, 0:1], None, ALU.mult, None, [sinb, sgn], [sinb])
        return cosb, sinb, emit

    def transpose_in(xb, xT, ns):
        for ck in range(16):
            pb = PBR.next()
            for s_ in range(ns):
                tr(pb.ap[:, s_ * 128:(s_ + 1) * 128], xb.ap[:, s_, ck * 128:(ck + 1) * 128], ident.ap, [xb, ident], [pb])
            evac(xT.ap[:, ck, :], pb.ap[:, 0:ns * 128], [pb], [xT])

    def proj_rms(R, Wt, col0, xT, n, tag):
        raw = R.alloc("raw" + tag, [128, 4, n], F32)
        sq = R.alloc("sq" + tag, [128, 4, n], BF16)
        nrm = R.alloc("nrm" + tag, [128, 4, n], BF16)
        Rs = R.alloc("Rs" + tag, [128, n], F32)

        def emit():
            for fc in range(4):
                pf = PFR.next()
                for ck in range(16):
                    mm(pf.ap[:, 0:n], Wt.ap[:, ck, col0 + fc * 128:col0 + (fc + 1) * 128], xT.ap[:, ck, :], ck == 0, ck == 15, [Wt, xT], [pf])
                act(sq.ap[:, fc, :], pf.ap[:, 0:n], AF.Square, [pf], [sq])
                cp(raw.ap[:, fc, :], pf.ap[:, 0:n], [pf], [raw])
            pf = PFR.next()
            for fc in range(4):
                mm(pf.ap[:, 0:n], ones_bf.ap, sq.ap[:, fc, :], fc == 0, fc == 3, [ones_bf, sq], [pf])
            act(Rs.ap, pf.ap[:, 0:n], AF.Sqrt, [pf, epsT], [Rs], bias=epsT.ap[:, 0:1], scale=1.0 / 512)
            S.op("dve", lambda e: e.reciprocal(Rs.ap, Rs.ap), r=[Rs], w=[Rs])
            for fc in range(4):
                tt(nrm.ap[:, fc, :], raw.ap[:, fc, :], Rs.ap, ALU.mult, [raw, Rs], [nrm])
        return nrm, emit

    R1 = Region(arena, 8, 190)
    Frow = R1.alloc("Frow", [8, NT], F32)
    Wm = R1.alloc("Wm", [128, 16, 576], BF16)
    Wf = R1.alloc("Wf", [128, 16, 2056], BF16)
    Wukv = R1.alloc("Wukv", [128, 4, 2048], BF16)
    Wkrsw = R1.alloc("Wkrsw", [128, 16, 64], BF16)
    mark = R1.off
    stg = R1.alloc("stg", [128, 4, 2048], F32)
    S.dma("pool", Wm, Wm.ap, None, w_in_v[:, :, 512:1088])
    S.dma("pool", Wf, Wf.ap[:, :, 0:1028], None, w_in_v[:, :, 2112:3140])
    S.dma("pool", Wf, Wf.ap[:, :, 1028:2056], None, w_in_v[:, :, 3140:4168])
    S.dma("sp", stg, stg.ap, None, w_ukv.rearrange("(c p) n -> p c n", p=128))
    for fc in range(4):
        ts(Wukv.ap[:, fc, :], stg.ap[:, fc, :], gkv.ap[:, fc:fc + 1], None, ALU.mult, None, [stg, gkv], [Wukv])
    cp(Wkrsw.ap[:, :, 0:32], Wm.ap[:, :, 544:576], [Wm], [Wkrsw])
    cp(Wkrsw.ap[:, :, 32:64], Wm.ap[:, :, 512:544], [Wm], [Wkrsw])
    S.barrier()
    R1.off = mark
    ns = TT // 128
    xb = R1.alloc("xb", [128, ns, D], BF16)
    xT = R1.alloc("xT", [128, 16, TT], BF16)
    ckvn, emit_ckv = proj_rms(R1, Wm, 0, xT, TT, "kv")
    kn_st = RR([R1.alloc("kn_st%d" % i, [128, 8, TT], BF16) for i in range(2)])
    fk_st = RR([R1.alloc("fk_st%d" % i, [128, 8, TT], BF16) for i in range(2)])
    v_st = R1.alloc("v_st", [128, ns, 1024], BF16)
    fv_st = R1.alloc("fv_st", [128, ns, 1024], BF16)
    cosb, sinb, emit_rope = rope_tables(R1, pos_kv, 0, TT, "kv")
    T1 = R1.alloc("T1", [64, TT], F32); T2 = R1.alloc("T2", [64, TT], F32)
    kr_st = R1.alloc("kr_st", [64, TT], BF16)
    exb = R1.alloc("exb", [8, TT], F32); lnb = R1.alloc("lnb", [8, TT], F32)

    def sec_load(T, t0):
        S.dma("pool", xb, xb.ap, None, xkv[t0:t0 + TT, :].rearrange("(s p) d -> p s d", p=128))
        transpose_in(xb, xT, ns)
    MLAK = int(os.environ.get('MLAK', '9'))

    def sec_mla(T, t0):
        emit_ckv()
        if MLAK < 2: return
        kst = kn_st.next()
        for h in range(8):
            pf = PFR.next()
            for fc in range(4):
                mm(pf.ap[:, 0:TT], Wukv.ap[:, fc, h * 256:h * 256 + 128], ckvn.ap[:, fc, :], fc == 0, fc == 3, [Wukv, ckvn], [pf])
            evac(kst.ap[:, h, :], pf.ap[:, 0:TT], [pf], [kst])
        S.dma("sp", D_kTm, kTm_d.rearrange("h d t -> d h t")[:, :, t0:t0 + TT], kst, kst.ap)
        if MLAK < 3: return
        for s_ in range(ns):
            for hh in range(2):
                pf = PFR.next()
                for fc in range(4):
                    rhs = Wukv.ap[:, fc, :].rearrange("p (h c) -> p h c", c=256)[:, hh * 4:(hh + 1) * 4, 128:256]
                    mm(pf.ap, ckvn.ap[:, fc, s_ * 128:(s_ + 1) * 128], rhs, fc == 0, fc == 3, [Wukv, ckvn], [pf])
                evac(v_st.ap[:, s_, hh * 512:(hh + 1) * 512], pf.ap, [pf], [v_st])
            blk = T * ns + s_
            S.dma("sp", D_vm, vm_d.rearrange("h p b d -> p b h d")[:, blk, :, :], v_st,
                  v_st.ap[:, s_, :].rearrange("p (h d) -> p h d", d=128))
    def sec_rope(T, t0):
        pk = PFR.next(); pks = PFR.next()
        for ck in range(16):
            mm(pk.ap[0:64, 0:TT], Wm.ap[:, ck, 512:576], xT.ap[:, ck, :], ck == 0, ck == 15, [Wm, xT], [pk])
        for ck in range(16):
            mm(pks.ap[0:64, 0:TT], Wkrsw.ap[:, ck, :], xT.ap[:, ck, :], ck == 0, ck == 15, [Wkrsw, xT], [pks])
        emit_rope(t0)
        tt(T1.ap, pk.ap[0:64, 0:TT], cosb.ap, ALU.mult, [pk, cosb], [T1])
        tt(T2.ap, pks.ap[0:64, 0:TT], sinb.ap, ALU.mult, [pks, sinb], [T2])
        tt(kr_st.ap, T1.ap, T2.ap, ALU.add, [T1, T2], [kr_st])
        S.dma("sp", D_krT, krT_d[:, t0:t0 + TT], kr_st, kr_st.ap)
    def sec_fox(T, t0):
        fst = fk_st.next()
        for h in range(8):
            pf = PFR.next()
            for ck in range(16):
                mm(pf.ap[:, 0:TT], Wf.ap[:, ck, h * 128:(h + 1) * 128], xT.ap[:, ck, :], ck == 0, ck == 15, [Wf, xT], [pf])
            evac(fst.ap[:, h, :], pf.ap[:, 0:TT], [pf], [fst])
        S.dma("sp", D_kTf, kTf_d.rearrange("h d t -> d h t")[:, :, t0:t0 + TT], fst, fst.ap)
        for s_ in range(ns):
            for hh in range(2):
                pf = PFR.next()
                for ck in range(16):
                    mm(pf.ap, xT.ap[:, ck, s_ * 128:(s_ + 1) * 128], Wf.ap[:, ck, 1024 + hh * 512:1024 + (hh + 1) * 512], ck == 0, ck == 15, [Wf, xT], [pf])
                evac(fv_st.ap[:, s_, hh * 512:(hh + 1) * 512], pf.ap, [pf], [fv_st])
            blk = T * ns + s_
            S.dma("sp", D_vf, vf_d.rearrange("h p b d -> p b h d")[:, blk, :, :], fv_st,
                  fv_st.ap[:, s_, :].rearrange("p (h d) -> p h d", d=128))
    def sec_forget(T, t0):
        pff = PFR.next()
        for ck in range(16):
            mm(pff.ap[0:8, 0:TT], Wf.ap[:, ck, 2048:2056], xT.ap[:, ck, :], ck == 0, ck == 15, [Wf, xT], [pff])
        act(exb.ap, pff.ap[0:8, 0:TT], AF.Exp, [pff, negb], [exb], bias=negb.ap[:, 0:1], scale=-1.0)
        act(lnb.ap, exb.ap, AF.Ln, [exb, oneT], [lnb], bias=oneT.ap[0:8, 0:1])
        init = 0.0 if T == 0 else Frow.ap[:, t0 - 1:t0]
        S.op("dve", lambda e, init=init, t0=t0: e.tensor_tensor_scan(Frow.ap[:, t0:t0 + TT], ones_f.ap[0:8, 0:TT], lnb.ap, init, ALU.mult, ALU.subtract),
             r=[Frow, ones_f, lnb], w=[Frow])
        for s_ in range(ns):
            blk = T * ns + s_
            pf = PFR.next()
            tr(pf.ap[:, 0:8], Frow.ap[0:8, blk * 128:(blk + 1) * 128], identf.ap[0:8, 0:8], [Frow, identf], [pf])
            ts(negF_tok.ap[:, blk, :], pf.ap[:, 0:8], -1.0, None, ALU.mult, None, [pf], [negF_tok])

    for T in range(NTILES or (NT // TT)):
        t0 = T * TT
        sec_load(T, t0)
        r0 = T * 1024
        S.dma("pool", D_UV, UVb_d[r0:r0 + 1024, 0, :], None, peer_u[r0:r0 + 1024, :])
        if 'mla' in SECT: sec_mla(T, t0)
        if 'rope' in SECT: sec_rope(T, t0)
        if 'fox' in SECT: sec_fox(T, t0)
        if 'forget' in SECT: sec_forget(T, t0)
    dbg("negF", negF_tok, [128, 32, 8], F32)
    dbg("Frow", Frow, [8, NT], F32)
    if stop_after <= 1:
        final_bufs.extend([D_kTm, D_krT, D_vm, D_kTf, D_vf])
        S.finish(final_bufs)
        return nc

    S.barrier()
    RP = Region(arena, 24, 88)
    QN = RP.alloc("QN", [128, 8, NQ], BF16)
    QR = RP.alloc("QR", [64, 8, NQ], BF16)
    FQ = RP.alloc("FQ", [128, 8, NQ], BF16)
    Fq_row = RP.alloc("Fq_row", [1, 8, NQ], BF16)
    R2 = Region(arena, 88, 190)
    Wqc = R2.alloc("Wqc", [128, 16, 512], BF16)
    Wqf = R2.alloc("Wqf", [128, 16, 1024], BF16)
    Wuq = R2.alloc("Wuq", [128, 4, 1536], BF16)
    Wuqsw = R2.alloc("Wuqsw", [128, 4, 8, 64], BF16)
    mark = R2.off
    stg2 = R2.alloc("stg2", [128, 4, 1536], F32)
    S.dma("pool", Wqc, Wqc.ap, None, w_in_v[:, :, 0:512])
    S.dma("pool", Wqf, Wqf.ap, None, w_in_v[:, :, 1088:2112])
    S.dma("sp", stg2, stg2.ap, None, w_uq.rearrange("(c p) n -> p c n", p=128))
    for fc in range(4):
        ts(Wuq.ap[:, fc, :], stg2.ap[:, fc, :], gq.ap[:, fc:fc + 1], None, ALU.mult, None, [stg2, gq], [Wuq])
    for fc in range(4):
        src = Wuq.ap[:, fc, :].rearrange("p (h c) -> p h c", c=192)
        cp(Wuqsw.ap[:, fc, :, 0:32], src[:, :, 160:192], [Wuq], [Wuqsw])
        cp(Wuqsw.ap[:, fc, :, 32:64], src[:, :, 128:160], [Wuq], [Wuqsw])
    S.barrier()
    R2.off = mark
    xb2 = R2.alloc("xb2", [128, ns, D], BF16)
    xT2 = R2.alloc("xT2", [128, 16, TT], BF16)
    cqn, emit_cq = proj_rms(R2, Wqc, 0, xT2, TT, "q")
    cosq, sinq, emit_ropeq = rope_tables(R2, pos_q, 0, TT, "q")
    T1q = R2.alloc("T1q", [64, TT], F32); T2q = R2.alloc("T2q", [64, TT], F32)
    selb = R2.alloc("selb", [128, 4, 128], F32)
    S.dma("sp", selb, selb.ap, None, sel_in)
    for T in range(NQ // TT):
        t0 = T * TT
        S.dma("pool", xb2, xb2.ap, None, xq[t0:t0 + TT, :].rearrange("(s p) d -> p s d", p=128))
        transpose_in(xb2, xT2, ns)
        emit_cq()
        emit_ropeq(t0)
        for h in range(8):
            pf = PFR.next()
            for fc in range(4):
                mm(pf.ap[:, 0:TT], Wuq.ap[:, fc, h * 192:h * 192 + 128], cqn.ap[:, fc, :], fc == 0, fc == 3, [Wuq, cqn], [pf])
            evac(QN.ap[:, h, t0:t0 + TT], pf.ap[:, 0:TT], [pf], [QN])
            pk = PFR.next(); pks = PFR.next()
            for fc in range(4):
                mm(pk.ap[0:64, 0:TT], Wuq.ap[:, fc, h * 192 + 128:h * 192 + 192], cqn.ap[:, fc, :], fc == 0, fc == 3, [Wuq, cqn], [pk])
            for fc in range(4):
                mm(pks.ap[0:64, 0:TT], Wuqsw.ap[:, fc, h, :], cqn.ap[:, fc, :], fc == 0, fc == 3, [Wuqsw, cqn], [pks])
            tt(T1q.ap, pk.ap[0:64, 0:TT], cosq.ap, ALU.mult, [pk, cosq], [T1q])
            tt(T2q.ap, pks.ap[0:64, 0:TT], sinq.ap, ALU.mult, [pks, sinq], [T2q])
            tt(QR.ap[:, h, t0:t0 + TT], T1q.ap, T2q.ap, ALU.add, [T1q, T2q], [QR])
            pf = PFR.next()
            for ck in range(16):
                mm(pf.ap[:, 0:TT], Wqf.ap[:, ck, h * 128:(h + 1) * 128], xT2.ap[:, ck, :], ck == 0, ck == 15, [Wqf, xT2], [pf])
            evac(FQ.ap[:, h, t0:t0 + TT], pf.ap[:, 0:TT], [pf], [FQ])
    for i in range(8):
        for h in range(8):
            pf = PFR.next()
            for m in range(4):
                mm(pf.ap[0:1, 0:128], negF_tok.ap[:, 4 * i + m, h:h + 1], selb.ap[:, m, :], m == 0, m == 3, [negF_tok, selb], [pf])
            ts(Fq_row.ap[0:1, h, i * 128:(i + 1) * 128], pf.ap[0:1, 0:128], -1.0 / SC_FOX, None, ALU.mult, None, [pf], [Fq_row])
    dbg("QN", QN, [128, 8, NQ], BF16)
    dbg("QR", QR, [64, 8, NQ], BF16)
    dbg("FQ", FQ, [128, 8, NQ], BF16)
    dbg("Fqrow", Fq_row, [1, 8, NQ], BF16)
    if stop_after <= 2:
        S.finish(final_bufs)
        return nc

    S.barrier()
    OT = Region(arena, 158, 190).alloc("OT", [128, 16, NQ], BF16)
    R3 = Region(arena, 88, 158)
    KT = RR([R3.alloc("KT%d" % i, [128, NT], BF16) for i in range(2)])
    KR = R3.alloc("KR", [64, NT], BF16)
    VV = RR([R3.alloc("V%d" % i, [128, 32, 129], BF16) for i in range(2)])
    PT = RR([R3.alloc("PT%d" % i, [128, 512], BF16) for i in range(4)])
    recb = RR([R3.alloc("recb%d" % i, [128, 512], F32) for i in range(2)])
    for v in VV.items:
        S.op("dve", lambda e, v=v: e.memset(v.ap[:, :, 128:129], 1.0), w=[v])
    S.dma("sp", KR, KR.ap, D_krT, krT_d)
    OACC = [PF[0], PF[1]]; DEN = [PF[2], PF[3]]
    SB1 = []
    for k in range(2):
        b_ = Buf("sb1_%d" % k, PB[k].ap.bitcast(F32))
        b_.excl = True
        SB1.append(b_)
    SSET = [[PF[4], PF[5]], SB1]
    NHD = int(os.environ.get("NHD", "16"))
    for hd in range(NHD):
        mla = hd < 8; h = hd % 8
        kt = KT.next(); v = VV.next()
        S.dma("sp", kt, kt.ap, (D_kTm if mla else D_kTf), (kTm_d if mla else kTf_d)[h])
        S.dma("sp", v, v.ap[:, :, 0:128], (D_vm if mla else D_vf), (vm_d if mla else vf_d)[h])
        S.dma("pool", D_UV, UVb_d[hd * 1024:(hd + 1) * 1024, 1, :], None, peer_v[hd * 1024:(hd + 1) * 1024, :])
        def emit_scores(kb):
            g = kb // 4; m = kb % 4
            parts = []
            if g < 4:
                parts.append((0, g * 128, 512))
                parts.append((1, 512, 1024))
            else:
                parts.append((1, g * 128, 1024))
            kcols = slice(kb * 128, (kb + 1) * 128)
            pts = []
            for (bk, c0, c1) in parts:
                n = c1 - c0
                ps = SSET[kb % 2][bk]
                out = ps.ap[:, 0:n]
                has_diag = (c0 == g * 128)
                if mla:
                    mm(out, kt.ap[:, kcols], QN.ap[:, h, c0:c1], True, False, [kt, QN], [ps])
                    mm(out, KR.ap[0:64, kcols], QR.ap[0:64, h, c0:c1], False, not has_diag, [KR, QR], [ps])
                else:
                    mm(out, kt.ap[:, kcols], FQ.ap[:, h, c0:c1], True, False, [kt, FQ], [ps])
                    mm(out, ones_bf.ap[0:1, 0:128], Fq_row.ap[0:1, h, c0:c1], False, not has_diag, [ones_bf, Fq_row], [ps])
                if has_diag:
                    mm(ps.ap[:, 0:128], ident.ap, negmask.ap[:, m, :], False, True, [ident, negmask], [ps])
                pt = PT.next()
                if mla:
                    act(pt.ap[:, 0:n], out, AF.Exp, [ps], [pt], scale=SC_MLA)
                else:
                    act(pt.ap[:, 0:n], out, AF.Exp, [ps, negF_tok], [pt], bias=negF_tok.ap[:, kb, h:h + 1], scale=SC_FOX)
                pts.append((bk, c0, c1, pt))
            return pts

        def emit_pv(kb, pts):
            for (bk, c0, c1, pt) in pts:
                n = c1 - c0
                o0 = c0 - bk * 512
                last = (kb == (15 if bk == 0 else 31))
                mm(OACC[bk].ap[:, o0:o0 + n], v.ap[:, kb, 0:128], pt.ap[:, 0:n], kb == 0, last, [v, pt], [OACC[bk]])
                mm(DEN[bk].ap[:, o0:o0 + n], ones_bf.ap, pt.ap[:, 0:n], kb == 0, last, [ones_bf, pt], [DEN[bk]])

        nxt = emit_scores(0)
        for kb in range(32):
            cur = nxt
            if kb + 1 < 32:
                nxt = emit_scores(kb + 1)
            emit_pv(kb, cur)
        for bk in range(2):
            rb = recb.next()
            S.op("dve", lambda e, rb=rb, bk=bk: e.reciprocal(rb.ap, DEN[bk].ap), r=[DEN[bk]], w=[rb])
            tt(OT.ap[:, hd, bk * 512:(bk + 1) * 512], OACC[bk].ap, rb.ap, ALU.mult, [OACC[bk], rb], [OT])
    dbg("OT", OT, [128, 16, NQ], BF16)
    if stop_after <= 3:
        S.finish(final_bufs)
        return nc

    S.barrier()
    H1 = Region(arena, 8, 72).alloc("H1", [128, 8, D], F32)
    R4 = Region(arena, 72, 158)
    WoutC = RR([R4.alloc("WoutC%d" % i, [128, 16, 512], BF16) for i in range(2)])
    xqt = RR([R4.alloc("xqt%d" % i, [128, D], F32) for i in range(2)])
    Gbc = R4.alloc("Gbc", [128, D], F32)
    Bbc = R4.alloc("Bbc", [128, D], F32)
    stats = R4.alloc("stats", [128, 4, 6], F32)
    mv = R4.alloc("mv", [128, 2], F32)
    rstd = R4.alloc("rstd", [128, 1], F32)
    w_out_v = w_out.rearrange("(c p) n -> p c n", p=128)

    def layer_norm_rows(Xap, Xbuf, Gb, Bb, stats, mv, rstd):
        for c4 in range(4):
            S.op("dve", lambda e, c4=c4, stats=stats: e.bn_stats(stats.ap[:, c4, :], Xap[:, c4 * 512:(c4 + 1) * 512]), r=[Xbuf], w=[stats])
        S.op("dve", lambda e, stats=stats, mv=mv: e.bn_aggr(mv.ap, stats.ap), r=[stats], w=[mv])
        act(rstd.ap, mv.ap[:, 1:2], AF.Sqrt, [mv, epsT], [rstd], bias=epsT.ap[:, 0:1])
        S.op("dve", lambda e, rstd=rstd: e.reciprocal(rstd.ap, rstd.ap), r=[rstd], w=[rstd])
        ts(Xap, Xap, mv.ap[:, 0:1], rstd.ap[:, 0:1], ALU.subtract, ALU.mult, [Xbuf, mv, rstd], [Xbuf])
        if Gb is not None:
            tt(Xap, Xap, Gb.ap, ALU.mult, [Xbuf, Gb], [Xbuf])
            tt(Xap, Xap, Bb.ap, ALU.add, [Xbuf, Bb], [Xbuf])

    S.dma("sp", Gbc, Gbc.ap, None, ln1g.partition_broadcast(128))
    S.dma("sp", Bbc, Bbc.ap, None, ln1b.partition_broadcast(128))
    for nf in range(4):
        wc = WoutC.next()
        for half in range(2):
            S.dma("pool", wc, wc.ap[:, half * 8:(half + 1) * 8, :], None, w_out_v[:, half * 8:(half + 1) * 8, nf * 512:(nf + 1) * 512])
        for i in range(8):
            if nf == 0:
                xt_ = xqt.next()
                S.dma("sp", xt_, xt_.ap, None, xq[i * 128:(i + 1) * 128, :])
                S.op("act", lambda e, xt_=xt_, i=i: e.mul(H1.ap[:, i, :], xt_.ap, ALPHA), r=[xt_], w=[H1])
            pf = PFR.next()
            for cc in range(16):
                mm(pf.ap, OT.ap[:, cc, i * 128:(i + 1) * 128], wc.ap[:, cc, :], cc == 0, cc == 15, [OT, wc], [pf])
            tt(H1.ap[:, i, nf * 512:(nf + 1) * 512], H1.ap[:, i, nf * 512:(nf + 1) * 512], pf.ap, ALU.add, [H1, pf], [H1])
    for i in range(8):
        layer_norm_rows(H1.ap[:, i, :], H1, Gbc, Bbc, stats, mv, rstd)
    dbg("H1", H1, [128, 8, D], F32)
    S.barrier()
    RB = Region(arena, 72, 136)
    H1B = RB.alloc("H1B", [128, 8, D], BF16)
    H1T = RB.alloc("H1T", [128, 16, NQ], BF16)
    for i in range(8):
        cp(H1B.ap[:, i, :], H1.ap[:, i, :], [H1], [H1B], eng=("act" if i % 2 else "dve"))
    for cc in range(16):
        for ig in range(2):
            pb = PBR.next()
            for k in range(4):
                i = ig * 4 + k
                tr(pb.ap[:, k * 128:(k + 1) * 128], H1B.ap[:, i, cc * 128:(cc + 1) * 128], ident.ap, [H1B, ident], [pb])
            evac(H1T.ap[:, cc, ig * 512:(ig + 1) * 512], pb.ap[:, 0:512], [pb], [H1T])
    if stop_after <= 4:
        S.finish(final_bufs)
        return nc

    R5 = Region(arena, 136, 190)
    WgC = RR([R5.alloc("WgC%d" % i, [128, 16, 512], BF16) for i in range(2)])
    Wp = R5.alloc("Wp", [128, 2, D], BF16)
    pT = R5.alloc("pT", [128, 2, NQ], BF16)
    pb16 = R5.alloc("pb16", [128, 8, 256], BF16)
    sig = RR([R5.alloc("sig%d" % i, [128, 512], F32) for i in range(2)])
    S.dma("pool", Wp, Wp.ap, None, ple_wproj.rearrange("(c p) n -> p c n", p=128))
    S.dma("pool", pb16, pb16.ap, None, pq.rearrange("(i p) d -> p i d", p=128))
    for c2 in range(2):
        for ig in range(2):
            pb = PBR.next()
            for k in range(4):
                i = ig * 4 + k
                tr(pb.ap[:, k * 128:(k + 1) * 128], pb16.ap[:, i, c2 * 128:(c2 + 1) * 128], ident.ap, [pb16, ident], [pb])
            evac(pT.ap[:, c2, ig * 512:(ig + 1) * 512], pb.ap[:, 0:512], [pb], [pT])
    wg_v = ple_wgate.rearrange("(c p) n -> p c n", p=128)
    for nf in range(4):
        wc = WgC.next()
        for half in range(2):
            S.dma("pool", wc, wc.ap[:, half * 8:(half + 1) * 8, :], None, wg_v[:, half * 8:(half + 1) * 8, nf * 512:(nf + 1) * 512])
        for i in range(8):
            pf = PFR.next()
            for cc in range(16):
                mm(pf.ap, H1T.ap[:, cc, i * 128:(i + 1) * 128], wc.ap[:, cc, :], cc == 0, cc == 15, [H1T, wc], [pf])
            sg = sig.next()
            act(sg.ap, pf.ap, AF.Sigmoid, [pf], [sg])
            pf2 = PFR.next()
            for c2 in range(2):
                mm(pf2.ap, pT.ap[:, c2, i * 128:(i + 1) * 128], Wp.ap[:, c2, nf * 512:(nf + 1) * 512], c2 == 0, c2 == 1, [pT, Wp], [pf2])
            tt(sg.ap, sg.ap, pf2.ap, ALU.mult, [sg, pf2], [sg])
            Hs = H1.ap[:, i, nf * 512:(nf + 1) * 512]
            stt(Hs, Hs, ALPHA, sg.ap, ALU.mult, ALU.add, [H1, sg], [H1])
    dbg("Y", H1, [128, 8, D], F32)
    if stop_after <= 5:
        S.finish(final_bufs)
        return nc

    S.barrier()
    KT12 = RC.alloc("KT12", [128, 8, 128], BF16)
    Rpre = Region(arena, 136, 190)
    Wpq = Rpre.alloc("Wpq", [128, 16, 1024], BF16)
    QP_all = Rpre.alloc("QP_all", [128, 8, NQ], BF16)
    k12 = Rpre.alloc("k12", [128, 128], BF16)
    pwq_v = peer_wq.rearrange("(c p) n -> p c n", p=128)
    for half in range(2):
        S.dma("pool", Wpq, Wpq.ap[:, half * 8:(half + 1) * 8, :], None, pwq_v[:, half * 8:(half + 1) * 8, :])
    for h in range(8):
        S.dma("pool", k12, k12.ap[:, 0:64], None, keys1[h])
        S.dma("pool", k12, k12.ap[:, 64:128], None, keys2[h])
        pb = PBR.next()
        tr(pb.ap[:, 0:128], k12.ap, ident.ap, [k12, ident], [pb])
        evac(KT12.ap[:, h, :], pb.ap[:, 0:128], [pb], [KT12])
    for h in range(8):
        for half in range(2):
            pf = PFR.next()
            for ck in range(16):
                mm(pf.ap, Wpq.ap[:, ck, h * 128:(h + 1) * 128], H1T.ap[:, ck, half * 512:(half + 1) * 512], ck == 0, ck == 15, [Wpq, H1T], [pf])
            evac(QP_all.ap[:, h, half * 512:(half + 1) * 512], pf.ap, [pf], [QP_all])

    S.barrier()
    NTI = int(os.environ.get("NTI", "8"))
    RE = Region(arena, 104, 112)
    EI_t = [RE.alloc("EI_t%d" % i, [128, 128], I32) for i in range(8)]
    GT_t = [RE.alloc("GT_t%d" % i, [128, 128], F32) for i in range(8)]
    EI_h = [[Buf("EIh%d_%d" % (i, h), EI_t[i].ap[:, h * 16:(h + 1) * 16]) for h in range(8)] for i in range(8)]
    GT_h = [[Buf("GTh%d_%d" % (i, h), GT_t[i].ap[:, h * 16:(h + 1) * 16]) for h in range(8)] for i in range(8)]
    R5b = Region(arena, 112, 120)
    SCb = R5b.alloc("SCb", [128, 256], F32)
    tmpS = R5b.alloc("tmpS", [128, 128], F32)
    V12 = R5b.alloc("V12", [128, 32], F32)
    I12 = R5b.alloc("I12", [128, 32], U32)
    I12f = R5b.alloc("I12f", [128, 32], F32)
    cand = R5b.alloc("cand", [128, 16, 16], F32)
    cidx = R5b.alloc("cidx", [128, 16, 16], F32)
    tmp256 = R5b.alloc("tmp256", [128, 256], F32)
    junk256 = R5b.alloc("junk256", [128, 256], F32)
    iota_f = R5b.alloc("iota_f", [128, 256], F32)
    SCv = R5b.alloc("SCv", [128, 16], F32)
    posu = R5b.alloc("posu", [128, 16], U32)
    posf2 = R5b.alloc("posf2", [128, 16], F32)
    EIf = R5b.alloc("EIf", [128, 16], F32)
    gexp = R5b.alloc("gexp", [128, 16], F32)
    negm = R5b.alloc("negm", [128, 1], F32)
    Zs = R5b.alloc("Zs", [128, 1], F32)
    iota_i_ap = tmp256.ap.bitcast(I32)
    S.op("pool", lambda e: e.iota(iota_i_ap, pattern=[[1, 256]], base=0, channel_multiplier=0), w=[tmp256])
    cp(iota_f.ap, iota_i_ap, [tmp256], [iota_f])
    cand_f = cand.ap.rearrange("p a b -> p (a b)")
    cidx_f = cidx.ap.rearrange("p a b -> p (a b)")

    def top16(vals_ap, vbuf, out_v, out_i, obufs, scratch, n):
        S.op("dve", lambda e: e.max(out_v[:, 0:8], vals_ap), r=[vbuf], w=[obufs[0]])
        S.op("dve", lambda e: e.max_index(out_i[:, 0:8], out_v[:, 0:8], vals_ap), r=[vbuf, obufs[0]], w=[obufs[1]])
        S.op("dve", lambda e: e.match_replace(scratch.ap[:, 0:n], out_v[:, 0:8], vals_ap, NEG), r=[vbuf, obufs[0]], w=[scratch])
        S.op("dve", lambda e: e.max(out_v[:, 8:16], scratch.ap[:, 0:n]), r=[scratch], w=[obufs[0]])
        S.op("dve", lambda e: e.max_index(out_i[:, 8:16], out_v[:, 8:16], scratch.ap[:, 0:n]), r=[scratch, obufs[0]], w=[obufs[1]])

    def routing_head(i, h):
        pf = PF[4]; pf2 = PF[5]
        qs = slice(i * 128, (i + 1) * 128)
        mm(pf.ap[:, 0:128], QP_all.ap[0:64, h, qs], KT12.ap[0:64, h, :], True, True, [QP_all, KT12], [pf])
        mm(pf2.ap[:, 0:128], QP_all.ap[64:128, h, qs], KT12.ap[64:128, h, :], True, True, [QP_all, KT12], [pf2])
        cp(SCb.ap[:, 0:128], pf.ap[:, 0:128], [pf], [SCb])
        cp(SCb.ap[:, 128:256], pf2.ap[:, 0:128], [pf2], [SCb])
        for half in range(2):
            top16(SCb.ap[:, half * 128:(half + 1) * 128], SCb, V12.ap[:, half * 16:(half + 1) * 16], I12.ap[:, half * 16:(half + 1) * 16],
                  (V12, I12), tmpS, 128)
        cp(I12f.ap, I12.ap, [I12], [I12f])
        v1b = V12.ap[:, 0:16].unsqueeze(2).to_broadcast([128, 16, 16])
        v2b = V12.ap[:, 16:32].unsqueeze(1).to_broadcast([128, 16, 16])
        i1b = I12f.ap[:, 0:16].unsqueeze(2).to_broadcast([128, 16, 16])
        i2b = I12f.ap[:, 16:32].unsqueeze(1).to_broadcast([128, 16, 16])
        tt(cand.ap, v1b, v2b, ALU.add, [V12], [cand])
        stt(cidx.ap, i1b, 128.0, i2b, ALU.mult, ALU.add, [I12f], [cidx])
        top16(cand_f, cand, SCv.ap, posu.ap, (SCv, posu), tmp256, 256)
        cp(posf2.ap, posu.ap, [posu], [posf2])
        for k in range(16):
            stt(junk256.ap, iota_f.ap, posf2.ap[:, k:k + 1], cidx_f, ALU.is_equal, ALU.mult, [iota_f, posf2, cidx], [junk256, EIf],
                accum=EIf.ap[:, k:k + 1])
        cp(EI_t[i].ap[:, h * 16:(h + 1) * 16], EIf.ap, [EIf], [EI_h[i][h]])
        ts(negm.ap, SCv.ap[:, 0:1], -1.0, None, ALU.mult, None, [SCv], [negm])
        act(gexp.ap, SCv.ap, AF.Exp, [SCv, negm], [gexp, Zs], bias=negm.ap[:, 0:1], accum=Zs.ap[:, 0:1])
        S.op("dve", lambda e: e.reciprocal(Zs.ap, Zs.ap), r=[Zs], w=[Zs])
        ts(GT_t[i].ap[:, h * 16:(h + 1) * 16], gexp.ap, Zs.ap[:, 0:1], None, ALU.mult, None, [gexp, Zs], [GT_h[i][h]])

    R5c = Region(arena, 120, 168)
    NUB = 5
    UVG = [R5c.alloc("UVG%d" % k, [128, 2 * D], BF16) for k in range(NUB)]
    PRD = R5c.alloc("PRD", [128, D], F32)
    UV_flat = UVb_d.rearrange("e t d -> e (t d)")
    R5s = Region(arena, 184, 190)
    Adot = RR([R5s.alloc("Adot%d" % k, [128, 1], F32) for k in range(8)])
    AW = RR([R5s.alloc("AW%d" % k, [128, 1], F32) for k in range(8)])
    DG = RR([R5s.alloc("DG%d" % k, [128, 128], BF16) for k in range(3)])
    stats2 = R5s.alloc("stats2", [128, 4, 6], F32)
    mv2 = R5s.alloc("mv2", [128, 2], F32)
    rstd2 = R5s.alloc("rstd2", [128, 1], F32)
    Gh = R5s.alloc("Gh", [128, 512], F32)
    Bh = R5s.alloc("Bh", [128, 512], F32)
    NSL = int(os.environ.get("NSL", "128"))
    LOOK = NUB - 1
    POOL_EVERY = int(os.environ.get("POOL_EVERY", "0"))
    for h in range(8):
        routing_head(0, h)

    def emit_gather(gidx):
        i, slot = divmod(gidx, NSL)
        uvg = UVG[gidx % NUB]
        S.op("pool", lambda e: e.indirect_dma_start(
            out=uvg.ap, out_offset=None, in_=UV_flat,
            in_offset=bass.IndirectOffsetOnAxis(ap=EI_t[i].ap[:, slot:slot + 1], axis=0)), r=[EI_h[i][slot // 16], D_UV], w=[uvg], dma=True)

    for gidx in range(min(LOOK, NTI * NSL)):
        emit_gather(gidx)
    for i in range(NTI):
        for slot in range(NSL):
            gidx = i * NSL + slot
            if gidx + LOOK < NTI * NSL:
                emit_gather(gidx + LOOK)
            uvg = UVG[gidx % NUB]; ad = Adot.next(); aw = AW.next(); dg = DG.next()
            if POOL_EVERY and (slot % POOL_EVERY == POOL_EVERY - 1):
                tt(PRD.ap, uvg.ap[:, 0:D], H1B.ap[:, i, :], ALU.mult, [uvg, H1B], [PRD], eng="pool")
                act(PRD.ap, PRD.ap, AF.Copy, [PRD], [PRD, ad], accum=ad.ap[:, 0:1])
            else:
                stt(uvg.ap[:, 0:D], uvg.ap[:, 0:D], 1.0, H1B.ap[:, i, :], ALU.mult, ALU.mult, [uvg, H1B], [uvg, ad], accum=ad.ap[:, 0:1])
            act(aw.ap, ad.ap, AF.Gelu, [ad], [aw])
            act(aw.ap, aw.ap, AF.Copy, [aw, GT_h[i][slot // 16]], [aw], scale=GT_t[i].ap[:, slot:slot + 1])
            act(dg.ap, ident.ap, AF.Copy, [ident, aw], [dg], scale=aw.ap[:, 0:1])
            for nf in range(4):
                mm(PF[nf].ap, dg.ap, uvg.ap[:, D + nf * 512:D + (nf + 1) * 512], slot == 0, slot == NSL - 1, [dg, uvg], [PF[nf]])
            if (slot % 14 == 13) and (slot // 14 < 8) and (i + 1 < NTI):
                routing_head(i + 1, slot // 14)
        for nf in range(4):
            Hs = H1.ap[:, i, nf * 512:(nf + 1) * 512]
            tt(Hs, Hs, PF[nf].ap, ALU.add, [H1, PF[nf]], [H1])
        layer_norm_rows(H1.ap[:, i, :], H1, None, None, stats2, mv2, rstd2)
        for qf in range(4):
            S.dma("sp", Gh, Gh.ap, None, ln2g[:, qf * 512:(qf + 1) * 512].partition_broadcast(128))
            S.dma("sp", Bh, Bh.ap, None, ln2b[:, qf * 512:(qf + 1) * 512].partition_broadcast(128))
            Xh = H1.ap[:, i, qf * 512:(qf + 1) * 512]
            tt(Xh, Xh, Gh.ap, ALU.mult, [H1, Gh], [H1])
            tt(Xh, Xh, Bh.ap, ALU.add, [H1, Bh], [H1])
        S.dma("sp", D_out, out_d[i * 128:(i + 1) * 128, :], H1, H1.ap[:, i, :])
    S.finish(final_bufs)
    return nc


def own_rows(j):
    return np.concatenate([np.arange((4 * i + j) * 128, (4 * i + j + 1) * 128) for i in range(8)])


def core_inputs(inp, c):
    b, j = c // 4, c % 4
    own = own_rows(j)
    f = lambda a: np.ascontiguousarray(a, dtype=np.float32)
    x = inp["x"][b]
    negmask = np.zeros((128, 4, 128), np.float32)
    sel = np.zeros((128, 4, 128), np.float32)
    kk = np.arange(128)[:, None]; qq = np.arange(128)[None, :]
    for m in range(4):
        if m == j:
            negmask[:, m, :] = np.where(kk <= qq, 0.0, NEG)
            sel[:, m, :] = np.eye(128, dtype=np.float32)
        elif m > j:
            negmask[:, m, :] = NEG
    inv_freq = (1.0 / (10000.0 ** (np.arange(0, 64, 2, dtype=np.float32) / 64))).astype(np.float32)
    invf = np.concatenate([inv_freq, inv_freq])[:, None]
    sgn = np.concatenate([-np.ones(32, np.float32), np.ones(32, np.float32)])[:, None]
    return {
        "xkv": f(x), "xq": f(x[own]), "pq": f(inp["p"][0, b][own]),
        "pos_kv": np.ascontiguousarray(inp["positions"][b][None, :], dtype=np.int32),
        "pos_q": np.ascontiguousarray(inp["positions"][b][own][None, :], dtype=np.int32),
        "w_in": f(inp["w_in"][0]), "w_uq": f(inp["w_uq"][0]), "w_ukv": f(inp["w_ukv"][0]),
        "w_out": f(inp["w_out"][0]), "peer_wq": f(inp["peer_wq"][0]),
        "keys1": f(inp["peer_keys1"][0]), "keys2": f(inp["peer_keys2"][0]),
        "peer_u": f(inp["peer_u"][0]), "peer_v": f(inp["peer_v"][0]),
        "ple_wgate": f(inp["ple_wgate"][0]), "ple_wproj": f(inp["ple_wproj"][0]),
        "gqT": f(inp["g_q_norm"][0].reshape(4, 128).T), "gkvT": f(inp["g_kv_norm"][0].reshape(4, 128).T),
        "bfg": f(inp["b_forget"][0][:, None]),
        "ln1g": f(inp["ln1_g"]), "ln1b": f(inp["ln1_b"]), "ln2g": f(inp["ln2_g"]), "ln2b": f(inp["ln2_b"]),
        "negmask": negmask, "sel": sel, "invf": f(invf), "sgn": f(sgn),
    }


_NC_CACHE = {}


def kernel(**inputs):
    inp = {k: np.asarray(v) for k, v in inputs.items()}
    if "nc" not in _NC_CACHE:
        _NC_CACHE["nc"] = build_nc()
    nc = _NC_CACHE["nc"]
    maps = [core_inputs(inp, c) for c in range(8)]
    res = run_bass_kernel_spmd(nc, maps, core_ids=list(range(8)))
    out = np.zeros((2, NT, D), np.float32)
    for c in range(8):
        b, j = c // 4, c % 4
        out[b, own_rows(j)] = np.asarray(res.results[c]["out"], dtype=np.float32)
    return out
```
